# Optimizing a Trainium2 kernel written in Bass

```python
import math
import jax
import jax.numpy as jnp
from jax import lax
import numpy as np

D_MODEL = 2048
BATCH = 4
SEQ = 2048
DEPTH = 4

CTX_LEN = 256
GRID_W = 64
N_MIXERS = 4
N_MOD = 9

N_HEADS = 16
N_KV_HEADS = 4
HEAD_DIM = D_MODEL // N_HEADS
GROUP = N_HEADS // N_KV_HEADS
Q_WIDTH = N_HEADS * HEAD_DIM
KV_WIDTH = N_KV_HEADS * HEAD_DIM
QKV_WIDTH = Q_WIDTH + 2 * KV_WIDTH
Q_BLOCK = 128
WINDOW = 128
ROPE_BASE = 10000.0
ROPE_PAIRS_PER_AXIS = HEAD_DIM // 4

S5_GROUP_SIZE = 16
S5_GROUPS = D_MODEL // S5_GROUP_SIZE
S5_STATE = 64
S5_DT_MIN = 1e-3
S5_DT_MAX = 1e-1
S5_MAX_REAL = -1e-4

MLSTM_HEADS = 8
MLSTM_DQK = D_MODEL // (2 * MLSTM_HEADS)
MLSTM_DV = D_MODEL // MLSTM_HEADS
MLSTM_QK_WIDTH = MLSTM_HEADS * MLSTM_DQK
MLSTM_V_WIDTH = MLSTM_HEADS * MLSTM_DV
MLSTM_IN_WIDTH = 2 * MLSTM_QK_WIDTH + 2 * MLSTM_V_WIDTH
MLSTM_CHUNK = 64
GATE_CAP = 15.0

D_FF = 5632
NORM_EPS = 1e-6

kernel_name = 'hybrid_interleaved_dit_backbone'


def rms_norm(x, gain):
    xf = x.astype(jnp.float32)
    y = xf * lax.rsqrt(jnp.mean(xf * xf, axis=-1, keepdims=True) + NORM_EPS)
    return y.astype(x.dtype) * gain


def modulate(x, gain, shift, scale):
    return rms_norm(x, gain) * (1 + scale) + shift


def swiglu(h, w_in, w_out):
    a, g = jnp.split(h @ w_in, 2, axis=-1)
    return (jax.nn.silu(g) * a) @ w_out


def axial_rope_tables(n_tokens):
    n_rows = n_tokens // GRID_W
    grid = jnp.arange(n_rows * GRID_W)
    row = (grid // GRID_W).astype(jnp.float32)
    col = (grid % GRID_W).astype(jnp.float32)
    inv = ROPE_BASE ** (-jnp.arange(ROPE_PAIRS_PER_AXIS, dtype=jnp.float32) / ROPE_PAIRS_PER_AXIS)
    ang = jnp.concatenate([row[:, None] * inv, col[:, None] * inv], axis=-1)
    ang = jnp.concatenate([ang, ang], axis=-1)
    return jnp.cos(ang), jnp.sin(ang)


def apply_rope(x, cos, sin):
    shape = (1, x.shape[1]) + (1,) * (x.ndim - 3) + (HEAD_DIM,)
    cos = cos.reshape(shape)
    sin = sin.reshape(shape)
    xf = x.astype(jnp.float32)
    x1, x2 = jnp.split(xf, 2, axis=-1)
    rot = jnp.concatenate([-x2, x1], axis=-1)
    return (xf * cos + rot * sin).astype(x.dtype)


def attn_project(h, w_qkv, qk_g, rope):
    b_, t_, _ = h.shape
    q, k, v = jnp.split(h @ w_qkv, [Q_WIDTH, Q_WIDTH + KV_WIDTH], axis=-1)
    q = rms_norm(q.reshape(b_, t_, N_KV_HEADS, GROUP, HEAD_DIM), qk_g[0]) * (HEAD_DIM ** -0.5)
    k = rms_norm(k.reshape(b_, t_, N_KV_HEADS, HEAD_DIM), qk_g[1])
    v = v.reshape(b_, t_, N_KV_HEADS, HEAD_DIM)
    if rope is not None:
        q = apply_rope(q, rope[0], rope[1])
        k = apply_rope(k, rope[0], rope[1])
    return q, k, v


def gqa_scores(q, k):
    return jnp.einsum('btkgd,bskd->bkgts', q, k, preferred_element_type=jnp.float32)


def gqa_values(p, v):
    return jnp.einsum('bkgts,bskd->btkgd', p.astype(v.dtype), v)


def full_attention(hc, hl, w_qkv, qk_g, w_o, rope, ctx_out):
    qc, kc, vc = attn_project(hc, w_qkv, qk_g, None)
    ql, kl, vl = attn_project(hl, w_qkv, qk_g, rope)
    b_, s_ = hl.shape[0], hl.shape[1]
    nb = s_ // Q_BLOCK
    k_all = jnp.concatenate([kc, kl], axis=1)
    v_all = jnp.concatenate([vc, vl], axis=1)

    def block(qb):
        p = jax.nn.softmax(gqa_scores(qb, k_all), axis=-1)
        return gqa_values(p, v_all)

    qb = jnp.moveaxis(ql.reshape(b_, nb, Q_BLOCK, N_KV_HEADS, GROUP, HEAD_DIM), 1, 0)
    ol = jnp.moveaxis(lax.map(block, qb), 0, 1).reshape(b_, s_, Q_WIDTH)
    yl = ol @ w_o
    yc = None
    if ctx_out:
        oc = gqa_values(jax.nn.softmax(gqa_scores(qc, kc), axis=-1), vc)
        yc = oc.reshape(b_, -1, Q_WIDTH) @ w_o
    return yc, yl


def window_attention(hc, hl, w_qkv, qk_g, sink, w_o, rope, ctx_out):
    qc, kc, vc = attn_project(hc, w_qkv, qk_g, None)
    ql, kl, vl = attn_project(hl, w_qkv, qk_g, rope)
    b_, s_ = hl.shape[0], hl.shape[1]
    n_ctx = kc.shape[1]
    nb = s_ // Q_BLOCK
    sink_logit = sink.astype(jnp.float32).reshape(N_KV_HEADS, GROUP, 1, 1)

    def softmax_with_sink(logits):
        sk = jnp.broadcast_to(sink_logit, logits.shape[:-1] + (1,))
        p = jax.nn.softmax(jnp.concatenate([logits, sk], axis=-1), axis=-1)
        return p[..., :-1]

    def band(a):
        ap = jnp.pad(a, ((0, 0), (Q_BLOCK, Q_BLOCK), (0, 0), (0, 0)))
        ap = ap.reshape(b_, nb + 2, Q_BLOCK, N_KV_HEADS, HEAD_DIM)
        w = jnp.concatenate([ap[:, :-2], ap[:, 1:-1], ap[:, 2:]], axis=2)
        return jnp.moveaxis(w, 1, 0)

    tq = jnp.arange(nb)[:, None] * Q_BLOCK + jnp.arange(Q_BLOCK)[None, :]
    tk = (jnp.arange(nb)[:, None] - 1) * Q_BLOCK + jnp.arange(3 * Q_BLOCK)[None, :]
    valid = ((jnp.abs(tq[:, :, None] - tk[:, None, :]) <= WINDOW)
             & (tk >= 0)[:, None, :] & (tk < s_)[:, None, :])

    def block(args):
        qb, kb, vb, vmask = args
        s_ctx = gqa_scores(qb, kc)
        s_loc = jnp.where(vmask, gqa_scores(qb, kb), -jnp.inf)
        p = softmax_with_sink(jnp.concatenate([s_ctx, s_loc], axis=-1))
        return gqa_values(p[..., :n_ctx], vc) + gqa_values(p[..., n_ctx:], vb)

    qb = jnp.moveaxis(ql.reshape(b_, nb, Q_BLOCK, N_KV_HEADS, GROUP, HEAD_DIM), 1, 0)
    ol = lax.map(block, (qb, band(kl), band(vl), valid))
    yl = jnp.moveaxis(ol, 0, 1).reshape(b_, s_, Q_WIDTH) @ w_o
    yc = None
    if ctx_out:
        oc = gqa_values(softmax_with_sink(gqa_scores(qc, kc)), vc)
        yc = oc.reshape(b_, -1, Q_WIDTH) @ w_o
    return yc, yl


def _complex_affine_combine(e1, e2):
    a1r, a1i, b1r, b1i = e1
    a2r, a2i, b2r, b2i = e2
    return (a2r * a1r - a2i * a1i, a2r * a1i + a2i * a1r,
            a2r * b1r - a2i * b1i + b2r, a2r * b1i + a2i * b1r + b2i)


def s5_discretize(lam_re, lam_im, log_dt):
    lr = jnp.minimum(lam_re.astype(jnp.float32), S5_MAX_REAL)
    li = lam_im.astype(jnp.float32)
    dt = jnp.exp(log_dt.astype(jnp.float32))[:, None]
    mag = jnp.exp(lr * dt)
    ar, ai = mag * jnp.cos(li * dt), mag * jnp.sin(li * dt)
    den = lr * lr + li * li
    kr = ((ar - 1) * lr + ai * li) / den
    ki = (ai * lr - (ar - 1) * li) / den
    return ar, ai, kr, ki


def s5_drive(u, b_re, b_im, kr, ki):
    ur = jnp.einsum('tbgc,gnc->tbgn', u, b_re.astype(jnp.float32))
    ui = jnp.einsum('tbgc,gnc->tbgn', u, b_im.astype(jnp.float32))
    return kr * ur - ki * ui, kr * ui + ki * ur


def s5_scan(ar, ai, br, bi, x0, reverse):
    if x0 is not None:
        idx = -1 if reverse else 0
        x0r, x0i = x0
        br = br.at[idx].add(ar * x0r - ai * x0i)
        bi = bi.at[idx].add(ar * x0i + ai * x0r)
    t_ = br.shape[0]
    arb = jnp.broadcast_to(ar, (t_, 1) + ar.shape)
    aib = jnp.broadcast_to(ai, (t_, 1) + ai.shape)
    _, _, xr, xi = lax.associative_scan(_complex_affine_combine, (arb, aib, br, bi), reverse=reverse, axis=0)
    return xr, xi


def s5_readout(xr, xi, c_re, c_im):
    y = (jnp.einsum('tbgn,gcn->tbgc', xr, c_re.astype(jnp.float32))
         - jnp.einsum('tbgn,gcn->tbgc', xi, c_im.astype(jnp.float32)))
    y = jnp.moveaxis(y, 0, 1)
    return y.reshape(y.shape[0], y.shape[1], D_MODEL)


def s5_mixer(hc, hl, lam_re, lam_im, log_dt, b_re, b_im, c_re, c_im, d_skip, w_glu, ctx_out):
    def groups(h):
        return jnp.moveaxis(h.astype(jnp.float32).reshape(h.shape[0], h.shape[1], S5_GROUPS, S5_GROUP_SIZE), 1, 0)

    uc, ul = groups(hc), groups(hl)
    dsk = d_skip.astype(jnp.float32)
    yl = dsk * hl.astype(jnp.float32)
    yc = dsk * hc.astype(jnp.float32)
    for dr in range(2):
        rev = dr == 1
        ar, ai, kr, ki = s5_discretize(lam_re[dr], lam_im[dr], log_dt[dr])
        xcr, xci = s5_scan(ar, ai, *s5_drive(uc, b_re[dr], b_im[dr], kr, ki), None, rev)
        end = 0 if rev else -1
        xlr, xli = s5_scan(ar, ai, *s5_drive(ul, b_re[dr], b_im[dr], kr, ki), (xcr[end], xci[end]), rev)
        yl = yl + s5_readout(xlr, xli, c_re[dr], c_im[dr])
        if ctx_out:
            yc = yc + s5_readout(xcr, xci, c_re[dr], c_im[dr])

    def glu(y):
        z = jax.nn.gelu(y).astype(hl.dtype)
        a, g = jnp.split(z @ w_glu, 2, axis=-1)
        return a * jax.nn.sigmoid(g)

    return (glu(yc) if ctx_out else None), glu(yl)


def mlstm_project(h, w_in, w_gate, b_gate):
    b_, t_, _ = h.shape
    q, k, v, o = jnp.split(h @ w_in, [MLSTM_QK_WIDTH, 2 * MLSTM_QK_WIDTH, 2 * MLSTM_QK_WIDTH + MLSTM_V_WIDTH], axis=-1)

    def heads(a, d):
        return jnp.moveaxis(a.astype(jnp.float32).reshape(b_, t_, MLSTM_HEADS, d), 2, 1)

    q = heads(q, MLSTM_DQK)
    k = heads(k, MLSTM_DQK) * (MLSTM_DQK ** -0.5)
    v = heads(v, MLSTM_DV)
    g = (h @ w_gate + b_gate).astype(jnp.float32)
    g = GATE_CAP * jnp.tanh(g / GATE_CAP)
    g = jnp.moveaxis(g.reshape(b_, t_, 2, 2, MLSTM_HEADS), 1, -1)
    return q, k, v, o, g


def mlstm_chunkwise(q, k, v, ig, fg, state, need_out):
    b_, h_, t_ = ig.shape
    nc = t_ // MLSTM_CHUNK

    def chunks(a):
        a = a.reshape((b_, h_, nc, MLSTM_CHUNK) + a.shape[3:])
        return jnp.moveaxis(a, 2, 0)

    lower = jnp.tril(jnp.ones((MLSTM_CHUNK, MLSTM_CHUNK), dtype=bool))

    def step(carry, xs):
        c_prev, n_prev, m_prev = carry
        qc, kc, vc, ic, fc = xs
        b = jnp.cumsum(jax.nn.log_sigmoid(fc), axis=-1)
        b_end = b[..., -1]
        w_log = b_end[..., None] - b + ic
        m_new = jnp.maximum(b_end + m_prev, jnp.max(w_log, axis=-1))
        decay = jnp.exp(b_end + m_prev - m_new)
        w = jnp.exp(w_log - m_new[..., None])
        c_new = decay[..., None, None] * c_prev + jnp.einsum('bhs,bhsv,bhsk->bhvk', w, vc, kc)
        n_new = decay[..., None] * n_prev + jnp.einsum('bhs,bhsk->bhk', w, kc)
        carry_new = (c_new, n_new, m_new)
        if not need_out:
            return carry_new, None
        d_log = jnp.where(lower, b[..., :, None] - b[..., None, :] + ic[..., None, :], -jnp.inf)
        inter_log = b + m_prev[..., None]
        m_t = jnp.maximum(jnp.max(d_log, axis=-1), inter_log)
        a_inter = jnp.exp(inter_log - m_t)
        s = jnp.einsum('bhtk,bhsk->bhts', qc, kc) * jnp.exp(d_log - m_t[..., None])
        num = jnp.einsum('bhts,bhsv->bhtv', s, vc) + a_inter[..., None] * jnp.einsum('bhvk,bhtk->bhtv', c_prev, qc)
        den = jnp.sum(s, axis=-1) + a_inter * jnp.einsum('bhk,bhtk->bht', n_prev, qc)
        h = num / jnp.maximum(jnp.abs(den), jnp.exp(-m_t))[..., None]
        return carry_new, h

    state, hs = lax.scan(step, state, tuple(chunks(a) for a in (q, k, v, ig, fg)))
    if not need_out:
        return state, None
    return state, jnp.moveaxis(hs, 0, 2).reshape(b_, h_, t_, MLSTM_DV)


def mlstm_mixer(hc, hl, w_in, w_gate, b_gate, norm_g, w_out, ctx_out):
    qc, kc, vc, oc, gc = mlstm_project(hc, w_in, w_gate, b_gate)
    ql, kl, vl, ol, gl = mlstm_project(hl, w_in, w_gate, b_gate)
    b_ = hl.shape[0]
    zero = (jnp.zeros((b_, MLSTM_HEADS, MLSTM_DV, MLSTM_DQK), jnp.float32),
            jnp.zeros((b_, MLSTM_HEADS, MLSTM_DQK), jnp.float32),
            jnp.zeros((b_, MLSTM_HEADS), jnp.float32))
    h_lat = 0.0
    h_ctx = 0.0
    for dr in range(2):
        if dr == 1:
            flip = lambda a: jnp.flip(a, axis=2)
        else:
            flip = lambda a: a
        st_c, hc_dir = mlstm_chunkwise(flip(qc), flip(kc), flip(vc), flip(gc[:, dr, 0]), flip(gc[:, dr, 1]), zero, ctx_out)
        _, hl_dir = mlstm_chunkwise(flip(ql), flip(kl), flip(vl), flip(gl[:, dr, 0]), flip(gl[:, dr, 1]), st_c, True)
        h_lat = h_lat + flip(hl_dir)
        if ctx_out:
            h_ctx = h_ctx + flip(hc_dir)

    def readout(h, o):
        h = jnp.moveaxis(h, 1, 2)
        h = rms_norm(h, norm_g.reshape(MLSTM_HEADS, MLSTM_DV))
        h = h.reshape(h.shape[0], h.shape[1], MLSTM_V_WIDTH).astype(o.dtype) * jax.nn.sigmoid(o)
        return h @ w_out

    return (readout(h_ctx, oc) if ctx_out else None), readout(h_lat, ol)


def setup_inputs(seed: int = 0) -> dict:
    key = jax.random.key(seed)
    ks = iter(jax.random.split(key, 32))
    f32 = jnp.float32

    def nrm(shape, std):
        return jax.random.normal(next(ks), shape, f32) * std

    n_a, n_b, n_c, n_d = [len(range(m, DEPTH, N_MIXERS)) for m in range(N_MIXERS)]
    d = D_MODEL
    x = nrm((BATCH, SEQ, d), 1.0)
    c = nrm((BATCH, d), 1.0)
    ctx = nrm((BATCH, CTX_LEN, d), 1.0)
    c_ctx = nrm((d,), 1.0)
    mod_w = nrm((DEPTH, d, N_MOD * d), 0.5 * d ** -0.5)
    mod_b = nrm((DEPTH, N_MOD * d), 0.02)
    norm_g = 1.0 + nrm((DEPTH, 3, d), 0.02)
    ffn_wi = nrm((DEPTH, 2, d, 2 * D_FF), d ** -0.5)
    ffn_wo = nrm((DEPTH, 2, D_FF, d), D_FF ** -0.5)
    a_wqkv = nrm((n_a, d, QKV_WIDTH), d ** -0.5)
    a_qk_g = 1.0 + nrm((n_a, 2, HEAD_DIM), 0.02)
    a_wo = nrm((n_a, Q_WIDTH, d), Q_WIDTH ** -0.5)
    s5_shape = (n_b, 2, S5_GROUPS, S5_STATE)
    s5_lam_re = -0.5 + nrm(s5_shape, 0.01)
    s5_lam_im = jnp.pi * jnp.arange(S5_STATE, dtype=f32) + nrm(s5_shape, 0.01)
    s5_log_dt = jax.random.uniform(next(ks), (n_b, 2, S5_GROUPS), f32, math.log(S5_DT_MIN), math.log(S5_DT_MAX))
    s5_b_re = nrm((n_b, 2, S5_GROUPS, S5_STATE, S5_GROUP_SIZE), (0.5 / S5_GROUP_SIZE) ** 0.5)
    s5_b_im = nrm((n_b, 2, S5_GROUPS, S5_STATE, S5_GROUP_SIZE), (0.5 / S5_GROUP_SIZE) ** 0.5)
    s5_c_re = nrm((n_b, 2, S5_GROUPS, S5_GROUP_SIZE, S5_STATE), (0.5 / S5_STATE) ** 0.5)
    s5_c_im = nrm((n_b, 2, S5_GROUPS, S5_GROUP_SIZE, S5_STATE), (0.5 / S5_STATE) ** 0.5)
    s5_d = nrm((n_b, d), 1.0)
    s5_w_glu = nrm((n_b, d, 2 * d), d ** -0.5)
    m_w_in = nrm((n_c, d, MLSTM_IN_WIDTH), d ** -0.5)
    m_w_gate = nrm((n_c, d, 4 * MLSTM_HEADS), 0.1 * d ** -0.5)
    i_bias = nrm((n_c, 2, 1, MLSTM_HEADS), 0.1)
    f_bias = jnp.linspace(3.0, 6.0, MLSTM_HEADS, dtype=f32) + nrm((n_c, 2, 1, MLSTM_HEADS), 0.1)
    m_b_gate = jnp.concatenate([i_bias, f_bias], axis=2).reshape(n_c, 4 * MLSTM_HEADS)
    m_norm_g = 1.0 + nrm((n_c, MLSTM_V_WIDTH), 0.02)
    m_w_out = nrm((n_c, MLSTM_V_WIDTH, d), MLSTM_V_WIDTH ** -0.5)
    w_wqkv = nrm((n_d, d, QKV_WIDTH), d ** -0.5)
    w_qk_g = 1.0 + nrm((n_d, 2, HEAD_DIM), 0.02)
    w_sink = nrm((n_d, N_HEADS), 0.5)
    w_wo = nrm((n_d, Q_WIDTH, d), Q_WIDTH ** -0.5)
    return {'x': x, 'c': c, 'ctx': ctx, 'c_ctx': c_ctx,
            'mod_w': mod_w, 'mod_b': mod_b, 'norm_g': norm_g, 'ffn_wi': ffn_wi, 'ffn_wo': ffn_wo,
            'a_wqkv': a_wqkv, 'a_qk_g': a_qk_g, 'a_wo': a_wo,
            's5_lam_re': s5_lam_re, 's5_lam_im': s5_lam_im, 's5_log_dt': s5_log_dt,
            's5_b_re': s5_b_re, 's5_b_im': s5_b_im, 's5_c_re': s5_c_re, 's5_c_im': s5_c_im,
            's5_d': s5_d, 's5_w_glu': s5_w_glu,
            'm_w_in': m_w_in, 'm_w_gate': m_w_gate, 'm_b_gate': m_b_gate, 'm_norm_g': m_norm_g, 'm_w_out': m_w_out,
            'w_wqkv': w_wqkv, 'w_qk_g': w_qk_g, 'w_sink': w_sink, 'w_wo': w_wo}


def reference(x, c, ctx, c_ctx, mod_w, mod_b, norm_g, ffn_wi, ffn_wo,
              a_wqkv, a_qk_g, a_wo,
              s5_lam_re, s5_lam_im, s5_log_dt, s5_b_re, s5_b_im, s5_c_re, s5_c_im, s5_d, s5_w_glu,
              m_w_in, m_w_gate, m_b_gate, m_norm_g, m_w_out,
              w_wqkv, w_qk_g, w_sink, w_wo):
    b_, s_ = x.shape[0], x.shape[1]
    rope = axial_rope_tables(s_)
    xl, xc = x, ctx
    cond_l = jax.nn.silu(c)
    cond_c = jax.nn.silu(c_ctx)
    for i in range(DEPTH):
        kind, occ, last = i % N_MIXERS, i // N_MIXERS, i == DEPTH - 1
        ml = (cond_l @ mod_w[i] + mod_b[i]).reshape(b_, 1, N_MOD, D_MODEL)
        mc = (cond_c @ mod_w[i] + mod_b[i]).reshape(N_MOD, D_MODEL)
        xl = xl + 0.5 * ml[..., 2, :] * swiglu(modulate(xl, norm_g[i, 0], ml[..., 0, :], ml[..., 1, :]), ffn_wi[i, 0], ffn_wo[i, 0])
        xc = xc + 0.5 * mc[..., 2, :] * swiglu(modulate(xc, norm_g[i, 0], mc[..., 0, :], mc[..., 1, :]), ffn_wi[i, 0], ffn_wo[i, 0])
        hl = modulate(xl, norm_g[i, 1], ml[..., 3, :], ml[..., 4, :])
        hc = modulate(xc, norm_g[i, 1], mc[..., 3, :], mc[..., 4, :])
        if kind == 0:
            yc, yl = full_attention(hc, hl, a_wqkv[occ], a_qk_g[occ], a_wo[occ], rope, not last)
        elif kind == 1:
            yc, yl = s5_mixer(hc, hl, s5_lam_re[occ], s5_lam_im[occ], s5_log_dt[occ], s5_b_re[occ], s5_b_im[occ],
                              s5_c_re[occ], s5_c_im[occ], s5_d[occ], s5_w_glu[occ], not last)
        elif kind == 2:
            yc, yl = mlstm_mixer(hc, hl, m_w_in[occ], m_w_gate[occ], m_b_gate[occ], m_norm_g[occ], m_w_out[occ], not last)
        else:
            yc, yl = window_attention(hc, hl, w_wqkv[occ], w_qk_g[occ], w_sink[occ], w_wo[occ], rope, not last)
        xl = xl + ml[..., 5, :] * yl
        xl = xl + 0.5 * ml[..., 8, :] * swiglu(modulate(xl, norm_g[i, 2], ml[..., 6, :], ml[..., 7, :]), ffn_wi[i, 1], ffn_wo[i, 1])
        if not last:
            xc = xc + mc[..., 5, :] * yc
            xc = xc + 0.5 * mc[..., 8, :] * swiglu(modulate(xc, norm_g[i, 2], mc[..., 6, :], mc[..., 7, :]), ffn_wi[i, 1], ffn_wo[i, 1])
    return xl
```

```python
import numpy as np
from contextlib import ExitStack
import concourse.bass as bass
import concourse.mybir as mybir
from concourse.bass_utils import run_bass_kernel_spmd

F32 = mybir.dt.float32
BF16 = mybir.dt.bfloat16
AF = mybir.ActivationFunctionType
ALU = mybir.AluOpType
AX = mybir.AxisListType

D = 2048
NCH = 16
DFF = 5632
NFC = 44
NCTX = 256
NLAT = 1024
TOK = NCTX + NLAT
SEQ = 2048
EPS = 1e-6
TBS = [(0, 256, 1), (256, 512, 0), (768, 512, 0)]
NCORES = 8


class DT:
    __slots__ = ("name", "w", "r", "dsem", "dcnt")

    def __init__(self, name=""):
        self.name = name
        self.w = {}
        self.r = {}
        self.dsem = None
        self.dcnt = 0


class KB:
    def __init__(self, nc, es):
        self.nc = nc
        self.es = es
        self.engs = {"pe": nc.tensor, "act": nc.scalar, "dve": nc.vector, "pool": nc.gpsimd, "sp": nc.sync}
        self.sem = {n: es.enter_context(nc.semaphore("s_" + n)) for n in ("pe", "act", "dve", "pool")}
        self.cnt = {n: 0 for n in self.sem}
        self.waited = {n: {} for n in self.engs}
        self.bound = []
        self.all_sems = []
        self.free_sems = []
        self.scount = {}
        self.final = []
        self.nsem = 0
        self.ninstr = 0
        self.uid = 0

    def _wait(self, e, deps):
        need = {}
        for d in deps:
            for s, v in d.items():
                if need.get(s, 0) < v:
                    need[s] = v
        w = self.waited[e]
        for s, v in need.items():
            if e == "pe" and s is self.sem["pe"]:
                continue
            if w.get(s, 0) >= v:
                continue
            self.engs[e].wait_ge(s, v)
            w[s] = v

    def op(self, e, fn, reads=(), writes=(), inc=True):
        deps = [t.w for t in reads]
        for t in writes:
            deps.append(t.w)
            deps.append(t.r)
        self._wait(e, deps)
        ins = fn(self.engs[e])
        self.ninstr += 1
        s = self.sem[e]
        if inc:
            self.cnt[e] += 1
            ins.then_inc(s, 1)
            v = self.cnt[e]
        else:
            v = self.cnt[e] + 1
        for t in reads:
            t.r[s] = v
        for t in writes:
            t.w[s] = v
            t.r = {}
        return ins

    def dma(self, q, out, in_, reads=(), writes=(), final=False):
        deps = [t.w for t in reads]
        for t in writes:
            deps.append(t.w)
            deps.append(t.r)
        self._wait(q, deps)
        t0 = (list(writes) + list(reads))[0]
        if t0.dsem is None:
            if self.free_sems:
                t0.dsem = self.free_sems.pop()
            else:
                t0.dsem = self.es.enter_context(self.nc.semaphore("d%d" % self.nsem))
                self.nsem += 1
                self.all_sems.append(t0.dsem)
                self.scount[t0.dsem] = 0
            self.bound.append(t0)
        self.scount[t0.dsem] += 16
        cnt = self.scount[t0.dsem]
        ins = self.engs[q].dma_start(out=out, in_=in_).then_inc(t0.dsem, 16)
        self.ninstr += 1
        for t in reads:
            t.r[t0.dsem] = cnt
        for t in writes:
            t.w[t0.dsem] = cnt
            t.r = {}
        if final:
            self.final.append({t0.dsem: cnt})
        return ins

    def barrier(self):
        allt = {self.sem[n]: self.cnt[n] for n in self.sem if self.cnt[n] > 0}
        for sm in self.all_sems:
            if self.scount[sm] > 0:
                allt[sm] = self.scount[sm]
        for e in self.engs:
            self._wait(e, [allt])
        for t in self.bound:
            t.dsem = None
        self.bound = []
        self.free_sems = list(self.all_sems)

    def finish(self):
        self._wait("sp", self.final)


class Stage:
    def __init__(self, k):
        self.k = k
        self.es = ExitStack()

    def __enter__(self):
        self.es.__enter__()
        return self

    def __exit__(self, *a):
        self.k.barrier()
        return self.es.__exit__(*a)

    def sb(self, name, shape, dt):
        self.k.uid += 1
        return self.es.enter_context(self.k.nc.sbuf_tensor("sb%d_%s" % (self.k.uid, name), list(shape), dt))

    def ps(self, name, shape, dt=F32):
        self.k.uid += 1
        return self.es.enter_context(self.k.nc.psum_tensor("ps%d_%s" % (self.k.uid, name), list(shape), dt))


class Ctx:
    pass


def setup_consts(k, st, C):
    nc = k.nc
    C.ones_f = st.sb("ones_f", [128, 128], F32)
    C.ones_b = st.sb("ones_b", [128, 128], BF16)
    C.t_const = DT("const")
    k.op("pool", lambda e: e.memset(C.ones_f[:], 1.0), writes=[C.t_const])
    k.op("pool", lambda e: e.memset(C.ones_b[:], 1.0), writes=[C.t_const])
    C.eps_col = st.sb("eps_col", [128, 2], F32)
    k.op("pool", lambda e: e.memset(C.eps_col[:], EPS), writes=[C.t_const])
    C.one_col = st.sb("one_col", [128, 2], F32)
    k.op("pool", lambda e: e.memset(C.one_col[:], 1.0), writes=[C.t_const])


def emit_modvec(k, C, st, condT_dram, modw_dram, modb_dram, ng_dram, i2_dram):
    C.t_mv = DT("mv")
    cond = st.sb("cond", [128, 2, NCH], F32)
    condb = st.sb("condb", [128, NCH, 2], BF16)
    t_cond = DT("cond")
    k.dma("sp", cond[:], condT_dram, writes=[t_cond])
    t_ng = DT("ng")
    k.dma("sp", C.NG[:], ng_dram, writes=[C.t_mv])
    k.dma("sp", C.MB[:], modb_dram, writes=[C.t_mv])
    t_condb = DT("condb")
    for v in range(2):
        k.op("act", lambda e: e.activation(out=condb[:, :, v], in_=cond[:, v, :], func=AF.Silu),
             reads=[t_cond], writes=[t_condb])
    NW = 3
    CW = 512
    wt = [st.sb("mw%d" % i, [128, NCH, CW], BF16) for i in range(NW)]
    wtt = [DT("mw%d" % i) for i in range(NW)]
    rows = st.sb("mvrows", [2, 9 * D], F32)
    t_rows = DT("mvrows")
    i2 = st.sb("mvi2", [2, 2], F32)
    k.dma("sp", i2[:], i2_dram, writes=[t_rows])
    pr = [st.ps("mvpr%d" % i, [128, 512], F32) for i in range(2)]
    prt = [DT("mvpr%d" % i) for i in range(2)]
    ps = st.ps("mvps", [128, 512], F32)
    pst = DT("mvps")
    modw_v = modw_dram.rearrange("(c p) n -> p c n", p=128)
    ntile = (9 * D) // CW
    for ti in range(ntile):
        b = ti % NW
        u = ti % 2
        k.dma("pool", wt[b][:], modw_v[:, :, ti * CW:(ti + 1) * CW], writes=[wtt[b]])
        for kc in range(NCH):
            k.op("pe", lambda e: e.matmul(pr[u][0:2, :], lhsT=condb[:, kc, :], rhs=wt[b][:, kc, :], start=(kc == 0),
                                          stop=(kc == NCH - 1)),
                 reads=[wtt[b], t_condb], writes=[prt[u]], inc=(kc == NCH - 1))
        k.op("act", lambda e: e.copy(out=rows[:, ti * CW:(ti + 1) * CW], in_=pr[u][0:2, :]), reads=[prt[u]], writes=[t_rows])
    for cc in range(9 * NCH):
        k.op("pe", lambda e: e.matmul(ps[:, cc * 2:cc * 2 + 2], lhsT=rows[:, cc * 128:(cc + 1) * 128], rhs=i2[:],
                                      start=True, stop=True), reads=[t_rows], writes=[pst], inc=(cc == 9 * NCH - 1))
    psv = ps[:, 0:288].rearrange("p (mc v) -> p v mc", v=2)
    for v in range(2):
        k.op("dve", lambda e: e.tensor_tensor(out=C.MV[:, v, :], in0=psv[:, v, :], in1=C.MB[:], op=ALU.add),
             reads=[pst, C.t_mv], writes=[C.t_mv])
    for v in range(2):
        for j in range(3):
            sc = C.MV[:, v, (3 * j + 1) * NCH:(3 * j + 2) * NCH]
            k.op("dve", lambda e: e.tensor_scalar(out=C.A[:, v, j, :], in0=sc, scalar1=1.0, scalar2=1.0,
                                                  op0=ALU.add, op1=ALU.mult), reads=[C.t_mv], writes=[C.t_mv])
            k.op("dve", lambda e: e.tensor_tensor(out=C.A[:, v, j, :], in0=C.A[:, v, j, :], in1=C.NG[:, j, :],
                                                  op=ALU.mult), reads=[C.t_mv], writes=[C.t_mv])
            gt = C.MV[:, v, (3 * j + 2) * NCH:(3 * j + 3) * NCH]
            k.op("dve", lambda e: e.tensor_scalar(out=C.G[:, v, j, :], in0=gt, scalar1=(1.0 if j == 1 else 0.5),
                                                  scalar2=None, op0=ALU.mult), reads=[C.t_mv], writes=[C.t_mv])


def emit_norm_mod(k, C, st, j, H, HT, pss, psst, tbs=None):
    X = C.X
    sq = [st.sb("sq%d_%d" % (j, i), [128, 512], F32) for i in range(2)]
    sqt = [DT("sq") for _ in range(2)]
    tmp = [st.sb("nt%d_%d" % (j, i), [128, 512], F32) for i in range(2)]
    tmpt = [DT("nt") for _ in range(2)]
    rstd = st.sb("rstd%d" % j, [128, TOK], F32)
    n = 0
    for bi, (t0, tl, v) in enumerate(TBS):
        if tbs is not None and bi not in tbs:
            continue
        ts = slice(t0, t0 + tl)
        pb = bi % 2
        for c in range(NCH):
            b = n % 2
            n += 1
            k.op("act", lambda e: e.activation(out=sq[b][:, :tl], in_=X[:, c, ts], func=AF.Square),
                 reads=[C.XT[bi]], writes=[sqt[b]])
            k.op("pe", lambda e: e.matmul(pss[pb][:, :tl], lhsT=C.ones_f[:], rhs=sq[b][:, :tl], start=(c == 0),
                                          stop=(c == NCH - 1)),
                 reads=[sqt[b], C.t_const], writes=[psst[pb]], inc=True)
        t_r = DT("rstd")
        k.op("act", lambda e: e.activation(out=rstd[:, ts], in_=pss[pb][:, :tl], func=AF.Sqrt,
                                           bias=C.eps_col[:, 0:1], scale=1.0 / D),
             reads=[psst[pb], C.t_const], writes=[t_r])
        k.op("dve", lambda e: e.reciprocal(out=rstd[:, ts], in_=rstd[:, ts]), reads=[t_r], writes=[t_r])
        for c in range(NCH):
            b = n % 2
            n += 1
            k.op("dve", lambda e: e.scalar_tensor_tensor(out=tmp[b][:, :tl], in0=X[:, c, ts],
                                                         scalar=C.A[:, v, j, c:c + 1], in1=rstd[:, ts],
                                                         op0=ALU.mult, op1=ALU.mult),
                 reads=[C.XT[bi], t_r, C.t_mv], writes=[tmpt[b]])
            k.op("act", lambda e: e.activation(out=H[:, c, ts], in_=tmp[b][:, :tl], func=AF.Identity,
                                               bias=C.MV[:, v, 3 * j * NCH + c:3 * j * NCH + c + 1], scale=1.0),
                 reads=[tmpt[b], C.t_mv], writes=[HT[bi]])


def emit_rstd(k, C, st, rstd, t_rs, pss, psst):
    X = C.X
    sq = [st.sb("rsq%d" % i, [128, 512], F32) for i in range(2)]
    sqt = [DT("rsq") for _ in range(2)]
    n = 0
    for bi, (t0, tl, v) in enumerate(TBS):
        ts = slice(t0, t0 + tl)
        pb = bi % 2
        for c in range(NCH):
            b = n % 2
            n += 1
            k.op("act", lambda e: e.activation(out=sq[b][:, :tl], in_=X[:, c, ts], func=AF.Square),
                 reads=[C.XT[bi]], writes=[sqt[b]])
            k.op("pe", lambda e: e.matmul(pss[pb][:, :tl], lhsT=C.ones_f[:], rhs=sq[b][:, :tl], start=(c == 0),
                                          stop=(c == NCH - 1)),
                 reads=[sqt[b], C.t_const], writes=[psst[pb]], inc=True)
        k.op("act", lambda e: e.activation(out=rstd[:, ts], in_=pss[pb][:, :tl], func=AF.Sqrt,
                                           bias=C.eps_col[:, 0:1], scale=1.0 / D),
             reads=[psst[pb], C.t_const], writes=[t_rs[bi]])
        k.op("dve", lambda e: e.reciprocal(out=rstd[:, ts], in_=rstd[:, ts]), reads=[t_rs[bi]], writes=[t_rs[bi]])


def emit_hchunk(k, C, j, c, rstd, t_rs, Hc, HcT, tmp, tmpt):
    X = C.X
    for bi, (t0, tl, v) in enumerate(TBS):
        ts = slice(t0, t0 + tl)
        b = bi % 2
        k.op("dve", lambda e: e.scalar_tensor_tensor(out=tmp[b][:, :tl], in0=X[:, c, ts],
                                                     scalar=C.A[:, v, j, c:c + 1], in1=rstd[:, ts],
                                                     op0=ALU.mult, op1=ALU.mult),
             reads=[C.XT[bi], t_rs[bi], C.t_mv], writes=[tmpt[b]])
        k.op("act", lambda e: e.activation(out=Hc[:, ts], in_=tmp[b][:, :tl], func=AF.Identity,
                                           bias=C.MV[:, v, 3 * j * NCH + c:3 * j * NCH + c + 1], scale=1.0),
             reads=[tmpt[b], C.t_mv], writes=[HcT])


def emit_ffn(k, C, j, wi_dram, wo_dram, lat_only=False):
    X = C.X
    NG_ = 4
    GC = NFC // NG_
    with Stage(k) as st:
        H = st.sb("ffn_h", [128, NCH, TOK], BF16)
        HT = [DT("h%d" % i) for i in range(3)]
        ps = [st.ps("fps%d" % i, [128, 512], F32) for i in range(8)]
        pst = [DT("fps%d" % i) for i in range(8)]
        tbs = [1, 2] if lat_only else [0, 1, 2]
        with Stage(k) as stn:
            emit_norm_mod(k, C, stn, j, H, HT, ps[6:8], pst[6:8], tbs=tbs)
        act = st.sb("ffn_act", [128, GC, TOK], BF16)
        actT = [DT("act%d" % i) for i in range(3)]
        NWI = 3
        wi = [st.sb("wi%d" % i, [128, NCH, 256], BF16) for i in range(NWI)]
        wit = [DT("wi%d" % i) for i in range(NWI)]
        NWO = 3
        WOC = 256
        wo = [st.sb("wo%d" % i, [128, GC, WOC], BF16) for i in range(NWO)]
        wot = [DT("wo%d" % i) for i in range(NWO)]
        sg = [st.sb("sg%d" % i, [128, 512], F32) for i in range(2)]
        sgt = [DT("sg") for _ in range(2)]
        wi_v = wi_dram.rearrange("(c p) n -> p c n", p=128)
        wo_v = wo_dram.rearrange("(f p) n -> p f n", p=128)
        nwi = 0
        nwo = 0
        nu = 0
        ny = 0
        for g in range(NG_):
            for fl in range(GC):
                f = g * GC + fl
                b = nwi % NWI
                nwi += 1
                k.dma("pool", wi[b][:, :, 0:128], wi_v[:, :, f * 128:(f + 1) * 128], writes=[wit[b]])
                k.dma("pool", wi[b][:, :, 128:256], wi_v[:, :, DFF + f * 128:DFF + (f + 1) * 128], writes=[wit[b]])
                for bi, (t0, tl, v) in enumerate(TBS):
                    if bi not in tbs:
                        continue
                    ts = slice(t0, t0 + tl)
                    pa = (nu % 3) * 2
                    pg = pa + 1
                    sb_ = nu % 2
                    nu += 1
                    for c in range(NCH):
                        k.op("pe", lambda e: e.matmul(ps[pa][:, :tl], lhsT=wi[b][:, c, 0:128], rhs=H[:, c, ts],
                                                      start=(c == 0), stop=(c == NCH - 1)),
                             reads=[wit[b], HT[bi]], writes=[pst[pa]], inc=(c == NCH - 1))
                    for c in range(NCH):
                        k.op("pe", lambda e: e.matmul(ps[pg][:, :tl], lhsT=wi[b][:, c, 128:256], rhs=H[:, c, ts],
                                                      start=(c == 0), stop=(c == NCH - 1)),
                             reads=[wit[b], HT[bi]], writes=[pst[pg]], inc=(c == NCH - 1))
                    k.op("act", lambda e: e.activation(out=sg[sb_][:, :tl], in_=ps[pg][:, :tl], func=AF.Silu),
                         reads=[pst[pg]], writes=[sgt[sb_]])
                    k.op("dve", lambda e: e.tensor_tensor(out=act[:, fl, ts], in0=ps[pa][:, :tl], in1=sg[sb_][:, :tl],
                                                          op=ALU.mult),
                         reads=[pst[pa], sgt[sb_]], writes=[actT[bi]])
            for dq in range(D // WOC):
                b = nwo % NWO
                nwo += 1
                k.dma("pool", wo[b][:], wo_v[:, g * GC:(g + 1) * GC, dq * WOC:(dq + 1) * WOC], writes=[wot[b]])
                for dl in range(WOC // 128):
                    dc = dq * (WOC // 128) + dl
                    for bi, (t0, tl, v) in enumerate(TBS):
                        if bi not in tbs:
                            continue
                        ts = slice(t0, t0 + tl)
                        p = ny % 6
                        ny += 1
                        for fl in range(GC):
                            k.op("pe", lambda e: e.matmul(ps[p][:, :tl], lhsT=wo[b][:, fl, dl * 128:(dl + 1) * 128],
                                                          rhs=act[:, fl, ts], start=(fl == 0), stop=(fl == GC - 1)),
                                 reads=[wot[b], actT[bi]], writes=[pst[p]], inc=(fl == GC - 1))
                        k.op("dve", lambda e: e.scalar_tensor_tensor(out=X[:, dc, ts], in0=ps[p][:, :tl],
                                                                     scalar=C.G[:, v, j, dc:dc + 1], in1=X[:, dc, ts],
                                                                     op0=ALU.mult, op1=ALU.add),
                             reads=[pst[p], C.t_mv, C.XT[bi]], writes=[C.XT[bi]])


def alloc_persistent(k, st, C):
    C.X = st.sb("X", [128, NCH, TOK], F32)
    C.XT = [DT("x%d" % i) for i in range(3)]
    C.MV = st.sb("MV", [128, 2, 9 * NCH], F32)
    C.MB = st.sb("MB", [128, 9 * NCH], F32)
    C.NG = st.sb("NG", [128, 3, NCH], F32)
    C.A = st.sb("A", [128, 2, 3, NCH], F32)
    C.G = st.sb("G", [128, 2, 3, NCH], F32)
    setup_consts(k, st, C)


def load_x(k, C, xT_dram):
    xv = xT_dram.rearrange("(c p) t -> p c t", p=128)
    for bi, (t0, tl, v) in enumerate(TBS):
        k.dma("sp", C.X[:, :, t0:t0 + tl], xv[:, :, t0:t0 + tl], writes=[C.XT[bi]])


def store_x(k, C, xo_dram, lat_only=False):
    xv = xo_dram.rearrange("(c p) t -> p c t", p=128)
    for bi, (t0, tl, v) in enumerate(TBS):
        if lat_only and bi == 0:
            continue
        o0 = t0 - (NCTX if lat_only else 0)
        k.dma("sp", xv[:, :, o0:o0 + tl], C.X[:, :, t0:t0 + tl], reads=[C.XT[bi]], final=True)


def build_test_ffn():
    nc = bass.Bass("TRN2", target_bir_lowering=False)
    xT = nc.dram_tensor("xT", [D, TOK], F32, kind="ExternalInput").ap()
    condT = nc.dram_tensor("condT", [128, 2, NCH], F32, kind="ExternalInput").ap()
    modw = nc.dram_tensor("modw", [D, 9 * D], F32, kind="ExternalInput").ap()
    modb = nc.dram_tensor("modb", [128, 9 * NCH], F32, kind="ExternalInput").ap()
    ng = nc.dram_tensor("ng", [128, 3, NCH], F32, kind="ExternalInput").ap()
    wi = nc.dram_tensor("wi", [D, 2 * DFF], F32, kind="ExternalInput").ap()
    wo = nc.dram_tensor("wo", [DFF, D], F32, kind="ExternalInput").ap()
    xo = nc.dram_tensor("xo", [D, TOK], F32, kind="ExternalOutput").ap()
    mvo = nc.dram_tensor("mvo", [128, 2 * 9 * NCH], F32, kind="ExternalOutput").ap()
    with ExitStack() as es:
        k = KB(nc, es)
        C = Ctx()
        with Stage(k) as st0:
            alloc_persistent(k, st0, C)
            load_x(k, C, xT)
            with Stage(k) as st:
                emit_modvec(k, C, st, condT, modw, modb, ng)
            k.dma("sp", mvo, C.MV[:].rearrange("p v m -> p (v m)"), reads=[C.t_mv], final=True)
            emit_ffn(k, C, 0, wi, wo)
            store_x(k, C, xo)
            k.finish()
        print("instructions:", k.ninstr, "dma sems:", k.nsem)
    return nc


def vec_pm(v):
    v = np.asarray(v)
    lead = v.shape[:-1]
    n = v.shape[-1] // 128
    a = v.reshape(lead + (n, 128))
    return np.ascontiguousarray(np.moveaxis(a, -1, 0))


def core_tokens_T(x, ctx, core):
    b, h = core // 2, core % 2
    cx, xl = ctx[b], x[b, h * NLAT:(h + 1) * NLAT]
    if h == 1:
        cx, xl = cx[::-1], xl[::-1]
    t = np.concatenate([cx, xl], axis=0)
    return np.ascontiguousarray(t.T)


NH = 16
NKV = 4
HD = 128
QKV_W = 3072
NKEY = NCTX + SEQ
NKB = NKEY // 128


def emit_qkv(k, C, wqkv_dram, qkg_dram, cos_dram, sin_dram, rm_dram, qT_d, kT_d, v_d):
    with Stage(k) as st:
        H = st.sb("qkv_h", [128, NCH, TOK], BF16)
        HT = [DT("h%d" % i) for i in range(3)]
        ps = [st.ps("qps%d" % i, [128, 512], F32) for i in range(8)]
        pst = [DT("qps%d" % i) for i in range(8)]
        with Stage(k) as stn:
            emit_norm_mod(k, C, stn, 1, H, HT, ps[6:8], pst[6:8])
        cs = st.sb("cs", [128, 2, NLAT], F32)
        rm = st.sb("rm", [128, 128], F32)
        g2 = st.sb("g2", [128, 2], F32)
        t_c = DT("qkvconst")
        k.dma("sp", cs[:, 0, :], cos_dram, writes=[t_c])
        k.dma("sp", cs[:, 1, :], sin_dram, writes=[t_c])
        k.dma("sp", rm[:], rm_dram, writes=[t_c])
        k.dma("sp", g2[:], qkg_dram, writes=[t_c])
        k.op("dve", lambda e: e.tensor_scalar(out=g2[:, 0:1], in0=g2[:, 0:1], scalar1=float(HD ** -0.5), scalar2=None,
                                              op0=ALU.mult), reads=[t_c], writes=[t_c])
        NW = 3
        wt = [st.sb("qw%d" % i, [128, NCH, 256], BF16) for i in range(NW)]
        wtt = [DT("qw%d" % i) for i in range(NW)]
        wv = st.sb("qwv", [128, NCH, 512], BF16)
        wvt = DT("qwv")
        w_v = wqkv_dram.rearrange("(c p) n -> p c n", p=128)
        k.dma("pool", wv[:], w_v[:, :, 2560:3072], writes=[wvt])
        sq = [st.sb("qsq%d" % i, [128, 512], F32) for i in range(2)]
        sqt = [DT("qsq") for _ in range(2)]
        rs = [st.sb("qrs%d" % i, [128, 512], F32) for i in range(2)]
        rst = [DT("qrs") for _ in range(2)]
        qn = [st.sb("qqn%d" % i, [128, 512], F32) for i in range(2)]
        qnt = [DT("qqn") for _ in range(2)]
        t1 = [st.sb("qt1%d" % i, [128, 512], F32) for i in range(2)]
        t1t = [DT("qt1") for _ in range(2)]
        ob = [st.sb("qob%d" % i, [128, 512], BF16) for i in range(3)]
        obt = [DT("qob%d" % i) for i in range(3)]
        nu = 0
        for ti in range(10):
            b = ti % NW
            k.dma("pool", wt[b][:], w_v[:, :, ti * 256:(ti + 1) * 256], writes=[wtt[b]])
            for s in range(2):
                fc = ti * 2 + s
                isq = fc < NH
                gcol = g2[:, 0:1] if isq else g2[:, 1:2]
                for bi, (t0, tl, v) in enumerate(TBS):
                    ts = slice(t0, t0 + tl)
                    u = nu % 2
                    o3 = nu % 3
                    nu += 1
                    pq, pss, pr = ps[u], ps[2 + u], ps[4 + u]
                    pqt, psst, prt = pst[u], pst[2 + u], pst[4 + u]
                    for c in range(NCH):
                        k.op("pe", lambda e: e.matmul(pq[:, :tl], lhsT=wt[b][:, c, s * 128:(s + 1) * 128], rhs=H[:, c, ts],
                                                      start=(c == 0), stop=(c == NCH - 1)),
                             reads=[wtt[b], HT[bi]], writes=[pqt], inc=(c == NCH - 1))
                    k.op("act", lambda e: e.activation(out=sq[u][:, :tl], in_=pq[:, :tl], func=AF.Square),
                         reads=[pqt], writes=[sqt[u]])
                    k.op("pe", lambda e: e.matmul(pss[:, :tl], lhsT=C.ones_f[:], rhs=sq[u][:, :tl], start=True, stop=True),
                         reads=[sqt[u], C.t_const], writes=[psst])
                    k.op("act", lambda e: e.activation(out=rs[u][:, :tl], in_=pss[:, :tl], func=AF.Sqrt,
                                                       bias=C.eps_col[:, 0:1], scale=1.0 / HD),
                         reads=[psst, C.t_const], writes=[rst[u]])
                    k.op("dve", lambda e: e.reciprocal(out=rs[u][:, :tl], in_=rs[u][:, :tl]), reads=[rst[u]], writes=[rst[u]])
                    k.op("dve", lambda e: e.scalar_tensor_tensor(out=qn[u][:, :tl], in0=pq[:, :tl], scalar=gcol,
                                                                 in1=rs[u][:, :tl], op0=ALU.mult, op1=ALU.mult),
                         reads=[pqt, rst[u], t_c], writes=[qnt[u]])
                    if bi == 0:
                        k.op("act", lambda e: e.copy(out=ob[o3][:, :tl], in_=qn[u][:, :tl]), reads=[qnt[u]], writes=[obt[o3]])
                    else:
                        ls = slice(t0 - NCTX, t0 - NCTX + tl)
                        k.op("pe", lambda e: e.matmul(pr[:, :tl], lhsT=rm[:], rhs=qn[u][:, :tl], start=True, stop=True),
                             reads=[qnt[u], t_c], writes=[prt])
                        k.op("dve", lambda e: e.tensor_tensor(out=t1[u][:, :tl], in0=qn[u][:, :tl], in1=cs[:, 0, ls],
                                                               op=ALU.mult), reads=[qnt[u], t_c], writes=[t1t[u]])
                        k.op("dve", lambda e: e.tensor_tensor(out=qn[u][:, :tl], in0=pr[:, :tl], in1=cs[:, 1, ls],
                                                              op=ALU.mult), reads=[prt, t_c], writes=[qnt[u]])
                        k.op("dve", lambda e: e.tensor_tensor(out=ob[o3][:, :tl], in0=qn[u][:, :tl], in1=t1[u][:, :tl],
                                                              op=ALU.add), reads=[qnt[u], t1t[u]], writes=[obt[o3]])
                    dst = qT_d[fc, :, ts] if isq else kT_d[fc - NH, :, ts]
                    k.dma("sp", dst, ob[o3][:, :tl], reads=[obt[o3]], final=True)
        for tb in range(TOK // 128):
            u = tb % 2
            o3 = nu % 3
            nu += 1
            bi = 0 if tb < 2 else (1 if tb < 6 else 2)
            for c in range(NCH):
                k.op("pe", lambda e: e.matmul(ps[u][:, :], lhsT=H[:, c, tb * 128:(tb + 1) * 128], rhs=wv[:, c, :],
                                              start=(c == 0), stop=(c == NCH - 1)),
                     reads=[wvt, HT[bi]], writes=[pst[u]], inc=(c == NCH - 1))
            k.op("act", lambda e: e.copy(out=ob[o3][:, :], in_=ps[u][:, :]), reads=[pst[u]], writes=[obt[o3]])
            k.dma("sp", v_d[tb * 128:(tb + 1) * 128, :], ob[o3][:, :], reads=[obt[o3]], final=True)


NKEYW = NCTX + 128 + NLAT + 128


def emit_attn(k, C, window, ctx_out, qT_d, kT_all_d, v_all_d, kv_other, esink_dram, masks_dram, wo_dram):
    X = C.X
    nkey = NKEYW if window else NKEY
    nkb = nkey // 128
    with Stage(k) as st:
        KT = st.sb("KT", [128, NKV, nkey], BF16)
        V = st.sb("V", [128, nkb, 512], BF16)
        t_kv = DT("kv")
        kT_me, v_me, kT_x, v_x = kT_all_d, v_all_d, kv_other[0], kv_other[1]
        vr = lambda a: a.rearrange("(b p) f -> p b f", p=128)
        nlb = NLAT // 128
        if not window:
            for g in range(NKV):
                k.dma("sp", KT[:, g, 0:NCTX], kT_me[g][:, 0:NCTX], writes=[t_kv])
                k.dma("sp", KT[:, g, NCTX:TOK], kT_x[0][g][:, NCTX:TOK], writes=[t_kv])
                k.dma("sp", KT[:, g, TOK:NKEY], kT_x[1][g][:, NCTX:TOK], writes=[t_kv])
            k.dma("sp", V[:, 0:2, :], vr(v_me[0:NCTX, :]), writes=[t_kv])
            k.dma("sp", V[:, 2:2 + nlb, :], vr(v_x[0][NCTX:TOK, :]), writes=[t_kv])
            k.dma("sp", V[:, 2 + nlb:2 + 2 * nlb, :], vr(v_x[1][NCTX:TOK, :]), writes=[t_kv])
        else:
            k.op("pool", lambda e: e.memset(KT[:, :, NCTX:NCTX + 128], 0.0), writes=[t_kv])
            k.op("pool", lambda e: e.memset(V[:, 2, :], 0.0), writes=[t_kv])
            for g in range(NKV):
                k.dma("sp", KT[:, g, 0:NCTX], kT_me[g][:, 0:NCTX], writes=[t_kv])
                k.dma("sp", KT[:, g, NCTX + 128:NCTX + 128 + NLAT], kT_me[g][:, NCTX:TOK], writes=[t_kv])
                k.dma("sp", KT[:, g, NCTX + 128 + NLAT:NKEYW], kT_x[g][:, TOK - 128:TOK], writes=[t_kv])
            k.dma("sp", V[:, 0:2, :], vr(v_me[0:NCTX, :]), writes=[t_kv])
            k.dma("sp", V[:, 3:3 + nlb, :], vr(v_me[NCTX:TOK, :]), writes=[t_kv])
            k.dma("sp", V[:, 3 + nlb:4 + nlb, :], vr(v_x[TOK - 128:TOK, :]), writes=[t_kv])
        QT = [st.sb("QT%d" % i, [128, 4, TOK], BF16) for i in range(2)]
        QTt = [DT("QT%d" % i) for i in range(2)]
        OT = [st.sb("OT%d" % i, [128, 4, TOK], BF16) for i in range(2)]
        OTt = [DT("OT%d" % i) for i in range(2)]
        wo = [st.sb("awo%d" % i, [128, 4, D], BF16) for i in range(2)]
        wot = [DT("awo%d" % i) for i in range(2)]
        pt = [st.sb("pt%d" % i, [128, 512], BF16) for i in range(3)]
        ptt = [DT("pt%d" % i) for i in range(3)]
        rd = [st.sb("rd%d" % i, [128, 512], F32) for i in range(2)]
        rdt = [DT("rd%d" % i) for i in range(2)]
        ps = [st.ps("aps%d" % i, [128, 512], F32) for i in range(8)]
        pst = [DT("aps%d" % i) for i in range(8)]
        t_c = DT("attnconst")
        if window:
            es_ = st.sb("esink", [128, NH], F32)
            mk = st.sb("masks", [128, 4, 128], BF16)
            k.dma("sp", es_[:], esink_dram, writes=[t_c])
            k.dma("sp", mk[:], masks_dram, writes=[t_c])
            k.op("act", lambda e: e.activation(out=es_[:], in_=es_[:], func=AF.Exp), reads=[t_c], writes=[t_c])
        wo_v = wo_dram.rearrange("(h p) n -> p h n", p=128)
        ns = 0
        nunit = 0
        ny = 0
        for g in range(NKV):
            gb_ = g % 2
            for hl in range(4):
                k.dma("sp", QT[gb_][:, hl, :], qT_d[4 * g + hl], writes=[QTt[gb_]])
            k.dma("pool", wo[gb_][:], wo_v[:, 4 * g:4 * g + 4, :], writes=[wot[gb_]])
            for hl in range(4):
                h = 4 * g + hl
                units = []
                if not window:
                    for (t0, tl, v) in TBS[1:]:
                        units.append((t0, tl, [(kb, None) for kb in range(NKB)]))
                    if ctx_out:
                        units.append((0, NCTX, [(0, None), (1, None)]))
                else:
                    nqb = NLAT // 128
                    for qb in range(nqb):
                        kl = [(0, None), (1, None), (2 + qb, 2 if qb == 0 else 0), (3 + qb, None),
                              (4 + qb, 3 if qb == nqb - 1 else 1)]
                        units.append((NCTX + qb * 128, 128, kl))
                    if ctx_out:
                        units.append((0, NCTX, [(0, None), (1, None)]))
                for (t0, tl, kl) in units:
                    ts = slice(t0, t0 + tl)
                    u = nunit % 2
                    nunit += 1
                    po, pd = ps[2 + u], ps[4 + u]
                    pot, pdt = pst[2 + u], pst[4 + u]
                    for ki, (kb, mi) in enumerate(kl):
                        s2 = ns % 2
                        p3 = ns % 3
                        ns += 1
                        k.op("pe", lambda e: e.matmul(ps[s2][:, :tl], lhsT=KT[:, g, kb * 128:(kb + 1) * 128],
                                                      rhs=QT[gb_][:, hl, ts], start=True, stop=True),
                             reads=[t_kv, QTt[gb_]], writes=[pst[s2]])
                        k.op("act", lambda e: e.activation(out=pt[p3][:, :tl], in_=ps[s2][:, :tl], func=AF.Exp),
                             reads=[pst[s2]], writes=[ptt[p3]])
                        if mi is not None:
                            k.op("dve", lambda e: e.tensor_tensor(out=pt[p3][:, :tl], in0=pt[p3][:, :tl], in1=mk[:, mi, :tl],
                                                                   op=ALU.mult), reads=[ptt[p3], t_c], writes=[ptt[p3]])
                        last = ki == len(kl) - 1
                        k.op("pe", lambda e: e.matmul(po[:, :tl], lhsT=V[:, kb, g * 128:(g + 1) * 128], rhs=pt[p3][:, :tl],
                                                      start=(ki == 0), stop=last),
                             reads=[t_kv, ptt[p3]], writes=[pot], inc=last)
                        k.op("pe", lambda e: e.matmul(pd[:, :tl], lhsT=C.ones_b[:], rhs=pt[p3][:, :tl],
                                                      start=(ki == 0), stop=last),
                             reads=[C.t_const, ptt[p3]], writes=[pdt], inc=last)
                    if window:
                        k.op("dve", lambda e: e.tensor_scalar(out=rd[u][:, :tl], in0=pd[:, :tl], scalar1=es_[:, h:h + 1],
                                                              scalar2=None, op0=ALU.add), reads=[pdt, t_c], writes=[rdt[u]])
                        k.op("dve", lambda e: e.reciprocal(out=rd[u][:, :tl], in_=rd[u][:, :tl]), reads=[rdt[u]], writes=[rdt[u]])
                    else:
                        k.op("dve", lambda e: e.reciprocal(out=rd[u][:, :tl], in_=pd[:, :tl]), reads=[pdt], writes=[rdt[u]])
                    k.op("dve", lambda e: e.tensor_tensor(out=OT[gb_][:, hl, ts], in0=po[:, :tl], in1=rd[u][:, :tl], op=ALU.mult),
                         reads=[pot, rdt[u]], writes=[OTt[gb_]])
            for dc in range(NCH):
                for bi, (t0, tl, v) in enumerate(TBS):
                    if bi == 0 and not ctx_out:
                        continue
                    ts = slice(t0, t0 + tl)
                    p = 6 + ny % 2
                    ny += 1
                    for hl in range(4):
                        k.op("pe", lambda e: e.matmul(ps[p][:, :tl], lhsT=wo[gb_][:, hl, dc * 128:(dc + 1) * 128],
                                                      rhs=OT[gb_][:, hl, ts], start=(hl == 0), stop=(hl == 3)),
                             reads=[wot[gb_], OTt[gb_]], writes=[pst[p]], inc=(hl == 3))
                    k.op("dve", lambda e: e.scalar_tensor_tensor(out=X[:, dc, ts], in0=ps[p][:, :tl],
                                                                 scalar=C.G[:, v, 1, dc:dc + 1], in1=X[:, dc, ts],
                                                                 op0=ALU.mult, op1=ALU.add),
                         reads=[pst[p], C.t_mv, C.XT[bi]], writes=[C.XT[bi]])


def rope_tables(half):
    t = np.arange(half * NLAT, (half + 1) * NLAT)
    if half == 1:
        t = t[::-1]
    row = (t // 64).astype(np.float32)
    col = (t % 64).astype(np.float32)
    inv = (10000.0 ** (-np.arange(32, dtype=np.float32) / 32)).astype(np.float32)
    ang = np.concatenate([row[:, None] * inv, col[:, None] * inv], axis=-1)
    ang = np.concatenate([ang, ang], axis=-1).astype(np.float32)
    return np.ascontiguousarray(np.cos(ang).T.astype(np.float32)), np.ascontiguousarray(np.sin(ang).T.astype(np.float32))


def rot_matrix():
    R = np.zeros((128, 128), np.float32)
    for m in range(64):
        R[m + 64, m] = -1.0
    for m in range(64, 128):
        R[m - 64, m] = 1.0
    return R


def win_masks(half=0):
    import ml_dtypes
    s = np.arange(128)[:, None]
    t = np.arange(128)[None, :]
    prev = (s >= t).astype(np.float32)
    nxt = (s <= t).astype(np.float32)
    z = np.zeros_like(prev)
    m = np.stack([prev, nxt, z, nxt[::-1]], axis=1)
    return np.ascontiguousarray(m).astype(ml_dtypes.bfloat16)


KINDS = ["a", "s", "m", "w"]


MH = 8
MDQK = 128
MDV = 256
NBLK = TOK // 128
LN_KSCALE = float(np.log(MDQK ** -0.5))


def emit_mlstm_proj(k, C, win_dram, wg_dram, bg_dram, qT_d, kT_d, ktok_d, vtok_d, so_d, gi_d):
    with Stage(k) as st:
        H = st.sb("m_h", [128, NCH, TOK], BF16)
        HT = [DT("h%d" % i) for i in range(3)]
        ps = [st.ps("mps%d" % i, [128, 512], F32) for i in range(8)]
        pst = [DT("mps%d" % i) for i in range(8)]
        with Stage(k) as stn:
            emit_norm_mod(k, C, stn, 1, H, HT, ps[6:8], pst[6:8])
        NW = 3
        wt = [st.sb("mw%d" % i, [128, NCH, 512], BF16) for i in range(NW)]
        wtt = [DT("mw%d" % i) for i in range(NW)]
        w_v = win_dram.rearrange("(c p) n -> p c n", p=128)
        ob = [st.sb("mob%d" % i, [128, 512], BF16) for i in range(3)]
        obt = [DT("mob%d" % i) for i in range(3)]
        of = [st.sb("mof%d" % i, [128, 512], F32) for i in range(2)]
        oft = [DT("mof%d" % i) for i in range(2)]
        nu = 0
        nw = 0
        for (c0, kind) in [(0, "q"), (512, "q"), (1024, "k"), (1536, "k"), (4096, "o"), (4608, "o"), (5120, "o"), (5632, "o")]:
            b = nw % NW
            nw += 1
            k.dma("pool", wt[b][:], w_v[:, :, c0:c0 + 512], writes=[wtt[b]])
            for s in range(4):
                fcol = c0 + s * 128
                for bi, (t0, tl, v) in enumerate(TBS):
                    ts = slice(t0, t0 + tl)
                    u = nu % 4
                    nu += 1
                    for c in range(NCH):
                        k.op("pe", lambda e: e.matmul(ps[u][:, :tl], lhsT=wt[b][:, c, s * 128:(s + 1) * 128], rhs=H[:, c, ts],
                                                      start=(c == 0), stop=(c == NCH - 1)),
                             reads=[wtt[b], HT[bi]], writes=[pst[u]], inc=(c == NCH - 1))
                    if kind == "o":
                        f2 = nu % 2
                        k.op("act", lambda e: e.activation(out=of[f2][:, :tl], in_=ps[u][:, :tl], func=AF.Sigmoid),
                             reads=[pst[u]], writes=[oft[f2]])
                        k.dma("sp", so_d[(fcol - 4096) // 128, :, ts], of[f2][:, :tl], reads=[oft[f2]], final=True)
                    else:
                        o3 = nu % 3
                        k.op("act", lambda e: e.copy(out=ob[o3][:, :tl], in_=ps[u][:, :tl]), reads=[pst[u]], writes=[obt[o3]])
                        dst = qT_d[fcol // 128, :, ts] if kind == "q" else kT_d[(fcol - 1024) // 128, :, ts]
                        k.dma("sp", dst, ob[o3][:, :tl], reads=[obt[o3]], final=True)
        wg = st.sb("m_wg", [128, NCH, 32], BF16)
        wgt = DT("m_wg")
        k.dma("pool", wg[:], wg_dram.rearrange("(c p) n -> p c n", p=128), writes=[wgt])
        bg = st.sb("m_bg", [128, 32], F32)
        k.dma("sp", bg[:], bg_dram, writes=[wgt])
        gt = [st.sb("m_gt%d" % i, [128, 32], F32) for i in range(2)]
        gtt = [DT("m_gt%d" % i) for i in range(2)]
        for (c0, kind) in [(1024, "k"), (1536, "k"), (2048, "v"), (2560, "v"), (3072, "v"), (3584, "v")]:
            b = nw % NW
            nw += 1
            k.dma("pool", wt[b][:], w_v[:, :, c0:c0 + 512], writes=[wtt[b]])
            for tb in range(NBLK):
                u = nu % 4
                o3 = nu % 3
                nu += 1
                bi = 0 if tb < 2 else (1 if tb < 6 else 2)
                for c in range(NCH):
                    k.op("pe", lambda e: e.matmul(ps[u][:, :], lhsT=H[:, c, tb * 128:(tb + 1) * 128], rhs=wt[b][:, c, :],
                                                  start=(c == 0), stop=(c == NCH - 1)),
                         reads=[wtt[b], HT[bi]], writes=[pst[u]], inc=(c == NCH - 1))
                k.op("act", lambda e: e.copy(out=ob[o3][:, :], in_=ps[u][:, :]), reads=[pst[u]], writes=[obt[o3]])
                dst = ktok_d[tb * 128:(tb + 1) * 128, c0 - 1024:c0 - 512] if kind == "k" else \
                    vtok_d[tb * 128:(tb + 1) * 128, c0 - 2048:c0 - 1536]
                k.dma("sp", dst, ob[o3][:, :], reads=[obt[o3]], final=True)
        for tb in range(NBLK):
            u = nu % 4
            f2 = nu % 2
            nu += 1
            bi = 0 if tb < 2 else (1 if tb < 6 else 2)
            for c in range(NCH):
                k.op("pe", lambda e: e.matmul(ps[u][:, 0:32], lhsT=H[:, c, tb * 128:(tb + 1) * 128], rhs=wg[:, c, :],
                                              start=(c == 0), stop=(c == NCH - 1)),
                     reads=[wgt, HT[bi]], writes=[pst[u]], inc=(c == NCH - 1))
            k.op("dve", lambda e: e.tensor_tensor(out=gt[f2][:], in0=ps[u][:, 0:32], in1=bg[:], op=ALU.add),
                 reads=[pst[u], wgt], writes=[gtt[f2]])
            k.op("act", lambda e: e.activation(out=gt[f2][:], in_=gt[f2][:], func=AF.Tanh, scale=1.0 / 15.0),
                 reads=[gtt[f2]], writes=[gtt[f2]])
            k.op("dve", lambda e: e.tensor_scalar(out=gt[f2][:], in0=gt[f2][:], scalar1=15.0, scalar2=None, op0=ALU.mult),
                 reads=[gtt[f2]], writes=[gtt[f2]])
            k.dma("sp", gi_d[tb * 128:(tb + 1) * 128, :], gt[f2][:], reads=[gtt[f2]], final=True)


def emit_mlstm_scan(k, C, st, phase, qT_d, kT_d, ktok_d, vtok_d, gi_d, tri_dram, madd_dram, state_in, state_out,
                    h_in, h_out):
    p = phase
    NB_ = 2
    QT = [st.sb("m_qT%d" % i, [128, MH, 128], BF16) for i in range(NB_)]
    KT = [st.sb("m_kT%d" % i, [128, MH, 128], BF16) for i in range(NB_)]
    KK = [st.sb("m_kk%d" % i, [128, MH * MDQK], BF16) for i in range(NB_)]
    VV = [st.sb("m_vv%d" % i, [128, MH * MDV], BF16) for i in range(NB_)]
    t_blk = [DT("m_blk%d" % i) for i in range(NB_)]
    GI = st.sb("m_gi", [128, NBLK, 32], F32)
    t_in = DT("m_in")
    k.dma("sp", GI[:], gi_d.rearrange("(b p) f -> p b f", p=128), writes=[t_in])
    tri = st.sb("m_tri", [128, 128], F32)
    madd = st.sb("m_madd", [128, 128], F32)
    t_c = DT("m_c")
    k.dma("sp", tri[:], tri_dram[p], writes=[t_c])
    k.dma("sp", madd[:], madd_dram[p], writes=[t_c])
    A_ = st.sb("m_a", [128, NBLK, MH], F32)
    IMB = st.sb("m_imb", [128, NBLK, MH], F32)
    t_g = DT("m_g")
    fsl = GI[:, :, p * 16 + 8:p * 16 + 16]
    isl = GI[:, :, p * 16:p * 16 + 8]
    k.op("act", lambda e: e.activation(out=A_[:], in_=fsl, func=AF.Exp, scale=-1.0), reads=[t_in], writes=[t_g])
    k.op("act", lambda e: e.activation(out=A_[:], in_=A_[:], func=AF.Ln, bias=C.one_col[:, 0:1], scale=1.0),
         reads=[t_g, C.t_const], writes=[t_g])
    k.op("dve", lambda e: e.tensor_scalar(out=A_[:], in0=A_[:], scalar1=-1.0, scalar2=None, op0=ALU.mult),
         reads=[t_g], writes=[t_g])
    ps = [st.ps("sps%d" % i, [128, 512], F32) for i in range(8)]
    pst = [DT("sps%d" % i) for i in range(8)]
    for blk in range(NBLK):
        u = blk % 2
        k.op("pe", lambda e: e.matmul(ps[u][:, 0:MH], lhsT=tri[:], rhs=A_[:, blk, :], start=True, stop=True),
             reads=[t_c, t_g], writes=[pst[u]])
        k.op("dve", lambda e: e.scalar_tensor_tensor(out=IMB[:, blk, :], in0=isl[:, blk, :], scalar=LN_KSCALE,
                                                     in1=ps[u][:, 0:MH], op0=ALU.add, op1=ALU.subtract),
             reads=[pst[u], t_in], writes=[t_g])
    Cf = st.sb("m_Cf", [128, MH, MDV], F32)
    Cb = st.sb("m_Cb", [128, MH, MDV], BF16)
    Nf = st.sb("m_Nf", [128, MH, 128], F32)
    Nb = st.sb("m_Nb", [128, MH, 128], BF16)
    St = [DT("m_st%d" % hh) for hh in range(MH)]
    NS = 2
    Ta = [st.sb("m_Ta%d" % i, [128, 128], F32) for i in range(NS)]
    Tat = [DT("Ta") for _ in range(NS)]
    tmp = [st.sb("m_tmp%d" % i, [128, 128], F32) for i in range(NS)]
    tmpt = [DT("tmp") for _ in range(NS)]
    Dm = [st.sb("m_Dm%d" % i, [128, 128], F32) for i in range(NS)]
    Dmt = [DT("Dm") for _ in range(NS)]
    eb = [st.sb("m_eb%d" % i, [128, 128], F32) for i in range(NS)]
    ebt = [DT("eb") for _ in range(NS)]
    Qp = [st.sb("m_Qp%d" % i, [128, 128], BF16) for i in range(NS)]
    Qpt = [DT("Qp") for _ in range(NS)]
    PT = [st.sb("m_PT%d" % i, [128, 128], BF16) for i in range(NS)]
    PTt = [DT("PT") for _ in range(NS)]
    rd = [st.sb("m_rd%d" % i, [128, 128], F32) for i in range(NS)]
    rdt = [DT("rd") for _ in range(NS)]
    wc = [st.sb("m_wc%d" % i, [128, 2], F32) for i in range(NS)]
    wct = [DT("wc") for _ in range(NS)]
    K2 = [st.sb("m_K2%d" % i, [128, 128], BF16) for i in range(NS)]
    K2t = [DT("K2") for _ in range(NS)]
    hb = [st.sb("m_hb%d" % i, [128, 2, 128], F32) for i in range(3)]
    hbt = [DT("hb%d" % i) for i in range(3)]
    hi = [st.sb("m_hi%d" % i, [128, 2, 128], F32) for i in range(3)]
    hit = [DT("hi%d" % i) for i in range(3)]

    def zero_state():
        for hh in range(MH):
            k.op("pool", lambda e: e.memset(Cf[:, hh, :], 0.0), writes=[St[hh]])
            k.op("pool", lambda e: e.memset(Cb[:, hh, :], 0.0), writes=[St[hh]])
            k.op("pool", lambda e: e.memset(Nf[:, hh, :], 0.0), writes=[St[hh]])
            k.op("pool", lambda e: e.memset(Nb[:, hh, :], 0.0), writes=[St[hh]])

    ecol = 127 if p == 0 else 0

    def make_unit(blk, bb, bs, hh, u, h3):
        pA, pS, pN, pC = ps[u], ps[2 + u], ps[4 + u], ps[6 + u]
        pAt, pSt, pNt, pCt = pst[u], pst[2 + u], pst[4 + u], pst[6 + u]

        def fa():
                k.op("act", lambda e: e.activation(out=Ta[u][:], in_=tri[:], func=AF.Identity, scale=A_[:, blk, hh:hh + 1]),
                     reads=[t_c, t_g], writes=[Tat[u]])
                k.op("pe", lambda e: e.matmul(pA[:, 0:128], lhsT=C.ones_f[:], rhs=Ta[u][:], start=True, stop=True),
                     reads=[Tat[u], C.t_const], writes=[pAt])
                k.op("dve", lambda e: e.tensor_tensor(out=tmp[u][:], in0=pA[:, 0:128], in1=madd[:], op=ALU.add),
                     reads=[pAt, t_c], writes=[tmpt[u]])
                k.op("act", lambda e: e.activation(out=Dm[u][:], in_=tmp[u][:], func=AF.Exp, bias=IMB[:, blk, hh:hh + 1], scale=1.0),
                     reads=[tmpt[u], t_g], writes=[Dmt[u]])
                k.op("act", lambda e: e.activation(out=eb[u][:], in_=pA[:, 0:128], func=AF.Exp), reads=[pAt], writes=[ebt[u]])
                k.op("dve", lambda e: e.tensor_tensor(out=Qp[u][:], in0=QT[bb][:, hh, :], in1=eb[u][:], op=ALU.mult),
                     reads=[t_blk[bb], ebt[u]], writes=[Qpt[u]])
                k.op("pe", lambda e: e.matmul(pS[:, 0:128], lhsT=KT[bb][:, hh, :], rhs=QT[bb][:, hh, :], start=True, stop=True),
                     reads=[t_blk[bb]], writes=[pSt])
                k.op("dve", lambda e: e.tensor_tensor(out=PT[u][:], in0=pS[:, 0:128], in1=Dm[u][:], op=ALU.mult),
                     reads=[pSt, Dmt[u]], writes=[PTt[u]])

        def fb():
                for dvh in range(2):
                    k.op("pe", lambda e: e.matmul(pN[:, dvh * 128:(dvh + 1) * 128], lhsT=VV[bb][:, hh * MDV + dvh * 128:hh * MDV + (dvh + 1) * 128],
                                                  rhs=PT[u][:], start=True, stop=False), reads=[t_blk[bb], PTt[u]], writes=[pNt], inc=False)
                    k.op("pe", lambda e: e.matmul(pN[:, dvh * 128:(dvh + 1) * 128], lhsT=Cb[:, hh, dvh * 128:(dvh + 1) * 128],
                                                  rhs=Qp[u][:], start=False, stop=True), reads=[St[hh], Qpt[u]], writes=[pNt], inc=False)
                k.op("pe", lambda e: e.matmul(pN[:, 256:384], lhsT=C.ones_b[:], rhs=PT[u][:], start=True, stop=False),
                     reads=[C.t_const, PTt[u]], writes=[pNt], inc=False)
                k.op("pe", lambda e: e.matmul(pN[:, 256:384], lhsT=Nb[:, hh, :], rhs=Qp[u][:], start=False, stop=True),
                     reads=[St[hh], Qpt[u]], writes=[pNt], inc=True)
                k.op("act", lambda e: e.activation(out=rd[u][:], in_=pN[:, 256:384], func=AF.Abs), reads=[pNt], writes=[rdt[u]])
                k.op("dve", lambda e: e.tensor_scalar(out=rd[u][:], in0=rd[u][:], scalar1=1.0, scalar2=None, op0=ALU.max),
                     reads=[rdt[u]], writes=[rdt[u]])
                k.op("dve", lambda e: e.reciprocal(out=rd[u][:], in_=rd[u][:]), reads=[rdt[u]], writes=[rdt[u]])
                if p == 1:
                    for dvh in range(2):
                        k.dma("sp", hi[h3][:, dvh, :], h_in[2 * hh + dvh, :, bs], writes=[hit[h3]])
                for dvh in range(2):
                    k.op("dve", lambda e: e.tensor_tensor(out=hb[h3][:, dvh, :], in0=pN[:, dvh * 128:(dvh + 1) * 128], in1=rd[u][:],
                                                          op=ALU.mult), reads=[pNt, rdt[u]], writes=[hbt[h3]])
                if p == 0:
                    for dvh in range(2):
                        k.dma("sp", h_out[2 * hh + dvh, :, bs], hb[h3][:, dvh, :], reads=[hbt[h3]], final=True)
                else:
                    k.op("dve", lambda e: e.tensor_tensor(out=hb[h3][:], in0=hb[h3][:], in1=hi[h3][:], op=ALU.add),
                         reads=[hit[h3]], writes=[hbt[h3]])
                    for dvh in range(2):
                        k.dma("sp", h_out[2 * hh + dvh, :, bs], hb[h3][:, dvh, :], reads=[hbt[h3]], final=True)

        def fc():
                k.op("dve", lambda e: e.tensor_tensor(out=wc[u][:, 0:1], in0=IMB[:, blk, hh:hh + 1], in1=pA[:, ecol:ecol + 1],
                                                      op=ALU.add), reads=[pAt, t_g], writes=[wct[u]])
                k.op("act", lambda e: e.activation(out=wc[u][:, 0:1], in_=wc[u][:, 0:1], func=AF.Exp), reads=[wct[u]], writes=[wct[u]])
                k.op("act", lambda e: e.activation(out=wc[u][:, 1:2], in_=pA[:, ecol:ecol + 1], func=AF.Exp), reads=[pAt, wct[u]],
                     writes=[wct[u]])
                k.op("act", lambda e: e.activation(out=K2[u][:], in_=KK[bb][:, hh * 128:(hh + 1) * 128], func=AF.Identity,
                                                   scale=wc[u][:, 0:1]), reads=[t_blk[bb], wct[u]], writes=[K2t[u]])
                k.op("pe", lambda e: e.matmul(pC[:, 0:256], lhsT=K2[u][:], rhs=VV[bb][:, hh * MDV:(hh + 1) * MDV], start=True, stop=True),
                     reads=[K2t[u], t_blk[bb]], writes=[pCt], inc=False)
                k.op("pe", lambda e: e.matmul(pC[:, 256:384], lhsT=K2[u][:], rhs=C.ones_b[:], start=True, stop=True),
                     reads=[K2t[u], C.t_const], writes=[pCt], inc=True)
                k.op("dve", lambda e: e.scalar_tensor_tensor(out=Cf[:, hh, :], in0=Cf[:, hh, :], scalar=wc[u][:, 1:2], in1=pC[:, 0:256],
                                                             op0=ALU.mult, op1=ALU.add), reads=[pCt, wct[u], St[hh]], writes=[St[hh]])
                k.op("act", lambda e: e.copy(out=Cb[:, hh, :], in_=Cf[:, hh, :]), reads=[St[hh]], writes=[St[hh]])
                k.op("dve", lambda e: e.scalar_tensor_tensor(out=Nf[:, hh, :], in0=Nf[:, hh, :], scalar=wc[u][:, 1:2], in1=pC[:, 256:384],
                                                             op0=ALU.mult, op1=ALU.add), reads=[pCt, wct[u], St[hh]], writes=[St[hh]])
                k.op("act", lambda e: e.copy(out=Nb[:, hh, :], in_=Nf[:, hh, :]), reads=[St[hh]], writes=[St[hh]])

        return fa, fb, fc

    if p == 0:
        order = [("z",)] + [("b", b) for b in range(NBLK)]
    else:
        order = [("z",), ("b", 1), ("b", 0), ("load",)] + [("b", b) for b in range(NBLK - 1, 1, -1)]
    n = 0
    nh = 0
    nblk = 0
    pending = []
    for it in order:
        if it[0] in ("z", "load"):
            for f in pending:
                f()
            pending = []
        if it[0] == "z":
            zero_state()
            continue
        if it[0] == "load":
            for hh in range(MH):
                k.dma("sp", Cf[:, hh, :], state_in[0][hh], writes=[St[hh]])
                k.dma("sp", Nf[:, hh, :], state_in[1][hh], writes=[St[hh]])
                k.op("act", lambda e: e.copy(out=Cb[:, hh, :], in_=Cf[:, hh, :]), reads=[St[hh]], writes=[St[hh]])
                k.op("act", lambda e: e.copy(out=Nb[:, hh, :], in_=Nf[:, hh, :]), reads=[St[hh]], writes=[St[hh]])
            continue
        blk = it[1]
        bs = slice(blk * 128, (blk + 1) * 128)
        bb = nblk % NB_
        nblk += 1
        k.dma("sp", QT[bb][:], qT_d[:, :, bs].rearrange("h p t -> p h t"), writes=[t_blk[bb]])
        k.dma("sp", KT[bb][:], kT_d[:, :, bs].rearrange("h p t -> p h t"), writes=[t_blk[bb]])
        k.dma("sp", KK[bb][:], ktok_d[bs, :], writes=[t_blk[bb]])
        k.dma("sp", VV[bb][:], vtok_d[bs, :], writes=[t_blk[bb]])
        for hh in range(MH):
            fa, fb, fc = make_unit(blk, bb, bs, hh, n % NS, nh % 3)
            n += 1
            nh += 1
            fa()
            for f in pending:
                f()
            pending = [fb, fc]
    for f in pending:
        f()
    if p == 0:
        for hh in range(MH):
            k.dma("sp", state_out[0][hh], Cf[:, hh, :], reads=[St[hh]], final=True)
            k.dma("sp", state_out[1][hh], Nf[:, hh, :], reads=[St[hh]], final=True)


def emit_mlstm_readout(k, C, st, hs_d, so_d, ng_dram, wout_dram, ctx_out):
    X = C.X
    HN = st.sb("m_hn", [128, NCH, TOK], BF16)
    HNT = [DT("hn%d" % i) for i in range(3)]
    gn = st.sb("m_gn", [128, NCH], F32)
    t_c = DT("m_rc")
    k.dma("sp", gn[:], ng_dram, writes=[t_c])
    ps = [st.ps("rps%d" % i, [128, 512], F32) for i in range(4)]
    pst = [DT("rps%d" % i) for i in range(4)]
    sq = [st.sb("m_sq%d" % i, [128, 512], F32) for i in range(2)]
    sqt = [DT("sq") for _ in range(2)]
    rs = [st.sb("m_rs%d" % i, [128, 512], F32) for i in range(2)]
    rst = [DT("rs") for _ in range(2)]
    so = [st.sb("m_so%d" % i, [128, 2, 512], F32) for i in range(2)]
    sot = [DT("so%d" % i) for i in range(2)]
    hh_ = [st.sb("m_hh%d" % i, [128, 2, 512], F32) for i in range(2)]
    hht = [DT("hh%d" % i) for i in range(2)]
    n = 0
    for hh in range(MH):
        for bi, (t0, tl, v) in enumerate(TBS):
            if bi == 0 and not ctx_out:
                continue
            ts = slice(t0, t0 + tl)
            u = n % 2
            n += 1
            for dvh in range(2):
                k.dma("sp", so[u][:, dvh, :tl], so_d[2 * hh + dvh, :, ts], writes=[sot[u]])
                k.dma("sp", hh_[u][:, dvh, :tl], hs_d[2 * hh + dvh, :, ts], writes=[hht[u]])
            for dvh in range(2):
                q = (2 * n + dvh) % 2
                k.op("act", lambda e: e.activation(out=sq[q][:, :tl], in_=hh_[u][:, dvh, :tl], func=AF.Square),
                     reads=[hht[u]], writes=[sqt[q]])
                k.op("pe", lambda e: e.matmul(ps[u][:, :tl], lhsT=C.ones_f[:], rhs=sq[q][:, :tl], start=(dvh == 0), stop=(dvh == 1)),
                     reads=[sqt[q], C.t_const], writes=[pst[u]])
            k.op("act", lambda e: e.activation(out=rs[u][:, :tl], in_=ps[u][:, :tl], func=AF.Sqrt, bias=C.eps_col[:, 0:1],
                                               scale=1.0 / MDV), reads=[pst[u], C.t_const], writes=[rst[u]])
            k.op("dve", lambda e: e.reciprocal(out=rs[u][:, :tl], in_=rs[u][:, :tl]), reads=[rst[u]], writes=[rst[u]])
            for dvh in range(2):
                c = 2 * hh + dvh
                k.op("dve", lambda e: e.scalar_tensor_tensor(out=so[u][:, dvh, :tl], in0=so[u][:, dvh, :tl], scalar=gn[:, c:c + 1],
                                                             in1=rs[u][:, :tl], op0=ALU.mult, op1=ALU.mult),
                     reads=[sot[u], rst[u], t_c], writes=[sot[u]])
                k.op("dve", lambda e: e.tensor_tensor(out=HN[:, c, ts], in0=so[u][:, dvh, :tl], in1=hh_[u][:, dvh, :tl], op=ALU.mult),
                     reads=[sot[u], hht[u]], writes=[HNT[bi]])
    emit_outproj(k, C, st, HN, HNT, wout_dram, ctx_out, ps[2:4], pst[2:4])


def emit_outproj(k, C, st, IN, INT, w_dram, ctx_out, ps, pst):
    X = C.X
    NW = 2
    wo = [st.sb("op_w%d" % i, [128, NCH, 256], BF16) for i in range(NW)]
    wot = [DT("op_w%d" % i) for i in range(NW)]
    w_v = w_dram.rearrange("(c p) n -> p c n", p=128)
    ny = 0
    for dq in range(D // 256):
        b = dq % NW
        k.dma("pool", wo[b][:], w_v[:, :, dq * 256:(dq + 1) * 256], writes=[wot[b]])
        for dl in range(2):
            dc = dq * 2 + dl
            for bi, (t0, tl, v) in enumerate(TBS):
                if bi == 0 and not ctx_out:
                    continue
                ts = slice(t0, t0 + tl)
                p = ny % 2
                ny += 1
                for c in range(NCH):
                    k.op("pe", lambda e: e.matmul(ps[p][:, :tl], lhsT=wo[b][:, c, dl * 128:(dl + 1) * 128], rhs=IN[:, c, ts],
                                                  start=(c == 0), stop=(c == NCH - 1)),
                         reads=[wot[b], INT[bi]], writes=[pst[p]], inc=(c == NCH - 1))
                k.op("dve", lambda e: e.scalar_tensor_tensor(out=X[:, dc, ts], in0=ps[p][:, :tl], scalar=C.G[:, v, 1, dc:dc + 1],
                                                             in1=X[:, dc, ts], op0=ALU.mult, op1=ALU.add),
                     reads=[pst[p], C.t_mv, C.XT[bi]], writes=[C.XT[bi]])


def mlstm_consts():
    kk = np.arange(128)[:, None]
    tt = np.arange(128)[None, :]
    tri = np.stack([(kk <= tt), (kk >= tt)]).astype(np.float32)
    madd = ((tri - 1.0) * 1.0e4).astype(np.float32)
    return tri, madd


S5P = 64
S5T = TOK


def _eslice(seg, a, tl):
    t0, n, rev = seg
    if not rev:
        return slice(a - t0, a - t0 + tl)
    hi = n - 1 - (a - t0)
    lo = hi - tl + 1
    return slice(hi, lo - 1 if lo > 0 else None, -1)


def emit_s5_phase(k, C, p, lam_dram, bd_dram, cT_dram, dsk_dram, state_in, state_out, y_in, y_out, Z, ZT):
    segs = [(0, TOK, False)] if p == 0 else [(0, NCTX, True), (NCTX, NLAT, True)]
    with Stage(k) as st:
        ps = [st.ps("s5ps%d" % i, [128, 512], F32) for i in range(8)]
        pst = [DT("s5ps%d" % i) for i in range(8)]
        rstd = st.sb("s_rstd", [128, TOK], F32)
        t_rs = [DT("s_rs%d" % i) for i in range(3)]
        with Stage(k) as stn:
            emit_rstd(k, C, stn, rstd, t_rs, ps[6:8], pst[6:8])
        Hd = [st.sb("s_hd%d" % i, [128, TOK], BF16) for i in range(2)]
        HdT = [DT("s_hd%d" % i) for i in range(2)]
        htmp = [st.sb("s_htmp%d" % i, [128, 512], F32) for i in range(2)]
        htmpt = [DT("s_htmp%d" % i) for i in range(2)]
        LM = st.sb("s_lam", [128, 3, S5P], F32)
        t_p = DT("s5prep")
        k.dma("sp", LM[:], lam_dram[p], writes=[t_p])
        W = st.sb("s_w", [128, 16, S5P], F32)
        LR, LI, DTT, MAG, TH, CS, SN, AR1, AI, DEN, KR, KI, T1, T2, NSN, T3 = [W[:, i, :] for i in range(16)]

        def dv(fn):
            k.op("dve", fn, reads=[t_p, C.t_const], writes=[t_p])

        def ac(fn):
            k.op("act", fn, reads=[t_p, C.t_const], writes=[t_p])
        dv(lambda e: e.tensor_scalar(out=LR, in0=LM[:, 0, :], scalar1=-1e-4, scalar2=None, op0=ALU.min))
        ac(lambda e: e.activation(out=DTT, in_=LM[:, 2, :], func=AF.Exp))
        dv(lambda e: e.tensor_tensor(out=T1, in0=LR, in1=DTT, op=ALU.mult))
        ac(lambda e: e.activation(out=MAG, in_=T1, func=AF.Exp))
        dv(lambda e: e.tensor_tensor(out=TH, in0=LM[:, 1, :], in1=DTT, op=ALU.mult))
        ac(lambda e: e.activation(out=SN, in_=TH, func=AF.Sin, scale=1.0 / 8))
        ac(lambda e: e.activation(out=T1, in_=TH, func=AF.Sin, scale=1.0 / 16))
        dv(lambda e: e.tensor_tensor(out=T1, in0=T1, in1=T1, op=ALU.mult))
        dv(lambda e: e.tensor_scalar(out=CS, in0=T1, scalar1=-2.0, scalar2=1.0, op0=ALU.mult, op1=ALU.add))
        for _ in range(3):
            dv(lambda e: e.tensor_tensor(out=T1, in0=CS, in1=CS, op=ALU.mult))
            dv(lambda e: e.tensor_tensor(out=T2, in0=SN, in1=SN, op=ALU.mult))
            dv(lambda e: e.tensor_tensor(out=T3, in0=CS, in1=SN, op=ALU.mult))
            dv(lambda e: e.tensor_tensor(out=CS, in0=T1, in1=T2, op=ALU.subtract))
            dv(lambda e: e.tensor_scalar(out=SN, in0=T3, scalar1=2.0, scalar2=None, op0=ALU.mult))
        dv(lambda e: e.tensor_tensor(out=AR1, in0=MAG, in1=CS, op=ALU.mult))
        dv(lambda e: e.tensor_scalar(out=AR1, in0=AR1, scalar1=-1.0, scalar2=None, op0=ALU.add))
        dv(lambda e: e.tensor_tensor(out=AI, in0=MAG, in1=SN, op=ALU.mult))
        dv(lambda e: e.tensor_tensor(out=T1, in0=LR, in1=LR, op=ALU.mult))
        dv(lambda e: e.tensor_tensor(out=T2, in0=LM[:, 1, :], in1=LM[:, 1, :], op=ALU.mult))
        dv(lambda e: e.tensor_tensor(out=DEN, in0=T1, in1=T2, op=ALU.add))
        dv(lambda e: e.reciprocal(out=DEN, in_=DEN))
        dv(lambda e: e.tensor_tensor(out=T1, in0=AR1, in1=LR, op=ALU.mult))
        dv(lambda e: e.tensor_tensor(out=T2, in0=AI, in1=LM[:, 1, :], op=ALU.mult))
        dv(lambda e: e.tensor_tensor(out=T1, in0=T1, in1=T2, op=ALU.add))
        dv(lambda e: e.tensor_tensor(out=KR, in0=T1, in1=DEN, op=ALU.mult))
        dv(lambda e: e.tensor_tensor(out=T1, in0=AI, in1=LR, op=ALU.mult))
        dv(lambda e: e.tensor_tensor(out=T2, in0=AR1, in1=LM[:, 1, :], op=ALU.mult))
        dv(lambda e: e.tensor_tensor(out=T1, in0=T1, in1=T2, op=ALU.subtract))
        dv(lambda e: e.tensor_tensor(out=KI, in0=T1, in1=DEN, op=ALU.mult))
        dv(lambda e: e.tensor_scalar(out=NSN, in0=SN, scalar1=-1.0, scalar2=None, op0=ALU.mult))
        NLV = 11
        UP = st.sb("s_up", [128, NLV, 3, S5P], F32)
        dv(lambda e: e.tensor_copy(out=UP[:, 0, 0, :], in_=CS))
        dv(lambda e: e.tensor_copy(out=UP[:, 0, 1, :], in_=SN))
        dv(lambda e: e.tensor_copy(out=UP[:, 0, 2, :], in_=NSN))
        for l in range(1, NLV):
            dv(lambda e: e.tensor_tensor(out=T1, in0=UP[:, l - 1, 0, :], in1=UP[:, l - 1, 0, :], op=ALU.mult))
            dv(lambda e: e.tensor_tensor(out=T2, in0=UP[:, l - 1, 1, :], in1=UP[:, l - 1, 1, :], op=ALU.mult))
            dv(lambda e: e.tensor_tensor(out=UP[:, l, 0, :], in0=T1, in1=T2, op=ALU.subtract))
            dv(lambda e: e.tensor_tensor(out=T3, in0=UP[:, l - 1, 0, :], in1=UP[:, l - 1, 1, :], op=ALU.mult))
            dv(lambda e: e.tensor_scalar(out=UP[:, l, 1, :], in0=T3, scalar1=2.0, scalar2=None, op0=ALU.mult))
            dv(lambda e: e.tensor_scalar(out=UP[:, l, 2, :], in0=T3, scalar1=-2.0, scalar2=None, op0=ALU.mult))
        CR = st.sb("s_cr", [128, S5P, 16], F32)
        CI = st.sb("s_ci", [128, S5P, 16], F32)
        with Stage(k) as stc:
            CT = stc.sb("s_cT", [128, 2, S5P, 16], F32)
            k.dma("sp", CT[:], cT_dram[p], writes=[t_p])
            CW = stc.sb("s_cw", [128, S5P, 16], F32)
            krb = W[:, 10:11, :].rearrange("p o s -> p s o").to_broadcast([128, S5P, 16])
            kib = W[:, 11:12, :].rearrange("p o s -> p s o").to_broadcast([128, S5P, 16])
            dv(lambda e: e.tensor_tensor(out=CR[:], in0=CT[:, 0, :, :], in1=krb, op=ALU.mult))
            dv(lambda e: e.tensor_tensor(out=CW[:], in0=CT[:, 1, :, :], in1=kib, op=ALU.mult))
            dv(lambda e: e.tensor_tensor(out=CR[:], in0=CR[:], in1=CW[:], op=ALU.subtract))
            dv(lambda e: e.tensor_tensor(out=CI[:], in0=CT[:, 0, :, :], in1=kib, op=ALU.mult))
            dv(lambda e: e.tensor_tensor(out=CW[:], in0=CT[:, 1, :, :], in1=krb, op=ALU.mult))
            dv(lambda e: e.tensor_tensor(out=CI[:], in0=CI[:], in1=CW[:], op=ALU.add))
            dv(lambda e: e.tensor_scalar(out=CI[:], in0=CI[:], scalar1=-1.0, scalar2=None, op0=ALU.mult))
        dsk = st.sb("s_dsk", [128, NCH], F32)
        k.dma("sp", dsk[:], dsk_dram, writes=[t_p])
        SI = None
        if p == 1:
            SI = st.sb("s_si", [128, S5P, 2], F32)
            k.dma("sp", SI[:], state_in, writes=[t_p])
        SO = st.sb("s_so", [128, S5P, 2], F32)
        t_so = DT("s_so")
        E2 = [st.sb("s_E%d" % i, [128, 2, TOK], F32) for i in range(2)]
        t_E2 = [DT("s_E%d" % i) for i in range(2)]
        MB2 = [st.sb("s_magb%d" % i, [128, TOK], F32) for i in range(1)] * 2
        t_mb2 = [DT("s_magb%d" % i) for i in range(1)] * 2
        BR = [st.sb("s_br%d" % i, [128, TOK], F32) for i in range(2)]
        BI = [st.sb("s_bi%d" % i, [128, TOK], F32) for i in range(2)]
        t_b = [DT("s_b%d" % i) for i in range(2)]
        XR = [st.sb("s_xr%d" % i, [128, TOK], BF16) for i in range(1)] * 2
        XI = [st.sb("s_xi%d" % i, [128, TOK], BF16) for i in range(1)] * 2
        t_x = [DT("s_x%d" % i) for i in range(1)] * 2
        sc = [st.sb("s_sc%d" % i, [128, 512], F32) for i in range(4)]
        sct = [DT("s_sc%d" % i) for i in range(4)]
        tsc = [st.sb("s_tsc%d" % i, [128, 640], F32) for i in range(2)] * 2
        tsct = [DT("s_tsc%d" % i) for i in range(2)] * 2
        ini = st.sb("s_ini", [128, 4], F32)
        t_ini = DT("s_ini")
        BD = [st.sb("s_bd%d" % i, [128, 8, 128], BF16) for i in range(2)]
        BDt = [DT("s_bd%d" % i) for i in range(2)]
        CD = [st.sb("s_cd%d" % i, [128, 8, 128], BF16) for i in range(2)]
        CDt = [DT("s_cd%d" % i) for i in range(2)]
        yb = [st.sb("s_yb%d" % i, [128, 512], F32) for i in range(2)]
        ybt = [DT("s_yb%d" % i) for i in range(2)]
        yi = [st.sb("s_yi%d" % i, [128, 512], F32) for i in range(2)]
        yit = [DT("s_yi%d" % i) for i in range(2)]
        zb = [st.sb("s_zb%d" % i, [128, 512], BF16) for i in range(2)]
        zbt = [DT("s_zb%d" % i) for i in range(2)]

        def table_gen(tn):
            nonlocal nsc
            E, t_E, MB_, t_mb = E2[tn % 2], t_E2[tn % 2], MB2[tn % 2], t_mb2[tn % 2]
            P_ = tn
            k.op("pool", lambda e: e.memset(E[:, 0, 0:1], 1.0), writes=[t_E])
            k.op("pool", lambda e: e.memset(E[:, 1, 0:1], 0.0), writes=[t_E])
            seg = 1
            l = 0
            emax = TOK if p == 0 else NLAT
            while seg < emax:
                n_ = min(seg, emax - seg)
                s4 = nsc % 4
                nsc += 1
                ur, ui, nui = UP[:, l, 0, P_:P_ + 1], UP[:, l, 1, P_:P_ + 1], UP[:, l, 2, P_:P_ + 1]
                k.op("act", lambda e: e.activation(out=tsc[s4][:, :n_], in_=E[:, 0, 0:n_], func=AF.Identity, scale=ur),
                     reads=[t_E, t_p], writes=[tsct[s4]])
                k.op("dve", lambda e: e.scalar_tensor_tensor(out=E[:, 0, seg:seg + n_], in0=E[:, 1, 0:n_], scalar=nui, in1=tsc[s4][:, :n_],
                                                             op0=ALU.mult, op1=ALU.add), reads=[t_E, t_p, tsct[s4]], writes=[t_E])
                s4 = nsc % 4
                nsc += 1
                k.op("act", lambda e: e.activation(out=tsc[s4][:, :n_], in_=E[:, 1, 0:n_], func=AF.Identity, scale=ur),
                     reads=[t_E, t_p], writes=[tsct[s4]])
                k.op("dve", lambda e: e.scalar_tensor_tensor(out=E[:, 1, seg:seg + n_], in0=E[:, 0, 0:n_], scalar=ui, in1=tsc[s4][:, :n_],
                                                             op0=ALU.mult, op1=ALU.add), reads=[t_E, t_p, tsct[s4]], writes=[t_E])
                seg *= 2
                l += 1
                yield

        nt = 0
        nsc = 0
        nyb = 0
        ntile = 0
        for dc in range(NCH):
            db = dc % 2
            emit_hchunk(k, C, 1, dc, rstd, t_rs, Hd[db], HdT[db], htmp, htmpt)
            k.dma("pool", BD[db][:], bd_dram[p, dc], writes=[BDt[db]])
            k.op("pool", lambda e: e.memset(CD[db][:], 0.0), writes=[CDt[db]])
            for j in range(4):
                P_ = 4 * dc + j
                for gp in range(2):
                    rows = slice(gp * 64, (gp + 1) * 64)
                    cols = slice((2 * j + gp) * 16, (2 * j + gp + 1) * 16)
                    k.op("act", lambda e: e.copy(out=CD[db][rows, j, cols], in_=CR[rows, P_, :]), reads=[t_p], writes=[CDt[db]])
                    k.op("act", lambda e: e.copy(out=CD[db][rows, 4 + j, cols], in_=CI[rows, P_, :]), reads=[t_p], writes=[CDt[db]])
            ypst = pst[4:7]
            for j in range(4):
                P_ = 4 * dc + j
                u = nt % 2
                nt += 1
                E, t_E, MB_, t_mb = E2[ntile % 2], t_E2[ntile % 2], MB2[ntile % 2], t_mb2[ntile % 2]
                if ntile == 0:
                    for _ in table_gen(0):
                        pass
                gen = table_gen(ntile + 1) if ntile + 1 < 4 * NCH else iter(())

                def step():
                    next(gen, None)
                ntile += 1
                k.op("act", lambda e: e.activation(out=MB_[:], in_=E[:, 0, :], func=AF.Identity, scale=0.0, bias=MAG[:, P_:P_ + 1]),
                     reads=[t_E, t_p], writes=[t_mb])
                for bi, (t0, tl, v) in enumerate(TBS):
                    ts = slice(t0, t0 + tl)
                    sg = segs[0] if (p == 0 or bi == 0) else segs[1]
                    es_ = _eslice(sg, t0, tl)
                    pu = (nsc % 2) * 2
                    k.op("pe", lambda e: e.matmul(ps[pu][:, :tl], lhsT=BD[db][:, j, :], rhs=Hd[db][:, ts], start=True, stop=True),
                         reads=[BDt[db], HdT[db]], writes=[pst[pu]])
                    k.op("pe", lambda e: e.matmul(ps[pu + 1][:, :tl], lhsT=BD[db][:, 4 + j, :], rhs=Hd[db][:, ts], start=True, stop=True),
                         reads=[BDt[db], HdT[db]], writes=[pst[pu + 1]])
                    a4, b4 = nsc % 4, (nsc + 1) % 4
                    nsc += 2
                    k.op("dve", lambda e: e.tensor_tensor(out=sc[a4][:, :tl], in0=ps[pu][:, :tl], in1=E[:, 0, es_], op=ALU.mult),
                         reads=[pst[pu], t_E], writes=[sct[a4]])
                    k.op("dve", lambda e: e.tensor_tensor(out=sc[b4][:, :tl], in0=ps[pu + 1][:, :tl], in1=E[:, 1, es_], op=ALU.mult),
                         reads=[pst[pu + 1], t_E], writes=[sct[b4]])
                    k.op("dve", lambda e: e.tensor_tensor(out=BR[u][:, ts], in0=sc[a4][:, :tl], in1=sc[b4][:, :tl], op=ALU.add),
                         reads=[sct[a4], sct[b4]], writes=[t_b[u]])
                    step()
                    a4, b4 = nsc % 4, (nsc + 1) % 4
                    nsc += 2
                    k.op("dve", lambda e: e.tensor_tensor(out=sc[a4][:, :tl], in0=ps[pu + 1][:, :tl], in1=E[:, 0, es_], op=ALU.mult),
                         reads=[pst[pu + 1], t_E], writes=[sct[a4]])
                    k.op("dve", lambda e: e.tensor_tensor(out=sc[b4][:, :tl], in0=ps[pu][:, :tl], in1=E[:, 1, es_], op=ALU.mult),
                         reads=[pst[pu], t_E], writes=[sct[b4]])
                    k.op("dve", lambda e: e.tensor_tensor(out=BI[u][:, ts], in0=sc[a4][:, :tl], in1=sc[b4][:, :tl], op=ALU.subtract),
                         reads=[sct[a4], sct[b4]], writes=[t_b[u]])
                    step()
                for si, sg in enumerate(segs):
                    t0, n_, rev = sg
                    vs = slice(t0 + n_ - 1, t0 - 1 if t0 > 0 else None, -1) if rev else slice(t0, t0 + n_)
                    if p == 1 and si == 1:
                        x0r, x0i = SI[:, P_, 0:1], SI[:, P_, 1:2]
                        ur, ui, nui = UP[:, 0, 0, P_:P_ + 1], UP[:, 0, 1, P_:P_ + 1], UP[:, 0, 2, P_:P_ + 1]
                        k.op("dve", lambda e: e.tensor_tensor(out=ini[:, 2:3], in0=x0r, in1=ur, op=ALU.mult), reads=[t_p], writes=[t_ini])
                        k.op("dve", lambda e: e.scalar_tensor_tensor(out=ini[:, 0:1], in0=x0i, scalar=nui, in1=ini[:, 2:3], op0=ALU.mult, op1=ALU.add),
                             reads=[t_p, t_ini], writes=[t_ini])
                        k.op("dve", lambda e: e.tensor_tensor(out=ini[:, 3:4], in0=x0i, in1=ur, op=ALU.mult), reads=[t_p, t_ini], writes=[t_ini])
                        k.op("dve", lambda e: e.scalar_tensor_tensor(out=ini[:, 1:2], in0=x0r, scalar=ui, in1=ini[:, 3:4], op0=ALU.mult, op1=ALU.add),
                             reads=[t_p, t_ini], writes=[t_ini])
                        i_r, i_i = ini[:, 0:1], ini[:, 1:2]
                    else:
                        i_r, i_i = 0.0, 0.0
                    k.op("dve", lambda e: e.tensor_tensor_scan(out=BR[u][:, vs], data0=MB_[:, 0:n_], data1=BR[u][:, vs], initial=i_r,
                                                               op0=ALU.mult, op1=ALU.add), reads=[t_mb, t_ini], writes=[t_b[u]])
                    k.op("dve", lambda e: e.tensor_tensor_scan(out=BI[u][:, vs], data0=MB_[:, 0:n_], data1=BI[u][:, vs], initial=i_i,
                                                               op0=ALU.mult, op1=ALU.add), reads=[t_mb, t_ini], writes=[t_b[u]])
                    step()
                for bi, (t0, tl, v) in enumerate(TBS):
                    ts = slice(t0, t0 + tl)
                    sg = segs[0] if (p == 0 or bi == 0) else segs[1]
                    es_ = _eslice(sg, t0, tl)
                    a4, b4 = nsc % 4, (nsc + 1) % 4
                    nsc += 2
                    k.op("dve", lambda e: e.tensor_tensor(out=sc[a4][:, :tl], in0=BR[u][:, ts], in1=E[:, 0, es_], op=ALU.mult),
                         reads=[t_b[u], t_E], writes=[sct[a4]])
                    k.op("dve", lambda e: e.tensor_tensor(out=sc[b4][:, :tl], in0=BI[u][:, ts], in1=E[:, 1, es_], op=ALU.mult),
                         reads=[t_b[u], t_E], writes=[sct[b4]])
                    k.op("dve", lambda e: e.tensor_tensor(out=XR[u][:, ts], in0=sc[a4][:, :tl], in1=sc[b4][:, :tl], op=ALU.subtract),
                         reads=[sct[a4], sct[b4]], writes=[t_x[u]])
                    step()
                    if p == 0 and bi == 2:
                        k.op("act", lambda e: e.activation(out=ini[:, 2:3], in_=sc[a4][:, tl - 1:tl], func=AF.Identity,
                                                           bias=sc[b4][:, tl - 1:tl], scale=1.0), reads=[sct[a4], sct[b4]], writes=[t_ini])
                        k.op("dve", lambda e: e.tensor_scalar(out=SO[:, P_, 0:1], in0=sc[b4][:, tl - 1:tl], scalar1=-2.0, scalar2=ini[:, 2:3],
                                                              op0=ALU.mult, op1=ALU.add), reads=[sct[b4], t_ini], writes=[t_so])
                    a4, b4 = nsc % 4, (nsc + 1) % 4
                    nsc += 2
                    k.op("dve", lambda e: e.tensor_tensor(out=sc[a4][:, :tl], in0=BI[u][:, ts], in1=E[:, 0, es_], op=ALU.mult),
                         reads=[t_b[u], t_E], writes=[sct[a4]])
                    k.op("dve", lambda e: e.tensor_tensor(out=sc[b4][:, :tl], in0=BR[u][:, ts], in1=E[:, 1, es_], op=ALU.mult),
                         reads=[t_b[u], t_E], writes=[sct[b4]])
                    k.op("dve", lambda e: e.tensor_tensor(out=XI[u][:, ts], in0=sc[a4][:, :tl], in1=sc[b4][:, :tl], op=ALU.add),
                         reads=[sct[a4], sct[b4]], writes=[t_x[u]])
                    step()
                    if p == 0 and bi == 2:
                        k.op("dve", lambda e: e.tensor_tensor(out=SO[:, P_, 1:2], in0=sc[a4][:, tl - 1:tl], in1=sc[b4][:, tl - 1:tl], op=ALU.add),
                             reads=[sct[a4], sct[b4]], writes=[t_so])
                    yp = ps[4 + bi]
                    k.op("pe", lambda e: e.matmul(yp[:, :tl], lhsT=CD[db][:, j, :], rhs=XR[u][:, ts], start=(j == 0), stop=False),
                         reads=[CDt[db], t_x[u]], writes=[ypst[bi]], inc=False)
                    k.op("pe", lambda e: e.matmul(yp[:, :tl], lhsT=CD[db][:, 4 + j, :], rhs=XI[u][:, ts], start=False, stop=(j == 3)),
                         reads=[CDt[db], t_x[u]], writes=[ypst[bi]], inc=True)
                for _ in gen:
                    pass
            for bi, (t0, tl, v) in enumerate(TBS):
                ts = slice(t0, t0 + tl)
                q = nyb % 2
                nyb += 1
                if p == 0:
                    k.op("dve", lambda e: e.scalar_tensor_tensor(out=yb[q][:, :tl], in0=Hd[db][:, ts], scalar=dsk[:, dc:dc + 1], in1=ps[4 + bi][:, :tl],
                                                                 op0=ALU.mult, op1=ALU.add), reads=[HdT[db], t_p, ypst[bi]], writes=[ybt[q]])
                    k.dma("sp", y_out[dc, :, ts], yb[q][:, :tl], reads=[ybt[q]], final=True)
                else:
                    k.dma("sp", yi[q][:, :tl], y_in[dc, :, ts], writes=[yit[q]])
                    k.op("dve", lambda e: e.tensor_tensor(out=yb[q][:, :tl], in0=ps[4 + bi][:, :tl], in1=yi[q][:, :tl], op=ALU.add),
                         reads=[ypst[bi], yit[q]], writes=[ybt[q]])
                    k.op("dve", lambda e: e.tensor_tensor(out=yi[q][:, :tl], in0=yb[q][:, :tl], in1=yb[q][:, :tl], op=ALU.mult),
                         reads=[ybt[q]], writes=[yit[q]])
                    k.op("dve", lambda e: e.tensor_scalar(out=yi[q][:, :tl], in0=yi[q][:, :tl], scalar1=0.044715, scalar2=1.0, op0=ALU.mult, op1=ALU.add),
                         reads=[], writes=[yit[q]])
                    k.op("dve", lambda e: e.tensor_tensor(out=yi[q][:, :tl], in0=yi[q][:, :tl], in1=yb[q][:, :tl], op=ALU.mult),
                         reads=[ybt[q]], writes=[yit[q]])
                    k.op("act", lambda e: e.activation(out=yi[q][:, :tl], in_=yi[q][:, :tl], func=AF.Tanh, scale=0.7978845608028654),
                         reads=[], writes=[yit[q]])
                    k.op("act", lambda e: e.mul(out=yb[q][:, :tl], in_=yb[q][:, :tl], mul=0.5),
                         reads=[], writes=[ybt[q]])
                    k.op("dve", lambda e: e.scalar_tensor_tensor(out=zb[q][:, :tl], in0=yi[q][:, :tl], scalar=1.0, in1=yb[q][:, :tl],
                                                                 op0=ALU.add, op1=ALU.mult), reads=[yit[q], ybt[q]], writes=[zbt[q]])
                    k.dma("sp", Z[dc, :, ts], zb[q][:, :tl], reads=[zbt[q]], final=True)
        if p == 0:
            k.dma("sp", state_out, SO[:], reads=[t_so], final=True)


def emit_s5_glu(k, C, st, Z, ZT, wglu_dram):
    X = C.X
    NW = 3
    wt = [st.sb("g_w%d" % i, [128, NCH, 256], BF16) for i in range(NW)]
    wtt = [DT("g_w%d" % i) for i in range(NW)]
    sg = [st.sb("g_sg%d" % i, [128, 512], F32) for i in range(2)]
    sgt = [DT("g_sg%d" % i) for i in range(2)]
    ps = [st.ps("gps%d" % i, [128, 512], F32) for i in range(4)]
    pst = [DT("gps%d" % i) for i in range(4)]
    w_v = wglu_dram.rearrange("(c p) n -> p c n", p=128)
    nu = 0
    for dc in range(NCH):
        b = dc % NW
        k.dma("pool", wt[b][:, :, 0:128], w_v[:, :, dc * 128:(dc + 1) * 128], writes=[wtt[b]])
        k.dma("pool", wt[b][:, :, 128:256], w_v[:, :, D + dc * 128:D + (dc + 1) * 128], writes=[wtt[b]])
        for bi, (t0, tl, v) in enumerate(TBS):
            ts = slice(t0, t0 + tl)
            u = nu % 2
            nu += 1
            pa, pg = ps[2 * u], ps[2 * u + 1]
            for c in range(NCH):
                k.op("pe", lambda e: e.matmul(pa[:, :tl], lhsT=wt[b][:, c, 0:128], rhs=Z[:, c, ts], start=(c == 0), stop=(c == NCH - 1)),
                     reads=[wtt[b], ZT[bi]], writes=[pst[2 * u]], inc=(c == NCH - 1))
            for c in range(NCH):
                k.op("pe", lambda e: e.matmul(pg[:, :tl], lhsT=wt[b][:, c, 128:256], rhs=Z[:, c, ts], start=(c == 0), stop=(c == NCH - 1)),
                     reads=[wtt[b], ZT[bi]], writes=[pst[2 * u + 1]], inc=(c == NCH - 1))
            k.op("act", lambda e: e.activation(out=sg[u][:, :tl], in_=pg[:, :tl], func=AF.Sigmoid), reads=[pst[2 * u + 1]], writes=[sgt[u]])
            k.op("dve", lambda e: e.tensor_tensor(out=sg[u][:, :tl], in0=pa[:, :tl], in1=sg[u][:, :tl], op=ALU.mult),
                 reads=[pst[2 * u], sgt[u]], writes=[sgt[u]])
            k.op("dve", lambda e: e.scalar_tensor_tensor(out=X[:, dc, ts], in0=sg[u][:, :tl], scalar=C.G[:, v, 1, dc:dc + 1], in1=X[:, dc, ts],
                                                          op0=ALU.mult, op1=ALU.add), reads=[sgt[u], C.t_mv, C.XT[bi]], writes=[C.XT[bi]])


def s5_host_arrays(I, half):
    occ = 0
    dirs = [0, 1] if half == 0 else [1, 0]
    lam = np.zeros((2, 128, 3, S5P), np.float32)
    bd = np.zeros((2, NCH, 128, 8, 128), np.float32)
    cT = np.zeros((2, 128, 2, S5P, 16), np.float32)
    for s, dr in enumerate(dirs):
        for nm, idx in (("s5_lam_re", 0), ("s5_lam_im", 1)):
            a = I[nm][occ, dr].reshape(S5P, 2, 64)
            lam[s, :, idx, :] = a.transpose(1, 2, 0).reshape(128, S5P)
        ld = np.broadcast_to(I["s5_log_dt"][occ, dr].reshape(S5P, 2, 1), (S5P, 2, 64))
        lam[s, :, 2, :] = ld.transpose(1, 2, 0).reshape(128, S5P)
        for ri, nm in enumerate(("s5_b_re", "s5_b_im")):
            B = I[nm][occ, dr]
            for dc in range(NCH):
                for j in range(4):
                    for gp in range(2):
                        g = 8 * dc + 2 * j + gp
                        gl = 2 * j + gp
                        bd[s, dc, gl * 16:(gl + 1) * 16, ri * 4 + j, gp * 64:(gp + 1) * 64] = B[g].T
        for ri, nm in enumerate(("s5_c_re", "s5_c_im")):
            Cc = I[nm][occ, dr].reshape(S5P, 2, 16, 64)
            cT[s, :, ri, :, :] = Cc.transpose(1, 3, 0, 2).reshape(128, S5P, 16)
    return lam, bd, cT


NVC = 2
NACT = 4


class FProg:
    def __init__(self):
        self.nc = bass.Bass("TRN2", target_bir_lowering=False)
        self.ins = {}
        self.scr = {}
        self.outs = {}

    def inp(self, name, shape, dt=F32):
        if name in self.scr:
            return self.scr[name]
        if name not in self.ins:
            self.ins[name] = self.nc.dram_tensor(name, list(shape), dt, kind="ExternalInput").ap()
        return self.ins[name]

    def tmp(self, name, shape, dt=F32):
        if name not in self.scr:
            self.scr[name] = self.nc.dram_tensor(name, list(shape), dt).ap()
        return self.scr[name]

    def out(self, name, shape, dt=F32):
        if name not in self.outs:
            self.outs[name] = self.nc.dram_tensor(name, list(shape), dt, kind="ExternalOutput").ap()
        return self.outs[name]


SEGMENTS = [
    [("modvec", 0), ("ffn", 0, 0), ("mix1", 0)],
    [("mix2", 0), ("ffn", 0, 1), ("modvec", 1), ("ffn", 1, 0), ("mix1", 1)],
    [("mix2", 1), ("ffn", 1, 1), ("modvec", 2), ("ffn", 2, 0), ("mix1", 2)],
    [("mix2", 2), ("ffn", 2, 1), ("modvec", 3), ("ffn", 3, 0), ("mix1", 3)],
    [("mix2", 3), ("ffn", 3, 1)],
]


def set_layer(C, i):
    m = C.mvsets[i % 2]
    C.MV, C.MB, C.NG, C.A, C.G, C.t_mv = m


def emit_step(k, C, P, step, v):
    op = step[0]
    sf = "_v%d" % v
    so = "_v%d" % (1 - v)
    if op == "modvec":
        i = step[1]
        if v != 0:
            return
        set_layer(C, i)
        with Stage(k) as st:
            emit_modvec(k, C, st, P.inp("condT", [128, 2, NCH]), P.inp("modw%d" % i, [D, 9 * D]),
                        P.inp("modb%d" % i, [128, 9 * NCH]), P.inp("ng%d" % i, [128, 3, NCH]), P.inp("eye2", [2, 2]))
    elif op == "ffn":
        i, j = step[1], step[2]
        set_layer(C, i)
        emit_ffn(k, C, 0 if j == 0 else 2, P.inp("wi%d_%d" % (i, j), [D, 2 * DFF]),
                 P.inp("wo%d_%d" % (i, j), [DFF, D]), lat_only=(i == 3 and j == 1))
    elif op == "mix1":
        i = step[1]
        set_layer(C, i)
        kind = KINDS[i % 4]
        if kind in ("a", "w"):
            emit_qkv(k, C, P.inp("wqkv%d" % i, [D, QKV_W]), P.inp("qkg%d" % i, [128, 2]),
                     P.inp("cos" + sf, [128, NLAT]), P.inp("sin" + sf, [128, NLAT]), P.inp("rm", [128, 128]),
                     P.tmp("qT%d" % i + sf, [NH, 128, TOK], BF16), P.tmp("kT%d" % i + sf, [NKV, 128, TOK], BF16),
                     P.tmp("v%d" % i + sf, [TOK, 512], BF16))
        elif kind == "m":
            mq = P.tmp("mq%d" % i + sf, [MH, 128, TOK], BF16)
            mk_ = P.tmp("mk%d" % i + sf, [MH, 128, TOK], BF16)
            mkt = P.tmp("mkt%d" % i + sf, [TOK, MH * MDQK], BF16)
            mvt = P.tmp("mvt%d" % i + sf, [TOK, MH * MDV], BF16)
            mso = P.tmp("mso%d" % i + sf, [NCH, 128, TOK], F32)
            mgi = P.tmp("mgi%d" % i + sf, [TOK, 32], F32)
            emit_mlstm_proj(k, C, P.inp("mwin%d" % i, [D, 6144]), P.inp("mwg%d" % i + sf, [D, 32]),
                            P.inp("mbg%d" % i + sf, [128, 32]), mq, mk_, mkt, mvt, mso, mgi)
            with Stage(k) as st:
                emit_mlstm_scan(k, C, st, 0, mq, mk_, mkt, mvt, mgi, P.inp("mtri", [2, 128, 128]),
                                P.inp("mmadd", [2, 128, 128]), None,
                                (P.tmp("mstC%d" % i + sf, [MH, 128, MDV]), P.tmp("mstN%d" % i + sf, [MH, 128, 128])),
                                None, P.tmp("mh1_%d" % i + sf, [NCH, 128, TOK]))
        elif kind == "s":
            emit_s5_phase(k, C, 0, P.inp("s5lam" + sf, [2, 128, 3, S5P]), P.inp("s5bd" + sf, [2, NCH, 128, 8, 128]),
                          P.inp("s5cT" + sf, [2, 128, 2, S5P, 16]), P.inp("s5dsk", [128, NCH]), None,
                          P.tmp("s5st%d" % i + sf, [128, S5P, 2]), None, P.tmp("s5y1_%d" % i + sf, [NCH, 128, TOK]), None, None)
    elif op == "mix2":
        i = step[1]
        set_layer(C, i)
        kind = KINDS[i % 4]
        if kind in ("a", "w"):
            kts = [P.scr["kT%d_v%d" % (i, q)] for q in range(NVC)]
            vs = [P.scr["v%d_v%d" % (i, q)] for q in range(NVC)]
            other = (kts, vs) if kind == "a" else (kts[1 - v], vs[1 - v])
            emit_attn(k, C, kind == "w", i != 3, P.scr["qT%d" % i + sf], kts[v], vs[v], other,
                      P.inp("esink%d" % i, [128, NH]) if kind == "w" else None,
                      P.inp("masks", [128, 4, 128], BF16) if kind == "w" else None,
                      P.inp("awo%d" % i, [D, D]))
        elif kind == "m":
            hs = P.tmp("mhs%d" % i + sf, [NCH, 128, TOK])
            with Stage(k) as st:
                emit_mlstm_scan(k, C, st, 1, P.scr["mq%d" % i + sf], P.scr["mk%d" % i + sf], P.scr["mkt%d" % i + sf],
                                P.scr["mvt%d" % i + sf], P.scr["mgi%d" % i + sf],
                                P.inp("mtri", [2, 128, 128]), P.inp("mmadd", [2, 128, 128]),
                                (P.scr["mstC%d" % i + so], P.scr["mstN%d" % i + so]),
                                None, P.scr["mh1_%d" % i + sf], hs)
            with Stage(k) as st:
                emit_mlstm_readout(k, C, st, hs, P.scr["mso%d" % i + sf],
                                   P.inp("mng%d" % i, [128, NCH]), P.inp("mwout%d" % i, [D, D]), True)
        elif kind == "s":
            zd = P.tmp("s5z%d" % i + sf, [NCH, 128, TOK], BF16)
            emit_s5_phase(k, C, 1, P.inp("s5lam" + sf, [2, 128, 3, S5P]), P.inp("s5bd" + sf, [2, NCH, 128, 8, 128]),
                          P.inp("s5cT" + sf, [2, 128, 2, S5P, 16]), P.inp("s5dsk", [128, NCH]),
                          P.scr["s5st%d" % i + so], None, P.scr["s5y1_%d" % i + sf], None, zd, None)
            with Stage(k) as st:
                Z = st.sb("s_z", [128, NCH, TOK], BF16)
                ZT = [DT("z%d" % q) for q in range(3)]
                for q, (t0, tl, vv) in enumerate(TBS):
                    k.dma("sp", Z[:, :, t0:t0 + tl], zd[:, :, t0:t0 + tl].rearrange("c p t -> p c t"), writes=[ZT[q]])
                emit_s5_glu(k, C, st, Z, ZT, P.inp("s5wglu", [D, 2 * D]))
    else:
        raise ValueError(op)


def build_fused(segments=None, nvc=NVC):
    segments = SEGMENTS if segments is None else segments
    P = FProg()
    nc = P.nc
    with ExitStack() as es:
        k = KB(nc, es)
        C = Ctx()
        with Stage(k) as st0:
            C.X = st0.sb("X", [128, NCH, TOK], F32)
            C.XT = [DT("x%d" % i) for i in range(3)]
            C.mvsets = []
            for q in range(2):
                C.mvsets.append((st0.sb("MV%d" % q, [128, 2, 9 * NCH], F32), st0.sb("MB%d" % q, [128, 9 * NCH], F32),
                                 st0.sb("NG%d" % q, [128, 3, NCH], F32), st0.sb("A%d" % q, [128, 2, 3, NCH], F32),
                                 st0.sb("G%d" % q, [128, 2, 3, NCH], F32), DT("mv%d" % q)))
            setup_consts(k, st0, C)
            for si, seg in enumerate(segments):
                last = si == len(segments) - 1
                for v in range(nvc):
                    sf = "_v%d" % v
                    src = P.inp("xT" + sf, [D, TOK]) if si == 0 else P.scr["xpark" + sf]
                    load_x(k, C, src)
                    for step in seg:
                        emit_step(k, C, P, step, v)
                    if last:
                        store_x(k, C, P.out("xo" + sf, [D, NLAT]), True)
                    else:
                        store_x(k, C, P.tmp("xpark" + sf, [D, TOK]), False)
                    k.barrier()
            k.finish()
        P.ninstr = k.ninstr
    return P


def host_inputs(I, b, names):
    out = {}
    s5 = None
    for name in names:
        v = None
        base = name
        if len(name) > 3 and name[-3:-1] == "_v":
            v = int(name[-1])
            base = name[:-3]
        if base == "xT":
            out[name] = core_tokens_T(I["x"], I["ctx"], 2 * b + v)
        elif base == "condT":
            out[name] = vec_pm(np.stack([I["c"][b], I["c_ctx"]]))
        elif base == "rm":
            out[name] = rot_matrix()
        elif base == "eye2":
            out[name] = np.eye(2, dtype=np.float32)
        elif base == "masks":
            out[name] = win_masks()
        elif base == "cos":
            out[name] = rope_tables(v)[0]
        elif base == "sin":
            out[name] = rope_tables(v)[1]
        elif base == "mtri":
            out[name] = mlstm_consts()[0]
        elif base == "mmadd":
            out[name] = mlstm_consts()[1]
        elif base in ("s5lam", "s5bd", "s5cT"):
            if s5 is None:
                s5 = [s5_host_arrays(I, h) for h in range(2)]
            out[name] = s5[v][("s5lam", "s5bd", "s5cT").index(base)]
        elif base == "s5dsk":
            out[name] = vec_pm(I["s5_d"][0])
        elif base == "s5wglu":
            out[name] = I["s5_w_glu"][0]
        else:
            i = int(base[-3]) if base[-2] == "_" else int(base[-1])
            bb = base[:-3] if base[-2] == "_" else base[:-1]
            pre = "a_" if i % 4 == 0 else "w_"
            if bb == "modw":
                out[name] = I["mod_w"][i]
            elif bb == "modb":
                out[name] = vec_pm(I["mod_b"][i])
            elif bb == "ng":
                out[name] = vec_pm(I["norm_g"][i])
            elif bb == "wi":
                out[name] = I["ffn_wi"][i, int(base[-1])]
            elif bb == "wo":
                out[name] = I["ffn_wo"][i, int(base[-1])]
            elif bb == "wqkv":
                out[name] = I[pre + "wqkv"][0]
            elif bb == "qkg":
                out[name] = np.ascontiguousarray(I[pre + "qk_g"][0].T)
            elif bb == "awo":
                out[name] = I[pre + "wo"][0]
            elif bb == "esink":
                out[name] = np.ascontiguousarray(np.broadcast_to(I["w_sink"][0][None, :], (128, NH)))
            elif bb == "mwin":
                out[name] = I["m_w_in"][0]
            elif bb in ("mwg", "mbg"):
                perm = np.arange(32) if v == 0 else np.concatenate([np.arange(16, 32), np.arange(0, 16)])
                if bb == "mwg":
                    out[name] = np.ascontiguousarray(I["m_w_gate"][0][:, perm])
                else:
                    out[name] = np.ascontiguousarray(np.broadcast_to(I["m_b_gate"][0][perm][None, :], (128, 32)))
            elif bb == "mng":
                out[name] = vec_pm(I["m_norm_g"][0])
            elif bb == "mwout":
                out[name] = I["m_w_out"][0]
            else:
                raise KeyError(name)
    return out


def kernel(**inputs):
    I = {k_: np.asarray(v) for k_, v in inputs.items()}
    P = build_fused()
    names = list(P.ins)
    in_maps = [host_inputs(I, b, names) for b in range(NACT)]
    res = run_bass_kernel_spmd(P.nc, in_maps, core_ids=list(range(NACT)))
    B = I["x"].shape[0]
    out = np.empty((B, SEQ, D), np.float32)
    for b in range(NACT):
        for v in range(NVC):
            y = res.results[b]["xo_v%d" % v].T
            if v == 1:
                y = y[::-1]
            out[b, v * NLAT:(v + 1) * NLAT] = y
    return out
```

```python
import numpy as np
from contextlib import ExitStack
import concourse.bass as bass
import concourse.mybir as mybir
from concourse.bass_utils import run_bass_kernel_spmd

F32 = mybir.dt.float32
BF16 = mybir.dt.bfloat16
AF = mybir.ActivationFunctionType
ALU = mybir.AluOpType
AX = mybir.AxisListType

D = 2048
NCH = 16
DFF = 5632
NFC = 44
NCTX = 256
NLAT = 1024
TOK = NCTX + NLAT
SEQ = 2048
EPS = 1e-6
TBS = [(0, 256, 1), (256, 512, 0), (768, 512, 0)]
NCORES = 8


class DT:
    __slots__ = ("name", "w", "r", "dsem", "dcnt")

    def __init__(self, name=""):
        self.name = name
        self.w = {}
        self.r = {}
        self.dsem = None
        self.dcnt = 0


class KB:
    def __init__(self, nc, es):
        self.nc = nc
        self.es = es
        self.engs = {"pe": nc.tensor, "act": nc.scalar, "dve": nc.vector, "pool": nc.gpsimd, "sp": nc.sync}
        self.sem = {n: es.enter_context(nc.semaphore("s_" + n)) for n in ("pe", "act", "dve", "pool")}
        self.cnt = {n: 0 for n in self.sem}
        self.waited = {n: {} for n in self.engs}
        self.bound = []
        self.all_sems = []
        self.free_sems = []
        self.scount = {}
        self.final = []
        self.nsem = 0
        self.ninstr = 0
        self.uid = 0

    def _wait(self, e, deps):
        need = {}
        for d in deps:
            for s, v in d.items():
                if need.get(s, 0) < v:
                    need[s] = v
        w = self.waited[e]
        for s, v in need.items():
            if e == "pe" and s is self.sem["pe"]:
                continue
            if w.get(s, 0) >= v:
                continue
            self.engs[e].wait_ge(s, v)
            w[s] = v

    def op(self, e, fn, reads=(), writes=(), inc=True):
        deps = [t.w for t in reads]
        for t in writes:
            deps.append(t.w)
            deps.append(t.r)
        self._wait(e, deps)
        ins = fn(self.engs[e])
        self.ninstr += 1
        s = self.sem[e]
        if inc:
            self.cnt[e] += 1
            ins.then_inc(s, 1)
            v = self.cnt[e]
        else:
            v = self.cnt[e] + 1
        for t in reads:
            t.r[s] = v
        for t in writes:
            t.w[s] = v
            t.r = {}
        return ins

    def dma(self, q, out, in_, reads=(), writes=(), final=False):
        deps = [t.w for t in reads]
        for t in writes:
            deps.append(t.w)
            deps.append(t.r)
        self._wait(q, deps)
        t0 = (list(writes) + list(reads))[0]
        if t0.dsem is None:
            if self.free_sems:
                t0.dsem = self.free_sems.pop()
            else:
                t0.dsem = self.es.enter_context(self.nc.semaphore("d%d" % self.nsem))
                self.nsem += 1
                self.all_sems.append(t0.dsem)
                self.scount[t0.dsem] = 0
            self.bound.append(t0)
        self.scount[t0.dsem] += 16
        cnt = self.scount[t0.dsem]
        ins = self.engs[q].dma_start(out=out, in_=in_).then_inc(t0.dsem, 16)
        self.ninstr += 1
        for t in reads:
            t.r[t0.dsem] = cnt
        for t in writes:
            t.w[t0.dsem] = cnt
            t.r = {}
        if final:
            self.final.append({t0.dsem: cnt})
        return ins

    def barrier(self):
        allt = {self.sem[n]: self.cnt[n] for n in self.sem if self.cnt[n] > 0}
        for sm in self.all_sems:
            if self.scount[sm] > 0:
                allt[sm] = self.scount[sm]
        for e in self.engs:
            self._wait(e, [allt])
        for t in self.bound:
            t.dsem = None
        self.bound = []
        self.free_sems = list(self.all_sems)

    def finish(self):
        self._wait("sp", self.final)


class Stage:
    def __init__(self, k):
        self.k = k
        self.es = ExitStack()

    def __enter__(self):
        self.es.__enter__()
        return self

    def __exit__(self, *a):
        self.k.barrier()
        return self.es.__exit__(*a)

    def sb(self, name, shape, dt):
        self.k.uid += 1
        return self.es.enter_context(self.k.nc.sbuf_tensor("sb%d_%s" % (self.k.uid, name), list(shape), dt))

    def ps(self, name, shape, dt=F32):
        self.k.uid += 1
        return self.es.enter_context(self.k.nc.psum_tensor("ps%d_%s" % (self.k.uid, name), list(shape), dt))


class Ctx:
    pass


def setup_consts(k, st, C):
    nc = k.nc
    C.ones_f = st.sb("ones_f", [128, 128], F32)
    C.ones_b = st.sb("ones_b", [128, 128], BF16)
    C.t_const = DT("const")
    k.op("pool", lambda e: e.memset(C.ones_f[:], 1.0), writes=[C.t_const])
    k.op("pool", lambda e: e.memset(C.ones_b[:], 1.0), writes=[C.t_const])
    C.eps_col = st.sb("eps_col", [128, 2], F32)
    k.op("pool", lambda e: e.memset(C.eps_col[:], EPS), writes=[C.t_const])
    C.one_col = st.sb("one_col", [128, 2], F32)
    k.op("pool", lambda e: e.memset(C.one_col[:], 1.0), writes=[C.t_const])


def emit_modvec(k, C, st, condT_dram, modw_dram, modb_dram, ng_dram, i2_dram):
    C.t_mv = DT("mv")
    cond = st.sb("cond", [128, 2, NCH], F32)
    condb = st.sb("condb", [128, NCH, 2], BF16)
    t_cond = DT("cond")
    k.dma("sp", cond[:], condT_dram, writes=[t_cond])
    t_ng = DT("ng")
    k.dma("sp", C.NG[:], ng_dram, writes=[C.t_mv])
    k.dma("sp", C.MB[:], modb_dram, writes=[C.t_mv])
    t_condb = DT("condb")
    for v in range(2):
        k.op("act", lambda e: e.activation(out=condb[:, :, v], in_=cond[:, v, :], func=AF.Silu),
             reads=[t_cond], writes=[t_condb])
    NW = 3
    CW = 512
    wt = [st.sb("mw%d" % i, [128, NCH, CW], BF16) for i in range(NW)]
    wtt = [DT("mw%d" % i) for i in range(NW)]
    rows = st.sb("mvrows", [2, 9 * D], F32)
    t_rows = DT("mvrows")
    i2 = st.sb("mvi2", [2, 2], F32)
    k.dma("sp", i2[:], i2_dram, writes=[t_rows])
    pr = [st.ps("mvpr%d" % i, [128, 512], F32) for i in range(2)]
    prt = [DT("mvpr%d" % i) for i in range(2)]
    ps = st.ps("mvps", [128, 512], F32)
    pst = DT("mvps")
    modw_v = modw_dram.rearrange("(c p) n -> p c n", p=128)
    ntile = (9 * D) // CW
    for ti in range(ntile):
        b = ti % NW
        u = ti % 2
        k.dma("pool", wt[b][:], modw_v[:, :, ti * CW:(ti + 1) * CW], writes=[wtt[b]])
        for kc in range(NCH):
            k.op("pe", lambda e: e.matmul(pr[u][0:2, :], lhsT=condb[:, kc, :], rhs=wt[b][:, kc, :], start=(kc == 0),
                                          stop=(kc == NCH - 1)),
                 reads=[wtt[b], t_condb], writes=[prt[u]], inc=(kc == NCH - 1))
        k.op("act", lambda e: e.copy(out=rows[:, ti * CW:(ti + 1) * CW], in_=pr[u][0:2, :]), reads=[prt[u]], writes=[t_rows])
    for cc in range(9 * NCH):
        k.op("pe", lambda e: e.matmul(ps[:, cc * 2:cc * 2 + 2], lhsT=rows[:, cc * 128:(cc + 1) * 128], rhs=i2[:],
                                      start=True, stop=True), reads=[t_rows], writes=[pst], inc=(cc == 9 * NCH - 1))
    psv = ps[:, 0:288].rearrange("p (mc v) -> p v mc", v=2)
    for v in range(2):
        k.op("dve", lambda e: e.tensor_tensor(out=C.MV[:, v, :], in0=psv[:, v, :], in1=C.MB[:], op=ALU.add),
             reads=[pst, C.t_mv], writes=[C.t_mv])
    for v in range(2):
        for j in range(3):
            sc = C.MV[:, v, (3 * j + 1) * NCH:(3 * j + 2) * NCH]
            k.op("dve", lambda e: e.tensor_scalar(out=C.A[:, v, j, :], in0=sc, scalar1=1.0, scalar2=1.0,
                                                  op0=ALU.add, op1=ALU.mult), reads=[C.t_mv], writes=[C.t_mv])
            k.op("dve", lambda e: e.tensor_tensor(out=C.A[:, v, j, :], in0=C.A[:, v, j, :], in1=C.NG[:, j, :],
                                                  op=ALU.mult), reads=[C.t_mv], writes=[C.t_mv])
            gt = C.MV[:, v, (3 * j + 2) * NCH:(3 * j + 3) * NCH]
            k.op("dve", lambda e: e.tensor_scalar(out=C.G[:, v, j, :], in0=gt, scalar1=(1.0 if j == 1 else 0.5),
                                                  scalar2=None, op0=ALU.mult), reads=[C.t_mv], writes=[C.t_mv])


def emit_norm_mod(k, C, st, j, H, HT, pss, psst, tbs=None):
    X = C.X
    sq = [st.sb("sq%d_%d" % (j, i), [128, 512], F32) for i in range(2)]
    sqt = [DT("sq") for _ in range(2)]
    tmp = [st.sb("nt%d_%d" % (j, i), [128, 512], F32) for i in range(2)]
    tmpt = [DT("nt") for _ in range(2)]
    rstd = st.sb("rstd%d" % j, [128, TOK], F32)
    n = 0
    for bi, (t0, tl, v) in enumerate(TBS):
        if tbs is not None and bi not in tbs:
            continue
        ts = slice(t0, t0 + tl)
        pb = bi % 2
        for c in range(NCH):
            b = n % 2
            n += 1
            k.op("act", lambda e: e.activation(out=sq[b][:, :tl], in_=X[:, c, ts], func=AF.Square),
                 reads=[C.XT[bi]], writes=[sqt[b]])
            k.op("pe", lambda e: e.matmul(pss[pb][:, :tl], lhsT=C.ones_f[:], rhs=sq[b][:, :tl], start=(c == 0),
                                          stop=(c == NCH - 1)),
                 reads=[sqt[b], C.t_const], writes=[psst[pb]], inc=True)
        t_r = DT("rstd")
        k.op("act", lambda e: e.activation(out=rstd[:, ts], in_=pss[pb][:, :tl], func=AF.Sqrt,
                                           bias=C.eps_col[:, 0:1], scale=1.0 / D),
             reads=[psst[pb], C.t_const], writes=[t_r])
        k.op("dve", lambda e: e.reciprocal(out=rstd[:, ts], in_=rstd[:, ts]), reads=[t_r], writes=[t_r])
        for c in range(NCH):
            b = n % 2
            n += 1
            k.op("dve", lambda e: e.scalar_tensor_tensor(out=tmp[b][:, :tl], in0=X[:, c, ts],
                                                         scalar=C.A[:, v, j, c:c + 1], in1=rstd[:, ts],
                                                         op0=ALU.mult, op1=ALU.mult),
                 reads=[C.XT[bi], t_r, C.t_mv], writes=[tmpt[b]])
            k.op("act", lambda e: e.activation(out=H[:, c, ts], in_=tmp[b][:, :tl], func=AF.Identity,
                                               bias=C.MV[:, v, 3 * j * NCH + c:3 * j * NCH + c + 1], scale=1.0),
                 reads=[tmpt[b], C.t_mv], writes=[HT[bi]])


def emit_rstd(k, C, st, rstd, t_rs, pss, psst):
    X = C.X
    sq = [st.sb("rsq%d" % i, [128, 512], F32) for i in range(2)]
    sqt = [DT("rsq") for _ in range(2)]
    n = 0
    for bi, (t0, tl, v) in enumerate(TBS):
        ts = slice(t0, t0 + tl)
        pb = bi % 2
        for c in range(NCH):
            b = n % 2
            n += 1
            k.op("act", lambda e: e.activation(out=sq[b][:, :tl], in_=X[:, c, ts], func=AF.Square),
                 reads=[C.XT[bi]], writes=[sqt[b]])
            k.op("pe", lambda e: e.matmul(pss[pb][:, :tl], lhsT=C.ones_f[:], rhs=sq[b][:, :tl], start=(c == 0),
                                          stop=(c == NCH - 1)),
                 reads=[sqt[b], C.t_const], writes=[psst[pb]], inc=True)
        k.op("act", lambda e: e.activation(out=rstd[:, ts], in_=pss[pb][:, :tl], func=AF.Sqrt,
                                           bias=C.eps_col[:, 0:1], scale=1.0 / D),
             reads=[psst[pb], C.t_const], writes=[t_rs[bi]])
        k.op("dve", lambda e: e.reciprocal(out=rstd[:, ts], in_=rstd[:, ts]), reads=[t_rs[bi]], writes=[t_rs[bi]])


def emit_hchunk(k, C, j, c, rstd, t_rs, Hc, HcT, tmp, tmpt):
    X = C.X
    for bi, (t0, tl, v) in enumerate(TBS):
        ts = slice(t0, t0 + tl)
        b = bi % 2
        k.op("dve", lambda e: e.scalar_tensor_tensor(out=tmp[b][:, :tl], in0=X[:, c, ts],
                                                     scalar=C.A[:, v, j, c:c + 1], in1=rstd[:, ts],
                                                     op0=ALU.mult, op1=ALU.mult),
             reads=[C.XT[bi], t_rs[bi], C.t_mv], writes=[tmpt[b]])
        k.op("act", lambda e: e.activation(out=Hc[:, ts], in_=tmp[b][:, :tl], func=AF.Identity,
                                           bias=C.MV[:, v, 3 * j * NCH + c:3 * j * NCH + c + 1], scale=1.0),
             reads=[tmpt[b], C.t_mv], writes=[HcT])


def emit_ffn(k, C, j, wi_dram, wo_dram, lat_only=False):
    X = C.X
    NG_ = 4
    GC = NFC // NG_
    with Stage(k) as st:
        H = st.sb("ffn_h", [128, NCH, TOK], BF16)
        HT = [DT("h%d" % i) for i in range(3)]
        ps = [st.ps("fps%d" % i, [128, 512], F32) for i in range(8)]
        pst = [DT("fps%d" % i) for i in range(8)]
        tbs = [1, 2] if lat_only else [0, 1, 2]
        with Stage(k) as stn:
            emit_norm_mod(k, C, stn, j, H, HT, ps[6:8], pst[6:8], tbs=tbs)
        act = st.sb("ffn_act", [128, GC, TOK], BF16)
        actT = [DT("act%d" % i) for i in range(3)]
        NWI = 3
        wi = [st.sb("wi%d" % i, [128, NCH, 256], BF16) for i in range(NWI)]
        wit = [DT("wi%d" % i) for i in range(NWI)]
        NWO = 3
        WOC = 256
        wo = [st.sb("wo%d" % i, [128, GC, WOC], BF16) for i in range(NWO)]
        wot = [DT("wo%d" % i) for i in range(NWO)]
        sg = [st.sb("sg%d" % i, [128, 512], F32) for i in range(2)]
        sgt = [DT("sg") for _ in range(2)]
        wi_v = wi_dram.rearrange("(c p) n -> p c n", p=128)
        wo_v = wo_dram.rearrange("(f p) n -> p f n", p=128)
        nwi = 0
        nwo = 0
        nu = 0
        ny = 0
        for g in range(NG_):
            for fl in range(GC):
                f = g * GC + fl
                b = nwi % NWI
                nwi += 1
                k.dma("pool", wi[b][:, :, 0:128], wi_v[:, :, f * 128:(f + 1) * 128], writes=[wit[b]])
                k.dma("pool", wi[b][:, :, 128:256], wi_v[:, :, DFF + f * 128:DFF + (f + 1) * 128], writes=[wit[b]])
                for bi, (t0, tl, v) in enumerate(TBS):
                    if bi not in tbs:
                        continue
                    ts = slice(t0, t0 + tl)
                    pa = (nu % 3) * 2
                    pg = pa + 1
                    sb_ = nu % 2
                    nu += 1
                    for c in range(NCH):
                        k.op("pe", lambda e: e.matmul(ps[pa][:, :tl], lhsT=wi[b][:, c, 0:128], rhs=H[:, c, ts],
                                                      start=(c == 0), stop=(c == NCH - 1)),
                             reads=[wit[b], HT[bi]], writes=[pst[pa]], inc=(c == NCH - 1))
                    for c in range(NCH):
                        k.op("pe", lambda e: e.matmul(ps[pg][:, :tl], lhsT=wi[b][:, c, 128:256], rhs=H[:, c, ts],
                                                      start=(c == 0), stop=(c == NCH - 1)),
                             reads=[wit[b], HT[bi]], writes=[pst[pg]], inc=(c == NCH - 1))
                    k.op("act", lambda e: e.activation(out=sg[sb_][:, :tl], in_=ps[pg][:, :tl], func=AF.Silu),
                         reads=[pst[pg]], writes=[sgt[sb_]])
                    k.op("dve", lambda e: e.tensor_tensor(out=act[:, fl, ts], in0=ps[pa][:, :tl], in1=sg[sb_][:, :tl],
                                                          op=ALU.mult),
                         reads=[pst[pa], sgt[sb_]], writes=[actT[bi]])
            for dq in range(D // WOC):
                b = nwo % NWO
                nwo += 1
                k.dma("pool", wo[b][:], wo_v[:, g * GC:(g + 1) * GC, dq * WOC:(dq + 1) * WOC], writes=[wot[b]])
                for dl in range(WOC // 128):
                    dc = dq * (WOC // 128) + dl
                    for bi, (t0, tl, v) in enumerate(TBS):
                        if bi not in tbs:
                            continue
                        ts = slice(t0, t0 + tl)
                        p = ny % 6
                        ny += 1
                        for fl in range(GC):
                            k.op("pe", lambda e: e.matmul(ps[p][:, :tl], lhsT=wo[b][:, fl, dl * 128:(dl + 1) * 128],
                                                          rhs=act[:, fl, ts], start=(fl == 0), stop=(fl == GC - 1)),
                                 reads=[wot[b], actT[bi]], writes=[pst[p]], inc=(fl == GC - 1))
                        k.op("dve", lambda e: e.scalar_tensor_tensor(out=X[:, dc, ts], in0=ps[p][:, :tl],
                                                                     scalar=C.G[:, v, j, dc:dc + 1], in1=X[:, dc, ts],
                                                                     op0=ALU.mult, op1=ALU.add),
                             reads=[pst[p], C.t_mv, C.XT[bi]], writes=[C.XT[bi]])


def alloc_persistent(k, st, C):
    C.X = st.sb("X", [128, NCH, TOK], F32)
    C.XT = [DT("x%d" % i) for i in range(3)]
    C.MV = st.sb("MV", [128, 2, 9 * NCH], F32)
    C.MB = st.sb("MB", [128, 9 * NCH], F32)
    C.NG = st.sb("NG", [128, 3, NCH], F32)
    C.A = st.sb("A", [128, 2, 3, NCH], F32)
    C.G = st.sb("G", [128, 2, 3, NCH], F32)
    setup_consts(k, st, C)


def load_x(k, C, xT_dram):
    xv = xT_dram.rearrange("(c p) t -> p c t", p=128)
    for bi, (t0, tl, v) in enumerate(TBS):
        k.dma("sp", C.X[:, :, t0:t0 + tl], xv[:, :, t0:t0 + tl], writes=[C.XT[bi]])


def store_x(k, C, xo_dram, lat_only=False):
    xv = xo_dram.rearrange("(c p) t -> p c t", p=128)
    for bi, (t0, tl, v) in enumerate(TBS):
        if lat_only and bi == 0:
            continue
        o0 = t0 - (NCTX if lat_only else 0)
        k.dma("sp", xv[:, :, o0:o0 + tl], C.X[:, :, t0:t0 + tl], reads=[C.XT[bi]], final=True)


def build_test_ffn():
    nc = bass.Bass("TRN2", target_bir_lowering=False)
    xT = nc.dram_tensor("xT", [D, TOK], F32, kind="ExternalInput").ap()
    condT = nc.dram_tensor("condT", [128, 2, NCH], F32, kind="ExternalInput").ap()
    modw = nc.dram_tensor("modw", [D, 9 * D], F32, kind="ExternalInput").ap()
    modb = nc.dram_tensor("modb", [128, 9 * NCH], F32, kind="ExternalInput").ap()
    ng = nc.dram_tensor("ng", [128, 3, NCH], F32, kind="ExternalInput").ap()
    wi = nc.dram_tensor("wi", [D, 2 * DFF], F32, kind="ExternalInput").ap()
    wo = nc.dram_tensor("wo", [DFF, D], F32, kind="ExternalInput").ap()
    xo = nc.dram_tensor("xo", [D, TOK], F32, kind="ExternalOutput").ap()
    mvo = nc.dram_tensor("mvo", [128, 2 * 9 * NCH], F32, kind="ExternalOutput").ap()
    with ExitStack() as es:
        k = KB(nc, es)
        C = Ctx()
        with Stage(k) as st0:
            alloc_persistent(k, st0, C)
            load_x(k, C, xT)
            with Stage(k) as st:
                emit_modvec(k, C, st, condT, modw, modb, ng)
            k.dma("sp", mvo, C.MV[:].rearrange("p v m -> p (v m)"), reads=[C.t_mv], final=True)
            emit_ffn(k, C, 0, wi, wo)
            store_x(k, C, xo)
            k.finish()
        print("instructions:", k.ninstr, "dma sems:", k.nsem)
    return nc


def vec_pm(v):
    v = np.asarray(v)
    lead = v.shape[:-1]
    n = v.shape[-1] // 128
    a = v.reshape(lead + (n, 128))
    return np.ascontiguousarray(np.moveaxis(a, -1, 0))


def core_tokens_T(x, ctx, core):
    b, h = core // 2, core % 2
    cx, xl = ctx[b], x[b, h * NLAT:(h + 1) * NLAT]
    if h == 1:
        cx, xl = cx[::-1], xl[::-1]
    t = np.concatenate([cx, xl], axis=0)
    return np.ascontiguousarray(t.T)


NH = 16
NKV = 4
HD = 128
QKV_W = 3072
NKEY = NCTX + SEQ
NKB = NKEY // 128


def emit_qkv(k, C, wqkv_dram, qkg_dram, cos_dram, sin_dram, rm_dram, qT_d, kT_d, v_d):
    with Stage(k) as st:
        H = st.sb("qkv_h", [128, NCH, TOK], BF16)
        HT = [DT("h%d" % i) for i in range(3)]
        ps = [st.ps("qps%d" % i, [128, 512], F32) for i in range(8)]
        pst = [DT("qps%d" % i) for i in range(8)]
        with Stage(k) as stn:
            emit_norm_mod(k, C, stn, 1, H, HT, ps[6:8], pst[6:8])
        cs = st.sb("cs", [128, 2, NLAT], F32)
        rm = st.sb("rm", [128, 128], F32)
        g2 = st.sb("g2", [128, 2], F32)
        t_c = DT("qkvconst")
        k.dma("sp", cs[:, 0, :], cos_dram, writes=[t_c])
        k.dma("sp", cs[:, 1, :], sin_dram, writes=[t_c])
        k.dma("sp", rm[:], rm_dram, writes=[t_c])
        k.dma("sp", g2[:], qkg_dram, writes=[t_c])
        k.op("dve", lambda e: e.tensor_scalar(out=g2[:, 0:1], in0=g2[:, 0:1], scalar1=float(HD ** -0.5), scalar2=None,
                                              op0=ALU.mult), reads=[t_c], writes=[t_c])
        NW = 3
        wt = [st.sb("qw%d" % i, [128, NCH, 256], BF16) for i in range(NW)]
        wtt = [DT("qw%d" % i) for i in range(NW)]
        wv = st.sb("qwv", [128, NCH, 512], BF16)
        wvt = DT("qwv")
        w_v = wqkv_dram.rearrange("(c p) n -> p c n", p=128)
        k.dma("pool", wv[:], w_v[:, :, 2560:3072], writes=[wvt])
        sq = [st.sb("qsq%d" % i, [128, 512], F32) for i in range(2)]
        sqt = [DT("qsq") for _ in range(2)]
        rs = [st.sb("qrs%d" % i, [128, 512], F32) for i in range(2)]
        rst = [DT("qrs") for _ in range(2)]
        qn = [st.sb("qqn%d" % i, [128, 512], F32) for i in range(2)]
        qnt = [DT("qqn") for _ in range(2)]
        t1 = [st.sb("qt1%d" % i, [128, 512], F32) for i in range(2)]
        t1t = [DT("qt1") for _ in range(2)]
        ob = [st.sb("qob%d" % i, [128, 512], BF16) for i in range(3)]
        obt = [DT("qob%d" % i) for i in range(3)]
        def qunit(b, s, fc, isq, gcol, bi, t0, tl, u, o3):
            ts = slice(t0, t0 + tl)
            pq, pss, pr = ps[u], ps[2 + u], ps[4 + u]
            pqt, psst, prt = pst[u], pst[2 + u], pst[4 + u]

            def f1():
                for c in range(NCH):
                    k.op("pe", lambda e: e.matmul(pq[:, :tl], lhsT=wt[b][:, c, s * 128:(s + 1) * 128], rhs=H[:, c, ts],
                                                  start=(c == 0), stop=(c == NCH - 1)),
                         reads=[wtt[b], HT[bi]], writes=[pqt], inc=(c == NCH - 1))

            def f2():
                k.op("act", lambda e: e.activation(out=sq[u][:, :tl], in_=pq[:, :tl], func=AF.Square),
                     reads=[pqt], writes=[sqt[u]])
                k.op("pe", lambda e: e.matmul(pss[:, :tl], lhsT=C.ones_f[:], rhs=sq[u][:, :tl], start=True, stop=True),
                     reads=[sqt[u], C.t_const], writes=[psst])
                k.op("act", lambda e: e.activation(out=rs[u][:, :tl], in_=pss[:, :tl], func=AF.Sqrt,
                                                   bias=C.eps_col[:, 0:1], scale=1.0 / HD),
                     reads=[psst, C.t_const], writes=[rst[u]])
                k.op("dve", lambda e: e.reciprocal(out=rs[u][:, :tl], in_=rs[u][:, :tl]), reads=[rst[u]], writes=[rst[u]])
                k.op("dve", lambda e: e.scalar_tensor_tensor(out=qn[u][:, :tl], in0=pq[:, :tl], scalar=gcol,
                                                             in1=rs[u][:, :tl], op0=ALU.mult, op1=ALU.mult),
                     reads=[pqt, rst[u], t_c], writes=[qnt[u]])

            def f3():
                if bi == 0:
                    k.op("act", lambda e: e.copy(out=ob[o3][:, :tl], in_=qn[u][:, :tl]), reads=[qnt[u]], writes=[obt[o3]])
                else:
                    ls = slice(t0 - NCTX, t0 - NCTX + tl)
                    k.op("pe", lambda e: e.matmul(pr[:, :tl], lhsT=rm[:], rhs=qn[u][:, :tl], start=True, stop=True),
                         reads=[qnt[u], t_c], writes=[prt])
                    k.op("dve", lambda e: e.tensor_tensor(out=t1[u][:, :tl], in0=qn[u][:, :tl], in1=cs[:, 0, ls],
                                                           op=ALU.mult), reads=[qnt[u], t_c], writes=[t1t[u]])
                    k.op("dve", lambda e: e.tensor_tensor(out=qn[u][:, :tl], in0=pr[:, :tl], in1=cs[:, 1, ls],
                                                          op=ALU.mult), reads=[prt, t_c], writes=[qnt[u]])
                    k.op("dve", lambda e: e.tensor_tensor(out=ob[o3][:, :tl], in0=qn[u][:, :tl], in1=t1[u][:, :tl],
                                                          op=ALU.add), reads=[qnt[u], t1t[u]], writes=[obt[o3]])
                dst = qT_d[fc, :, ts] if isq else kT_d[fc - NH, :, ts]
                k.dma("sp", dst, ob[o3][:, :tl], reads=[obt[o3]], final=True)

            return f1, f2, f3

        pend2 = None
        pend3 = None
        nu = 0
        for ti in range(10):
            b = ti % NW
            k.dma("pool", wt[b][:], w_v[:, :, ti * 256:(ti + 1) * 256], writes=[wtt[b]])
            for s in range(2):
                fc = ti * 2 + s
                isq = fc < NH
                gcol = g2[:, 0:1] if isq else g2[:, 1:2]
                for bi, (t0, tl, v) in enumerate(TBS):
                    f1, f2, f3 = qunit(b, s, fc, isq, gcol, bi, t0, tl, nu % 2, nu % 3)
                    nu += 1
                    f1()
                    if pend3 is not None:
                        pend3()
                    pend3 = None
                    if pend2 is not None:
                        pend2[0]()
                        pend3 = pend2[1]
                    pend2 = (f2, f3)
        if pend3 is not None:
            pend3()
        if pend2 is not None:
            pend2[0]()
            pend2[1]()
        for tb in range(TOK // 128):
            u = tb % 2
            o3 = nu % 3
            nu += 1
            bi = 0 if tb < 2 else (1 if tb < 6 else 2)
            for c in range(NCH):
                k.op("pe", lambda e: e.matmul(ps[u][:, :], lhsT=H[:, c, tb * 128:(tb + 1) * 128], rhs=wv[:, c, :],
                                              start=(c == 0), stop=(c == NCH - 1)),
                     reads=[wvt, HT[bi]], writes=[pst[u]], inc=(c == NCH - 1))
            k.op("act", lambda e: e.copy(out=ob[o3][:, :], in_=ps[u][:, :]), reads=[pst[u]], writes=[obt[o3]])
            k.dma("sp", v_d[tb * 128:(tb + 1) * 128, :], ob[o3][:, :], reads=[obt[o3]], final=True)


NKEYW = NCTX + 128 + NLAT + 128


def emit_attn(k, C, window, ctx_out, qT_d, kT_all_d, v_all_d, kv_other, esink_dram, masks_dram, wo_dram):
    X = C.X
    nkey = NKEYW if window else NKEY
    nkb = nkey // 128
    with Stage(k) as st:
        KT = st.sb("KT", [128, NKV, nkey], BF16)
        V = st.sb("V", [128, nkb, 512], BF16)
        t_kv = DT("kv")
        kT_me, v_me, kT_x, v_x = kT_all_d, v_all_d, kv_other[0], kv_other[1]
        vr = lambda a: a.rearrange("(b p) f -> p b f", p=128)
        nlb = NLAT // 128
        if not window:
            for g in range(NKV):
                k.dma("sp", KT[:, g, 0:NCTX], kT_me[g][:, 0:NCTX], writes=[t_kv])
                k.dma("sp", KT[:, g, NCTX:TOK], kT_x[0][g][:, NCTX:TOK], writes=[t_kv])
                k.dma("sp", KT[:, g, TOK:NKEY], kT_x[1][g][:, NCTX:TOK], writes=[t_kv])
            k.dma("sp", V[:, 0:2, :], vr(v_me[0:NCTX, :]), writes=[t_kv])
            k.dma("sp", V[:, 2:2 + nlb, :], vr(v_x[0][NCTX:TOK, :]), writes=[t_kv])
            k.dma("sp", V[:, 2 + nlb:2 + 2 * nlb, :], vr(v_x[1][NCTX:TOK, :]), writes=[t_kv])
        else:
            k.op("pool", lambda e: e.memset(KT[:, :, NCTX:NCTX + 128], 0.0), writes=[t_kv])
            k.op("pool", lambda e: e.memset(V[:, 2, :], 0.0), writes=[t_kv])
            for g in range(NKV):
                k.dma("sp", KT[:, g, 0:NCTX], kT_me[g][:, 0:NCTX], writes=[t_kv])
                k.dma("sp", KT[:, g, NCTX + 128:NCTX + 128 + NLAT], kT_me[g][:, NCTX:TOK], writes=[t_kv])
                k.dma("sp", KT[:, g, NCTX + 128 + NLAT:NKEYW], kT_x[g][:, TOK - 128:TOK], writes=[t_kv])
            k.dma("sp", V[:, 0:2, :], vr(v_me[0:NCTX, :]), writes=[t_kv])
            k.dma("sp", V[:, 3:3 + nlb, :], vr(v_me[NCTX:TOK, :]), writes=[t_kv])
            k.dma("sp", V[:, 3 + nlb:4 + nlb, :], vr(v_x[TOK - 128:TOK, :]), writes=[t_kv])
        QT = [st.sb("QT%d" % i, [128, 4, TOK], BF16) for i in range(2)]
        QTt = [DT("QT%d" % i) for i in range(2)]
        OT = [st.sb("OT%d" % i, [128, 4, TOK], BF16) for i in range(2)]
        OTt = [DT("OT%d" % i) for i in range(2)]
        wo = [st.sb("awo%d" % i, [128, 4, D], BF16) for i in range(2)]
        wot = [DT("awo%d" % i) for i in range(2)]
        pt = [st.sb("pt%d" % i, [128, 512], BF16) for i in range(3)]
        ptt = [DT("pt%d" % i) for i in range(3)]
        rd = [st.sb("rd%d" % i, [128, 512], F32) for i in range(2)]
        rdt = [DT("rd%d" % i) for i in range(2)]
        ps = [st.ps("aps%d" % i, [128, 512], F32) for i in range(8)]
        pst = [DT("aps%d" % i) for i in range(8)]
        t_c = DT("attnconst")
        if window:
            es_ = st.sb("esink", [128, NH], F32)
            mk = st.sb("masks", [128, 4, 128], BF16)
            k.dma("sp", es_[:], esink_dram, writes=[t_c])
            k.dma("sp", mk[:], masks_dram, writes=[t_c])
            k.op("act", lambda e: e.activation(out=es_[:], in_=es_[:], func=AF.Exp), reads=[t_c], writes=[t_c])
        wo_v = wo_dram.rearrange("(h p) n -> p h n", p=128)
        ns = 0
        nunit = 0
        ny = 0
        for g in range(NKV):
            gb_ = g % 2
            for hl in range(4):
                k.dma("sp", QT[gb_][:, hl, :], qT_d[4 * g + hl], writes=[QTt[gb_]])
            k.dma("pool", wo[gb_][:], wo_v[:, 4 * g:4 * g + 4, :], writes=[wot[gb_]])
            for hl in range(4):
                h = 4 * g + hl
                units = []
                if not window:
                    for (t0, tl, v) in TBS[1:]:
                        units.append((t0, tl, [(kb, None) for kb in range(NKB)]))
                    if ctx_out:
                        units.append((0, NCTX, [(0, None), (1, None)]))
                else:
                    nqb = NLAT // 128
                    for qb in range(nqb):
                        kl = [(0, None), (1, None), (2 + qb, 2 if qb == 0 else 0), (3 + qb, None),
                              (4 + qb, 3 if qb == nqb - 1 else 1)]
                        units.append((NCTX + qb * 128, 128, kl))
                    if ctx_out:
                        units.append((0, NCTX, [(0, None), (1, None)]))
                for (t0, tl, kl) in units:
                    ts = slice(t0, t0 + tl)
                    u = nunit % 2
                    nunit += 1
                    po, pd = ps[2 + u], ps[4 + u]
                    pot, pdt = pst[2 + u], pst[4 + u]
                    def front(ki, kb, mi, s2, p3):
                        k.op("pe", lambda e: e.matmul(ps[s2][:, :tl], lhsT=KT[:, g, kb * 128:(kb + 1) * 128],
                                                      rhs=QT[gb_][:, hl, ts], start=True, stop=True),
                             reads=[t_kv, QTt[gb_]], writes=[pst[s2]])
                        k.op("act", lambda e: e.activation(out=pt[p3][:, :tl], in_=ps[s2][:, :tl], func=AF.Exp),
                             reads=[pst[s2]], writes=[ptt[p3]])
                        if mi is not None:
                            k.op("dve", lambda e: e.tensor_tensor(out=pt[p3][:, :tl], in0=pt[p3][:, :tl], in1=mk[:, mi, :tl],
                                                                  op=ALU.mult), reads=[ptt[p3], t_c], writes=[ptt[p3]])

                    def back(ki, kb, p3):
                        last = ki == len(kl) - 1
                        k.op("pe", lambda e: e.matmul(po[:, :tl], lhsT=V[:, kb, g * 128:(g + 1) * 128], rhs=pt[p3][:, :tl],
                                                      start=(ki == 0), stop=last),
                             reads=[t_kv, ptt[p3]], writes=[pot], inc=last)
                        k.op("pe", lambda e: e.matmul(pd[:, :tl], lhsT=C.ones_b[:], rhs=pt[p3][:, :tl],
                                                      start=(ki == 0), stop=last),
                             reads=[C.t_const, ptt[p3]], writes=[pdt], inc=last)
                    prev = None
                    for ki, (kb, mi) in enumerate(kl):
                        s2 = ns % 2
                        p3 = ns % 3
                        ns += 1
                        front(ki, kb, mi, s2, p3)
                        if prev is not None:
                            back(*prev)
                        prev = (ki, kb, p3)
                    back(*prev)
                    if window:
                        k.op("dve", lambda e: e.tensor_scalar(out=rd[u][:, :tl], in0=pd[:, :tl], scalar1=es_[:, h:h + 1],
                                                              scalar2=None, op0=ALU.add), reads=[pdt, t_c], writes=[rdt[u]])
                        k.op("dve", lambda e: e.reciprocal(out=rd[u][:, :tl], in_=rd[u][:, :tl]), reads=[rdt[u]], writes=[rdt[u]])
                    else:
                        k.op("dve", lambda e: e.reciprocal(out=rd[u][:, :tl], in_=pd[:, :tl]), reads=[pdt], writes=[rdt[u]])
                    k.op("dve", lambda e: e.tensor_tensor(out=OT[gb_][:, hl, ts], in0=po[:, :tl], in1=rd[u][:, :tl], op=ALU.mult),
                         reads=[pot, rdt[u]], writes=[OTt[gb_]])
            for dc in range(NCH):
                for bi, (t0, tl, v) in enumerate(TBS):
                    if bi == 0 and not ctx_out:
                        continue
                    ts = slice(t0, t0 + tl)
                    p = 6 + ny % 2
                    ny += 1
                    for hl in range(4):
                        k.op("pe", lambda e: e.matmul(ps[p][:, :tl], lhsT=wo[gb_][:, hl, dc * 128:(dc + 1) * 128],
                                                      rhs=OT[gb_][:, hl, ts], start=(hl == 0), stop=(hl == 3)),
                             reads=[wot[gb_], OTt[gb_]], writes=[pst[p]], inc=(hl == 3))
                    k.op("dve", lambda e: e.scalar_tensor_tensor(out=X[:, dc, ts], in0=ps[p][:, :tl],
                                                                 scalar=C.G[:, v, 1, dc:dc + 1], in1=X[:, dc, ts],
                                                                 op0=ALU.mult, op1=ALU.add),
                         reads=[pst[p], C.t_mv, C.XT[bi]], writes=[C.XT[bi]])


def rope_tables(half):
    t = np.arange(half * NLAT, (half + 1) * NLAT)
    if half == 1:
        t = t[::-1]
    row = (t // 64).astype(np.float32)
    col = (t % 64).astype(np.float32)
    inv = (10000.0 ** (-np.arange(32, dtype=np.float32) / 32)).astype(np.float32)
    ang = np.concatenate([row[:, None] * inv, col[:, None] * inv], axis=-1)
    ang = np.concatenate([ang, ang], axis=-1).astype(np.float32)
    return np.ascontiguousarray(np.cos(ang).T.astype(np.float32)), np.ascontiguousarray(np.sin(ang).T.astype(np.float32))


def rot_matrix():
    R = np.zeros((128, 128), np.float32)
    for m in range(64):
        R[m + 64, m] = -1.0
    for m in range(64, 128):
        R[m - 64, m] = 1.0
    return R


def win_masks(half=0):
    import ml_dtypes
    s = np.arange(128)[:, None]
    t = np.arange(128)[None, :]
    prev = (s >= t).astype(np.float32)
    nxt = (s <= t).astype(np.float32)
    z = np.zeros_like(prev)
    m = np.stack([prev, nxt, z, nxt[::-1]], axis=1)
    return np.ascontiguousarray(m).astype(ml_dtypes.bfloat16)


KINDS = ["a", "s", "m", "w"]


MH = 8
MDQK = 128
MDV = 256
NBLK = TOK // 128
LN_KSCALE = float(np.log(MDQK ** -0.5))


def emit_mlstm_proj(k, C, win_dram, wg_dram, bg_dram, qT_d, kT_d, ktok_d, vtok_d, so_d, gi_d):
    with Stage(k) as st:
        H = st.sb("m_h", [128, NCH, TOK], BF16)
        HT = [DT("h%d" % i) for i in range(3)]
        ps = [st.ps("mps%d" % i, [128, 512], F32) for i in range(8)]
        pst = [DT("mps%d" % i) for i in range(8)]
        with Stage(k) as stn:
            emit_norm_mod(k, C, stn, 1, H, HT, ps[6:8], pst[6:8])
        NW = 3
        wt = [st.sb("mw%d" % i, [128, NCH, 512], BF16) for i in range(NW)]
        wtt = [DT("mw%d" % i) for i in range(NW)]
        w_v = win_dram.rearrange("(c p) n -> p c n", p=128)
        ob = [st.sb("mob%d" % i, [128, 512], BF16) for i in range(3)]
        obt = [DT("mob%d" % i) for i in range(3)]
        of = [st.sb("mof%d" % i, [128, 512], F32) for i in range(2)]
        oft = [DT("mof%d" % i) for i in range(2)]
        nu = 0
        nw = 0
        for (c0, kind) in [(0, "q"), (512, "q"), (1024, "k"), (1536, "k"), (4096, "o"), (4608, "o"), (5120, "o"), (5632, "o")]:
            b = nw % NW
            nw += 1
            k.dma("pool", wt[b][:], w_v[:, :, c0:c0 + 512], writes=[wtt[b]])
            for s in range(4):
                fcol = c0 + s * 128
                for bi, (t0, tl, v) in enumerate(TBS):
                    ts = slice(t0, t0 + tl)
                    u = nu % 4
                    nu += 1
                    for c in range(NCH):
                        k.op("pe", lambda e: e.matmul(ps[u][:, :tl], lhsT=wt[b][:, c, s * 128:(s + 1) * 128], rhs=H[:, c, ts],
                                                      start=(c == 0), stop=(c == NCH - 1)),
                             reads=[wtt[b], HT[bi]], writes=[pst[u]], inc=(c == NCH - 1))
                    if kind == "o":
                        f2 = nu % 2
                        k.op("act", lambda e: e.activation(out=of[f2][:, :tl], in_=ps[u][:, :tl], func=AF.Sigmoid),
                             reads=[pst[u]], writes=[oft[f2]])
                        k.dma("sp", so_d[(fcol - 4096) // 128, :, ts], of[f2][:, :tl], reads=[oft[f2]], final=True)
                    else:
                        o3 = nu % 3
                        k.op("act", lambda e: e.copy(out=ob[o3][:, :tl], in_=ps[u][:, :tl]), reads=[pst[u]], writes=[obt[o3]])
                        dst = qT_d[fcol // 128, :, ts] if kind == "q" else kT_d[(fcol - 1024) // 128, :, ts]
                        k.dma("sp", dst, ob[o3][:, :tl], reads=[obt[o3]], final=True)
        wg = st.sb("m_wg", [128, NCH, 32], BF16)
        wgt = DT("m_wg")
        k.dma("pool", wg[:], wg_dram.rearrange("(c p) n -> p c n", p=128), writes=[wgt])
        bg = st.sb("m_bg", [128, 32], F32)
        k.dma("sp", bg[:], bg_dram, writes=[wgt])
        gt = [st.sb("m_gt%d" % i, [128, 32], F32) for i in range(2)]
        gtt = [DT("m_gt%d" % i) for i in range(2)]
        for (c0, kind) in [(1024, "k"), (1536, "k"), (2048, "v"), (2560, "v"), (3072, "v"), (3584, "v")]:
            b = nw % NW
            nw += 1
            k.dma("pool", wt[b][:], w_v[:, :, c0:c0 + 512], writes=[wtt[b]])
            for tb in range(NBLK):
                u = nu % 4
                o3 = nu % 3
                nu += 1
                bi = 0 if tb < 2 else (1 if tb < 6 else 2)
                for c in range(NCH):
                    k.op("pe", lambda e: e.matmul(ps[u][:, :], lhsT=H[:, c, tb * 128:(tb + 1) * 128], rhs=wt[b][:, c, :],
                                                  start=(c == 0), stop=(c == NCH - 1)),
                         reads=[wtt[b], HT[bi]], writes=[pst[u]], inc=(c == NCH - 1))
                k.op("act", lambda e: e.copy(out=ob[o3][:, :], in_=ps[u][:, :]), reads=[pst[u]], writes=[obt[o3]])
                dst = ktok_d[tb * 128:(tb + 1) * 128, c0 - 1024:c0 - 512] if kind == "k" else \
                    vtok_d[tb * 128:(tb + 1) * 128, c0 - 2048:c0 - 1536]
                k.dma("sp", dst, ob[o3][:, :], reads=[obt[o3]], final=True)
        for tb in range(NBLK):
            u = nu % 4
            f2 = nu % 2
            nu += 1
            bi = 0 if tb < 2 else (1 if tb < 6 else 2)
            for c in range(NCH):
                k.op("pe", lambda e: e.matmul(ps[u][:, 0:32], lhsT=H[:, c, tb * 128:(tb + 1) * 128], rhs=wg[:, c, :],
                                              start=(c == 0), stop=(c == NCH - 1)),
                     reads=[wgt, HT[bi]], writes=[pst[u]], inc=(c == NCH - 1))
            k.op("dve", lambda e: e.tensor_tensor(out=gt[f2][:], in0=ps[u][:, 0:32], in1=bg[:], op=ALU.add),
                 reads=[pst[u], wgt], writes=[gtt[f2]])
            k.op("act", lambda e: e.activation(out=gt[f2][:], in_=gt[f2][:], func=AF.Tanh, scale=1.0 / 15.0),
                 reads=[gtt[f2]], writes=[gtt[f2]])
            k.op("dve", lambda e: e.tensor_scalar(out=gt[f2][:], in0=gt[f2][:], scalar1=15.0, scalar2=None, op0=ALU.mult),
                 reads=[gtt[f2]], writes=[gtt[f2]])
            k.dma("sp", gi_d[tb * 128:(tb + 1) * 128, :], gt[f2][:], reads=[gtt[f2]], final=True)


def emit_mlstm_scan(k, C, st, phase, qT_d, kT_d, ktok_d, vtok_d, gi_d, tri_dram, madd_dram, state_in, state_out,
                    h_in, h_out):
    p = phase
    NB_ = 2
    QT = [st.sb("m_qT%d" % i, [128, MH, 128], BF16) for i in range(NB_)]
    KT = [st.sb("m_kT%d" % i, [128, MH, 128], BF16) for i in range(NB_)]
    KK = [st.sb("m_kk%d" % i, [128, MH * MDQK], BF16) for i in range(NB_)]
    VV = [st.sb("m_vv%d" % i, [128, MH * MDV], BF16) for i in range(NB_)]
    t_blk = [DT("m_blk%d" % i) for i in range(NB_)]
    GI = st.sb("m_gi", [128, NBLK, 32], F32)
    t_in = DT("m_in")
    k.dma("sp", GI[:], gi_d.rearrange("(b p) f -> p b f", p=128), writes=[t_in])
    tri = st.sb("m_tri", [128, 128], F32)
    madd = st.sb("m_madd", [128, 128], F32)
    t_c = DT("m_c")
    k.dma("sp", tri[:], tri_dram[p], writes=[t_c])
    k.dma("sp", madd[:], madd_dram[p], writes=[t_c])
    A_ = st.sb("m_a", [128, NBLK, MH], F32)
    IMB = st.sb("m_imb", [128, NBLK, MH], F32)
    t_g = DT("m_g")
    fsl = GI[:, :, p * 16 + 8:p * 16 + 16]
    isl = GI[:, :, p * 16:p * 16 + 8]
    k.op("act", lambda e: e.activation(out=A_[:], in_=fsl, func=AF.Exp, scale=-1.0), reads=[t_in], writes=[t_g])
    k.op("act", lambda e: e.activation(out=A_[:], in_=A_[:], func=AF.Ln, bias=C.one_col[:, 0:1], scale=1.0),
         reads=[t_g, C.t_const], writes=[t_g])
    k.op("dve", lambda e: e.tensor_scalar(out=A_[:], in0=A_[:], scalar1=-1.0, scalar2=None, op0=ALU.mult),
         reads=[t_g], writes=[t_g])
    ps = [st.ps("sps%d" % i, [128, 512], F32) for i in range(8)]
    pst = [DT("sps%d" % i) for i in range(8)]
    for blk in range(NBLK):
        u = blk % 2
        k.op("pe", lambda e: e.matmul(ps[u][:, 0:MH], lhsT=tri[:], rhs=A_[:, blk, :], start=True, stop=True),
             reads=[t_c, t_g], writes=[pst[u]])
        k.op("dve", lambda e: e.scalar_tensor_tensor(out=IMB[:, blk, :], in0=isl[:, blk, :], scalar=LN_KSCALE,
                                                     in1=ps[u][:, 0:MH], op0=ALU.add, op1=ALU.subtract),
             reads=[pst[u], t_in], writes=[t_g])
    Cf = st.sb("m_Cf", [128, MH, MDV], F32)
    Cb = st.sb("m_Cb", [128, MH, MDV], BF16)
    Nf = st.sb("m_Nf", [128, MH, 128], F32)
    Nb = st.sb("m_Nb", [128, MH, 128], BF16)
    St = [DT("m_st%d" % hh) for hh in range(MH)]
    NS = 2
    Ta = [st.sb("m_Ta%d" % i, [128, 128], F32) for i in range(NS)]
    Tat = [DT("Ta") for _ in range(NS)]
    tmp = [st.sb("m_tmp%d" % i, [128, 128], F32) for i in range(NS)]
    tmpt = [DT("tmp") for _ in range(NS)]
    Dm = [st.sb("m_Dm%d" % i, [128, 128], F32) for i in range(NS)]
    Dmt = [DT("Dm") for _ in range(NS)]
    eb = [st.sb("m_eb%d" % i, [128, 128], F32) for i in range(NS)]
    ebt = [DT("eb") for _ in range(NS)]
    Qp = [st.sb("m_Qp%d" % i, [128, 128], BF16) for i in range(NS)]
    Qpt = [DT("Qp") for _ in range(NS)]
    PT = [st.sb("m_PT%d" % i, [128, 128], BF16) for i in range(NS)]
    PTt = [DT("PT") for _ in range(NS)]
    rd = [st.sb("m_rd%d" % i, [128, 128], F32) for i in range(NS)]
    rdt = [DT("rd") for _ in range(NS)]
    wc = [st.sb("m_wc%d" % i, [128, 2], F32) for i in range(NS)]
    wct = [DT("wc") for _ in range(NS)]
    K2 = [st.sb("m_K2%d" % i, [128, 128], BF16) for i in range(NS)]
    K2t = [DT("K2") for _ in range(NS)]
    hb = [st.sb("m_hb%d" % i, [128, 2, 128], F32) for i in range(3)]
    hbt = [DT("hb%d" % i) for i in range(3)]
    hi = [st.sb("m_hi%d" % i, [128, 2, 128], F32) for i in range(3)]
    hit = [DT("hi%d" % i) for i in range(3)]

    def zero_state():
        for hh in range(MH):
            k.op("pool", lambda e: e.memset(Cf[:, hh, :], 0.0), writes=[St[hh]])
            k.op("pool", lambda e: e.memset(Cb[:, hh, :], 0.0), writes=[St[hh]])
            k.op("pool", lambda e: e.memset(Nf[:, hh, :], 0.0), writes=[St[hh]])
            k.op("pool", lambda e: e.memset(Nb[:, hh, :], 0.0), writes=[St[hh]])

    ecol = 127 if p == 0 else 0

    def make_unit(blk, bb, bs, hh, u, h3):
        pA, pS, pN, pC = ps[u], ps[2 + u], ps[4 + u], ps[6 + u]
        pAt, pSt, pNt, pCt = pst[u], pst[2 + u], pst[4 + u], pst[6 + u]

        def fa():
                k.op("act", lambda e: e.activation(out=Ta[u][:], in_=tri[:], func=AF.Identity, scale=A_[:, blk, hh:hh + 1]),
                     reads=[t_c, t_g], writes=[Tat[u]])
                k.op("pe", lambda e: e.matmul(pA[:, 0:128], lhsT=C.ones_f[:], rhs=Ta[u][:], start=True, stop=True),
                     reads=[Tat[u], C.t_const], writes=[pAt])
                k.op("dve", lambda e: e.tensor_tensor(out=tmp[u][:], in0=pA[:, 0:128], in1=madd[:], op=ALU.add),
                     reads=[pAt, t_c], writes=[tmpt[u]])
                k.op("act", lambda e: e.activation(out=Dm[u][:], in_=tmp[u][:], func=AF.Exp, bias=IMB[:, blk, hh:hh + 1], scale=1.0),
                     reads=[tmpt[u], t_g], writes=[Dmt[u]])
                k.op("act", lambda e: e.activation(out=eb[u][:], in_=pA[:, 0:128], func=AF.Exp), reads=[pAt], writes=[ebt[u]])
                k.op("dve", lambda e: e.tensor_tensor(out=Qp[u][:], in0=QT[bb][:, hh, :], in1=eb[u][:], op=ALU.mult),
                     reads=[t_blk[bb], ebt[u]], writes=[Qpt[u]])
                k.op("pe", lambda e: e.matmul(pS[:, 0:128], lhsT=KT[bb][:, hh, :], rhs=QT[bb][:, hh, :], start=True, stop=True),
                     reads=[t_blk[bb]], writes=[pSt])
                k.op("dve", lambda e: e.tensor_tensor(out=PT[u][:], in0=pS[:, 0:128], in1=Dm[u][:], op=ALU.mult),
                     reads=[pSt, Dmt[u]], writes=[PTt[u]])

        def fb():
                for dvh in range(2):
                    k.op("pe", lambda e: e.matmul(pN[:, dvh * 128:(dvh + 1) * 128], lhsT=VV[bb][:, hh * MDV + dvh * 128:hh * MDV + (dvh + 1) * 128],
                                                  rhs=PT[u][:], start=True, stop=False), reads=[t_blk[bb], PTt[u]], writes=[pNt], inc=False)
                    k.op("pe", lambda e: e.matmul(pN[:, dvh * 128:(dvh + 1) * 128], lhsT=Cb[:, hh, dvh * 128:(dvh + 1) * 128],
                                                  rhs=Qp[u][:], start=False, stop=True), reads=[St[hh], Qpt[u]], writes=[pNt], inc=False)
                k.op("pe", lambda e: e.matmul(pN[:, 256:384], lhsT=C.ones_b[:], rhs=PT[u][:], start=True, stop=False),
                     reads=[C.t_const, PTt[u]], writes=[pNt], inc=False)
                k.op("pe", lambda e: e.matmul(pN[:, 256:384], lhsT=Nb[:, hh, :], rhs=Qp[u][:], start=False, stop=True),
                     reads=[St[hh], Qpt[u]], writes=[pNt], inc=True)
                k.op("act", lambda e: e.activation(out=rd[u][:], in_=pN[:, 256:384], func=AF.Abs), reads=[pNt], writes=[rdt[u]])
                k.op("dve", lambda e: e.tensor_scalar(out=rd[u][:], in0=rd[u][:], scalar1=1.0, scalar2=None, op0=ALU.max),
                     reads=[rdt[u]], writes=[rdt[u]])
                k.op("dve", lambda e: e.reciprocal(out=rd[u][:], in_=rd[u][:]), reads=[rdt[u]], writes=[rdt[u]])
                if p == 1:
                    for dvh in range(2):
                        k.dma("sp", hi[h3][:, dvh, :], h_in[2 * hh + dvh, :, bs], writes=[hit[h3]])
                for dvh in range(2):
                    k.op("dve", lambda e: e.tensor_tensor(out=hb[h3][:, dvh, :], in0=pN[:, dvh * 128:(dvh + 1) * 128], in1=rd[u][:],
                                                          op=ALU.mult), reads=[pNt, rdt[u]], writes=[hbt[h3]])
                if p == 0:
                    for dvh in range(2):
                        k.dma("sp", h_out[2 * hh + dvh, :, bs], hb[h3][:, dvh, :], reads=[hbt[h3]], final=True)
                else:
                    k.op("dve", lambda e: e.tensor_tensor(out=hb[h3][:], in0=hb[h3][:], in1=hi[h3][:], op=ALU.add),
                         reads=[hit[h3]], writes=[hbt[h3]])
                    for dvh in range(2):
                        k.dma("sp", h_out[2 * hh + dvh, :, bs], hb[h3][:, dvh, :], reads=[hbt[h3]], final=True)

        def fc():
                k.op("dve", lambda e: e.tensor_tensor(out=wc[u][:, 0:1], in0=IMB[:, blk, hh:hh + 1], in1=pA[:, ecol:ecol + 1],
                                                      op=ALU.add), reads=[pAt, t_g], writes=[wct[u]])
                k.op("act", lambda e: e.activation(out=wc[u][:, 0:1], in_=wc[u][:, 0:1], func=AF.Exp), reads=[wct[u]], writes=[wct[u]])
                k.op("act", lambda e: e.activation(out=wc[u][:, 1:2], in_=pA[:, ecol:ecol + 1], func=AF.Exp), reads=[pAt, wct[u]],
                     writes=[wct[u]])
                k.op("act", lambda e: e.activation(out=K2[u][:], in_=KK[bb][:, hh * 128:(hh + 1) * 128], func=AF.Identity,
                                                   scale=wc[u][:, 0:1]), reads=[t_blk[bb], wct[u]], writes=[K2t[u]])
                k.op("pe", lambda e: e.matmul(pC[:, 0:256], lhsT=K2[u][:], rhs=VV[bb][:, hh * MDV:(hh + 1) * MDV], start=True, stop=True),
                     reads=[K2t[u], t_blk[bb]], writes=[pCt], inc=False)
                k.op("pe", lambda e: e.matmul(pC[:, 256:384], lhsT=K2[u][:], rhs=C.ones_b[:], start=True, stop=True),
                     reads=[K2t[u], C.t_const], writes=[pCt], inc=True)
                k.op("dve", lambda e: e.scalar_tensor_tensor(out=Cf[:, hh, :], in0=Cf[:, hh, :], scalar=wc[u][:, 1:2], in1=pC[:, 0:256],
                                                             op0=ALU.mult, op1=ALU.add), reads=[pCt, wct[u], St[hh]], writes=[St[hh]])
                k.op("act", lambda e: e.copy(out=Cb[:, hh, :], in_=Cf[:, hh, :]), reads=[St[hh]], writes=[St[hh]])
                k.op("dve", lambda e: e.scalar_tensor_tensor(out=Nf[:, hh, :], in0=Nf[:, hh, :], scalar=wc[u][:, 1:2], in1=pC[:, 256:384],
                                                             op0=ALU.mult, op1=ALU.add), reads=[pCt, wct[u], St[hh]], writes=[St[hh]])
                k.op("act", lambda e: e.copy(out=Nb[:, hh, :], in_=Nf[:, hh, :]), reads=[St[hh]], writes=[St[hh]])

        return fa, fb, fc

    if p == 0:
        order = [("z",)] + [("b", b) for b in range(NBLK)]
    else:
        order = [("z",), ("b", 1), ("b", 0), ("load",)] + [("b", b) for b in range(NBLK - 1, 1, -1)]
    n = 0
    nh = 0
    nblk = 0
    pending = []
    for it in order:
        if it[0] in ("z", "load"):
            for f in pending:
                f()
            pending = []
        if it[0] == "z":
            zero_state()
            continue
        if it[0] == "load":
            for hh in range(MH):
                k.dma("sp", Cf[:, hh, :], state_in[0][hh], writes=[St[hh]])
                k.dma("sp", Nf[:, hh, :], state_in[1][hh], writes=[St[hh]])
                k.op("act", lambda e: e.copy(out=Cb[:, hh, :], in_=Cf[:, hh, :]), reads=[St[hh]], writes=[St[hh]])
                k.op("act", lambda e: e.copy(out=Nb[:, hh, :], in_=Nf[:, hh, :]), reads=[St[hh]], writes=[St[hh]])
            continue
        blk = it[1]
        bs = slice(blk * 128, (blk + 1) * 128)
        bb = nblk % NB_
        nblk += 1
        k.dma("sp", QT[bb][:], qT_d[:, :, bs].rearrange("h p t -> p h t"), writes=[t_blk[bb]])
        k.dma("sp", KT[bb][:], kT_d[:, :, bs].rearrange("h p t -> p h t"), writes=[t_blk[bb]])
        k.dma("sp", KK[bb][:], ktok_d[bs, :], writes=[t_blk[bb]])
        k.dma("sp", VV[bb][:], vtok_d[bs, :], writes=[t_blk[bb]])
        for hh in range(MH):
            fa, fb, fc = make_unit(blk, bb, bs, hh, n % NS, nh % 3)
            n += 1
            nh += 1
            fa()
            for f in pending:
                f()
            pending = [fb, fc]
    for f in pending:
        f()
    if p == 0:
        for hh in range(MH):
            k.dma("sp", state_out[0][hh], Cf[:, hh, :], reads=[St[hh]], final=True)
            k.dma("sp", state_out[1][hh], Nf[:, hh, :], reads=[St[hh]], final=True)


def emit_mlstm_readout(k, C, st, hs_d, so_d, ng_dram, wout_dram, ctx_out):
    X = C.X
    HN = st.sb("m_hn", [128, NCH, TOK], BF16)
    HNT = [DT("hn%d" % i) for i in range(3)]
    gn = st.sb("m_gn", [128, NCH], F32)
    t_c = DT("m_rc")
    k.dma("sp", gn[:], ng_dram, writes=[t_c])
    ps = [st.ps("rps%d" % i, [128, 512], F32) for i in range(4)]
    pst = [DT("rps%d" % i) for i in range(4)]
    sq = [st.sb("m_sq%d" % i, [128, 512], F32) for i in range(2)]
    sqt = [DT("sq") for _ in range(2)]
    rs = [st.sb("m_rs%d" % i, [128, 512], F32) for i in range(2)]
    rst = [DT("rs") for _ in range(2)]
    so = [st.sb("m_so%d" % i, [128, 2, 512], F32) for i in range(2)]
    sot = [DT("so%d" % i) for i in range(2)]
    hh_ = [st.sb("m_hh%d" % i, [128, 2, 512], F32) for i in range(2)]
    hht = [DT("hh%d" % i) for i in range(2)]
    n = 0
    for hh in range(MH):
        for bi, (t0, tl, v) in enumerate(TBS):
            if bi == 0 and not ctx_out:
                continue
            ts = slice(t0, t0 + tl)
            u = n % 2
            n += 1
            for dvh in range(2):
                k.dma("sp", so[u][:, dvh, :tl], so_d[2 * hh + dvh, :, ts], writes=[sot[u]])
                k.dma("sp", hh_[u][:, dvh, :tl], hs_d[2 * hh + dvh, :, ts], writes=[hht[u]])
            for dvh in range(2):
                q = (2 * n + dvh) % 2
                k.op("act", lambda e: e.activation(out=sq[q][:, :tl], in_=hh_[u][:, dvh, :tl], func=AF.Square),
                     reads=[hht[u]], writes=[sqt[q]])
                k.op("pe", lambda e: e.matmul(ps[u][:, :tl], lhsT=C.ones_f[:], rhs=sq[q][:, :tl], start=(dvh == 0), stop=(dvh == 1)),
                     reads=[sqt[q], C.t_const], writes=[pst[u]])
            k.op("act", lambda e: e.activation(out=rs[u][:, :tl], in_=ps[u][:, :tl], func=AF.Sqrt, bias=C.eps_col[:, 0:1],
                                               scale=1.0 / MDV), reads=[pst[u], C.t_const], writes=[rst[u]])
            k.op("dve", lambda e: e.reciprocal(out=rs[u][:, :tl], in_=rs[u][:, :tl]), reads=[rst[u]], writes=[rst[u]])
            for dvh in range(2):
                c = 2 * hh + dvh
                k.op("dve", lambda e: e.scalar_tensor_tensor(out=so[u][:, dvh, :tl], in0=so[u][:, dvh, :tl], scalar=gn[:, c:c + 1],
                                                             in1=rs[u][:, :tl], op0=ALU.mult, op1=ALU.mult),
                     reads=[sot[u], rst[u], t_c], writes=[sot[u]])
                k.op("dve", lambda e: e.tensor_tensor(out=HN[:, c, ts], in0=so[u][:, dvh, :tl], in1=hh_[u][:, dvh, :tl], op=ALU.mult),
                     reads=[sot[u], hht[u]], writes=[HNT[bi]])
    emit_outproj(k, C, st, HN, HNT, wout_dram, ctx_out, ps[2:4], pst[2:4])


def emit_outproj(k, C, st, IN, INT, w_dram, ctx_out, ps, pst):
    X = C.X
    NW = 2
    wo = [st.sb("op_w%d" % i, [128, NCH, 256], BF16) for i in range(NW)]
    wot = [DT("op_w%d" % i) for i in range(NW)]
    w_v = w_dram.rearrange("(c p) n -> p c n", p=128)
    ny = 0
    for dq in range(D // 256):
        b = dq % NW
        k.dma("pool", wo[b][:], w_v[:, :, dq * 256:(dq + 1) * 256], writes=[wot[b]])
        for dl in range(2):
            dc = dq * 2 + dl
            for bi, (t0, tl, v) in enumerate(TBS):
                if bi == 0 and not ctx_out:
                    continue
                ts = slice(t0, t0 + tl)
                p = ny % 2
                ny += 1
                for c in range(NCH):
                    k.op("pe", lambda e: e.matmul(ps[p][:, :tl], lhsT=wo[b][:, c, dl * 128:(dl + 1) * 128], rhs=IN[:, c, ts],
                                                  start=(c == 0), stop=(c == NCH - 1)),
                         reads=[wot[b], INT[bi]], writes=[pst[p]], inc=(c == NCH - 1))
                k.op("dve", lambda e: e.scalar_tensor_tensor(out=X[:, dc, ts], in0=ps[p][:, :tl], scalar=C.G[:, v, 1, dc:dc + 1],
                                                             in1=X[:, dc, ts], op0=ALU.mult, op1=ALU.add),
                     reads=[pst[p], C.t_mv, C.XT[bi]], writes=[C.XT[bi]])


def mlstm_consts():
    kk = np.arange(128)[:, None]
    tt = np.arange(128)[None, :]
    tri = np.stack([(kk <= tt), (kk >= tt)]).astype(np.float32)
    madd = ((tri - 1.0) * 1.0e4).astype(np.float32)
    return tri, madd


S5P = 64
S5T = TOK


def _eslice(seg, a, tl):
    t0, n, rev = seg
    if not rev:
        return slice(a - t0, a - t0 + tl)
    hi = n - 1 - (a - t0)
    lo = hi - tl + 1
    return slice(hi, lo - 1 if lo > 0 else None, -1)


def emit_s5_phase(k, C, p, lam_dram, bd_dram, cT_dram, dsk_dram, state_in, state_out, y_in, y_out, Z, ZT):
    segs = [(0, TOK, False)] if p == 0 else [(0, NCTX, True), (NCTX, NLAT, True)]
    with Stage(k) as st:
        ps = [st.ps("s5ps%d" % i, [128, 512], F32) for i in range(8)]
        pst = [DT("s5ps%d" % i) for i in range(8)]
        rstd = st.sb("s_rstd", [128, TOK], F32)
        t_rs = [DT("s_rs%d" % i) for i in range(3)]
        with Stage(k) as stn:
            emit_rstd(k, C, stn, rstd, t_rs, ps[6:8], pst[6:8])
        Hd = [st.sb("s_hd%d" % i, [128, TOK], BF16) for i in range(2)]
        HdT = [DT("s_hd%d" % i) for i in range(2)]
        htmp = [st.sb("s_htmp%d" % i, [128, 512], F32) for i in range(2)]
        htmpt = [DT("s_htmp%d" % i) for i in range(2)]
        LM = st.sb("s_lam", [128, 3, S5P], F32)
        t_p = DT("s5prep")
        k.dma("sp", LM[:], lam_dram[p], writes=[t_p])
        W = st.sb("s_w", [128, 16, S5P], F32)
        LR, LI, DTT, MAG, TH, CS, SN, AR1, AI, DEN, KR, KI, T1, T2, NSN, T3 = [W[:, i, :] for i in range(16)]

        def dv(fn):
            k.op("dve", fn, reads=[t_p, C.t_const], writes=[t_p])

        def ac(fn):
            k.op("act", fn, reads=[t_p, C.t_const], writes=[t_p])
        dv(lambda e: e.tensor_scalar(out=LR, in0=LM[:, 0, :], scalar1=-1e-4, scalar2=None, op0=ALU.min))
        ac(lambda e: e.activation(out=DTT, in_=LM[:, 2, :], func=AF.Exp))
        dv(lambda e: e.tensor_tensor(out=T1, in0=LR, in1=DTT, op=ALU.mult))
        ac(lambda e: e.activation(out=MAG, in_=T1, func=AF.Exp))
        dv(lambda e: e.tensor_tensor(out=TH, in0=LM[:, 1, :], in1=DTT, op=ALU.mult))
        ac(lambda e: e.activation(out=SN, in_=TH, func=AF.Sin, scale=1.0 / 8))
        ac(lambda e: e.activation(out=T1, in_=TH, func=AF.Sin, scale=1.0 / 16))
        dv(lambda e: e.tensor_tensor(out=T1, in0=T1, in1=T1, op=ALU.mult))
        dv(lambda e: e.tensor_scalar(out=CS, in0=T1, scalar1=-2.0, scalar2=1.0, op0=ALU.mult, op1=ALU.add))
        for _ in range(3):
            dv(lambda e: e.tensor_tensor(out=T1, in0=CS, in1=CS, op=ALU.mult))
            dv(lambda e: e.tensor_tensor(out=T2, in0=SN, in1=SN, op=ALU.mult))
            dv(lambda e: e.tensor_tensor(out=T3, in0=CS, in1=SN, op=ALU.mult))
            dv(lambda e: e.tensor_tensor(out=CS, in0=T1, in1=T2, op=ALU.subtract))
            dv(lambda e: e.tensor_scalar(out=SN, in0=T3, scalar1=2.0, scalar2=None, op0=ALU.mult))
        dv(lambda e: e.tensor_tensor(out=AR1, in0=MAG, in1=CS, op=ALU.mult))
        dv(lambda e: e.tensor_scalar(out=AR1, in0=AR1, scalar1=-1.0, scalar2=None, op0=ALU.add))
        dv(lambda e: e.tensor_tensor(out=AI, in0=MAG, in1=SN, op=ALU.mult))
        dv(lambda e: e.tensor_tensor(out=T1, in0=LR, in1=LR, op=ALU.mult))
        dv(lambda e: e.tensor_tensor(out=T2, in0=LM[:, 1, :], in1=LM[:, 1, :], op=ALU.mult))
        dv(lambda e: e.tensor_tensor(out=DEN, in0=T1, in1=T2, op=ALU.add))
        dv(lambda e: e.reciprocal(out=DEN, in_=DEN))
        dv(lambda e: e.tensor_tensor(out=T1, in0=AR1, in1=LR, op=ALU.mult))
        dv(lambda e: e.tensor_tensor(out=T2, in0=AI, in1=LM[:, 1, :], op=ALU.mult))
        dv(lambda e: e.tensor_tensor(out=T1, in0=T1, in1=T2, op=ALU.add))
        dv(lambda e: e.tensor_tensor(out=KR, in0=T1, in1=DEN, op=ALU.mult))
        dv(lambda e: e.tensor_tensor(out=T1, in0=AI, in1=LR, op=ALU.mult))
        dv(lambda e: e.tensor_tensor(out=T2, in0=AR1, in1=LM[:, 1, :], op=ALU.mult))
        dv(lambda e: e.tensor_tensor(out=T1, in0=T1, in1=T2, op=ALU.subtract))
        dv(lambda e: e.tensor_tensor(out=KI, in0=T1, in1=DEN, op=ALU.mult))
        dv(lambda e: e.tensor_scalar(out=NSN, in0=SN, scalar1=-1.0, scalar2=None, op0=ALU.mult))
        NLV = 11
        UP = st.sb("s_up", [128, NLV, 3, S5P], F32)
        dv(lambda e: e.tensor_copy(out=UP[:, 0, 0, :], in_=CS))
        dv(lambda e: e.tensor_copy(out=UP[:, 0, 1, :], in_=SN))
        dv(lambda e: e.tensor_copy(out=UP[:, 0, 2, :], in_=NSN))
        for l in range(1, NLV):
            dv(lambda e: e.tensor_tensor(out=T1, in0=UP[:, l - 1, 0, :], in1=UP[:, l - 1, 0, :], op=ALU.mult))
            dv(lambda e: e.tensor_tensor(out=T2, in0=UP[:, l - 1, 1, :], in1=UP[:, l - 1, 1, :], op=ALU.mult))
            dv(lambda e: e.tensor_tensor(out=UP[:, l, 0, :], in0=T1, in1=T2, op=ALU.subtract))
            dv(lambda e: e.tensor_tensor(out=T3, in0=UP[:, l - 1, 0, :], in1=UP[:, l - 1, 1, :], op=ALU.mult))
            dv(lambda e: e.tensor_scalar(out=UP[:, l, 1, :], in0=T3, scalar1=2.0, scalar2=None, op0=ALU.mult))
            dv(lambda e: e.tensor_scalar(out=UP[:, l, 2, :], in0=T3, scalar1=-2.0, scalar2=None, op0=ALU.mult))
        CR = st.sb("s_cr", [128, S5P, 16], F32)
        CI = st.sb("s_ci", [128, S5P, 16], F32)
        with Stage(k) as stc:
            CT = stc.sb("s_cT", [128, 2, S5P, 16], F32)
            k.dma("sp", CT[:], cT_dram[p], writes=[t_p])
            CW = stc.sb("s_cw", [128, S5P, 16], F32)
            krb = W[:, 10:11, :].rearrange("p o s -> p s o").to_broadcast([128, S5P, 16])
            kib = W[:, 11:12, :].rearrange("p o s -> p s o").to_broadcast([128, S5P, 16])
            dv(lambda e: e.tensor_tensor(out=CR[:], in0=CT[:, 0, :, :], in1=krb, op=ALU.mult))
            dv(lambda e: e.tensor_tensor(out=CW[:], in0=CT[:, 1, :, :], in1=kib, op=ALU.mult))
            dv(lambda e: e.tensor_tensor(out=CR[:], in0=CR[:], in1=CW[:], op=ALU.subtract))
            dv(lambda e: e.tensor_tensor(out=CI[:], in0=CT[:, 0, :, :], in1=kib, op=ALU.mult))
            dv(lambda e: e.tensor_tensor(out=CW[:], in0=CT[:, 1, :, :], in1=krb, op=ALU.mult))
            dv(lambda e: e.tensor_tensor(out=CI[:], in0=CI[:], in1=CW[:], op=ALU.add))
            dv(lambda e: e.tensor_scalar(out=CI[:], in0=CI[:], scalar1=-1.0, scalar2=None, op0=ALU.mult))
        dsk = st.sb("s_dsk", [128, NCH], F32)
        k.dma("sp", dsk[:], dsk_dram, writes=[t_p])
        SI = None
        if p == 1:
            SI = st.sb("s_si", [128, S5P, 2], F32)
            k.dma("sp", SI[:], state_in, writes=[t_p])
        SO = st.sb("s_so", [128, S5P, 2], F32)
        t_so = DT("s_so")
        E2 = [st.sb("s_E%d" % i, [128, 2, TOK], F32) for i in range(2)]
        t_E2 = [DT("s_E%d" % i) for i in range(2)]
        MB2 = [st.sb("s_magb%d" % i, [128, TOK], F32) for i in range(1)] * 2
        t_mb2 = [DT("s_magb%d" % i) for i in range(1)] * 2
        BR = [st.sb("s_br%d" % i, [128, TOK], F32) for i in range(2)]
        BI = [st.sb("s_bi%d" % i, [128, TOK], F32) for i in range(2)]
        t_b = [DT("s_b%d" % i) for i in range(2)]
        XR = [st.sb("s_xr%d" % i, [128, TOK], BF16) for i in range(1)] * 2
        XI = [st.sb("s_xi%d" % i, [128, TOK], BF16) for i in range(1)] * 2
        t_x = [DT("s_x%d" % i) for i in range(1)] * 2
        sc = [st.sb("s_sc%d" % i, [128, 512], F32) for i in range(4)]
        sct = [DT("s_sc%d" % i) for i in range(4)]
        tsc = [st.sb("s_tsc%d" % i, [128, 640], F32) for i in range(2)] * 2
        tsct = [DT("s_tsc%d" % i) for i in range(2)] * 2
        ini = st.sb("s_ini", [128, 4], F32)
        t_ini = DT("s_ini")
        BD = [st.sb("s_bd%d" % i, [128, 8, 128], BF16) for i in range(2)]
        BDt = [DT("s_bd%d" % i) for i in range(2)]
        CD = [st.sb("s_cd%d" % i, [128, 8, 128], BF16) for i in range(2)]
        CDt = [DT("s_cd%d" % i) for i in range(2)]
        yb = [st.sb("s_yb%d" % i, [128, 512], F32) for i in range(2)]
        ybt = [DT("s_yb%d" % i) for i in range(2)]
        yi = [st.sb("s_yi%d" % i, [128, 512], F32) for i in range(2)]
        yit = [DT("s_yi%d" % i) for i in range(2)]
        zb = [st.sb("s_zb%d" % i, [128, 512], BF16) for i in range(2)]
        zbt = [DT("s_zb%d" % i) for i in range(2)]

        def table_gen(tn):
            nonlocal nsc
            E, t_E, MB_, t_mb = E2[tn % 2], t_E2[tn % 2], MB2[tn % 2], t_mb2[tn % 2]
            P_ = tn
            k.op("pool", lambda e: e.memset(E[:, 0, 0:1], 1.0), writes=[t_E])
            k.op("pool", lambda e: e.memset(E[:, 1, 0:1], 0.0), writes=[t_E])
            seg = 1
            l = 0
            emax = TOK if p == 0 else NLAT
            while seg < emax:
                n_ = min(seg, emax - seg)
                s4 = nsc % 4
                nsc += 1
                ur, ui, nui = UP[:, l, 0, P_:P_ + 1], UP[:, l, 1, P_:P_ + 1], UP[:, l, 2, P_:P_ + 1]
                k.op("act", lambda e: e.activation(out=tsc[s4][:, :n_], in_=E[:, 0, 0:n_], func=AF.Identity, scale=ur),
                     reads=[t_E, t_p], writes=[tsct[s4]])
                k.op("dve", lambda e: e.scalar_tensor_tensor(out=E[:, 0, seg:seg + n_], in0=E[:, 1, 0:n_], scalar=nui, in1=tsc[s4][:, :n_],
                                                             op0=ALU.mult, op1=ALU.add), reads=[t_E, t_p, tsct[s4]], writes=[t_E])
                s4 = nsc % 4
                nsc += 1
                k.op("act", lambda e: e.activation(out=tsc[s4][:, :n_], in_=E[:, 1, 0:n_], func=AF.Identity, scale=ur),
                     reads=[t_E, t_p], writes=[tsct[s4]])
                k.op("dve", lambda e: e.scalar_tensor_tensor(out=E[:, 1, seg:seg + n_], in0=E[:, 0, 0:n_], scalar=ui, in1=tsc[s4][:, :n_],
                                                             op0=ALU.mult, op1=ALU.add), reads=[t_E, t_p, tsct[s4]], writes=[t_E])
                seg *= 2
                l += 1
                yield

        nt = 0
        nsc = 0
        nyb = 0
        ntile = 0
        for dc in range(NCH):
            db = dc % 2
            emit_hchunk(k, C, 1, dc, rstd, t_rs, Hd[db], HdT[db], htmp, htmpt)
            k.dma("pool", BD[db][:], bd_dram[p, dc], writes=[BDt[db]])
            k.op("pool", lambda e: e.memset(CD[db][:], 0.0), writes=[CDt[db]])
            for j in range(4):
                P_ = 4 * dc + j
                for gp in range(2):
                    rows = slice(gp * 64, (gp + 1) * 64)
                    cols = slice((2 * j + gp) * 16, (2 * j + gp + 1) * 16)
                    k.op("act", lambda e: e.copy(out=CD[db][rows, j, cols], in_=CR[rows, P_, :]), reads=[t_p], writes=[CDt[db]])
                    k.op("act", lambda e: e.copy(out=CD[db][rows, 4 + j, cols], in_=CI[rows, P_, :]), reads=[t_p], writes=[CDt[db]])
            ypst = pst[4:7]
            for j in range(4):
                P_ = 4 * dc + j
                u = nt % 2
                nt += 1
                E, t_E, MB_, t_mb = E2[ntile % 2], t_E2[ntile % 2], MB2[ntile % 2], t_mb2[ntile % 2]
                if ntile == 0:
                    for _ in table_gen(0):
                        pass
                gen = table_gen(ntile + 1) if ntile + 1 < 4 * NCH else iter(())

                def step():
                    next(gen, None)
                ntile += 1
                k.op("act", lambda e: e.activation(out=MB_[:], in_=E[:, 0, :], func=AF.Identity, scale=0.0, bias=MAG[:, P_:P_ + 1]),
                     reads=[t_E, t_p], writes=[t_mb])
                for bi, (t0, tl, v) in enumerate(TBS):
                    ts = slice(t0, t0 + tl)
                    sg = segs[0] if (p == 0 or bi == 0) else segs[1]
                    es_ = _eslice(sg, t0, tl)
                    pu = (nsc % 2) * 2
                    k.op("pe", lambda e: e.matmul(ps[pu][:, :tl], lhsT=BD[db][:, j, :], rhs=Hd[db][:, ts], start=True, stop=True),
                         reads=[BDt[db], HdT[db]], writes=[pst[pu]])
                    k.op("pe", lambda e: e.matmul(ps[pu + 1][:, :tl], lhsT=BD[db][:, 4 + j, :], rhs=Hd[db][:, ts], start=True, stop=True),
                         reads=[BDt[db], HdT[db]], writes=[pst[pu + 1]])
                    a4, b4 = nsc % 4, (nsc + 1) % 4
                    nsc += 2
                    k.op("dve", lambda e: e.tensor_tensor(out=sc[a4][:, :tl], in0=ps[pu][:, :tl], in1=E[:, 0, es_], op=ALU.mult),
                         reads=[pst[pu], t_E], writes=[sct[a4]])
                    k.op("dve", lambda e: e.tensor_tensor(out=sc[b4][:, :tl], in0=ps[pu + 1][:, :tl], in1=E[:, 1, es_], op=ALU.mult),
                         reads=[pst[pu + 1], t_E], writes=[sct[b4]])
                    k.op("dve", lambda e: e.tensor_tensor(out=BR[u][:, ts], in0=sc[a4][:, :tl], in1=sc[b4][:, :tl], op=ALU.add),
                         reads=[sct[a4], sct[b4]], writes=[t_b[u]])
                    step()
                    a4, b4 = nsc % 4, (nsc + 1) % 4
                    nsc += 2
                    k.op("dve", lambda e: e.tensor_tensor(out=sc[a4][:, :tl], in0=ps[pu + 1][:, :tl], in1=E[:, 0, es_], op=ALU.mult),
                         reads=[pst[pu + 1], t_E], writes=[sct[a4]])
                    k.op("dve", lambda e: e.tensor_tensor(out=sc[b4][:, :tl], in0=ps[pu][:, :tl], in1=E[:, 1, es_], op=ALU.mult),
                         reads=[pst[pu], t_E], writes=[sct[b4]])
                    k.op("dve", lambda e: e.tensor_tensor(out=BI[u][:, ts], in0=sc[a4][:, :tl], in1=sc[b4][:, :tl], op=ALU.subtract),
                         reads=[sct[a4], sct[b4]], writes=[t_b[u]])
                    step()
                for si, sg in enumerate(segs):
                    t0, n_, rev = sg
                    vs = slice(t0 + n_ - 1, t0 - 1 if t0 > 0 else None, -1) if rev else slice(t0, t0 + n_)
                    if p == 1 and si == 1:
                        x0r, x0i = SI[:, P_, 0:1], SI[:, P_, 1:2]
                        ur, ui, nui = UP[:, 0, 0, P_:P_ + 1], UP[:, 0, 1, P_:P_ + 1], UP[:, 0, 2, P_:P_ + 1]
                        k.op("dve", lambda e: e.tensor_tensor(out=ini[:, 2:3], in0=x0r, in1=ur, op=ALU.mult), reads=[t_p], writes=[t_ini])
                        k.op("dve", lambda e: e.scalar_tensor_tensor(out=ini[:, 0:1], in0=x0i, scalar=nui, in1=ini[:, 2:3], op0=ALU.mult, op1=ALU.add),
                             reads=[t_p, t_ini], writes=[t_ini])
                        k.op("dve", lambda e: e.tensor_tensor(out=ini[:, 3:4], in0=x0i, in1=ur, op=ALU.mult), reads=[t_p, t_ini], writes=[t_ini])
                        k.op("dve", lambda e: e.scalar_tensor_tensor(out=ini[:, 1:2], in0=x0r, scalar=ui, in1=ini[:, 3:4], op0=ALU.mult, op1=ALU.add),
                             reads=[t_p, t_ini], writes=[t_ini])
                        i_r, i_i = ini[:, 0:1], ini[:, 1:2]
                    else:
                        i_r, i_i = 0.0, 0.0
                    k.op("dve", lambda e: e.tensor_tensor_scan(out=BR[u][:, vs], data0=MB_[:, 0:n_], data1=BR[u][:, vs], initial=i_r,
                                                               op0=ALU.mult, op1=ALU.add), reads=[t_mb, t_ini], writes=[t_b[u]])
                    k.op("dve", lambda e: e.tensor_tensor_scan(out=BI[u][:, vs], data0=MB_[:, 0:n_], data1=BI[u][:, vs], initial=i_i,
                                                               op0=ALU.mult, op1=ALU.add), reads=[t_mb, t_ini], writes=[t_b[u]])
                    step()
                for bi, (t0, tl, v) in enumerate(TBS):
                    ts = slice(t0, t0 + tl)
                    sg = segs[0] if (p == 0 or bi == 0) else segs[1]
                    es_ = _eslice(sg, t0, tl)
                    a4, b4 = nsc % 4, (nsc + 1) % 4
                    nsc += 2
                    k.op("dve", lambda e: e.tensor_tensor(out=sc[a4][:, :tl], in0=BR[u][:, ts], in1=E[:, 0, es_], op=ALU.mult),
                         reads=[t_b[u], t_E], writes=[sct[a4]])
                    k.op("dve", lambda e: e.tensor_tensor(out=sc[b4][:, :tl], in0=BI[u][:, ts], in1=E[:, 1, es_], op=ALU.mult),
                         reads=[t_b[u], t_E], writes=[sct[b4]])
                    k.op("dve", lambda e: e.tensor_tensor(out=XR[u][:, ts], in0=sc[a4][:, :tl], in1=sc[b4][:, :tl], op=ALU.subtract),
                         reads=[sct[a4], sct[b4]], writes=[t_x[u]])
                    step()
                    if p == 0 and bi == 2:
                        k.op("act", lambda e: e.activation(out=ini[:, 2:3], in_=sc[a4][:, tl - 1:tl], func=AF.Identity,
                                                           bias=sc[b4][:, tl - 1:tl], scale=1.0), reads=[sct[a4], sct[b4]], writes=[t_ini])
                        k.op("dve", lambda e: e.tensor_scalar(out=SO[:, P_, 0:1], in0=sc[b4][:, tl - 1:tl], scalar1=-2.0, scalar2=ini[:, 2:3],
                                                              op0=ALU.mult, op1=ALU.add), reads=[sct[b4], t_ini], writes=[t_so])
                    a4, b4 = nsc % 4, (nsc + 1) % 4
                    nsc += 2
                    k.op("dve", lambda e: e.tensor_tensor(out=sc[a4][:, :tl], in0=BI[u][:, ts], in1=E[:, 0, es_], op=ALU.mult),
                         reads=[t_b[u], t_E], writes=[sct[a4]])
                    k.op("dve", lambda e: e.tensor_tensor(out=sc[b4][:, :tl], in0=BR[u][:, ts], in1=E[:, 1, es_], op=ALU.mult),
                         reads=[t_b[u], t_E], writes=[sct[b4]])
                    k.op("dve", lambda e: e.tensor_tensor(out=XI[u][:, ts], in0=sc[a4][:, :tl], in1=sc[b4][:, :tl], op=ALU.add),
                         reads=[sct[a4], sct[b4]], writes=[t_x[u]])
                    step()
                    if p == 0 and bi == 2:
                        k.op("dve", lambda e: e.tensor_tensor(out=SO[:, P_, 1:2], in0=sc[a4][:, tl - 1:tl], in1=sc[b4][:, tl - 1:tl], op=ALU.add),
                             reads=[sct[a4], sct[b4]], writes=[t_so])
                    yp = ps[4 + bi]
                    k.op("pe", lambda e: e.matmul(yp[:, :tl], lhsT=CD[db][:, j, :], rhs=XR[u][:, ts], start=(j == 0), stop=False),
                         reads=[CDt[db], t_x[u]], writes=[ypst[bi]], inc=False)
                    k.op("pe", lambda e: e.matmul(yp[:, :tl], lhsT=CD[db][:, 4 + j, :], rhs=XI[u][:, ts], start=False, stop=(j == 3)),
                         reads=[CDt[db], t_x[u]], writes=[ypst[bi]], inc=True)
                for _ in gen:
                    pass
            for bi, (t0, tl, v) in enumerate(TBS):
                ts = slice(t0, t0 + tl)
                q = nyb % 2
                nyb += 1
                if p == 0:
                    k.op("dve", lambda e: e.scalar_tensor_tensor(out=yb[q][:, :tl], in0=Hd[db][:, ts], scalar=dsk[:, dc:dc + 1], in1=ps[4 + bi][:, :tl],
                                                                 op0=ALU.mult, op1=ALU.add), reads=[HdT[db], t_p, ypst[bi]], writes=[ybt[q]])
                    k.dma("sp", y_out[dc, :, ts], yb[q][:, :tl], reads=[ybt[q]], final=True)
                else:
                    k.dma("sp", yi[q][:, :tl], y_in[dc, :, ts], writes=[yit[q]])
                    k.op("dve", lambda e: e.tensor_tensor(out=yb[q][:, :tl], in0=ps[4 + bi][:, :tl], in1=yi[q][:, :tl], op=ALU.add),
                         reads=[ypst[bi], yit[q]], writes=[ybt[q]])
                    k.op("dve", lambda e: e.tensor_tensor(out=yi[q][:, :tl], in0=yb[q][:, :tl], in1=yb[q][:, :tl], op=ALU.mult),
                         reads=[ybt[q]], writes=[yit[q]])
                    k.op("dve", lambda e: e.tensor_scalar(out=yi[q][:, :tl], in0=yi[q][:, :tl], scalar1=0.044715, scalar2=1.0, op0=ALU.mult, op1=ALU.add),
                         reads=[], writes=[yit[q]])
                    k.op("dve", lambda e: e.tensor_tensor(out=yi[q][:, :tl], in0=yi[q][:, :tl], in1=yb[q][:, :tl], op=ALU.mult),
                         reads=[ybt[q]], writes=[yit[q]])
                    k.op("act", lambda e: e.activation(out=yi[q][:, :tl], in_=yi[q][:, :tl], func=AF.Tanh, scale=0.7978845608028654),
                         reads=[], writes=[yit[q]])
                    k.op("act", lambda e: e.mul(out=yb[q][:, :tl], in_=yb[q][:, :tl], mul=0.5),
                         reads=[], writes=[ybt[q]])
                    k.op("dve", lambda e: e.scalar_tensor_tensor(out=zb[q][:, :tl], in0=yi[q][:, :tl], scalar=1.0, in1=yb[q][:, :tl],
                                                                 op0=ALU.add, op1=ALU.mult), reads=[yit[q], ybt[q]], writes=[zbt[q]])
                    k.dma("sp", Z[dc, :, ts], zb[q][:, :tl], reads=[zbt[q]], final=True)
        if p == 0:
            k.dma("sp", state_out, SO[:], reads=[t_so], final=True)


def emit_s5_glu(k, C, st, Z, ZT, wglu_dram):
    X = C.X
    NW = 3
    wt = [st.sb("g_w%d" % i, [128, NCH, 256], BF16) for i in range(NW)]
    wtt = [DT("g_w%d" % i) for i in range(NW)]
    sg = [st.sb("g_sg%d" % i, [128, 512], F32) for i in range(2)]
    sgt = [DT("g_sg%d" % i) for i in range(2)]
    ps = [st.ps("gps%d" % i, [128, 512], F32) for i in range(4)]
    pst = [DT("gps%d" % i) for i in range(4)]
    w_v = wglu_dram.rearrange("(c p) n -> p c n", p=128)
    nu = 0
    for dc in range(NCH):
        b = dc % NW
        k.dma("pool", wt[b][:, :, 0:128], w_v[:, :, dc * 128:(dc + 1) * 128], writes=[wtt[b]])
        k.dma("pool", wt[b][:, :, 128:256], w_v[:, :, D + dc * 128:D + (dc + 1) * 128], writes=[wtt[b]])
        for bi, (t0, tl, v) in enumerate(TBS):
            ts = slice(t0, t0 + tl)
            u = nu % 2
            nu += 1
            pa, pg = ps[2 * u], ps[2 * u + 1]
            for c in range(NCH):
                k.op("pe", lambda e: e.matmul(pa[:, :tl], lhsT=wt[b][:, c, 0:128], rhs=Z[:, c, ts], start=(c == 0), stop=(c == NCH - 1)),
                     reads=[wtt[b], ZT[bi]], writes=[pst[2 * u]], inc=(c == NCH - 1))
            for c in range(NCH):
                k.op("pe", lambda e: e.matmul(pg[:, :tl], lhsT=wt[b][:, c, 128:256], rhs=Z[:, c, ts], start=(c == 0), stop=(c == NCH - 1)),
                     reads=[wtt[b], ZT[bi]], writes=[pst[2 * u + 1]], inc=(c == NCH - 1))
            k.op("act", lambda e: e.activation(out=sg[u][:, :tl], in_=pg[:, :tl], func=AF.Sigmoid), reads=[pst[2 * u + 1]], writes=[sgt[u]])
            k.op("dve", lambda e: e.tensor_tensor(out=sg[u][:, :tl], in0=pa[:, :tl], in1=sg[u][:, :tl], op=ALU.mult),
                 reads=[pst[2 * u], sgt[u]], writes=[sgt[u]])
            k.op("dve", lambda e: e.scalar_tensor_tensor(out=X[:, dc, ts], in0=sg[u][:, :tl], scalar=C.G[:, v, 1, dc:dc + 1], in1=X[:, dc, ts],
                                                          op0=ALU.mult, op1=ALU.add), reads=[sgt[u], C.t_mv, C.XT[bi]], writes=[C.XT[bi]])


def s5_host_arrays(I, half):
    occ = 0
    dirs = [0, 1] if half == 0 else [1, 0]
    lam = np.zeros((2, 128, 3, S5P), np.float32)
    bd = np.zeros((2, NCH, 128, 8, 128), np.float32)
    cT = np.zeros((2, 128, 2, S5P, 16), np.float32)
    for s, dr in enumerate(dirs):
        for nm, idx in (("s5_lam_re", 0), ("s5_lam_im", 1)):
            a = I[nm][occ, dr].reshape(S5P, 2, 64)
            lam[s, :, idx, :] = a.transpose(1, 2, 0).reshape(128, S5P)
        ld = np.broadcast_to(I["s5_log_dt"][occ, dr].reshape(S5P, 2, 1), (S5P, 2, 64))
        lam[s, :, 2, :] = ld.transpose(1, 2, 0).reshape(128, S5P)
        for ri, nm in enumerate(("s5_b_re", "s5_b_im")):
            B = I[nm][occ, dr]
            for dc in range(NCH):
                for j in range(4):
                    for gp in range(2):
                        g = 8 * dc + 2 * j + gp
                        gl = 2 * j + gp
                        bd[s, dc, gl * 16:(gl + 1) * 16, ri * 4 + j, gp * 64:(gp + 1) * 64] = B[g].T
        for ri, nm in enumerate(("s5_c_re", "s5_c_im")):
            Cc = I[nm][occ, dr].reshape(S5P, 2, 16, 64)
            cT[s, :, ri, :, :] = Cc.transpose(1, 3, 0, 2).reshape(128, S5P, 16)
    return lam, bd, cT


NVC = 2
NACT = 4


class FProg:
    def __init__(self):
        self.nc = bass.Bass("TRN2", target_bir_lowering=False)
        self.ins = {}
        self.scr = {}
        self.outs = {}

    def inp(self, name, shape, dt=F32):
        if name in self.scr:
            return self.scr[name]
        if name not in self.ins:
            self.ins[name] = self.nc.dram_tensor(name, list(shape), dt, kind="ExternalInput").ap()
        return self.ins[name]

    def tmp(self, name, shape, dt=F32):
        if name not in self.scr:
            self.scr[name] = self.nc.dram_tensor(name, list(shape), dt).ap()
        return self.scr[name]

    def out(self, name, shape, dt=F32):
        if name not in self.outs:
            self.outs[name] = self.nc.dram_tensor(name, list(shape), dt, kind="ExternalOutput").ap()
        return self.outs[name]


SEGMENTS = [
    [("modvec", 0), ("ffn", 0, 0), ("mix1", 0)],
    [("mix2", 0), ("ffn", 0, 1), ("modvec", 1), ("ffn", 1, 0), ("mix1", 1)],
    [("mix2", 1), ("ffn", 1, 1), ("modvec", 2), ("ffn", 2, 0), ("mix1", 2)],
    [("mix2", 2), ("ffn", 2, 1), ("modvec", 3), ("ffn", 3, 0), ("mix1", 3)],
    [("mix2", 3), ("ffn", 3, 1)],
]


def set_layer(C, i):
    m = C.mvsets[i % 2]
    C.MV, C.MB, C.NG, C.A, C.G, C.t_mv = m


def emit_step(k, C, P, step, v):
    op = step[0]
    sf = "_v%d" % v
    so = "_v%d" % (1 - v)
    if op == "modvec":
        i = step[1]
        if v != 0:
            return
        set_layer(C, i)
        with Stage(k) as st:
            emit_modvec(k, C, st, P.inp("condT", [128, 2, NCH]), P.inp("modw%d" % i, [D, 9 * D]),
                        P.inp("modb%d" % i, [128, 9 * NCH]), P.inp("ng%d" % i, [128, 3, NCH]), P.inp("eye2", [2, 2]))
    elif op == "ffn":
        i, j = step[1], step[2]
        set_layer(C, i)
        emit_ffn(k, C, 0 if j == 0 else 2, P.inp("wi%d_%d" % (i, j), [D, 2 * DFF]),
                 P.inp("wo%d_%d" % (i, j), [DFF, D]), lat_only=(i == 3 and j == 1))
    elif op == "mix1":
        i = step[1]
        set_layer(C, i)
        kind = KINDS[i % 4]
        if kind in ("a", "w"):
            emit_qkv(k, C, P.inp("wqkv%d" % i, [D, QKV_W]), P.inp("qkg%d" % i, [128, 2]),
                     P.inp("cos" + sf, [128, NLAT]), P.inp("sin" + sf, [128, NLAT]), P.inp("rm", [128, 128]),
                     P.tmp("qT%d" % i + sf, [NH, 128, TOK], BF16), P.tmp("kT%d" % i + sf, [NKV, 128, TOK], BF16),
                     P.tmp("v%d" % i + sf, [TOK, 512], BF16))
        elif kind == "m":
            mq = P.tmp("mq%d" % i + sf, [MH, 128, TOK], BF16)
            mk_ = P.tmp("mk%d" % i + sf, [MH, 128, TOK], BF16)
            mkt = P.tmp("mkt%d" % i + sf, [TOK, MH * MDQK], BF16)
            mvt = P.tmp("mvt%d" % i + sf, [TOK, MH * MDV], BF16)
            mso = P.tmp("mso%d" % i + sf, [NCH, 128, TOK], F32)
            mgi = P.tmp("mgi%d" % i + sf, [TOK, 32], F32)
            emit_mlstm_proj(k, C, P.inp("mwin%d" % i, [D, 6144]), P.inp("mwg%d" % i + sf, [D, 32]),
                            P.inp("mbg%d" % i + sf, [128, 32]), mq, mk_, mkt, mvt, mso, mgi)
            with Stage(k) as st:
                emit_mlstm_scan(k, C, st, 0, mq, mk_, mkt, mvt, mgi, P.inp("mtri", [2, 128, 128]),
                                P.inp("mmadd", [2, 128, 128]), None,
                                (P.tmp("mstC%d" % i + sf, [MH, 128, MDV]), P.tmp("mstN%d" % i + sf, [MH, 128, 128])),
                                None, P.tmp("mh1_%d" % i + sf, [NCH, 128, TOK]))
        elif kind == "s":
            emit_s5_phase(k, C, 0, P.inp("s5lam" + sf, [2, 128, 3, S5P]), P.inp("s5bd" + sf, [2, NCH, 128, 8, 128]),
                          P.inp("s5cT" + sf, [2, 128, 2, S5P, 16]), P.inp("s5dsk", [128, NCH]), None,
                          P.tmp("s5st%d" % i + sf, [128, S5P, 2]), None, P.tmp("s5y1_%d" % i + sf, [NCH, 128, TOK]), None, None)
    elif op == "mix2":
        i = step[1]
        set_layer(C, i)
        kind = KINDS[i % 4]
        if kind in ("a", "w"):
            kts = [P.scr["kT%d_v%d" % (i, q)] for q in range(NVC)]
            vs = [P.scr["v%d_v%d" % (i, q)] for q in range(NVC)]
            other = (kts, vs) if kind == "a" else (kts[1 - v], vs[1 - v])
            emit_attn(k, C, kind == "w", i != 3, P.scr["qT%d" % i + sf], kts[v], vs[v], other,
                      P.inp("esink%d" % i, [128, NH]) if kind == "w" else None,
                      P.inp("masks", [128, 4, 128], BF16) if kind == "w" else None,
                      P.inp("awo%d" % i, [D, D]))
        elif kind == "m":
            hs = P.tmp("mhs%d" % i + sf, [NCH, 128, TOK])
            with Stage(k) as st:
                emit_mlstm_scan(k, C, st, 1, P.scr["mq%d" % i + sf], P.scr["mk%d" % i + sf], P.scr["mkt%d" % i + sf],
                                P.scr["mvt%d" % i + sf], P.scr["mgi%d" % i + sf],
                                P.inp("mtri", [2, 128, 128]), P.inp("mmadd", [2, 128, 128]),
                                (P.scr["mstC%d" % i + so], P.scr["mstN%d" % i + so]),
                                None, P.scr["mh1_%d" % i + sf], hs)
            with Stage(k) as st:
                emit_mlstm_readout(k, C, st, hs, P.scr["mso%d" % i + sf],
                                   P.inp("mng%d" % i, [128, NCH]), P.inp("mwout%d" % i, [D, D]), True)
        elif kind == "s":
            zd = P.tmp("s5z%d" % i + sf, [NCH, 128, TOK], BF16)
            emit_s5_phase(k, C, 1, P.inp("s5lam" + sf, [2, 128, 3, S5P]), P.inp("s5bd" + sf, [2, NCH, 128, 8, 128]),
                          P.inp("s5cT" + sf, [2, 128, 2, S5P, 16]), P.inp("s5dsk", [128, NCH]),
                          P.scr["s5st%d" % i + so], None, P.scr["s5y1_%d" % i + sf], None, zd, None)
            with Stage(k) as st:
                Z = st.sb("s_z", [128, NCH, TOK], BF16)
                ZT = [DT("z%d" % q) for q in range(3)]
                for q, (t0, tl, vv) in enumerate(TBS):
                    k.dma("sp", Z[:, :, t0:t0 + tl], zd[:, :, t0:t0 + tl].rearrange("c p t -> p c t"), writes=[ZT[q]])
                emit_s5_glu(k, C, st, Z, ZT, P.inp("s5wglu", [D, 2 * D]))
    else:
        raise ValueError(op)


def build_fused(segments=None, nvc=NVC):
    segments = SEGMENTS if segments is None else segments
    P = FProg()
    nc = P.nc
    with ExitStack() as es:
        k = KB(nc, es)
        C = Ctx()
        with Stage(k) as st0:
            C.X = st0.sb("X", [128, NCH, TOK], F32)
            C.XT = [DT("x%d" % i) for i in range(3)]
            C.mvsets = []
            for q in range(2):
                C.mvsets.append((st0.sb("MV%d" % q, [128, 2, 9 * NCH], F32), st0.sb("MB%d" % q, [128, 9 * NCH], F32),
                                 st0.sb("NG%d" % q, [128, 3, NCH], F32), st0.sb("A%d" % q, [128, 2, 3, NCH], F32),
                                 st0.sb("G%d" % q, [128, 2, 3, NCH], F32), DT("mv%d" % q)))
            setup_consts(k, st0, C)
            for si, seg in enumerate(segments):
                last = si == len(segments) - 1
                for v in range(nvc):
                    sf = "_v%d" % v
                    src = P.inp("xT" + sf, [D, TOK]) if si == 0 else P.scr["xpark" + sf]
                    load_x(k, C, src)
                    for step in seg:
                        emit_step(k, C, P, step, v)
                    if last:
                        store_x(k, C, P.out("xo" + sf, [D, NLAT]), True)
                    else:
                        store_x(k, C, P.tmp("xpark" + sf, [D, TOK]), False)
                    k.barrier()
            k.finish()
        P.ninstr = k.ninstr
    return P


def host_inputs(I, b, names):
    out = {}
    s5 = None
    for name in names:
        v = None
        base = name
        if len(name) > 3 and name[-3:-1] == "_v":
            v = int(name[-1])
            base = name[:-3]
        if base == "xT":
            out[name] = core_tokens_T(I["x"], I["ctx"], 2 * b + v)
        elif base == "condT":
            out[name] = vec_pm(np.stack([I["c"][b], I["c_ctx"]]))
        elif base == "rm":
            out[name] = rot_matrix()
        elif base == "eye2":
            out[name] = np.eye(2, dtype=np.float32)
        elif base == "masks":
            out[name] = win_masks()
        elif base == "cos":
            out[name] = rope_tables(v)[0]
        elif base == "sin":
            out[name] = rope_tables(v)[1]
        elif base == "mtri":
            out[name] = mlstm_consts()[0]
        elif base == "mmadd":
            out[name] = mlstm_consts()[1]
        elif base in ("s5lam", "s5bd", "s5cT"):
            if s5 is None:
                s5 = [s5_host_arrays(I, h) for h in range(2)]
            out[name] = s5[v][("s5lam", "s5bd", "s5cT").index(base)]
        elif base == "s5dsk":
            out[name] = vec_pm(I["s5_d"][0])
        elif base == "s5wglu":
            out[name] = I["s5_w_glu"][0]
        else:
            i = int(base[-3]) if base[-2] == "_" else int(base[-1])
            bb = base[:-3] if base[-2] == "_" else base[:-1]
            pre = "a_" if i % 4 == 0 else "w_"
            if bb == "modw":
                out[name] = I["mod_w"][i]
            elif bb == "modb":
                out[name] = vec_pm(I["mod_b"][i])
            elif bb == "ng":
                out[name] = vec_pm(I["norm_g"][i])
            elif bb == "wi":
                out[name] = I["ffn_wi"][i, int(base[-1])]
            elif bb == "wo":
                out[name] = I["ffn_wo"][i, int(base[-1])]
            elif bb == "wqkv":
                out[name] = I[pre + "wqkv"][0]
            elif bb == "qkg":
                out[name] = np.ascontiguousarray(I[pre + "qk_g"][0].T)
            elif bb == "awo":
                out[name] = I[pre + "wo"][0]
            elif bb == "esink":
                out[name] = np.ascontiguousarray(np.broadcast_to(I["w_sink"][0][None, :], (128, NH)))
            elif bb == "mwin":
                out[name] = I["m_w_in"][0]
            elif bb in ("mwg", "mbg"):
                perm = np.arange(32) if v == 0 else np.concatenate([np.arange(16, 32), np.arange(0, 16)])
                if bb == "mwg":
                    out[name] = np.ascontiguousarray(I["m_w_gate"][0][:, perm])
                else:
                    out[name] = np.ascontiguousarray(np.broadcast_to(I["m_b_gate"][0][perm][None, :], (128, 32)))
            elif bb == "mng":
                out[name] = vec_pm(I["m_norm_g"][0])
            elif bb == "mwout":
                out[name] = I["m_w_out"][0]
            else:
                raise KeyError(name)
    return out


def kernel(**inputs):
    I = {k_: np.asarray(v) for k_, v in inputs.items()}
    P = build_fused()
    names = list(P.ins)
    in_maps = [host_inputs(I, b, names) for b in range(NACT)]
    res = run_bass_kernel_spmd(P.nc, in_maps, core_ids=list(range(NACT)))
    B = I["x"].shape[0]
    out = np.empty((B, SEQ, D), np.float32)
    for b in range(NACT):
        for v in range(NVC):
            y = res.results[b]["xo_v%d" % v].T
            if v == 1:
                y = y[::-1]
            out[b, v * NLAT:(v + 1) * NLAT] = y
    return out
```

```python
import numpy as np
from contextlib import ExitStack
import concourse.bass as bass
import concourse.mybir as mybir
from concourse.bass_utils import run_bass_kernel_spmd

F32 = mybir.dt.float32
BF16 = mybir.dt.bfloat16
AF = mybir.ActivationFunctionType
ALU = mybir.AluOpType
AX = mybir.AxisListType

D = 2048
NCH = 16
DFF = 5632
NFC = 44
NCTX = 256
NLAT = 1024
TOK = NCTX + NLAT
SEQ = 2048
EPS = 1e-6
TBS = [(0, 256, 1), (256, 512, 0), (768, 512, 0)]
NCORES = 8


class DT:
    __slots__ = ("name", "w", "r", "dsem", "dcnt")

    def __init__(self, name=""):
        self.name = name
        self.w = {}
        self.r = {}
        self.dsem = None
        self.dcnt = 0


class KB:
    def __init__(self, nc, es):
        self.nc = nc
        self.es = es
        self.engs = {"pe": nc.tensor, "act": nc.scalar, "dve": nc.vector, "pool": nc.gpsimd, "sp": nc.sync}
        self.sem = {n: es.enter_context(nc.semaphore("s_" + n)) for n in ("pe", "act", "dve", "pool")}
        self.cnt = {n: 0 for n in self.sem}
        self.waited = {n: {} for n in self.engs}
        self.bound = []
        self.all_sems = []
        self.free_sems = []
        self.scount = {}
        self.final = []
        self.nsem = 0
        self.ninstr = 0
        self.uid = 0

    def _wait(self, e, deps):
        need = {}
        for d in deps:
            for s, v in d.items():
                if need.get(s, 0) < v:
                    need[s] = v
        w = self.waited[e]
        for s, v in need.items():
            if e == "pe" and s is self.sem["pe"]:
                continue
            if w.get(s, 0) >= v:
                continue
            self.engs[e].wait_ge(s, v)
            w[s] = v

    def op(self, e, fn, reads=(), writes=(), inc=True):
        deps = [t.w for t in reads]
        for t in writes:
            deps.append(t.w)
            deps.append(t.r)
        self._wait(e, deps)
        ins = fn(self.engs[e])
        self.ninstr += 1
        s = self.sem[e]
        if inc:
            self.cnt[e] += 1
            ins.then_inc(s, 1)
            v = self.cnt[e]
        else:
            v = self.cnt[e] + 1
        for t in reads:
            t.r[s] = v
        for t in writes:
            t.w[s] = v
            t.r = {}
        return ins

    def dma(self, q, out, in_, reads=(), writes=(), final=False):
        deps = [t.w for t in reads]
        for t in writes:
            deps.append(t.w)
            deps.append(t.r)
        self._wait(q, deps)
        t0 = (list(writes) + list(reads))[0]
        if t0.dsem is None:
            if self.free_sems:
                t0.dsem = self.free_sems.pop()
            else:
                t0.dsem = self.es.enter_context(self.nc.semaphore("d%d" % self.nsem))
                self.nsem += 1
                self.all_sems.append(t0.dsem)
                self.scount[t0.dsem] = 0
            self.bound.append(t0)
        self.scount[t0.dsem] += 16
        cnt = self.scount[t0.dsem]
        ins = self.engs[q].dma_start(out=out, in_=in_).then_inc(t0.dsem, 16)
        self.ninstr += 1
        for t in reads:
            t.r[t0.dsem] = cnt
        for t in writes:
            t.w[t0.dsem] = cnt
            t.r = {}
        if final:
            self.final.append({t0.dsem: cnt})
        return ins

    def barrier(self):
        allt = {self.sem[n]: self.cnt[n] for n in self.sem if self.cnt[n] > 0}
        for sm in self.all_sems:
            if self.scount[sm] > 0:
                allt[sm] = self.scount[sm]
        for e in self.engs:
            self._wait(e, [allt])
        for t in self.bound:
            t.dsem = None
        self.bound = []
        self.free_sems = list(self.all_sems)

    def finish(self):
        self._wait("sp", self.final)


class Stage:
    def __init__(self, k):
        self.k = k
        self.es = ExitStack()

    def __enter__(self):
        self.es.__enter__()
        return self

    def __exit__(self, *a):
        self.k.barrier()
        return self.es.__exit__(*a)

    def sb(self, name, shape, dt):
        self.k.uid += 1
        return self.es.enter_context(self.k.nc.sbuf_tensor("sb%d_%s" % (self.k.uid, name), list(shape), dt))

    def ps(self, name, shape, dt=F32):
        self.k.uid += 1
        return self.es.enter_context(self.k.nc.psum_tensor("ps%d_%s" % (self.k.uid, name), list(shape), dt))


class Ctx:
    pass


def setup_consts(k, st, C):
    nc = k.nc
    C.ones_f = st.sb("ones_f", [128, 128], F32)
    C.ones_b = st.sb("ones_b", [128, 128], BF16)
    C.t_const = DT("const")
    k.op("pool", lambda e: e.memset(C.ones_f[:], 1.0), writes=[C.t_const])
    k.op("pool", lambda e: e.memset(C.ones_b[:], 1.0), writes=[C.t_const])
    C.eps_col = st.sb("eps_col", [128, 2], F32)
    k.op("pool", lambda e: e.memset(C.eps_col[:], EPS), writes=[C.t_const])
    C.one_col = st.sb("one_col", [128, 2], F32)
    k.op("pool", lambda e: e.memset(C.one_col[:], 1.0), writes=[C.t_const])


def emit_modvec(k, C, st, condT_dram, modw_dram, modb_dram, ng_dram, i2_dram):
    C.t_mv = DT("mv")
    cond = st.sb("cond", [128, 2, NCH], F32)
    condb = st.sb("condb", [128, NCH, 2], BF16)
    t_cond = DT("cond")
    k.dma("sp", cond[:], condT_dram, writes=[t_cond])
    t_ng = DT("ng")
    k.dma("sp", C.NG[:], ng_dram, writes=[C.t_mv])
    k.dma("sp", C.MB[:], modb_dram, writes=[C.t_mv])
    t_condb = DT("condb")
    for v in range(2):
        k.op("act", lambda e: e.activation(out=condb[:, :, v], in_=cond[:, v, :], func=AF.Silu),
             reads=[t_cond], writes=[t_condb])
    NW = 3
    CW = 512
    wt = [st.sb("mw%d" % i, [128, NCH, CW], BF16) for i in range(NW)]
    wtt = [DT("mw%d" % i) for i in range(NW)]
    rows = st.sb("mvrows", [2, 9 * D], F32)
    t_rows = DT("mvrows")
    i2 = st.sb("mvi2", [2, 2], F32)
    k.dma("sp", i2[:], i2_dram, writes=[t_rows])
    pr = [st.ps("mvpr%d" % i, [128, 512], F32) for i in range(2)]
    prt = [DT("mvpr%d" % i) for i in range(2)]
    ps = st.ps("mvps", [128, 512], F32)
    pst = DT("mvps")
    modw_v = modw_dram.rearrange("(c p) n -> p c n", p=128)
    ntile = (9 * D) // CW
    for ti in range(ntile):
        b = ti % NW
        u = ti % 2
        k.dma("pool", wt[b][:], modw_v[:, :, ti * CW:(ti + 1) * CW], writes=[wtt[b]])
        for kc in range(NCH):
            k.op("pe", lambda e: e.matmul(pr[u][0:2, :], lhsT=condb[:, kc, :], rhs=wt[b][:, kc, :], start=(kc == 0),
                                          stop=(kc == NCH - 1)),
                 reads=[wtt[b], t_condb], writes=[prt[u]], inc=(kc == NCH - 1))
        k.op("act", lambda e: e.copy(out=rows[:, ti * CW:(ti + 1) * CW], in_=pr[u][0:2, :]), reads=[prt[u]], writes=[t_rows])
    for cc in range(9 * NCH):
        k.op("pe", lambda e: e.matmul(ps[:, cc * 2:cc * 2 + 2], lhsT=rows[:, cc * 128:(cc + 1) * 128], rhs=i2[:],
                                      start=True, stop=True), reads=[t_rows], writes=[pst], inc=(cc == 9 * NCH - 1))
    psv = ps[:, 0:288].rearrange("p (mc v) -> p v mc", v=2)
    for v in range(2):
        k.op("dve", lambda e: e.tensor_tensor(out=C.MV[:, v, :], in0=psv[:, v, :], in1=C.MB[:], op=ALU.add),
             reads=[pst, C.t_mv], writes=[C.t_mv])
    for v in range(2):
        for j in range(3):
            sc = C.MV[:, v, (3 * j + 1) * NCH:(3 * j + 2) * NCH]
            k.op("dve", lambda e: e.tensor_scalar(out=C.A[:, v, j, :], in0=sc, scalar1=1.0, scalar2=1.0,
                                                  op0=ALU.add, op1=ALU.mult), reads=[C.t_mv], writes=[C.t_mv])
            k.op("dve", lambda e: e.tensor_tensor(out=C.A[:, v, j, :], in0=C.A[:, v, j, :], in1=C.NG[:, j, :],
                                                  op=ALU.mult), reads=[C.t_mv], writes=[C.t_mv])
            gt = C.MV[:, v, (3 * j + 2) * NCH:(3 * j + 3) * NCH]
            k.op("dve", lambda e: e.tensor_scalar(out=C.G[:, v, j, :], in0=gt, scalar1=(1.0 if j == 1 else 0.5),
                                                  scalar2=None, op0=ALU.mult), reads=[C.t_mv], writes=[C.t_mv])


def emit_norm_mod(k, C, st, j, H, HT, pss, psst, tbs=None):
    X = C.X
    sq = [st.sb("sq%d_%d" % (j, i), [128, 512], F32) for i in range(2)]
    sqt = [DT("sq") for _ in range(2)]
    tmp = [st.sb("nt%d_%d" % (j, i), [128, 512], F32) for i in range(2)]
    tmpt = [DT("nt") for _ in range(2)]
    rstd = st.sb("rstd%d" % j, [128, TOK], F32)
    n = 0
    for bi, (t0, tl, v) in enumerate(TBS):
        if tbs is not None and bi not in tbs:
            continue
        ts = slice(t0, t0 + tl)
        pb = bi % 2
        for c in range(NCH):
            b = n % 2
            n += 1
            k.op("act", lambda e: e.activation(out=sq[b][:, :tl], in_=X[:, c, ts], func=AF.Square),
                 reads=[C.XT[bi]], writes=[sqt[b]])
            k.op("pe", lambda e: e.matmul(pss[pb][:, :tl], lhsT=C.ones_f[:], rhs=sq[b][:, :tl], start=(c == 0),
                                          stop=(c == NCH - 1)),
                 reads=[sqt[b], C.t_const], writes=[psst[pb]], inc=True)
        t_r = DT("rstd")
        k.op("act", lambda e: e.activation(out=rstd[:, ts], in_=pss[pb][:, :tl], func=AF.Sqrt,
                                           bias=C.eps_col[:, 0:1], scale=1.0 / D),
             reads=[psst[pb], C.t_const], writes=[t_r])
        k.op("dve", lambda e: e.reciprocal(out=rstd[:, ts], in_=rstd[:, ts]), reads=[t_r], writes=[t_r])
        for c in range(NCH):
            b = n % 2
            n += 1
            k.op("dve", lambda e: e.scalar_tensor_tensor(out=tmp[b][:, :tl], in0=X[:, c, ts],
                                                         scalar=C.A[:, v, j, c:c + 1], in1=rstd[:, ts],
                                                         op0=ALU.mult, op1=ALU.mult),
                 reads=[C.XT[bi], t_r, C.t_mv], writes=[tmpt[b]])
            k.op("act", lambda e: e.activation(out=H[:, c, ts], in_=tmp[b][:, :tl], func=AF.Identity,
                                               bias=C.MV[:, v, 3 * j * NCH + c:3 * j * NCH + c + 1], scale=1.0),
                 reads=[tmpt[b], C.t_mv], writes=[HT[bi]])


def emit_rstd(k, C, st, rstd, t_rs, pss, psst):
    X = C.X
    sq = [st.sb("rsq%d" % i, [128, 512], F32) for i in range(2)]
    sqt = [DT("rsq") for _ in range(2)]
    n = 0
    for bi, (t0, tl, v) in enumerate(TBS):
        ts = slice(t0, t0 + tl)
        pb = bi % 2
        for c in range(NCH):
            b = n % 2
            n += 1
            k.op("act", lambda e: e.activation(out=sq[b][:, :tl], in_=X[:, c, ts], func=AF.Square),
                 reads=[C.XT[bi]], writes=[sqt[b]])
            k.op("pe", lambda e: e.matmul(pss[pb][:, :tl], lhsT=C.ones_f[:], rhs=sq[b][:, :tl], start=(c == 0),
                                          stop=(c == NCH - 1)),
                 reads=[sqt[b], C.t_const], writes=[psst[pb]], inc=True)
        k.op("act", lambda e: e.activation(out=rstd[:, ts], in_=pss[pb][:, :tl], func=AF.Sqrt,
                                           bias=C.eps_col[:, 0:1], scale=1.0 / D),
             reads=[psst[pb], C.t_const], writes=[t_rs[bi]])
        k.op("dve", lambda e: e.reciprocal(out=rstd[:, ts], in_=rstd[:, ts]), reads=[t_rs[bi]], writes=[t_rs[bi]])


def emit_hchunk(k, C, j, c, rstd, t_rs, Hc, HcT, tmp, tmpt):
    X = C.X
    for bi, (t0, tl, v) in enumerate(TBS):
        ts = slice(t0, t0 + tl)
        b = bi % 2
        k.op("dve", lambda e: e.scalar_tensor_tensor(out=tmp[b][:, :tl], in0=X[:, c, ts],
                                                     scalar=C.A[:, v, j, c:c + 1], in1=rstd[:, ts],
                                                     op0=ALU.mult, op1=ALU.mult),
             reads=[C.XT[bi], t_rs[bi], C.t_mv], writes=[tmpt[b]])
        k.op("act", lambda e: e.activation(out=Hc[:, ts], in_=tmp[b][:, :tl], func=AF.Identity,
                                           bias=C.MV[:, v, 3 * j * NCH + c:3 * j * NCH + c + 1], scale=1.0),
             reads=[tmpt[b], C.t_mv], writes=[HcT])


def emit_ffn(k, C, j, wi_dram, wo_dram, lat_only=False):
    X = C.X
    NG_ = 4
    GC = NFC // NG_
    with Stage(k) as st:
        H = st.sb("ffn_h", [128, NCH, TOK], BF16)
        HT = [DT("h%d" % i) for i in range(3)]
        ps = [st.ps("fps%d" % i, [128, 512], F32) for i in range(8)]
        pst = [DT("fps%d" % i) for i in range(8)]
        tbs = [1, 2] if lat_only else [0, 1, 2]
        with Stage(k) as stn:
            emit_norm_mod(k, C, stn, j, H, HT, ps[6:8], pst[6:8], tbs=tbs)
        act = st.sb("ffn_act", [128, GC, TOK], BF16)
        actT = [DT("act%d" % i) for i in range(3)]
        NWI = 3
        wi = [st.sb("wi%d" % i, [128, NCH, 256], BF16) for i in range(NWI)]
        wit = [DT("wi%d" % i) for i in range(NWI)]
        NWO = 3
        WOC = 256
        wo = [st.sb("wo%d" % i, [128, GC, WOC], BF16) for i in range(NWO)]
        wot = [DT("wo%d" % i) for i in range(NWO)]
        sg = [st.sb("sg%d" % i, [128, 512], F32) for i in range(2)]
        sgt = [DT("sg") for _ in range(2)]
        wi_v = wi_dram.rearrange("(c p) n -> p c n", p=128)
        wo_v = wo_dram.rearrange("(f p) n -> p f n", p=128)
        nwi = 0
        nwo = 0
        nu = 0
        ny = 0
        for g in range(NG_):
            for fl in range(GC):
                f = g * GC + fl
                b = nwi % NWI
                nwi += 1
                k.dma("pool", wi[b][:, :, 0:128], wi_v[:, :, f * 128:(f + 1) * 128], writes=[wit[b]])
                k.dma("pool", wi[b][:, :, 128:256], wi_v[:, :, DFF + f * 128:DFF + (f + 1) * 128], writes=[wit[b]])
                for bi, (t0, tl, v) in enumerate(TBS):
                    if bi not in tbs:
                        continue
                    ts = slice(t0, t0 + tl)
                    pa = (nu % 3) * 2
                    pg = pa + 1
                    sb_ = nu % 2
                    nu += 1
                    for c in range(NCH):
                        k.op("pe", lambda e: e.matmul(ps[pa][:, :tl], lhsT=wi[b][:, c, 0:128], rhs=H[:, c, ts],
                                                      start=(c == 0), stop=(c == NCH - 1)),
                             reads=[wit[b], HT[bi]], writes=[pst[pa]], inc=(c == NCH - 1))
                    for c in range(NCH):
                        k.op("pe", lambda e: e.matmul(ps[pg][:, :tl], lhsT=wi[b][:, c, 128:256], rhs=H[:, c, ts],
                                                      start=(c == 0), stop=(c == NCH - 1)),
                             reads=[wit[b], HT[bi]], writes=[pst[pg]], inc=(c == NCH - 1))
                    k.op("act", lambda e: e.activation(out=sg[sb_][:, :tl], in_=ps[pg][:, :tl], func=AF.Silu),
                         reads=[pst[pg]], writes=[sgt[sb_]])
                    k.op("dve", lambda e: e.tensor_tensor(out=act[:, fl, ts], in0=ps[pa][:, :tl], in1=sg[sb_][:, :tl],
                                                          op=ALU.mult),
                         reads=[pst[pa], sgt[sb_]], writes=[actT[bi]])
            for dq in range(D // WOC):
                b = nwo % NWO
                nwo += 1
                k.dma("pool", wo[b][:], wo_v[:, g * GC:(g + 1) * GC, dq * WOC:(dq + 1) * WOC], writes=[wot[b]])
                for dl in range(WOC // 128):
                    dc = dq * (WOC // 128) + dl
                    for bi, (t0, tl, v) in enumerate(TBS):
                        if bi not in tbs:
                            continue
                        ts = slice(t0, t0 + tl)
                        p = ny % 6
                        ny += 1
                        for fl in range(GC):
                            k.op("pe", lambda e: e.matmul(ps[p][:, :tl], lhsT=wo[b][:, fl, dl * 128:(dl + 1) * 128],
                                                          rhs=act[:, fl, ts], start=(fl == 0), stop=(fl == GC - 1)),
                                 reads=[wot[b], actT[bi]], writes=[pst[p]], inc=(fl == GC - 1))
                        k.op("dve", lambda e: e.scalar_tensor_tensor(out=X[:, dc, ts], in0=ps[p][:, :tl],
                                                                     scalar=C.G[:, v, j, dc:dc + 1], in1=X[:, dc, ts],
                                                                     op0=ALU.mult, op1=ALU.add),
                             reads=[pst[p], C.t_mv, C.XT[bi]], writes=[C.XT[bi]])


def alloc_persistent(k, st, C):
    C.X = st.sb("X", [128, NCH, TOK], F32)
    C.XT = [DT("x%d" % i) for i in range(3)]
    C.MV = st.sb("MV", [128, 2, 9 * NCH], F32)
    C.MB = st.sb("MB", [128, 9 * NCH], F32)
    C.NG = st.sb("NG", [128, 3, NCH], F32)
    C.A = st.sb("A", [128, 2, 3, NCH], F32)
    C.G = st.sb("G", [128, 2, 3, NCH], F32)
    setup_consts(k, st, C)


def load_x(k, C, xT_dram):
    xv = xT_dram.rearrange("(c p) t -> p c t", p=128)
    for bi, (t0, tl, v) in enumerate(TBS):
        k.dma("sp", C.X[:, :, t0:t0 + tl], xv[:, :, t0:t0 + tl], writes=[C.XT[bi]])


def store_x(k, C, xo_dram, lat_only=False):
    xv = xo_dram.rearrange("(c p) t -> p c t", p=128)
    for bi, (t0, tl, v) in enumerate(TBS):
        if lat_only and bi == 0:
            continue
        o0 = t0 - (NCTX if lat_only else 0)
        k.dma("sp", xv[:, :, o0:o0 + tl], C.X[:, :, t0:t0 + tl], reads=[C.XT[bi]], final=True)


def build_test_ffn():
    nc = bass.Bass("TRN2", target_bir_lowering=False)
    xT = nc.dram_tensor("xT", [D, TOK], F32, kind="ExternalInput").ap()
    condT = nc.dram_tensor("condT", [128, 2, NCH], F32, kind="ExternalInput").ap()
    modw = nc.dram_tensor("modw", [D, 9 * D], F32, kind="ExternalInput").ap()
    modb = nc.dram_tensor("modb", [128, 9 * NCH], F32, kind="ExternalInput").ap()
    ng = nc.dram_tensor("ng", [128, 3, NCH], F32, kind="ExternalInput").ap()
    wi = nc.dram_tensor("wi", [D, 2 * DFF], F32, kind="ExternalInput").ap()
    wo = nc.dram_tensor("wo", [DFF, D], F32, kind="ExternalInput").ap()
    xo = nc.dram_tensor("xo", [D, TOK], F32, kind="ExternalOutput").ap()
    mvo = nc.dram_tensor("mvo", [128, 2 * 9 * NCH], F32, kind="ExternalOutput").ap()
    with ExitStack() as es:
        k = KB(nc, es)
        C = Ctx()
        with Stage(k) as st0:
            alloc_persistent(k, st0, C)
            load_x(k, C, xT)
            with Stage(k) as st:
                emit_modvec(k, C, st, condT, modw, modb, ng)
            k.dma("sp", mvo, C.MV[:].rearrange("p v m -> p (v m)"), reads=[C.t_mv], final=True)
            emit_ffn(k, C, 0, wi, wo)
            store_x(k, C, xo)
            k.finish()
        print("instructions:", k.ninstr, "dma sems:", k.nsem)
    return nc


def vec_pm(v):
    v = np.asarray(v)
    lead = v.shape[:-1]
    n = v.shape[-1] // 128
    a = v.reshape(lead + (n, 128))
    return np.ascontiguousarray(np.moveaxis(a, -1, 0))


def core_tokens_T(x, ctx, core):
    b, h = core // 2, core % 2
    cx, xl = ctx[b], x[b, h * NLAT:(h + 1) * NLAT]
    if h == 1:
        cx, xl = cx[::-1], xl[::-1]
    t = np.concatenate([cx, xl], axis=0)
    return np.ascontiguousarray(t.T)


NH = 16
NKV = 4
HD = 128
QKV_W = 3072
NKEY = NCTX + SEQ
NKB = NKEY // 128


def emit_qkv(k, C, wqkv_dram, qkg_dram, cos_dram, sin_dram, rm_dram, qT_d, kT_d, v_d):
    with Stage(k) as st:
        H = st.sb("qkv_h", [128, NCH, TOK], BF16)
        HT = [DT("h%d" % i) for i in range(3)]
        ps = [st.ps("qps%d" % i, [128, 512], F32) for i in range(8)]
        pst = [DT("qps%d" % i) for i in range(8)]
        with Stage(k) as stn:
            emit_norm_mod(k, C, stn, 1, H, HT, ps[6:8], pst[6:8])
        cs = st.sb("cs", [128, 2, NLAT], F32)
        rm = st.sb("rm", [128, 128], F32)
        g2 = st.sb("g2", [128, 2], F32)
        t_c = DT("qkvconst")
        k.dma("sp", cs[:, 0, :], cos_dram, writes=[t_c])
        k.dma("sp", cs[:, 1, :], sin_dram, writes=[t_c])
        k.dma("sp", rm[:], rm_dram, writes=[t_c])
        k.dma("sp", g2[:], qkg_dram, writes=[t_c])
        k.op("dve", lambda e: e.tensor_scalar(out=g2[:, 0:1], in0=g2[:, 0:1], scalar1=float(HD ** -0.5), scalar2=None,
                                              op0=ALU.mult), reads=[t_c], writes=[t_c])
        NW = 3
        wt = [st.sb("qw%d" % i, [128, NCH, 256], BF16) for i in range(NW)]
        wtt = [DT("qw%d" % i) for i in range(NW)]
        wv = st.sb("qwv", [128, NCH, 512], BF16)
        wvt = DT("qwv")
        w_v = wqkv_dram.rearrange("(c p) n -> p c n", p=128)
        k.dma("pool", wv[:], w_v[:, :, 2560:3072], writes=[wvt])
        sq = [st.sb("qsq%d" % i, [128, 512], F32) for i in range(2)]
        sqt = [DT("qsq") for _ in range(2)]
        rs = [st.sb("qrs%d" % i, [128, 512], F32) for i in range(2)]
        rst = [DT("qrs") for _ in range(2)]
        qn = [st.sb("qqn%d" % i, [128, 512], F32) for i in range(2)]
        qnt = [DT("qqn") for _ in range(2)]
        t1 = [st.sb("qt1%d" % i, [128, 512], F32) for i in range(2)]
        t1t = [DT("qt1") for _ in range(2)]
        ob = [st.sb("qob%d" % i, [128, 512], BF16) for i in range(3)]
        obt = [DT("qob%d" % i) for i in range(3)]
        def qunit(b, s, fc, isq, gcol, bi, t0, tl, u, o3):
            ts = slice(t0, t0 + tl)
            pq, pss, pr = ps[u], ps[2 + u], ps[4 + u]
            pqt, psst, prt = pst[u], pst[2 + u], pst[4 + u]

            def f1():
                for c in range(NCH):
                    k.op("pe", lambda e: e.matmul(pq[:, :tl], lhsT=wt[b][:, c, s * 128:(s + 1) * 128], rhs=H[:, c, ts],
                                                  start=(c == 0), stop=(c == NCH - 1)),
                         reads=[wtt[b], HT[bi]], writes=[pqt], inc=(c == NCH - 1))

            def f2():
                k.op("act", lambda e: e.activation(out=sq[u][:, :tl], in_=pq[:, :tl], func=AF.Square),
                     reads=[pqt], writes=[sqt[u]])
                k.op("pe", lambda e: e.matmul(pss[:, :tl], lhsT=C.ones_f[:], rhs=sq[u][:, :tl], start=True, stop=True),
                     reads=[sqt[u], C.t_const], writes=[psst])
                k.op("act", lambda e: e.activation(out=rs[u][:, :tl], in_=pss[:, :tl], func=AF.Sqrt,
                                                   bias=C.eps_col[:, 0:1], scale=1.0 / HD),
                     reads=[psst, C.t_const], writes=[rst[u]])
                k.op("dve", lambda e: e.reciprocal(out=rs[u][:, :tl], in_=rs[u][:, :tl]), reads=[rst[u]], writes=[rst[u]])
                k.op("dve", lambda e: e.scalar_tensor_tensor(out=qn[u][:, :tl], in0=pq[:, :tl], scalar=gcol,
                                                             in1=rs[u][:, :tl], op0=ALU.mult, op1=ALU.mult),
                     reads=[pqt, rst[u], t_c], writes=[qnt[u]])

            def f3():
                if bi == 0:
                    k.op("act", lambda e: e.copy(out=ob[o3][:, :tl], in_=qn[u][:, :tl]), reads=[qnt[u]], writes=[obt[o3]])
                else:
                    ls = slice(t0 - NCTX, t0 - NCTX + tl)
                    k.op("pe", lambda e: e.matmul(pr[:, :tl], lhsT=rm[:], rhs=qn[u][:, :tl], start=True, stop=True),
                         reads=[qnt[u], t_c], writes=[prt])
                    k.op("dve", lambda e: e.tensor_tensor(out=t1[u][:, :tl], in0=qn[u][:, :tl], in1=cs[:, 0, ls],
                                                           op=ALU.mult), reads=[qnt[u], t_c], writes=[t1t[u]])
                    k.op("dve", lambda e: e.tensor_tensor(out=qn[u][:, :tl], in0=pr[:, :tl], in1=cs[:, 1, ls],
                                                          op=ALU.mult), reads=[prt, t_c], writes=[qnt[u]])
                    k.op("dve", lambda e: e.tensor_tensor(out=ob[o3][:, :tl], in0=qn[u][:, :tl], in1=t1[u][:, :tl],
                                                          op=ALU.add), reads=[qnt[u], t1t[u]], writes=[obt[o3]])
                dst = qT_d[fc, :, ts] if isq else kT_d[fc - NH, :, ts]
                k.dma("sp", dst, ob[o3][:, :tl], reads=[obt[o3]], final=True)

            return f1, f2, f3

        pend2 = None
        pend3 = None
        nu = 0
        for ti in range(10):
            b = ti % NW
            k.dma("pool", wt[b][:], w_v[:, :, ti * 256:(ti + 1) * 256], writes=[wtt[b]])
            for s in range(2):
                fc = ti * 2 + s
                isq = fc < NH
                gcol = g2[:, 0:1] if isq else g2[:, 1:2]
                for bi, (t0, tl, v) in enumerate(TBS):
                    f1, f2, f3 = qunit(b, s, fc, isq, gcol, bi, t0, tl, nu % 2, nu % 3)
                    nu += 1
                    f1()
                    if pend3 is not None:
                        pend3()
                    pend3 = None
                    if pend2 is not None:
                        pend2[0]()
                        pend3 = pend2[1]
                    pend2 = (f2, f3)
        if pend3 is not None:
            pend3()
        if pend2 is not None:
            pend2[0]()
            pend2[1]()
        for tb in range(TOK // 128):
            u = tb % 2
            o3 = nu % 3
            nu += 1
            bi = 0 if tb < 2 else (1 if tb < 6 else 2)
            for c in range(NCH):
                k.op("pe", lambda e: e.matmul(ps[u][:, :], lhsT=H[:, c, tb * 128:(tb + 1) * 128], rhs=wv[:, c, :],
                                              start=(c == 0), stop=(c == NCH - 1)),
                     reads=[wvt, HT[bi]], writes=[pst[u]], inc=(c == NCH - 1))
            k.op("act", lambda e: e.copy(out=ob[o3][:, :], in_=ps[u][:, :]), reads=[pst[u]], writes=[obt[o3]])
            k.dma("sp", v_d[tb * 128:(tb + 1) * 128, :], ob[o3][:, :], reads=[obt[o3]], final=True)


NKEYW = NCTX + 128 + NLAT + 128


def emit_attn(k, C, window, ctx_out, qT_d, kT_all_d, v_all_d, kv_other, esink_dram, masks_dram, wo_dram):
    X = C.X
    nkey = NKEYW if window else NKEY
    nkb = nkey // 128
    with Stage(k) as st:
        KT = st.sb("KT", [128, NKV, nkey], BF16)
        V = st.sb("V", [128, nkb, 512], BF16)
        t_kv = DT("kv")
        kT_me, v_me, kT_x, v_x = kT_all_d, v_all_d, kv_other[0], kv_other[1]
        vr = lambda a: a.rearrange("(b p) f -> p b f", p=128)
        nlb = NLAT // 128
        if not window:
            for g in range(NKV):
                k.dma("sp", KT[:, g, 0:NCTX], kT_me[g][:, 0:NCTX], writes=[t_kv])
                k.dma("sp", KT[:, g, NCTX:TOK], kT_x[0][g][:, NCTX:TOK], writes=[t_kv])
                k.dma("sp", KT[:, g, TOK:NKEY], kT_x[1][g][:, NCTX:TOK], writes=[t_kv])
            k.dma("sp", V[:, 0:2, :], vr(v_me[0:NCTX, :]), writes=[t_kv])
            k.dma("sp", V[:, 2:2 + nlb, :], vr(v_x[0][NCTX:TOK, :]), writes=[t_kv])
            k.dma("sp", V[:, 2 + nlb:2 + 2 * nlb, :], vr(v_x[1][NCTX:TOK, :]), writes=[t_kv])
        else:
            k.op("pool", lambda e: e.memset(KT[:, :, NCTX:NCTX + 128], 0.0), writes=[t_kv])
            k.op("pool", lambda e: e.memset(V[:, 2, :], 0.0), writes=[t_kv])
            for g in range(NKV):
                k.dma("sp", KT[:, g, 0:NCTX], kT_me[g][:, 0:NCTX], writes=[t_kv])
                k.dma("sp", KT[:, g, NCTX + 128:NCTX + 128 + NLAT], kT_me[g][:, NCTX:TOK], writes=[t_kv])
                k.dma("sp", KT[:, g, NCTX + 128 + NLAT:NKEYW], kT_x[g][:, TOK - 128:TOK], writes=[t_kv])
            k.dma("sp", V[:, 0:2, :], vr(v_me[0:NCTX, :]), writes=[t_kv])
            k.dma("sp", V[:, 3:3 + nlb, :], vr(v_me[NCTX:TOK, :]), writes=[t_kv])
            k.dma("sp", V[:, 3 + nlb:4 + nlb, :], vr(v_x[TOK - 128:TOK, :]), writes=[t_kv])
        QT = [st.sb("QT%d" % i, [128, 4, TOK], BF16) for i in range(2)]
        QTt = [DT("QT%d" % i) for i in range(2)]
        OT = [st.sb("OT%d" % i, [128, 4, TOK], BF16) for i in range(2)]
        OTt = [DT("OT%d" % i) for i in range(2)]
        wo = [st.sb("awo%d" % i, [128, 4, D], BF16) for i in range(2)]
        wot = [DT("awo%d" % i) for i in range(2)]
        pt = [st.sb("pt%d" % i, [128, 512], BF16) for i in range(3)]
        ptt = [DT("pt%d" % i) for i in range(3)]
        rd = [st.sb("rd%d" % i, [128, 512], F32) for i in range(2)]
        rdt = [DT("rd%d" % i) for i in range(2)]
        ps = [st.ps("aps%d" % i, [128, 512], F32) for i in range(8)]
        pst = [DT("aps%d" % i) for i in range(8)]
        t_c = DT("attnconst")
        if window:
            es_ = st.sb("esink", [128, NH], F32)
            mk = st.sb("masks", [128, 4, 128], BF16)
            k.dma("sp", es_[:], esink_dram, writes=[t_c])
            k.dma("sp", mk[:], masks_dram, writes=[t_c])
            k.op("act", lambda e: e.activation(out=es_[:], in_=es_[:], func=AF.Exp), reads=[t_c], writes=[t_c])
        wo_v = wo_dram.rearrange("(h p) n -> p h n", p=128)
        ns = 0
        nunit = 0
        ny = 0
        for g in range(NKV):
            gb_ = g % 2
            for hl in range(4):
                k.dma("sp", QT[gb_][:, hl, :], qT_d[4 * g + hl], writes=[QTt[gb_]])
            k.dma("pool", wo[gb_][:], wo_v[:, 4 * g:4 * g + 4, :], writes=[wot[gb_]])
            for hl in range(4):
                h = 4 * g + hl
                units = []
                if not window:
                    for (t0, tl, v) in TBS[1:]:
                        units.append((t0, tl, [(kb, None) for kb in range(NKB)]))
                    if ctx_out:
                        units.append((0, NCTX, [(0, None), (1, None)]))
                else:
                    nqb = NLAT // 128
                    for qb in range(nqb):
                        kl = [(0, None), (1, None), (2 + qb, 2 if qb == 0 else 0), (3 + qb, None),
                              (4 + qb, 3 if qb == nqb - 1 else 1)]
                        units.append((NCTX + qb * 128, 128, kl))
                    if ctx_out:
                        units.append((0, NCTX, [(0, None), (1, None)]))
                for (t0, tl, kl) in units:
                    ts = slice(t0, t0 + tl)
                    u = nunit % 2
                    nunit += 1
                    po, pd = ps[2 + u], ps[4 + u]
                    pot, pdt = pst[2 + u], pst[4 + u]
                    def front(ki, kb, mi, s2, p3):
                        k.op("pe", lambda e: e.matmul(ps[s2][:, :tl], lhsT=KT[:, g, kb * 128:(kb + 1) * 128],
                                                      rhs=QT[gb_][:, hl, ts], start=True, stop=True),
                             reads=[t_kv, QTt[gb_]], writes=[pst[s2]])
                        k.op("act", lambda e: e.activation(out=pt[p3][:, :tl], in_=ps[s2][:, :tl], func=AF.Exp),
                             reads=[pst[s2]], writes=[ptt[p3]])
                        if mi is not None:
                            k.op("dve", lambda e: e.tensor_tensor(out=pt[p3][:, :tl], in0=pt[p3][:, :tl], in1=mk[:, mi, :tl],
                                                                  op=ALU.mult), reads=[ptt[p3], t_c], writes=[ptt[p3]])

                    def back(ki, kb, p3):
                        last = ki == len(kl) - 1
                        k.op("pe", lambda e: e.matmul(po[:, :tl], lhsT=V[:, kb, g * 128:(g + 1) * 128], rhs=pt[p3][:, :tl],
                                                      start=(ki == 0), stop=last),
                             reads=[t_kv, ptt[p3]], writes=[pot], inc=last)
                        k.op("pe", lambda e: e.matmul(pd[:, :tl], lhsT=C.ones_b[:], rhs=pt[p3][:, :tl],
                                                      start=(ki == 0), stop=last),
                             reads=[C.t_const, ptt[p3]], writes=[pdt], inc=last)
                    prev = None
                    for ki, (kb, mi) in enumerate(kl):
                        s2 = ns % 2
                        p3 = ns % 3
                        ns += 1
                        front(ki, kb, mi, s2, p3)
                        if prev is not None:
                            back(*prev)
                        prev = (ki, kb, p3)
                    back(*prev)
                    if window:
                        k.op("dve", lambda e: e.tensor_scalar(out=rd[u][:, :tl], in0=pd[:, :tl], scalar1=es_[:, h:h + 1],
                                                              scalar2=None, op0=ALU.add), reads=[pdt, t_c], writes=[rdt[u]])
                        k.op("dve", lambda e: e.reciprocal(out=rd[u][:, :tl], in_=rd[u][:, :tl]), reads=[rdt[u]], writes=[rdt[u]])
                    else:
                        k.op("dve", lambda e: e.reciprocal(out=rd[u][:, :tl], in_=pd[:, :tl]), reads=[pdt], writes=[rdt[u]])
                    k.op("dve", lambda e: e.tensor_tensor(out=OT[gb_][:, hl, ts], in0=po[:, :tl], in1=rd[u][:, :tl], op=ALU.mult),
                         reads=[pot, rdt[u]], writes=[OTt[gb_]])
            for dc in range(NCH):
                for bi, (t0, tl, v) in enumerate(TBS):
                    if bi == 0 and not ctx_out:
                        continue
                    ts = slice(t0, t0 + tl)
                    p = 6 + ny % 2
                    ny += 1
                    for hl in range(4):
                        k.op("pe", lambda e: e.matmul(ps[p][:, :tl], lhsT=wo[gb_][:, hl, dc * 128:(dc + 1) * 128],
                                                      rhs=OT[gb_][:, hl, ts], start=(hl == 0), stop=(hl == 3)),
                             reads=[wot[gb_], OTt[gb_]], writes=[pst[p]], inc=(hl == 3))
                    k.op("dve", lambda e: e.scalar_tensor_tensor(out=X[:, dc, ts], in0=ps[p][:, :tl],
                                                                 scalar=C.G[:, v, 1, dc:dc + 1], in1=X[:, dc, ts],
                                                                 op0=ALU.mult, op1=ALU.add),
                         reads=[pst[p], C.t_mv, C.XT[bi]], writes=[C.XT[bi]])


def rope_tables(half):
    t = np.arange(half * NLAT, (half + 1) * NLAT)
    if half == 1:
        t = t[::-1]
    row = (t // 64).astype(np.float32)
    col = (t % 64).astype(np.float32)
    inv = (10000.0 ** (-np.arange(32, dtype=np.float32) / 32)).astype(np.float32)
    ang = np.concatenate([row[:, None] * inv, col[:, None] * inv], axis=-1)
    ang = np.concatenate([ang, ang], axis=-1).astype(np.float32)
    return np.ascontiguousarray(np.cos(ang).T.astype(np.float32)), np.ascontiguousarray(np.sin(ang).T.astype(np.float32))


def rot_matrix():
    R = np.zeros((128, 128), np.float32)
    for m in range(64):
        R[m + 64, m] = -1.0
    for m in range(64, 128):
        R[m - 64, m] = 1.0
    return R


def win_masks(half=0):
    import ml_dtypes
    s = np.arange(128)[:, None]
    t = np.arange(128)[None, :]
    prev = (s >= t).astype(np.float32)
    nxt = (s <= t).astype(np.float32)
    z = np.zeros_like(prev)
    m = np.stack([prev, nxt, z, nxt[::-1]], axis=1)
    return np.ascontiguousarray(m).astype(ml_dtypes.bfloat16)


KINDS = ["a", "s", "m", "w"]


MH = 8
MDQK = 128
MDV = 256
NBLK = TOK // 128
LN_KSCALE = float(np.log(MDQK ** -0.5))


def emit_mlstm_proj(k, C, win_dram, wg_dram, bg_dram, qT_d, kT_d, ktok_d, vtok_d, so_d, gi_d):
    with Stage(k) as st:
        H = st.sb("m_h", [128, NCH, TOK], BF16)
        HT = [DT("h%d" % i) for i in range(3)]
        ps = [st.ps("mps%d" % i, [128, 512], F32) for i in range(8)]
        pst = [DT("mps%d" % i) for i in range(8)]
        with Stage(k) as stn:
            emit_norm_mod(k, C, stn, 1, H, HT, ps[6:8], pst[6:8])
        NW = 3
        wt = [st.sb("mw%d" % i, [128, NCH, 512], BF16) for i in range(NW)]
        wtt = [DT("mw%d" % i) for i in range(NW)]
        w_v = win_dram.rearrange("(c p) n -> p c n", p=128)
        ob = [st.sb("mob%d" % i, [128, 512], BF16) for i in range(3)]
        obt = [DT("mob%d" % i) for i in range(3)]
        of = [st.sb("mof%d" % i, [128, 512], F32) for i in range(2)]
        oft = [DT("mof%d" % i) for i in range(2)]
        nu = 0
        nw = 0
        for (c0, kind) in [(0, "q"), (512, "q"), (1024, "k"), (1536, "k"), (4096, "o"), (4608, "o"), (5120, "o"), (5632, "o")]:
            b = nw % NW
            nw += 1
            k.dma("pool", wt[b][:], w_v[:, :, c0:c0 + 512], writes=[wtt[b]])
            for s in range(4):
                fcol = c0 + s * 128
                for bi, (t0, tl, v) in enumerate(TBS):
                    ts = slice(t0, t0 + tl)
                    u = nu % 4
                    nu += 1
                    for c in range(NCH):
                        k.op("pe", lambda e: e.matmul(ps[u][:, :tl], lhsT=wt[b][:, c, s * 128:(s + 1) * 128], rhs=H[:, c, ts],
                                                      start=(c == 0), stop=(c == NCH - 1)),
                             reads=[wtt[b], HT[bi]], writes=[pst[u]], inc=(c == NCH - 1))
                    if kind == "o":
                        f2 = nu % 2
                        k.op("act", lambda e: e.activation(out=of[f2][:, :tl], in_=ps[u][:, :tl], func=AF.Sigmoid),
                             reads=[pst[u]], writes=[oft[f2]])
                        k.dma("sp", so_d[(fcol - 4096) // 128, :, ts], of[f2][:, :tl], reads=[oft[f2]], final=True)
                    else:
                        o3 = nu % 3
                        k.op("act", lambda e: e.copy(out=ob[o3][:, :tl], in_=ps[u][:, :tl]), reads=[pst[u]], writes=[obt[o3]])
                        dst = qT_d[fcol // 128, :, ts] if kind == "q" else kT_d[(fcol - 1024) // 128, :, ts]
                        k.dma("sp", dst, ob[o3][:, :tl], reads=[obt[o3]], final=True)
        wg = st.sb("m_wg", [128, NCH, 32], BF16)
        wgt = DT("m_wg")
        k.dma("pool", wg[:], wg_dram.rearrange("(c p) n -> p c n", p=128), writes=[wgt])
        bg = st.sb("m_bg", [128, 32], F32)
        k.dma("sp", bg[:], bg_dram, writes=[wgt])
        gt = [st.sb("m_gt%d" % i, [128, 32], F32) for i in range(2)]
        gtt = [DT("m_gt%d" % i) for i in range(2)]
        for (c0, kind) in [(1024, "k"), (1536, "k"), (2048, "v"), (2560, "v"), (3072, "v"), (3584, "v")]:
            b = nw % NW
            nw += 1
            k.dma("pool", wt[b][:], w_v[:, :, c0:c0 + 512], writes=[wtt[b]])
            for tb in range(NBLK):
                u = nu % 4
                o3 = nu % 3
                nu += 1
                bi = 0 if tb < 2 else (1 if tb < 6 else 2)
                for c in range(NCH):
                    k.op("pe", lambda e: e.matmul(ps[u][:, :], lhsT=H[:, c, tb * 128:(tb + 1) * 128], rhs=wt[b][:, c, :],
                                                  start=(c == 0), stop=(c == NCH - 1)),
                         reads=[wtt[b], HT[bi]], writes=[pst[u]], inc=(c == NCH - 1))
                k.op("act", lambda e: e.copy(out=ob[o3][:, :], in_=ps[u][:, :]), reads=[pst[u]], writes=[obt[o3]])
                dst = ktok_d[tb * 128:(tb + 1) * 128, c0 - 1024:c0 - 512] if kind == "k" else \
                    vtok_d[tb * 128:(tb + 1) * 128, c0 - 2048:c0 - 1536]
                k.dma("sp", dst, ob[o3][:, :], reads=[obt[o3]], final=True)
        for tb in range(NBLK):
            u = nu % 4
            f2 = nu % 2
            nu += 1
            bi = 0 if tb < 2 else (1 if tb < 6 else 2)
            for c in range(NCH):
                k.op("pe", lambda e: e.matmul(ps[u][:, 0:32], lhsT=H[:, c, tb * 128:(tb + 1) * 128], rhs=wg[:, c, :],
                                              start=(c == 0), stop=(c == NCH - 1)),
                     reads=[wgt, HT[bi]], writes=[pst[u]], inc=(c == NCH - 1))
            k.op("dve", lambda e: e.tensor_tensor(out=gt[f2][:], in0=ps[u][:, 0:32], in1=bg[:], op=ALU.add),
                 reads=[pst[u], wgt], writes=[gtt[f2]])
            k.op("act", lambda e: e.activation(out=gt[f2][:], in_=gt[f2][:], func=AF.Tanh, scale=1.0 / 15.0),
                 reads=[gtt[f2]], writes=[gtt[f2]])
            k.op("dve", lambda e: e.tensor_scalar(out=gt[f2][:], in0=gt[f2][:], scalar1=15.0, scalar2=None, op0=ALU.mult),
                 reads=[gtt[f2]], writes=[gtt[f2]])
            k.dma("sp", gi_d[tb * 128:(tb + 1) * 128, :], gt[f2][:], reads=[gtt[f2]], final=True)


def emit_mlstm_scan(k, C, st, phase, qT_d, kT_d, ktok_d, vtok_d, gi_d, tri_dram, madd_dram, state_in, state_out,
                    h_in, h_out):
    p = phase
    NB_ = 2
    QT = [st.sb("m_qT%d" % i, [128, MH, 128], BF16) for i in range(NB_)]
    KT = [st.sb("m_kT%d" % i, [128, MH, 128], BF16) for i in range(NB_)]
    KK = [st.sb("m_kk%d" % i, [128, MH * MDQK], BF16) for i in range(NB_)]
    VV = [st.sb("m_vv%d" % i, [128, MH * MDV], BF16) for i in range(NB_)]
    t_blk = [DT("m_blk%d" % i) for i in range(NB_)]
    GI = st.sb("m_gi", [128, NBLK, 32], F32)
    t_in = DT("m_in")
    k.dma("sp", GI[:], gi_d.rearrange("(b p) f -> p b f", p=128), writes=[t_in])
    tri = st.sb("m_tri", [128, 128], F32)
    madd = st.sb("m_madd", [128, 128], F32)
    t_c = DT("m_c")
    k.dma("sp", tri[:], tri_dram[p], writes=[t_c])
    k.dma("sp", madd[:], madd_dram[p], writes=[t_c])
    A_ = st.sb("m_a", [128, NBLK, MH], F32)
    IMB = st.sb("m_imb", [128, NBLK, MH], F32)
    t_g = DT("m_g")
    fsl = GI[:, :, p * 16 + 8:p * 16 + 16]
    isl = GI[:, :, p * 16:p * 16 + 8]
    k.op("act", lambda e: e.activation(out=A_[:], in_=fsl, func=AF.Exp, scale=-1.0), reads=[t_in], writes=[t_g])
    k.op("act", lambda e: e.activation(out=A_[:], in_=A_[:], func=AF.Ln, bias=C.one_col[:, 0:1], scale=1.0),
         reads=[t_g, C.t_const], writes=[t_g])
    k.op("dve", lambda e: e.tensor_scalar(out=A_[:], in0=A_[:], scalar1=-1.0, scalar2=None, op0=ALU.mult),
         reads=[t_g], writes=[t_g])
    ps = [st.ps("sps%d" % i, [128, 512], F32) for i in range(8)]
    pst = [DT("sps%d" % i) for i in range(8)]
    for blk in range(NBLK):
        u = blk % 2
        k.op("pe", lambda e: e.matmul(ps[u][:, 0:MH], lhsT=tri[:], rhs=A_[:, blk, :], start=True, stop=True),
             reads=[t_c, t_g], writes=[pst[u]])
        k.op("dve", lambda e: e.scalar_tensor_tensor(out=IMB[:, blk, :], in0=isl[:, blk, :], scalar=LN_KSCALE,
                                                     in1=ps[u][:, 0:MH], op0=ALU.add, op1=ALU.subtract),
             reads=[pst[u], t_in], writes=[t_g])
    Cf = st.sb("m_Cf", [128, MH, MDV], F32)
    Cb = st.sb("m_Cb", [128, MH, MDV], BF16)
    Nf = st.sb("m_Nf", [128, MH, 128], F32)
    Nb = st.sb("m_Nb", [128, MH, 128], BF16)
    St = [DT("m_st%d" % hh) for hh in range(MH)]
    NS = 2
    Ta = [st.sb("m_Ta%d" % i, [128, 128], F32) for i in range(NS)]
    Tat = [DT("Ta") for _ in range(NS)]
    tmp = [st.sb("m_tmp%d" % i, [128, 128], F32) for i in range(NS)]
    tmpt = [DT("tmp") for _ in range(NS)]
    Dm = [st.sb("m_Dm%d" % i, [128, 128], F32) for i in range(NS)]
    Dmt = [DT("Dm") for _ in range(NS)]
    eb = [st.sb("m_eb%d" % i, [128, 128], F32) for i in range(NS)]
    ebt = [DT("eb") for _ in range(NS)]
    Qp = [st.sb("m_Qp%d" % i, [128, 128], BF16) for i in range(NS)]
    Qpt = [DT("Qp") for _ in range(NS)]
    PT = [st.sb("m_PT%d" % i, [128, 128], BF16) for i in range(NS)]
    PTt = [DT("PT") for _ in range(NS)]
    rd = [st.sb("m_rd%d" % i, [128, 128], F32) for i in range(NS)]
    rdt = [DT("rd") for _ in range(NS)]
    wc = [st.sb("m_wc%d" % i, [128, 2], F32) for i in range(NS)]
    wct = [DT("wc") for _ in range(NS)]
    K2 = [st.sb("m_K2%d" % i, [128, 128], BF16) for i in range(NS)]
    K2t = [DT("K2") for _ in range(NS)]
    hb = [st.sb("m_hb%d" % i, [128, 2, 128], F32) for i in range(3)]
    hbt = [DT("hb%d" % i) for i in range(3)]
    hi = [st.sb("m_hi%d" % i, [128, 2, 128], F32) for i in range(3)]
    hit = [DT("hi%d" % i) for i in range(3)]

    def zero_state():
        for hh in range(MH):
            k.op("pool", lambda e: e.memset(Cf[:, hh, :], 0.0), writes=[St[hh]])
            k.op("pool", lambda e: e.memset(Cb[:, hh, :], 0.0), writes=[St[hh]])
            k.op("pool", lambda e: e.memset(Nf[:, hh, :], 0.0), writes=[St[hh]])
            k.op("pool", lambda e: e.memset(Nb[:, hh, :], 0.0), writes=[St[hh]])

    ecol = 127 if p == 0 else 0

    pSt2 = [DT("m_pS%d" % i) for i in range(2)]
    Ta3 = [st.sb("m_Ta3_%d" % i, [128, 128], F32) for i in range(3)]
    Tat3 = [DT("Ta3") for _ in range(3)]

    def make_unit(blk, bb, bs, hh, u, h3, u3):
        pA, pN, pC = ps[u3], ps[3 + u], ps[5 + u]
        pAt, pNt, pCt = pst[u3], pst[3 + u], pst[5 + u]
        pSv, pSt = pN[:, 384:512], pSt2[u]
        TaU, TatU = Ta3[u3], Tat3[u3]

        def fa1():
                k.op("act", lambda e: e.activation(out=TaU[:], in_=tri[:], func=AF.Identity, scale=A_[:, blk, hh:hh + 1]),
                     reads=[t_c, t_g], writes=[TatU])
                k.op("pe", lambda e: e.matmul(pA[:, 0:128], lhsT=C.ones_f[:], rhs=TaU[:], start=True, stop=True),
                     reads=[TatU, C.t_const], writes=[pAt])

        def fa2():
                k.op("dve", lambda e: e.tensor_tensor(out=tmp[u][:], in0=pA[:, 0:128], in1=madd[:], op=ALU.add),
                     reads=[pAt, t_c], writes=[tmpt[u]])
                k.op("act", lambda e: e.activation(out=Dm[u][:], in_=tmp[u][:], func=AF.Exp, bias=IMB[:, blk, hh:hh + 1], scale=1.0),
                     reads=[tmpt[u], t_g], writes=[Dmt[u]])
                k.op("act", lambda e: e.activation(out=eb[u][:], in_=pA[:, 0:128], func=AF.Exp), reads=[pAt], writes=[ebt[u]])
                k.op("dve", lambda e: e.tensor_tensor(out=Qp[u][:], in0=QT[bb][:, hh, :], in1=eb[u][:], op=ALU.mult),
                     reads=[t_blk[bb], ebt[u]], writes=[Qpt[u]])
                k.op("pe", lambda e: e.matmul(pSv, lhsT=KT[bb][:, hh, :], rhs=QT[bb][:, hh, :], start=True, stop=True),
                     reads=[t_blk[bb]], writes=[pSt])
                k.op("dve", lambda e: e.tensor_tensor(out=PT[u][:], in0=pSv, in1=Dm[u][:], op=ALU.mult),
                     reads=[pSt, Dmt[u]], writes=[PTt[u]])

        def fb():
                for dvh in range(2):
                    k.op("pe", lambda e: e.matmul(pN[:, dvh * 128:(dvh + 1) * 128], lhsT=VV[bb][:, hh * MDV + dvh * 128:hh * MDV + (dvh + 1) * 128],
                                                  rhs=PT[u][:], start=True, stop=False), reads=[t_blk[bb], PTt[u]], writes=[pNt], inc=False)
                    k.op("pe", lambda e: e.matmul(pN[:, dvh * 128:(dvh + 1) * 128], lhsT=Cb[:, hh, dvh * 128:(dvh + 1) * 128],
                                                  rhs=Qp[u][:], start=False, stop=True), reads=[St[hh], Qpt[u]], writes=[pNt], inc=False)
                k.op("pe", lambda e: e.matmul(pN[:, 256:384], lhsT=C.ones_b[:], rhs=PT[u][:], start=True, stop=False),
                     reads=[C.t_const, PTt[u]], writes=[pNt], inc=False)
                k.op("pe", lambda e: e.matmul(pN[:, 256:384], lhsT=Nb[:, hh, :], rhs=Qp[u][:], start=False, stop=True),
                     reads=[St[hh], Qpt[u]], writes=[pNt], inc=True)
                k.op("act", lambda e: e.activation(out=rd[u][:], in_=pN[:, 256:384], func=AF.Abs), reads=[pNt], writes=[rdt[u]])
                k.op("dve", lambda e: e.tensor_scalar(out=rd[u][:], in0=rd[u][:], scalar1=1.0, scalar2=None, op0=ALU.max),
                     reads=[rdt[u]], writes=[rdt[u]])
                k.op("dve", lambda e: e.reciprocal(out=rd[u][:], in_=rd[u][:]), reads=[rdt[u]], writes=[rdt[u]])
                if p == 1:
                    for dvh in range(2):
                        k.dma("sp", hi[h3][:, dvh, :], h_in[2 * hh + dvh, :, bs], writes=[hit[h3]])
                for dvh in range(2):
                    k.op("dve", lambda e: e.tensor_tensor(out=hb[h3][:, dvh, :], in0=pN[:, dvh * 128:(dvh + 1) * 128], in1=rd[u][:],
                                                          op=ALU.mult), reads=[pNt, rdt[u]], writes=[hbt[h3]])
                if p == 0:
                    for dvh in range(2):
                        k.dma("sp", h_out[2 * hh + dvh, :, bs], hb[h3][:, dvh, :], reads=[hbt[h3]], final=True)
                else:
                    k.op("dve", lambda e: e.tensor_tensor(out=hb[h3][:], in0=hb[h3][:], in1=hi[h3][:], op=ALU.add),
                         reads=[hit[h3]], writes=[hbt[h3]])
                    for dvh in range(2):
                        k.dma("sp", h_out[2 * hh + dvh, :, bs], hb[h3][:, dvh, :], reads=[hbt[h3]], final=True)

        def fc1():
                k.op("dve", lambda e: e.tensor_tensor(out=wc[u][:, 0:1], in0=IMB[:, blk, hh:hh + 1], in1=pA[:, ecol:ecol + 1],
                                                      op=ALU.add), reads=[pAt, t_g], writes=[wct[u]])
                k.op("act", lambda e: e.activation(out=wc[u][:, 0:1], in_=wc[u][:, 0:1], func=AF.Exp), reads=[wct[u]], writes=[wct[u]])
                k.op("act", lambda e: e.activation(out=wc[u][:, 1:2], in_=pA[:, ecol:ecol + 1], func=AF.Exp), reads=[pAt, wct[u]],
                     writes=[wct[u]])
                k.op("act", lambda e: e.activation(out=K2[u][:], in_=KK[bb][:, hh * 128:(hh + 1) * 128], func=AF.Identity,
                                                   scale=wc[u][:, 0:1]), reads=[t_blk[bb], wct[u]], writes=[K2t[u]])

        def fc2():
                k.op("pe", lambda e: e.matmul(pC[:, 0:256], lhsT=K2[u][:], rhs=VV[bb][:, hh * MDV:(hh + 1) * MDV], start=True, stop=True),
                     reads=[K2t[u], t_blk[bb]], writes=[pCt], inc=False)
                k.op("pe", lambda e: e.matmul(pC[:, 256:384], lhsT=K2[u][:], rhs=C.ones_b[:], start=True, stop=True),
                     reads=[K2t[u], C.t_const], writes=[pCt], inc=True)
                k.op("dve", lambda e: e.scalar_tensor_tensor(out=Cf[:, hh, :], in0=Cf[:, hh, :], scalar=wc[u][:, 1:2], in1=pC[:, 0:256],
                                                             op0=ALU.mult, op1=ALU.add), reads=[pCt, wct[u], St[hh]], writes=[St[hh]])
                k.op("act", lambda e: e.copy(out=Cb[:, hh, :], in_=Cf[:, hh, :]), reads=[St[hh]], writes=[St[hh]])
                k.op("dve", lambda e: e.scalar_tensor_tensor(out=Nf[:, hh, :], in0=Nf[:, hh, :], scalar=wc[u][:, 1:2], in1=pC[:, 256:384],
                                                             op0=ALU.mult, op1=ALU.add), reads=[pCt, wct[u], St[hh]], writes=[St[hh]])
                k.op("act", lambda e: e.copy(out=Nb[:, hh, :], in_=Nf[:, hh, :]), reads=[St[hh]], writes=[St[hh]])

        return fa1, fa2, fb, fc1, fc2

    if p == 0:
        order = [("z",)] + [("b", b) for b in range(NBLK)]
    else:
        order = [("z",), ("b", 1), ("b", 0), ("load",)] + [("b", b) for b in range(NBLK - 1, 1, -1)]
    n = 0
    nh = 0
    nblk = 0
    pipe = []
    LAG = [0, 1, 2, 2, 3]

    def advance(flush=False):
        nonlocal pipe
        newest = len(pipe) - 1
        for idx, stages in enumerate(pipe):
            age = newest - idx
            done = 5 - len(stages)
            while stages and (flush or LAG[done] <= age):
                stages.pop(0)()
                done += 1
        pipe = [st_ for st_ in pipe if st_]

    for it in order:
        if it[0] in ("z", "load"):
            advance(flush=True)
        if it[0] == "z":
            zero_state()
            continue
        if it[0] == "load":
            for hh in range(MH):
                k.dma("sp", Cf[:, hh, :], state_in[0][hh], writes=[St[hh]])
                k.dma("sp", Nf[:, hh, :], state_in[1][hh], writes=[St[hh]])
                k.op("act", lambda e: e.copy(out=Cb[:, hh, :], in_=Cf[:, hh, :]), reads=[St[hh]], writes=[St[hh]])
                k.op("act", lambda e: e.copy(out=Nb[:, hh, :], in_=Nf[:, hh, :]), reads=[St[hh]], writes=[St[hh]])
            continue
        blk = it[1]
        bs = slice(blk * 128, (blk + 1) * 128)
        bb = nblk % NB_
        nblk += 1
        k.dma("sp", QT[bb][:], qT_d[:, :, bs].rearrange("h p t -> p h t"), writes=[t_blk[bb]])
        k.dma("sp", KT[bb][:], kT_d[:, :, bs].rearrange("h p t -> p h t"), writes=[t_blk[bb]])
        k.dma("sp", KK[bb][:], ktok_d[bs, :], writes=[t_blk[bb]])
        k.dma("sp", VV[bb][:], vtok_d[bs, :], writes=[t_blk[bb]])
        for hh in range(MH):
            fa1, fa2, fb, fc1, fc2 = make_unit(blk, bb, bs, hh, n % NS, nh % 3, n % 3)
            n += 1
            nh += 1
            pipe.append([fa1, fa2, fb, fc1, fc2])
            advance()
    advance(flush=True)
    if p == 0:
        for hh in range(MH):
            k.dma("sp", state_out[0][hh], Cf[:, hh, :], reads=[St[hh]], final=True)
            k.dma("sp", state_out[1][hh], Nf[:, hh, :], reads=[St[hh]], final=True)


def emit_mlstm_readout(k, C, st, hs_d, so_d, ng_dram, wout_dram, ctx_out):
    X = C.X
    HN = st.sb("m_hn", [128, NCH, TOK], BF16)
    HNT = [DT("hn%d" % i) for i in range(3)]
    gn = st.sb("m_gn", [128, NCH], F32)
    t_c = DT("m_rc")
    k.dma("sp", gn[:], ng_dram, writes=[t_c])
    ps = [st.ps("rps%d" % i, [128, 512], F32) for i in range(4)]
    pst = [DT("rps%d" % i) for i in range(4)]
    sq = [st.sb("m_sq%d" % i, [128, 512], F32) for i in range(2)]
    sqt = [DT("sq") for _ in range(2)]
    rs = [st.sb("m_rs%d" % i, [128, 512], F32) for i in range(2)]
    rst = [DT("rs") for _ in range(2)]
    so = [st.sb("m_so%d" % i, [128, 2, 512], F32) for i in range(2)]
    sot = [DT("so%d" % i) for i in range(2)]
    hh_ = [st.sb("m_hh%d" % i, [128, 2, 512], F32) for i in range(2)]
    hht = [DT("hh%d" % i) for i in range(2)]
    n = 0
    for hh in range(MH):
        for bi, (t0, tl, v) in enumerate(TBS):
            if bi == 0 and not ctx_out:
                continue
            ts = slice(t0, t0 + tl)
            u = n % 2
            n += 1
            for dvh in range(2):
                k.dma("sp", so[u][:, dvh, :tl], so_d[2 * hh + dvh, :, ts], writes=[sot[u]])
                k.dma("sp", hh_[u][:, dvh, :tl], hs_d[2 * hh + dvh, :, ts], writes=[hht[u]])
            for dvh in range(2):
                q = (2 * n + dvh) % 2
                k.op("act", lambda e: e.activation(out=sq[q][:, :tl], in_=hh_[u][:, dvh, :tl], func=AF.Square),
                     reads=[hht[u]], writes=[sqt[q]])
                k.op("pe", lambda e: e.matmul(ps[u][:, :tl], lhsT=C.ones_f[:], rhs=sq[q][:, :tl], start=(dvh == 0), stop=(dvh == 1)),
                     reads=[sqt[q], C.t_const], writes=[pst[u]])
            k.op("act", lambda e: e.activation(out=rs[u][:, :tl], in_=ps[u][:, :tl], func=AF.Sqrt, bias=C.eps_col[:, 0:1],
                                               scale=1.0 / MDV), reads=[pst[u], C.t_const], writes=[rst[u]])
            k.op("dve", lambda e: e.reciprocal(out=rs[u][:, :tl], in_=rs[u][:, :tl]), reads=[rst[u]], writes=[rst[u]])
            for dvh in range(2):
                c = 2 * hh + dvh
                k.op("dve", lambda e: e.scalar_tensor_tensor(out=so[u][:, dvh, :tl], in0=so[u][:, dvh, :tl], scalar=gn[:, c:c + 1],
                                                             in1=rs[u][:, :tl], op0=ALU.mult, op1=ALU.mult),
                     reads=[sot[u], rst[u], t_c], writes=[sot[u]])
                k.op("dve", lambda e: e.tensor_tensor(out=HN[:, c, ts], in0=so[u][:, dvh, :tl], in1=hh_[u][:, dvh, :tl], op=ALU.mult),
                     reads=[sot[u], hht[u]], writes=[HNT[bi]])
    emit_outproj(k, C, st, HN, HNT, wout_dram, ctx_out, ps[2:4], pst[2:4])


def emit_outproj(k, C, st, IN, INT, w_dram, ctx_out, ps, pst):
    X = C.X
    NW = 2
    wo = [st.sb("op_w%d" % i, [128, NCH, 256], BF16) for i in range(NW)]
    wot = [DT("op_w%d" % i) for i in range(NW)]
    w_v = w_dram.rearrange("(c p) n -> p c n", p=128)
    ny = 0
    for dq in range(D // 256):
        b = dq % NW
        k.dma("pool", wo[b][:], w_v[:, :, dq * 256:(dq + 1) * 256], writes=[wot[b]])
        for dl in range(2):
            dc = dq * 2 + dl
            for bi, (t0, tl, v) in enumerate(TBS):
                if bi == 0 and not ctx_out:
                    continue
                ts = slice(t0, t0 + tl)
                p = ny % 2
                ny += 1
                for c in range(NCH):
                    k.op("pe", lambda e: e.matmul(ps[p][:, :tl], lhsT=wo[b][:, c, dl * 128:(dl + 1) * 128], rhs=IN[:, c, ts],
                                                  start=(c == 0), stop=(c == NCH - 1)),
                         reads=[wot[b], INT[bi]], writes=[pst[p]], inc=(c == NCH - 1))
                k.op("dve", lambda e: e.scalar_tensor_tensor(out=X[:, dc, ts], in0=ps[p][:, :tl], scalar=C.G[:, v, 1, dc:dc + 1],
                                                             in1=X[:, dc, ts], op0=ALU.mult, op1=ALU.add),
                     reads=[pst[p], C.t_mv, C.XT[bi]], writes=[C.XT[bi]])


def mlstm_consts():
    kk = np.arange(128)[:, None]
    tt = np.arange(128)[None, :]
    tri = np.stack([(kk <= tt), (kk >= tt)]).astype(np.float32)
    madd = ((tri - 1.0) * 1.0e4).astype(np.float32)
    return tri, madd


S5P = 64
S5T = TOK


def _eslice(seg, a, tl):
    t0, n, rev = seg
    if not rev:
        return slice(a - t0, a - t0 + tl)
    hi = n - 1 - (a - t0)
    lo = hi - tl + 1
    return slice(hi, lo - 1 if lo > 0 else None, -1)


def emit_s5_phase(k, C, p, lam_dram, bd_dram, cT_dram, dsk_dram, state_in, state_out, y_in, y_out, Z, ZT):
    segs = [(0, TOK, False)] if p == 0 else [(0, NCTX, True), (NCTX, NLAT, True)]
    with Stage(k) as st:
        ps = [st.ps("s5ps%d" % i, [128, 512], F32) for i in range(8)]
        pst = [DT("s5ps%d" % i) for i in range(8)]
        rstd = st.sb("s_rstd", [128, TOK], F32)
        t_rs = [DT("s_rs%d" % i) for i in range(3)]
        with Stage(k) as stn:
            emit_rstd(k, C, stn, rstd, t_rs, ps[6:8], pst[6:8])
        Hd = [st.sb("s_hd%d" % i, [128, TOK], BF16) for i in range(2)]
        HdT = [DT("s_hd%d" % i) for i in range(2)]
        htmp = [st.sb("s_htmp%d" % i, [128, 512], F32) for i in range(2)]
        htmpt = [DT("s_htmp%d" % i) for i in range(2)]
        LM = st.sb("s_lam", [128, 3, S5P], F32)
        t_p = DT("s5prep")
        k.dma("sp", LM[:], lam_dram[p], writes=[t_p])
        W = st.sb("s_w", [128, 16, S5P], F32)
        LR, LI, DTT, MAG, TH, CS, SN, AR1, AI, DEN, KR, KI, T1, T2, NSN, T3 = [W[:, i, :] for i in range(16)]

        def dv(fn):
            k.op("dve", fn, reads=[t_p, C.t_const], writes=[t_p])

        def ac(fn):
            k.op("act", fn, reads=[t_p, C.t_const], writes=[t_p])
        dv(lambda e: e.tensor_scalar(out=LR, in0=LM[:, 0, :], scalar1=-1e-4, scalar2=None, op0=ALU.min))
        ac(lambda e: e.activation(out=DTT, in_=LM[:, 2, :], func=AF.Exp))
        dv(lambda e: e.tensor_tensor(out=T1, in0=LR, in1=DTT, op=ALU.mult))
        ac(lambda e: e.activation(out=MAG, in_=T1, func=AF.Exp))
        dv(lambda e: e.tensor_tensor(out=TH, in0=LM[:, 1, :], in1=DTT, op=ALU.mult))
        ac(lambda e: e.activation(out=SN, in_=TH, func=AF.Sin, scale=1.0 / 8))
        ac(lambda e: e.activation(out=T1, in_=TH, func=AF.Sin, scale=1.0 / 16))
        dv(lambda e: e.tensor_tensor(out=T1, in0=T1, in1=T1, op=ALU.mult))
        dv(lambda e: e.tensor_scalar(out=CS, in0=T1, scalar1=-2.0, scalar2=1.0, op0=ALU.mult, op1=ALU.add))
        for _ in range(3):
            dv(lambda e: e.tensor_tensor(out=T1, in0=CS, in1=CS, op=ALU.mult))
            dv(lambda e: e.tensor_tensor(out=T2, in0=SN, in1=SN, op=ALU.mult))
            dv(lambda e: e.tensor_tensor(out=T3, in0=CS, in1=SN, op=ALU.mult))
            dv(lambda e: e.tensor_tensor(out=CS, in0=T1, in1=T2, op=ALU.subtract))
            dv(lambda e: e.tensor_scalar(out=SN, in0=T3, scalar1=2.0, scalar2=None, op0=ALU.mult))
        dv(lambda e: e.tensor_tensor(out=AR1, in0=MAG, in1=CS, op=ALU.mult))
        dv(lambda e: e.tensor_scalar(out=AR1, in0=AR1, scalar1=-1.0, scalar2=None, op0=ALU.add))
        dv(lambda e: e.tensor_tensor(out=AI, in0=MAG, in1=SN, op=ALU.mult))
        dv(lambda e: e.tensor_tensor(out=T1, in0=LR, in1=LR, op=ALU.mult))
        dv(lambda e: e.tensor_tensor(out=T2, in0=LM[:, 1, :], in1=LM[:, 1, :], op=ALU.mult))
        dv(lambda e: e.tensor_tensor(out=DEN, in0=T1, in1=T2, op=ALU.add))
        dv(lambda e: e.reciprocal(out=DEN, in_=DEN))
        dv(lambda e: e.tensor_tensor(out=T1, in0=AR1, in1=LR, op=ALU.mult))
        dv(lambda e: e.tensor_tensor(out=T2, in0=AI, in1=LM[:, 1, :], op=ALU.mult))
        dv(lambda e: e.tensor_tensor(out=T1, in0=T1, in1=T2, op=ALU.add))
        dv(lambda e: e.tensor_tensor(out=KR, in0=T1, in1=DEN, op=ALU.mult))
        dv(lambda e: e.tensor_tensor(out=T1, in0=AI, in1=LR, op=ALU.mult))
        dv(lambda e: e.tensor_tensor(out=T2, in0=AR1, in1=LM[:, 1, :], op=ALU.mult))
        dv(lambda e: e.tensor_tensor(out=T1, in0=T1, in1=T2, op=ALU.subtract))
        dv(lambda e: e.tensor_tensor(out=KI, in0=T1, in1=DEN, op=ALU.mult))
        dv(lambda e: e.tensor_scalar(out=NSN, in0=SN, scalar1=-1.0, scalar2=None, op0=ALU.mult))
        NLV = 11
        UP = st.sb("s_up", [128, NLV, 3, S5P], F32)
        dv(lambda e: e.tensor_copy(out=UP[:, 0, 0, :], in_=CS))
        dv(lambda e: e.tensor_copy(out=UP[:, 0, 1, :], in_=SN))
        dv(lambda e: e.tensor_copy(out=UP[:, 0, 2, :], in_=NSN))
        for l in range(1, NLV):
            dv(lambda e: e.tensor_tensor(out=T1, in0=UP[:, l - 1, 0, :], in1=UP[:, l - 1, 0, :], op=ALU.mult))
            dv(lambda e: e.tensor_tensor(out=T2, in0=UP[:, l - 1, 1, :], in1=UP[:, l - 1, 1, :], op=ALU.mult))
            dv(lambda e: e.tensor_tensor(out=UP[:, l, 0, :], in0=T1, in1=T2, op=ALU.subtract))
            dv(lambda e: e.tensor_tensor(out=T3, in0=UP[:, l - 1, 0, :], in1=UP[:, l - 1, 1, :], op=ALU.mult))
            dv(lambda e: e.tensor_scalar(out=UP[:, l, 1, :], in0=T3, scalar1=2.0, scalar2=None, op0=ALU.mult))
            dv(lambda e: e.tensor_scalar(out=UP[:, l, 2, :], in0=T3, scalar1=-2.0, scalar2=None, op0=ALU.mult))
        CR = st.sb("s_cr", [128, S5P, 16], F32)
        CI = st.sb("s_ci", [128, S5P, 16], F32)
        with Stage(k) as stc:
            CT = stc.sb("s_cT", [128, 2, S5P, 16], F32)
            k.dma("sp", CT[:], cT_dram[p], writes=[t_p])
            CW = stc.sb("s_cw", [128, S5P, 16], F32)
            krb = W[:, 10:11, :].rearrange("p o s -> p s o").to_broadcast([128, S5P, 16])
            kib = W[:, 11:12, :].rearrange("p o s -> p s o").to_broadcast([128, S5P, 16])
            dv(lambda e: e.tensor_tensor(out=CR[:], in0=CT[:, 0, :, :], in1=krb, op=ALU.mult))
            dv(lambda e: e.tensor_tensor(out=CW[:], in0=CT[:, 1, :, :], in1=kib, op=ALU.mult))
            dv(lambda e: e.tensor_tensor(out=CR[:], in0=CR[:], in1=CW[:], op=ALU.subtract))
            dv(lambda e: e.tensor_tensor(out=CI[:], in0=CT[:, 0, :, :], in1=kib, op=ALU.mult))
            dv(lambda e: e.tensor_tensor(out=CW[:], in0=CT[:, 1, :, :], in1=krb, op=ALU.mult))
            dv(lambda e: e.tensor_tensor(out=CI[:], in0=CI[:], in1=CW[:], op=ALU.add))
            dv(lambda e: e.tensor_scalar(out=CI[:], in0=CI[:], scalar1=-1.0, scalar2=None, op0=ALU.mult))
        dsk = st.sb("s_dsk", [128, NCH], F32)
        k.dma("sp", dsk[:], dsk_dram, writes=[t_p])
        SI = None
        if p == 1:
            SI = st.sb("s_si", [128, S5P, 2], F32)
            k.dma("sp", SI[:], state_in, writes=[t_p])
        SO = st.sb("s_so", [128, S5P, 2], F32)
        t_so = DT("s_so")
        E2 = [st.sb("s_E%d" % i, [128, 2, TOK], F32) for i in range(2)]
        t_E2 = [DT("s_E%d" % i) for i in range(2)]
        MB2 = [st.sb("s_magb%d" % i, [128, TOK], F32) for i in range(1)] * 2
        t_mb2 = [DT("s_magb%d" % i) for i in range(1)] * 2
        BR = [st.sb("s_br%d" % i, [128, TOK], F32) for i in range(2)]
        BI = [st.sb("s_bi%d" % i, [128, TOK], F32) for i in range(2)]
        t_b = [DT("s_b%d" % i) for i in range(2)]
        XR = [st.sb("s_xr%d" % i, [128, TOK], BF16) for i in range(1)] * 2
        XI = [st.sb("s_xi%d" % i, [128, TOK], BF16) for i in range(1)] * 2
        t_x = [DT("s_x%d" % i) for i in range(1)] * 2
        sc = [st.sb("s_sc%d" % i, [128, 512], F32) for i in range(4)]
        sct = [DT("s_sc%d" % i) for i in range(4)]
        tsc = [st.sb("s_tsc%d" % i, [128, 640], F32) for i in range(2)] * 2
        tsct = [DT("s_tsc%d" % i) for i in range(2)] * 2
        ini = st.sb("s_ini", [128, 4], F32)
        t_ini = DT("s_ini")
        BD = [st.sb("s_bd%d" % i, [128, 8, 128], BF16) for i in range(2)]
        BDt = [DT("s_bd%d" % i) for i in range(2)]
        CD = [st.sb("s_cd%d" % i, [128, 8, 128], BF16) for i in range(2)]
        CDt = [DT("s_cd%d" % i) for i in range(2)]
        yb = [st.sb("s_yb%d" % i, [128, 512], F32) for i in range(2)]
        ybt = [DT("s_yb%d" % i) for i in range(2)]
        yi = [st.sb("s_yi%d" % i, [128, 512], F32) for i in range(2)]
        yit = [DT("s_yi%d" % i) for i in range(2)]
        zb = [st.sb("s_zb%d" % i, [128, 512], BF16) for i in range(2)]
        zbt = [DT("s_zb%d" % i) for i in range(2)]

        def table_gen(tn):
            nonlocal nsc
            E, t_E, MB_, t_mb = E2[tn % 2], t_E2[tn % 2], MB2[tn % 2], t_mb2[tn % 2]
            P_ = tn
            k.op("pool", lambda e: e.memset(E[:, 0, 0:1], 1.0), writes=[t_E])
            k.op("pool", lambda e: e.memset(E[:, 1, 0:1], 0.0), writes=[t_E])
            seg = 1
            l = 0
            emax = TOK if p == 0 else NLAT
            while seg < emax:
                n_ = min(seg, emax - seg)
                s4 = nsc % 4
                nsc += 1
                ur, ui, nui = UP[:, l, 0, P_:P_ + 1], UP[:, l, 1, P_:P_ + 1], UP[:, l, 2, P_:P_ + 1]
                k.op("act", lambda e: e.activation(out=tsc[s4][:, :n_], in_=E[:, 0, 0:n_], func=AF.Identity, scale=ur),
                     reads=[t_E, t_p], writes=[tsct[s4]])
                k.op("dve", lambda e: e.scalar_tensor_tensor(out=E[:, 0, seg:seg + n_], in0=E[:, 1, 0:n_], scalar=nui, in1=tsc[s4][:, :n_],
                                                             op0=ALU.mult, op1=ALU.add), reads=[t_E, t_p, tsct[s4]], writes=[t_E])
                s4 = nsc % 4
                nsc += 1
                k.op("act", lambda e: e.activation(out=tsc[s4][:, :n_], in_=E[:, 1, 0:n_], func=AF.Identity, scale=ur),
                     reads=[t_E, t_p], writes=[tsct[s4]])
                k.op("dve", lambda e: e.scalar_tensor_tensor(out=E[:, 1, seg:seg + n_], in0=E[:, 0, 0:n_], scalar=ui, in1=tsc[s4][:, :n_],
                                                             op0=ALU.mult, op1=ALU.add), reads=[t_E, t_p, tsct[s4]], writes=[t_E])
                seg *= 2
                l += 1
                yield

        nt = 0
        nsc = 0
        nyb = 0
        ntile = 0
        for dc in range(NCH):
            db = dc % 2
            emit_hchunk(k, C, 1, dc, rstd, t_rs, Hd[db], HdT[db], htmp, htmpt)
            k.dma("pool", BD[db][:], bd_dram[p, dc], writes=[BDt[db]])
            k.op("pool", lambda e: e.memset(CD[db][:], 0.0), writes=[CDt[db]])
            for j in range(4):
                P_ = 4 * dc + j
                for gp in range(2):
                    rows = slice(gp * 64, (gp + 1) * 64)
                    cols = slice((2 * j + gp) * 16, (2 * j + gp + 1) * 16)
                    k.op("act", lambda e: e.copy(out=CD[db][rows, j, cols], in_=CR[rows, P_, :]), reads=[t_p], writes=[CDt[db]])
                    k.op("act", lambda e: e.copy(out=CD[db][rows, 4 + j, cols], in_=CI[rows, P_, :]), reads=[t_p], writes=[CDt[db]])
            ypst = pst[4:7]
            for j in range(4):
                P_ = 4 * dc + j
                u = nt % 2
                nt += 1
                E, t_E, MB_, t_mb = E2[ntile % 2], t_E2[ntile % 2], MB2[ntile % 2], t_mb2[ntile % 2]
                if ntile == 0:
                    for _ in table_gen(0):
                        pass
                gen = table_gen(ntile + 1) if ntile + 1 < 4 * NCH else iter(())

                def step():
                    next(gen, None)
                ntile += 1
                k.op("act", lambda e: e.activation(out=MB_[:], in_=E[:, 0, :], func=AF.Identity, scale=0.0, bias=MAG[:, P_:P_ + 1]),
                     reads=[t_E, t_p], writes=[t_mb])
                for bi, (t0, tl, v) in enumerate(TBS):
                    ts = slice(t0, t0 + tl)
                    sg = segs[0] if (p == 0 or bi == 0) else segs[1]
                    es_ = _eslice(sg, t0, tl)
                    pu = (nsc % 2) * 2
                    k.op("pe", lambda e: e.matmul(ps[pu][:, :tl], lhsT=BD[db][:, j, :], rhs=Hd[db][:, ts], start=True, stop=True),
                         reads=[BDt[db], HdT[db]], writes=[pst[pu]])
                    k.op("pe", lambda e: e.matmul(ps[pu + 1][:, :tl], lhsT=BD[db][:, 4 + j, :], rhs=Hd[db][:, ts], start=True, stop=True),
                         reads=[BDt[db], HdT[db]], writes=[pst[pu + 1]])
                    a4, b4 = nsc % 4, (nsc + 1) % 4
                    nsc += 2
                    k.op("dve", lambda e: e.tensor_tensor(out=sc[a4][:, :tl], in0=ps[pu][:, :tl], in1=E[:, 0, es_], op=ALU.mult),
                         reads=[pst[pu], t_E], writes=[sct[a4]])
                    k.op("dve", lambda e: e.tensor_tensor(out=sc[b4][:, :tl], in0=ps[pu + 1][:, :tl], in1=E[:, 1, es_], op=ALU.mult),
                         reads=[pst[pu + 1], t_E], writes=[sct[b4]])
                    k.op("dve", lambda e: e.tensor_tensor(out=BR[u][:, ts], in0=sc[a4][:, :tl], in1=sc[b4][:, :tl], op=ALU.add),
                         reads=[sct[a4], sct[b4]], writes=[t_b[u]])
                    step()
                    a4, b4 = nsc % 4, (nsc + 1) % 4
                    nsc += 2
                    k.op("dve", lambda e: e.tensor_tensor(out=sc[a4][:, :tl], in0=ps[pu + 1][:, :tl], in1=E[:, 0, es_], op=ALU.mult),
                         reads=[pst[pu + 1], t_E], writes=[sct[a4]])
                    k.op("dve", lambda e: e.tensor_tensor(out=sc[b4][:, :tl], in0=ps[pu][:, :tl], in1=E[:, 1, es_], op=ALU.mult),
                         reads=[pst[pu], t_E], writes=[sct[b4]])
                    k.op("dve", lambda e: e.tensor_tensor(out=BI[u][:, ts], in0=sc[a4][:, :tl], in1=sc[b4][:, :tl], op=ALU.subtract),
                         reads=[sct[a4], sct[b4]], writes=[t_b[u]])
                    step()
                for si, sg in enumerate(segs):
                    t0, n_, rev = sg
                    vs = slice(t0 + n_ - 1, t0 - 1 if t0 > 0 else None, -1) if rev else slice(t0, t0 + n_)
                    if p == 1 and si == 1:
                        x0r, x0i = SI[:, P_, 0:1], SI[:, P_, 1:2]
                        ur, ui, nui = UP[:, 0, 0, P_:P_ + 1], UP[:, 0, 1, P_:P_ + 1], UP[:, 0, 2, P_:P_ + 1]
                        k.op("dve", lambda e: e.tensor_tensor(out=ini[:, 2:3], in0=x0r, in1=ur, op=ALU.mult), reads=[t_p], writes=[t_ini])
                        k.op("dve", lambda e: e.scalar_tensor_tensor(out=ini[:, 0:1], in0=x0i, scalar=nui, in1=ini[:, 2:3], op0=ALU.mult, op1=ALU.add),
                             reads=[t_p, t_ini], writes=[t_ini])
                        k.op("dve", lambda e: e.tensor_tensor(out=ini[:, 3:4], in0=x0i, in1=ur, op=ALU.mult), reads=[t_p, t_ini], writes=[t_ini])
                        k.op("dve", lambda e: e.scalar_tensor_tensor(out=ini[:, 1:2], in0=x0r, scalar=ui, in1=ini[:, 3:4], op0=ALU.mult, op1=ALU.add),
                             reads=[t_p, t_ini], writes=[t_ini])
                        i_r, i_i = ini[:, 0:1], ini[:, 1:2]
                    else:
                        i_r, i_i = 0.0, 0.0
                    k.op("dve", lambda e: e.tensor_tensor_scan(out=BR[u][:, vs], data0=MB_[:, 0:n_], data1=BR[u][:, vs], initial=i_r,
                                                               op0=ALU.mult, op1=ALU.add), reads=[t_mb, t_ini], writes=[t_b[u]])
                    k.op("dve", lambda e: e.tensor_tensor_scan(out=BI[u][:, vs], data0=MB_[:, 0:n_], data1=BI[u][:, vs], initial=i_i,
                                                               op0=ALU.mult, op1=ALU.add), reads=[t_mb, t_ini], writes=[t_b[u]])
                    step()
                for bi, (t0, tl, v) in enumerate(TBS):
                    ts = slice(t0, t0 + tl)
                    sg = segs[0] if (p == 0 or bi == 0) else segs[1]
                    es_ = _eslice(sg, t0, tl)
                    a4, b4 = nsc % 4, (nsc + 1) % 4
                    nsc += 2
                    k.op("dve", lambda e: e.tensor_tensor(out=sc[a4][:, :tl], in0=BR[u][:, ts], in1=E[:, 0, es_], op=ALU.mult),
                         reads=[t_b[u], t_E], writes=[sct[a4]])
                    k.op("dve", lambda e: e.tensor_tensor(out=sc[b4][:, :tl], in0=BI[u][:, ts], in1=E[:, 1, es_], op=ALU.mult),
                         reads=[t_b[u], t_E], writes=[sct[b4]])
                    k.op("dve", lambda e: e.tensor_tensor(out=XR[u][:, ts], in0=sc[a4][:, :tl], in1=sc[b4][:, :tl], op=ALU.subtract),
                         reads=[sct[a4], sct[b4]], writes=[t_x[u]])
                    step()
                    if p == 0 and bi == 2:
                        k.op("act", lambda e: e.activation(out=ini[:, 2:3], in_=sc[a4][:, tl - 1:tl], func=AF.Identity,
                                                           bias=sc[b4][:, tl - 1:tl], scale=1.0), reads=[sct[a4], sct[b4]], writes=[t_ini])
                        k.op("dve", lambda e: e.tensor_scalar(out=SO[:, P_, 0:1], in0=sc[b4][:, tl - 1:tl], scalar1=-2.0, scalar2=ini[:, 2:3],
                                                              op0=ALU.mult, op1=ALU.add), reads=[sct[b4], t_ini], writes=[t_so])
                    a4, b4 = nsc % 4, (nsc + 1) % 4
                    nsc += 2
                    k.op("dve", lambda e: e.tensor_tensor(out=sc[a4][:, :tl], in0=BI[u][:, ts], in1=E[:, 0, es_], op=ALU.mult),
                         reads=[t_b[u], t_E], writes=[sct[a4]])
                    k.op("dve", lambda e: e.tensor_tensor(out=sc[b4][:, :tl], in0=BR[u][:, ts], in1=E[:, 1, es_], op=ALU.mult),
                         reads=[t_b[u], t_E], writes=[sct[b4]])
                    k.op("dve", lambda e: e.tensor_tensor(out=XI[u][:, ts], in0=sc[a4][:, :tl], in1=sc[b4][:, :tl], op=ALU.add),
                         reads=[sct[a4], sct[b4]], writes=[t_x[u]])
                    step()
                    if p == 0 and bi == 2:
                        k.op("dve", lambda e: e.tensor_tensor(out=SO[:, P_, 1:2], in0=sc[a4][:, tl - 1:tl], in1=sc[b4][:, tl - 1:tl], op=ALU.add),
                             reads=[sct[a4], sct[b4]], writes=[t_so])
                    yp = ps[4 + bi]
                    k.op("pe", lambda e: e.matmul(yp[:, :tl], lhsT=CD[db][:, j, :], rhs=XR[u][:, ts], start=(j == 0), stop=False),
                         reads=[CDt[db], t_x[u]], writes=[ypst[bi]], inc=False)
                    k.op("pe", lambda e: e.matmul(yp[:, :tl], lhsT=CD[db][:, 4 + j, :], rhs=XI[u][:, ts], start=False, stop=(j == 3)),
                         reads=[CDt[db], t_x[u]], writes=[ypst[bi]], inc=True)
                for _ in gen:
                    pass
            for bi, (t0, tl, v) in enumerate(TBS):
                ts = slice(t0, t0 + tl)
                q = nyb % 2
                nyb += 1
                if p == 0:
                    k.op("dve", lambda e: e.scalar_tensor_tensor(out=yb[q][:, :tl], in0=Hd[db][:, ts], scalar=dsk[:, dc:dc + 1], in1=ps[4 + bi][:, :tl],
                                                                 op0=ALU.mult, op1=ALU.add), reads=[HdT[db], t_p, ypst[bi]], writes=[ybt[q]])
                    k.dma("sp", y_out[dc, :, ts], yb[q][:, :tl], reads=[ybt[q]], final=True)
                else:
                    k.dma("sp", yi[q][:, :tl], y_in[dc, :, ts], writes=[yit[q]])
                    k.op("dve", lambda e: e.tensor_tensor(out=yb[q][:, :tl], in0=ps[4 + bi][:, :tl], in1=yi[q][:, :tl], op=ALU.add),
                         reads=[ypst[bi], yit[q]], writes=[ybt[q]])
                    k.op("dve", lambda e: e.tensor_tensor(out=yi[q][:, :tl], in0=yb[q][:, :tl], in1=yb[q][:, :tl], op=ALU.mult),
                         reads=[ybt[q]], writes=[yit[q]])
                    k.op("dve", lambda e: e.tensor_scalar(out=yi[q][:, :tl], in0=yi[q][:, :tl], scalar1=0.044715, scalar2=1.0, op0=ALU.mult, op1=ALU.add),
                         reads=[], writes=[yit[q]])
                    k.op("dve", lambda e: e.tensor_tensor(out=yi[q][:, :tl], in0=yi[q][:, :tl], in1=yb[q][:, :tl], op=ALU.mult),
                         reads=[ybt[q]], writes=[yit[q]])
                    k.op("act", lambda e: e.activation(out=yi[q][:, :tl], in_=yi[q][:, :tl], func=AF.Tanh, scale=0.7978845608028654),
                         reads=[], writes=[yit[q]])
                    k.op("act", lambda e: e.mul(out=yb[q][:, :tl], in_=yb[q][:, :tl], mul=0.5),
                         reads=[], writes=[ybt[q]])
                    k.op("dve", lambda e: e.scalar_tensor_tensor(out=zb[q][:, :tl], in0=yi[q][:, :tl], scalar=1.0, in1=yb[q][:, :tl],
                                                                 op0=ALU.add, op1=ALU.mult), reads=[yit[q], ybt[q]], writes=[zbt[q]])
                    k.dma("sp", Z[dc, :, ts], zb[q][:, :tl], reads=[zbt[q]], final=True)
        if p == 0:
            k.dma("sp", state_out, SO[:], reads=[t_so], final=True)


def emit_s5_glu(k, C, st, Z, ZT, wglu_dram):
    X = C.X
    NW = 3
    wt = [st.sb("g_w%d" % i, [128, NCH, 256], BF16) for i in range(NW)]
    wtt = [DT("g_w%d" % i) for i in range(NW)]
    sg = [st.sb("g_sg%d" % i, [128, 512], F32) for i in range(2)]
    sgt = [DT("g_sg%d" % i) for i in range(2)]
    ps = [st.ps("gps%d" % i, [128, 512], F32) for i in range(4)]
    pst = [DT("gps%d" % i) for i in range(4)]
    w_v = wglu_dram.rearrange("(c p) n -> p c n", p=128)
    nu = 0
    for dc in range(NCH):
        b = dc % NW
        k.dma("pool", wt[b][:, :, 0:128], w_v[:, :, dc * 128:(dc + 1) * 128], writes=[wtt[b]])
        k.dma("pool", wt[b][:, :, 128:256], w_v[:, :, D + dc * 128:D + (dc + 1) * 128], writes=[wtt[b]])
        for bi, (t0, tl, v) in enumerate(TBS):
            ts = slice(t0, t0 + tl)
            u = nu % 2
            nu += 1
            pa, pg = ps[2 * u], ps[2 * u + 1]
            for c in range(NCH):
                k.op("pe", lambda e: e.matmul(pa[:, :tl], lhsT=wt[b][:, c, 0:128], rhs=Z[:, c, ts], start=(c == 0), stop=(c == NCH - 1)),
                     reads=[wtt[b], ZT[bi]], writes=[pst[2 * u]], inc=(c == NCH - 1))
            for c in range(NCH):
                k.op("pe", lambda e: e.matmul(pg[:, :tl], lhsT=wt[b][:, c, 128:256], rhs=Z[:, c, ts], start=(c == 0), stop=(c == NCH - 1)),
                     reads=[wtt[b], ZT[bi]], writes=[pst[2 * u + 1]], inc=(c == NCH - 1))
            k.op("act", lambda e: e.activation(out=sg[u][:, :tl], in_=pg[:, :tl], func=AF.Sigmoid), reads=[pst[2 * u + 1]], writes=[sgt[u]])
            k.op("dve", lambda e: e.tensor_tensor(out=sg[u][:, :tl], in0=pa[:, :tl], in1=sg[u][:, :tl], op=ALU.mult),
                 reads=[pst[2 * u], sgt[u]], writes=[sgt[u]])
            k.op("dve", lambda e: e.scalar_tensor_tensor(out=X[:, dc, ts], in0=sg[u][:, :tl], scalar=C.G[:, v, 1, dc:dc + 1], in1=X[:, dc, ts],
                                                          op0=ALU.mult, op1=ALU.add), reads=[sgt[u], C.t_mv, C.XT[bi]], writes=[C.XT[bi]])


def s5_host_arrays(I, half):
    occ = 0
    dirs = [0, 1] if half == 0 else [1, 0]
    lam = np.zeros((2, 128, 3, S5P), np.float32)
    bd = np.zeros((2, NCH, 128, 8, 128), np.float32)
    cT = np.zeros((2, 128, 2, S5P, 16), np.float32)
    for s, dr in enumerate(dirs):
        for nm, idx in (("s5_lam_re", 0), ("s5_lam_im", 1)):
            a = I[nm][occ, dr].reshape(S5P, 2, 64)
            lam[s, :, idx, :] = a.transpose(1, 2, 0).reshape(128, S5P)
        ld = np.broadcast_to(I["s5_log_dt"][occ, dr].reshape(S5P, 2, 1), (S5P, 2, 64))
        lam[s, :, 2, :] = ld.transpose(1, 2, 0).reshape(128, S5P)
        for ri, nm in enumerate(("s5_b_re", "s5_b_im")):
            B = I[nm][occ, dr]
            for dc in range(NCH):
                for j in range(4):
                    for gp in range(2):
                        g = 8 * dc + 2 * j + gp
                        gl = 2 * j + gp
                        bd[s, dc, gl * 16:(gl + 1) * 16, ri * 4 + j, gp * 64:(gp + 1) * 64] = B[g].T
        for ri, nm in enumerate(("s5_c_re", "s5_c_im")):
            Cc = I[nm][occ, dr].reshape(S5P, 2, 16, 64)
            cT[s, :, ri, :, :] = Cc.transpose(1, 3, 0, 2).reshape(128, S5P, 16)
    return lam, bd, cT


NVC = 2
NACT = 4


class FProg:
    def __init__(self):
        self.nc = bass.Bass("TRN2", target_bir_lowering=False)
        self.ins = {}
        self.scr = {}
        self.outs = {}

    def inp(self, name, shape, dt=F32):
        if name in self.scr:
            return self.scr[name]
        if name not in self.ins:
            self.ins[name] = self.nc.dram_tensor(name, list(shape), dt, kind="ExternalInput").ap()
        return self.ins[name]

    def tmp(self, name, shape, dt=F32):
        if name not in self.scr:
            self.scr[name] = self.nc.dram_tensor(name, list(shape), dt).ap()
        return self.scr[name]

    def out(self, name, shape, dt=F32):
        if name not in self.outs:
            self.outs[name] = self.nc.dram_tensor(name, list(shape), dt, kind="ExternalOutput").ap()
        return self.outs[name]


SEGMENTS = [
    [("modvec", 0), ("ffn", 0, 0), ("mix1", 0)],
    [("mix2", 0), ("ffn", 0, 1), ("modvec", 1), ("ffn", 1, 0), ("mix1", 1)],
    [("mix2", 1), ("ffn", 1, 1), ("modvec", 2), ("ffn", 2, 0), ("mix1", 2)],
    [("mix2", 2), ("ffn", 2, 1), ("modvec", 3), ("ffn", 3, 0), ("mix1", 3)],
    [("mix2", 3), ("ffn", 3, 1)],
]


def set_layer(C, i):
    m = C.mvsets[i % 2]
    C.MV, C.MB, C.NG, C.A, C.G, C.t_mv = m


def emit_step(k, C, P, step, v):
    op = step[0]
    sf = "_v%d" % v
    so = "_v%d" % (1 - v)
    if op == "modvec":
        i = step[1]
        if v != 0:
            return
        set_layer(C, i)
        with Stage(k) as st:
            emit_modvec(k, C, st, P.inp("condT", [128, 2, NCH]), P.inp("modw%d" % i, [D, 9 * D]),
                        P.inp("modb%d" % i, [128, 9 * NCH]), P.inp("ng%d" % i, [128, 3, NCH]), P.inp("eye2", [2, 2]))
    elif op == "ffn":
        i, j = step[1], step[2]
        set_layer(C, i)
        emit_ffn(k, C, 0 if j == 0 else 2, P.inp("wi%d_%d" % (i, j), [D, 2 * DFF]),
                 P.inp("wo%d_%d" % (i, j), [DFF, D]), lat_only=(i == 3 and j == 1))
    elif op == "mix1":
        i = step[1]
        set_layer(C, i)
        kind = KINDS[i % 4]
        if kind in ("a", "w"):
            emit_qkv(k, C, P.inp("wqkv%d" % i, [D, QKV_W]), P.inp("qkg%d" % i, [128, 2]),
                     P.inp("cos" + sf, [128, NLAT]), P.inp("sin" + sf, [128, NLAT]), P.inp("rm", [128, 128]),
                     P.tmp("qT%d" % i + sf, [NH, 128, TOK], BF16), P.tmp("kT%d" % i + sf, [NKV, 128, TOK], BF16),
                     P.tmp("v%d" % i + sf, [TOK, 512], BF16))
        elif kind == "m":
            mq = P.tmp("mq%d" % i + sf, [MH, 128, TOK], BF16)
            mk_ = P.tmp("mk%d" % i + sf, [MH, 128, TOK], BF16)
            mkt = P.tmp("mkt%d" % i + sf, [TOK, MH * MDQK], BF16)
            mvt = P.tmp("mvt%d" % i + sf, [TOK, MH * MDV], BF16)
            mso = P.tmp("mso%d" % i + sf, [NCH, 128, TOK], F32)
            mgi = P.tmp("mgi%d" % i + sf, [TOK, 32], F32)
            emit_mlstm_proj(k, C, P.inp("mwin%d" % i, [D, 6144]), P.inp("mwg%d" % i + sf, [D, 32]),
                            P.inp("mbg%d" % i + sf, [128, 32]), mq, mk_, mkt, mvt, mso, mgi)
            with Stage(k) as st:
                emit_mlstm_scan(k, C, st, 0, mq, mk_, mkt, mvt, mgi, P.inp("mtri", [2, 128, 128]),
                                P.inp("mmadd", [2, 128, 128]), None,
                                (P.tmp("mstC%d" % i + sf, [MH, 128, MDV]), P.tmp("mstN%d" % i + sf, [MH, 128, 128])),
                                None, P.tmp("mh1_%d" % i + sf, [NCH, 128, TOK]))
        elif kind == "s":
            emit_s5_phase(k, C, 0, P.inp("s5lam" + sf, [2, 128, 3, S5P]), P.inp("s5bd" + sf, [2, NCH, 128, 8, 128]),
                          P.inp("s5cT" + sf, [2, 128, 2, S5P, 16]), P.inp("s5dsk", [128, NCH]), None,
                          P.tmp("s5st%d" % i + sf, [128, S5P, 2]), None, P.tmp("s5y1_%d" % i + sf, [NCH, 128, TOK]), None, None)
    elif op == "mix2":
        i = step[1]
        set_layer(C, i)
        kind = KINDS[i % 4]
        if kind in ("a", "w"):
            kts = [P.scr["kT%d_v%d" % (i, q)] for q in range(NVC)]
            vs = [P.scr["v%d_v%d" % (i, q)] for q in range(NVC)]
            other = (kts, vs) if kind == "a" else (kts[1 - v], vs[1 - v])
            emit_attn(k, C, kind == "w", i != 3, P.scr["qT%d" % i + sf], kts[v], vs[v], other,
                      P.inp("esink%d" % i, [128, NH]) if kind == "w" else None,
                      P.inp("masks", [128, 4, 128], BF16) if kind == "w" else None,
                      P.inp("awo%d" % i, [D, D]))
        elif kind == "m":
            hs = P.tmp("mhs%d" % i + sf, [NCH, 128, TOK])
            with Stage(k) as st:
                emit_mlstm_scan(k, C, st, 1, P.scr["mq%d" % i + sf], P.scr["mk%d" % i + sf], P.scr["mkt%d" % i + sf],
                                P.scr["mvt%d" % i + sf], P.scr["mgi%d" % i + sf],
                                P.inp("mtri", [2, 128, 128]), P.inp("mmadd", [2, 128, 128]),
                                (P.scr["mstC%d" % i + so], P.scr["mstN%d" % i + so]),
                                None, P.scr["mh1_%d" % i + sf], hs)
            with Stage(k) as st:
                emit_mlstm_readout(k, C, st, hs, P.scr["mso%d" % i + sf],
                                   P.inp("mng%d" % i, [128, NCH]), P.inp("mwout%d" % i, [D, D]), True)
        elif kind == "s":
            zd = P.tmp("s5z%d" % i + sf, [NCH, 128, TOK], BF16)
            emit_s5_phase(k, C, 1, P.inp("s5lam" + sf, [2, 128, 3, S5P]), P.inp("s5bd" + sf, [2, NCH, 128, 8, 128]),
                          P.inp("s5cT" + sf, [2, 128, 2, S5P, 16]), P.inp("s5dsk", [128, NCH]),
                          P.scr["s5st%d" % i + so], None, P.scr["s5y1_%d" % i + sf], None, zd, None)
            with Stage(k) as st:
                Z = st.sb("s_z", [128, NCH, TOK], BF16)
                ZT = [DT("z%d" % q) for q in range(3)]
                for q, (t0, tl, vv) in enumerate(TBS):
                    k.dma("sp", Z[:, :, t0:t0 + tl], zd[:, :, t0:t0 + tl].rearrange("c p t -> p c t"), writes=[ZT[q]])
                emit_s5_glu(k, C, st, Z, ZT, P.inp("s5wglu", [D, 2 * D]))
    else:
        raise ValueError(op)


def build_fused(segments=None, nvc=NVC):
    segments = SEGMENTS if segments is None else segments
    P = FProg()
    nc = P.nc
    with ExitStack() as es:
        k = KB(nc, es)
        C = Ctx()
        with Stage(k) as st0:
            C.X = st0.sb("X", [128, NCH, TOK], F32)
            C.XT = [DT("x%d" % i) for i in range(3)]
            C.mvsets = []
            for q in range(2):
                C.mvsets.append((st0.sb("MV%d" % q, [128, 2, 9 * NCH], F32), st0.sb("MB%d" % q, [128, 9 * NCH], F32),
                                 st0.sb("NG%d" % q, [128, 3, NCH], F32), st0.sb("A%d" % q, [128, 2, 3, NCH], F32),
                                 st0.sb("G%d" % q, [128, 2, 3, NCH], F32), DT("mv%d" % q)))
            setup_consts(k, st0, C)
            for si, seg in enumerate(segments):
                last = si == len(segments) - 1
                for v in range(nvc):
                    sf = "_v%d" % v
                    src = P.inp("xT" + sf, [D, TOK]) if si == 0 else P.scr["xpark" + sf]
                    load_x(k, C, src)
                    for step in seg:
                        emit_step(k, C, P, step, v)
                    if last:
                        store_x(k, C, P.out("xo" + sf, [D, NLAT]), True)
                    else:
                        store_x(k, C, P.tmp("xpark" + sf, [D, TOK]), False)
                    k.barrier()
            k.finish()
        P.ninstr = k.ninstr
    return P


def host_inputs(I, b, names):
    out = {}
    s5 = None
    for name in names:
        v = None
        base = name
        if len(name) > 3 and name[-3:-1] == "_v":
            v = int(name[-1])
            base = name[:-3]
        if base == "xT":
            out[name] = core_tokens_T(I["x"], I["ctx"], 2 * b + v)
        elif base == "condT":
            out[name] = vec_pm(np.stack([I["c"][b], I["c_ctx"]]))
        elif base == "rm":
            out[name] = rot_matrix()
        elif base == "eye2":
            out[name] = np.eye(2, dtype=np.float32)
        elif base == "masks":
            out[name] = win_masks()
        elif base == "cos":
            out[name] = rope_tables(v)[0]
        elif base == "sin":
            out[name] = rope_tables(v)[1]
        elif base == "mtri":
            out[name] = mlstm_consts()[0]
        elif base == "mmadd":
            out[name] = mlstm_consts()[1]
        elif base in ("s5lam", "s5bd", "s5cT"):
            if s5 is None:
                s5 = [s5_host_arrays(I, h) for h in range(2)]
            out[name] = s5[v][("s5lam", "s5bd", "s5cT").index(base)]
        elif base == "s5dsk":
            out[name] = vec_pm(I["s5_d"][0])
        elif base == "s5wglu":
            out[name] = I["s5_w_glu"][0]
        else:
            i = int(base[-3]) if base[-2] == "_" else int(base[-1])
            bb = base[:-3] if base[-2] == "_" else base[:-1]
            pre = "a_" if i % 4 == 0 else "w_"
            if bb == "modw":
                out[name] = I["mod_w"][i]
            elif bb == "modb":
                out[name] = vec_pm(I["mod_b"][i])
            elif bb == "ng":
                out[name] = vec_pm(I["norm_g"][i])
            elif bb == "wi":
                out[name] = I["ffn_wi"][i, int(base[-1])]
            elif bb == "wo":
                out[name] = I["ffn_wo"][i, int(base[-1])]
            elif bb == "wqkv":
                out[name] = I[pre + "wqkv"][0]
            elif bb == "qkg":
                out[name] = np.ascontiguousarray(I[pre + "qk_g"][0].T)
            elif bb == "awo":
                out[name] = I[pre + "wo"][0]
            elif bb == "esink":
                out[name] = np.ascontiguousarray(np.broadcast_to(I["w_sink"][0][None, :], (128, NH)))
            elif bb == "mwin":
                out[name] = I["m_w_in"][0]
            elif bb in ("mwg", "mbg"):
                perm = np.arange(32) if v == 0 else np.concatenate([np.arange(16, 32), np.arange(0, 16)])
                if bb == "mwg":
                    out[name] = np.ascontiguousarray(I["m_w_gate"][0][:, perm])
                else:
                    out[name] = np.ascontiguousarray(np.broadcast_to(I["m_b_gate"][0][perm][None, :], (128, 32)))
            elif bb == "mng":
                out[name] = vec_pm(I["m_norm_g"][0])
            elif bb == "mwout":
                out[name] = I["m_w_out"][0]
            else:
                raise KeyError(name)
    return out


def kernel(**inputs):
    I = {k_: np.asarray(v) for k_, v in inputs.items()}
    P = build_fused()
    names = list(P.ins)
    in_maps = [host_inputs(I, b, names) for b in range(NACT)]
    res = run_bass_kernel_spmd(P.nc, in_maps, core_ids=list(range(NACT)))
    B = I["x"].shape[0]
    out = np.empty((B, SEQ, D), np.float32)
    for b in range(NACT):
        for v in range(NVC):
            y = res.results[b]["xo_v%d" % v].T
            if v == 1:
                y = y[::-1]
            out[b, v * NLAT:(v + 1) * NLAT] = y
    return out
```

```python
import numpy as np
from contextlib import ExitStack
import concourse.bass as bass
import concourse.mybir as mybir
from concourse.bass_utils import run_bass_kernel_spmd

F32 = mybir.dt.float32
BF16 = mybir.dt.bfloat16
AF = mybir.ActivationFunctionType
ALU = mybir.AluOpType
AX = mybir.AxisListType

D = 2048
NCH = 16
DFF = 5632
NFC = 44
NCTX = 256
NLAT = 1024
TOK = NCTX + NLAT
SEQ = 2048
EPS = 1e-6
TBS = [(0, 256, 1), (256, 512, 0), (768, 512, 0)]
NCORES = 8


class DT:
    __slots__ = ("name", "w", "r", "dsem", "dcnt")

    def __init__(self, name=""):
        self.name = name
        self.w = {}
        self.r = {}
        self.dsem = None
        self.dcnt = 0


class KB:
    def __init__(self, nc, es):
        self.nc = nc
        self.es = es
        self.engs = {"pe": nc.tensor, "act": nc.scalar, "dve": nc.vector, "pool": nc.gpsimd, "sp": nc.sync}
        self.sem = {n: es.enter_context(nc.semaphore("s_" + n)) for n in ("pe", "act", "dve", "pool")}
        self.cnt = {n: 0 for n in self.sem}
        self.waited = {n: {} for n in self.engs}
        self.bound = []
        self.all_sems = []
        self.free_sems = []
        self.scount = {}
        self.final = []
        self.nsem = 0
        self.ninstr = 0
        self.uid = 0

    def _wait(self, e, deps):
        need = {}
        for d in deps:
            for s, v in d.items():
                if need.get(s, 0) < v:
                    need[s] = v
        w = self.waited[e]
        for s, v in need.items():
            if e == "pe" and s is self.sem["pe"]:
                continue
            if w.get(s, 0) >= v:
                continue
            self.engs[e].wait_ge(s, v)
            w[s] = v

    def op(self, e, fn, reads=(), writes=(), inc=True):
        deps = [t.w for t in reads]
        for t in writes:
            deps.append(t.w)
            deps.append(t.r)
        self._wait(e, deps)
        ins = fn(self.engs[e])
        self.ninstr += 1
        s = self.sem[e]
        if inc:
            self.cnt[e] += 1
            ins.then_inc(s, 1)
            v = self.cnt[e]
        else:
            v = self.cnt[e] + 1
        for t in reads:
            t.r[s] = v
        for t in writes:
            t.w[s] = v
            t.r = {}
        return ins

    def dma(self, q, out, in_, reads=(), writes=(), final=False):
        deps = [t.w for t in reads]
        for t in writes:
            deps.append(t.w)
            deps.append(t.r)
        self._wait(q, deps)
        t0 = (list(writes) + list(reads))[0]
        if t0.dsem is None:
            if self.free_sems:
                t0.dsem = self.free_sems.pop()
            else:
                t0.dsem = self.es.enter_context(self.nc.semaphore("d%d" % self.nsem))
                self.nsem += 1
                self.all_sems.append(t0.dsem)
                self.scount[t0.dsem] = 0
            self.bound.append(t0)
        self.scount[t0.dsem] += 16
        cnt = self.scount[t0.dsem]
        ins = self.engs[q].dma_start(out=out, in_=in_).then_inc(t0.dsem, 16)
        self.ninstr += 1
        for t in reads:
            t.r[t0.dsem] = cnt
        for t in writes:
            t.w[t0.dsem] = cnt
            t.r = {}
        if final:
            self.final.append({t0.dsem: cnt})
        return ins

    def barrier(self):
        allt = {self.sem[n]: self.cnt[n] for n in self.sem if self.cnt[n] > 0}
        for sm in self.all_sems:
            if self.scount[sm] > 0:
                allt[sm] = self.scount[sm]
        for e in self.engs:
            self._wait(e, [allt])
        for t in self.bound:
            t.dsem = None
        self.bound = []
        self.free_sems = list(self.all_sems)

    def finish(self):
        self._wait("sp", self.final)


class Stage:
    def __init__(self, k):
        self.k = k
        self.es = ExitStack()

    def __enter__(self):
        self.es.__enter__()
        return self

    def __exit__(self, *a):
        self.k.barrier()
        return self.es.__exit__(*a)

    def sb(self, name, shape, dt):
        self.k.uid += 1
        return self.es.enter_context(self.k.nc.sbuf_tensor("sb%d_%s" % (self.k.uid, name), list(shape), dt))

    def ps(self, name, shape, dt=F32):
        self.k.uid += 1
        return self.es.enter_context(self.k.nc.psum_tensor("ps%d_%s" % (self.k.uid, name), list(shape), dt))


class Ctx:
    pass


def setup_consts(k, st, C):
    nc = k.nc
    C.ones_f = st.sb("ones_f", [128, 128], F32)
    C.ones_b = st.sb("ones_b", [128, 128], BF16)
    C.t_const = DT("const")
    k.op("pool", lambda e: e.memset(C.ones_f[:], 1.0), writes=[C.t_const])
    k.op("pool", lambda e: e.memset(C.ones_b[:], 1.0), writes=[C.t_const])
    C.eps_col = st.sb("eps_col", [128, 2], F32)
    k.op("pool", lambda e: e.memset(C.eps_col[:], EPS), writes=[C.t_const])
    C.one_col = st.sb("one_col", [128, 2], F32)
    k.op("pool", lambda e: e.memset(C.one_col[:], 1.0), writes=[C.t_const])


def emit_modvec(k, C, st, condT_dram, modw_dram, modb_dram, ng_dram, i2_dram):
    C.t_mv = DT("mv")
    cond = st.sb("cond", [128, 2, NCH], F32)
    condb = st.sb("condb", [128, NCH, 2], BF16)
    t_cond = DT("cond")
    k.dma("sp", cond[:], condT_dram, writes=[t_cond])
    t_ng = DT("ng")
    k.dma("sp", C.NG[:], ng_dram, writes=[C.t_mv])
    k.dma("sp", C.MB[:], modb_dram, writes=[C.t_mv])
    t_condb = DT("condb")
    for v in range(2):
        k.op("act", lambda e: e.activation(out=condb[:, :, v], in_=cond[:, v, :], func=AF.Silu),
             reads=[t_cond], writes=[t_condb])
    NW = 3
    CW = 512
    wt = [st.sb("mw%d" % i, [128, NCH, CW], BF16) for i in range(NW)]
    wtt = [DT("mw%d" % i) for i in range(NW)]
    rows = st.sb("mvrows", [2, 9 * D], F32)
    t_rows = DT("mvrows")
    i2 = st.sb("mvi2", [2, 2], F32)
    k.dma("sp", i2[:], i2_dram, writes=[t_rows])
    pr = [st.ps("mvpr%d" % i, [128, 512], F32) for i in range(2)]
    prt = [DT("mvpr%d" % i) for i in range(2)]
    ps = st.ps("mvps", [128, 512], F32)
    pst = DT("mvps")
    modw_v = modw_dram.rearrange("(c p) n -> p c n", p=128)
    ntile = (9 * D) // CW
    for ti in range(ntile):
        b = ti % NW
        u = ti % 2
        k.dma("pool", wt[b][:], modw_v[:, :, ti * CW:(ti + 1) * CW], writes=[wtt[b]])
        for kc in range(NCH):
            k.op("pe", lambda e: e.matmul(pr[u][0:2, :], lhsT=condb[:, kc, :], rhs=wt[b][:, kc, :], start=(kc == 0),
                                          stop=(kc == NCH - 1)),
                 reads=[wtt[b], t_condb], writes=[prt[u]], inc=(kc == NCH - 1))
        k.op("act", lambda e: e.copy(out=rows[:, ti * CW:(ti + 1) * CW], in_=pr[u][0:2, :]), reads=[prt[u]], writes=[t_rows])
    for cc in range(9 * NCH):
        k.op("pe", lambda e: e.matmul(ps[:, cc * 2:cc * 2 + 2], lhsT=rows[:, cc * 128:(cc + 1) * 128], rhs=i2[:],
                                      start=True, stop=True), reads=[t_rows], writes=[pst], inc=(cc == 9 * NCH - 1))
    psv = ps[:, 0:288].rearrange("p (mc v) -> p v mc", v=2)
    for v in range(2):
        k.op("dve", lambda e: e.tensor_tensor(out=C.MV[:, v, :], in0=psv[:, v, :], in1=C.MB[:], op=ALU.add),
             reads=[pst, C.t_mv], writes=[C.t_mv])
    for v in range(2):
        for j in range(3):
            sc = C.MV[:, v, (3 * j + 1) * NCH:(3 * j + 2) * NCH]
            k.op("dve", lambda e: e.tensor_scalar(out=C.A[:, v, j, :], in0=sc, scalar1=1.0, scalar2=1.0,
                                                  op0=ALU.add, op1=ALU.mult), reads=[C.t_mv], writes=[C.t_mv])
            k.op("dve", lambda e: e.tensor_tensor(out=C.A[:, v, j, :], in0=C.A[:, v, j, :], in1=C.NG[:, j, :],
                                                  op=ALU.mult), reads=[C.t_mv], writes=[C.t_mv])
            gt = C.MV[:, v, (3 * j + 2) * NCH:(3 * j + 3) * NCH]
            k.op("dve", lambda e: e.tensor_scalar(out=C.G[:, v, j, :], in0=gt, scalar1=(1.0 if j == 1 else 0.5),
                                                  scalar2=None, op0=ALU.mult), reads=[C.t_mv], writes=[C.t_mv])


def emit_norm_mod(k, C, st, j, H, HT, pss, psst, tbs=None):
    X = C.X
    sq = [st.sb("sq%d_%d" % (j, i), [128, 512], F32) for i in range(2)]
    sqt = [DT("sq") for _ in range(2)]
    tmp = [st.sb("nt%d_%d" % (j, i), [128, 512], F32) for i in range(2)]
    tmpt = [DT("nt") for _ in range(2)]
    rstd = st.sb("rstd%d" % j, [128, TOK], F32)
    n = 0
    for bi, (t0, tl, v) in enumerate(TBS):
        if tbs is not None and bi not in tbs:
            continue
        ts = slice(t0, t0 + tl)
        pb = bi % 2
        for c in range(NCH):
            b = n % 2
            n += 1
            k.op("act", lambda e: e.activation(out=sq[b][:, :tl], in_=X[:, c, ts], func=AF.Square),
                 reads=[C.XT[bi]], writes=[sqt[b]])
            k.op("pe", lambda e: e.matmul(pss[pb][:, :tl], lhsT=C.ones_f[:], rhs=sq[b][:, :tl], start=(c == 0),
                                          stop=(c == NCH - 1)),
                 reads=[sqt[b], C.t_const], writes=[psst[pb]], inc=True)
        t_r = DT("rstd")
        k.op("act", lambda e: e.activation(out=rstd[:, ts], in_=pss[pb][:, :tl], func=AF.Sqrt,
                                           bias=C.eps_col[:, 0:1], scale=1.0 / D),
             reads=[psst[pb], C.t_const], writes=[t_r])
        k.op("dve", lambda e: e.reciprocal(out=rstd[:, ts], in_=rstd[:, ts]), reads=[t_r], writes=[t_r])
        for c in range(NCH):
            b = n % 2
            n += 1
            k.op("dve", lambda e: e.scalar_tensor_tensor(out=tmp[b][:, :tl], in0=X[:, c, ts],
                                                         scalar=C.A[:, v, j, c:c + 1], in1=rstd[:, ts],
                                                         op0=ALU.mult, op1=ALU.mult),
                 reads=[C.XT[bi], t_r, C.t_mv], writes=[tmpt[b]])
            k.op("act", lambda e: e.activation(out=H[:, c, ts], in_=tmp[b][:, :tl], func=AF.Identity,
                                               bias=C.MV[:, v, 3 * j * NCH + c:3 * j * NCH + c + 1], scale=1.0),
                 reads=[tmpt[b], C.t_mv], writes=[HT[bi]])


def emit_rstd(k, C, st, rstd, t_rs, pss, psst):
    X = C.X
    sq = [st.sb("rsq%d" % i, [128, 512], F32) for i in range(2)]
    sqt = [DT("rsq") for _ in range(2)]
    n = 0
    for bi, (t0, tl, v) in enumerate(TBS):
        ts = slice(t0, t0 + tl)
        pb = bi % 2
        for c in range(NCH):
            b = n % 2
            n += 1
            k.op("act", lambda e: e.activation(out=sq[b][:, :tl], in_=X[:, c, ts], func=AF.Square),
                 reads=[C.XT[bi]], writes=[sqt[b]])
            k.op("pe", lambda e: e.matmul(pss[pb][:, :tl], lhsT=C.ones_f[:], rhs=sq[b][:, :tl], start=(c == 0),
                                          stop=(c == NCH - 1)),
                 reads=[sqt[b], C.t_const], writes=[psst[pb]], inc=True)
        k.op("act", lambda e: e.activation(out=rstd[:, ts], in_=pss[pb][:, :tl], func=AF.Sqrt,
                                           bias=C.eps_col[:, 0:1], scale=1.0 / D),
             reads=[psst[pb], C.t_const], writes=[t_rs[bi]])
        k.op("dve", lambda e: e.reciprocal(out=rstd[:, ts], in_=rstd[:, ts]), reads=[t_rs[bi]], writes=[t_rs[bi]])


def emit_hchunk(k, C, j, c, rstd, t_rs, Hc, HcT, tmp, tmpt):
    X = C.X
    for bi, (t0, tl, v) in enumerate(TBS):
        ts = slice(t0, t0 + tl)
        b = bi % 2
        k.op("dve", lambda e: e.scalar_tensor_tensor(out=tmp[b][:, :tl], in0=X[:, c, ts],
                                                     scalar=C.A[:, v, j, c:c + 1], in1=rstd[:, ts],
                                                     op0=ALU.mult, op1=ALU.mult),
             reads=[C.XT[bi], t_rs[bi], C.t_mv], writes=[tmpt[b]])
        k.op("act", lambda e: e.activation(out=Hc[:, ts], in_=tmp[b][:, :tl], func=AF.Identity,
                                           bias=C.MV[:, v, 3 * j * NCH + c:3 * j * NCH + c + 1], scale=1.0),
             reads=[tmpt[b], C.t_mv], writes=[HcT])


def emit_ffn(k, C, j, wi_dram, wo_dram, lat_only=False):
    X = C.X
    NG_ = 4
    GC = NFC // NG_
    with Stage(k) as st:
        H = st.sb("ffn_h", [128, NCH, TOK], BF16)
        HT = [DT("h%d" % i) for i in range(3)]
        ps = [st.ps("fps%d" % i, [128, 512], F32) for i in range(8)]
        pst = [DT("fps%d" % i) for i in range(8)]
        tbs = [1, 2] if lat_only else [0, 1, 2]
        with Stage(k) as stn:
            emit_norm_mod(k, C, stn, j, H, HT, ps[6:8], pst[6:8], tbs=tbs)
        act = st.sb("ffn_act", [128, GC, TOK], BF16)
        actT = [DT("act%d" % i) for i in range(3)]
        NWI = 3
        wi = [st.sb("wi%d" % i, [128, NCH, 256], BF16) for i in range(NWI)]
        wit = [DT("wi%d" % i) for i in range(NWI)]
        NWO = 3
        WOC = 256
        wo = [st.sb("wo%d" % i, [128, GC, WOC], BF16) for i in range(NWO)]
        wot = [DT("wo%d" % i) for i in range(NWO)]
        sg = [st.sb("sg%d" % i, [128, 512], F32) for i in range(2)]
        sgt = [DT("sg") for _ in range(2)]
        wi_v = wi_dram.rearrange("(c p) n -> p c n", p=128)
        wo_v = wo_dram.rearrange("(f p) n -> p f n", p=128)
        nwi = 0
        nwo = 0
        nu = 0
        ny = 0
        for g in range(NG_):
            for fl in range(GC):
                f = g * GC + fl
                b = nwi % NWI
                nwi += 1
                k.dma("pool", wi[b][:, :, 0:128], wi_v[:, :, f * 128:(f + 1) * 128], writes=[wit[b]])
                k.dma("pool", wi[b][:, :, 128:256], wi_v[:, :, DFF + f * 128:DFF + (f + 1) * 128], writes=[wit[b]])
                for bi, (t0, tl, v) in enumerate(TBS):
                    if bi not in tbs:
                        continue
                    ts = slice(t0, t0 + tl)
                    pa = (nu % 3) * 2
                    pg = pa + 1
                    sb_ = nu % 2
                    nu += 1
                    for c in range(NCH):
                        k.op("pe", lambda e: e.matmul(ps[pa][:, :tl], lhsT=wi[b][:, c, 0:128], rhs=H[:, c, ts],
                                                      start=(c == 0), stop=(c == NCH - 1)),
                             reads=[wit[b], HT[bi]], writes=[pst[pa]], inc=(c == NCH - 1))
                    for c in range(NCH):
                        k.op("pe", lambda e: e.matmul(ps[pg][:, :tl], lhsT=wi[b][:, c, 128:256], rhs=H[:, c, ts],
                                                      start=(c == 0), stop=(c == NCH - 1)),
                             reads=[wit[b], HT[bi]], writes=[pst[pg]], inc=(c == NCH - 1))
                    k.op("act", lambda e: e.activation(out=sg[sb_][:, :tl], in_=ps[pg][:, :tl], func=AF.Silu),
                         reads=[pst[pg]], writes=[sgt[sb_]])
                    k.op("dve", lambda e: e.tensor_tensor(out=act[:, fl, ts], in0=ps[pa][:, :tl], in1=sg[sb_][:, :tl],
                                                          op=ALU.mult),
                         reads=[pst[pa], sgt[sb_]], writes=[actT[bi]])
            for dq in range(D // WOC):
                b = nwo % NWO
                nwo += 1
                k.dma("pool", wo[b][:], wo_v[:, g * GC:(g + 1) * GC, dq * WOC:(dq + 1) * WOC], writes=[wot[b]])
                for dl in range(WOC // 128):
                    dc = dq * (WOC // 128) + dl
                    for bi, (t0, tl, v) in enumerate(TBS):
                        if bi not in tbs:
                            continue
                        ts = slice(t0, t0 + tl)
                        p = ny % 6
                        ny += 1
                        for fl in range(GC):
                            k.op("pe", lambda e: e.matmul(ps[p][:, :tl], lhsT=wo[b][:, fl, dl * 128:(dl + 1) * 128],
                                                          rhs=act[:, fl, ts], start=(fl == 0), stop=(fl == GC - 1)),
                                 reads=[wot[b], actT[bi]], writes=[pst[p]], inc=(fl == GC - 1))
                        k.op("dve", lambda e: e.scalar_tensor_tensor(out=X[:, dc, ts], in0=ps[p][:, :tl],
                                                                     scalar=C.G[:, v, j, dc:dc + 1], in1=X[:, dc, ts],
                                                                     op0=ALU.mult, op1=ALU.add),
                             reads=[pst[p], C.t_mv, C.XT[bi]], writes=[C.XT[bi]])


def alloc_persistent(k, st, C):
    C.X = st.sb("X", [128, NCH, TOK], F32)
    C.XT = [DT("x%d" % i) for i in range(3)]
    C.MV = st.sb("MV", [128, 2, 9 * NCH], F32)
    C.MB = st.sb("MB", [128, 9 * NCH], F32)
    C.NG = st.sb("NG", [128, 3, NCH], F32)
    C.A = st.sb("A", [128, 2, 3, NCH], F32)
    C.G = st.sb("G", [128, 2, 3, NCH], F32)
    setup_consts(k, st, C)


def load_x(k, C, xT_dram):
    xv = xT_dram.rearrange("(c p) t -> p c t", p=128)
    for bi, (t0, tl, v) in enumerate(TBS):
        k.dma("sp", C.X[:, :, t0:t0 + tl], xv[:, :, t0:t0 + tl], writes=[C.XT[bi]])


def store_x(k, C, xo_dram, lat_only=False):
    xv = xo_dram.rearrange("(c p) t -> p c t", p=128)
    for bi, (t0, tl, v) in enumerate(TBS):
        if lat_only and bi == 0:
            continue
        o0 = t0 - (NCTX if lat_only else 0)
        k.dma("sp", xv[:, :, o0:o0 + tl], C.X[:, :, t0:t0 + tl], reads=[C.XT[bi]], final=True)


def build_test_ffn():
    nc = bass.Bass("TRN2", target_bir_lowering=False)
    xT = nc.dram_tensor("xT", [D, TOK], F32, kind="ExternalInput").ap()
    condT = nc.dram_tensor("condT", [128, 2, NCH], F32, kind="ExternalInput").ap()
    modw = nc.dram_tensor("modw", [D, 9 * D], F32, kind="ExternalInput").ap()
    modb = nc.dram_tensor("modb", [128, 9 * NCH], F32, kind="ExternalInput").ap()
    ng = nc.dram_tensor("ng", [128, 3, NCH], F32, kind="ExternalInput").ap()
    wi = nc.dram_tensor("wi", [D, 2 * DFF], F32, kind="ExternalInput").ap()
    wo = nc.dram_tensor("wo", [DFF, D], F32, kind="ExternalInput").ap()
    xo = nc.dram_tensor("xo", [D, TOK], F32, kind="ExternalOutput").ap()
    mvo = nc.dram_tensor("mvo", [128, 2 * 9 * NCH], F32, kind="ExternalOutput").ap()
    with ExitStack() as es:
        k = KB(nc, es)
        C = Ctx()
        with Stage(k) as st0:
            alloc_persistent(k, st0, C)
            load_x(k, C, xT)
            with Stage(k) as st:
                emit_modvec(k, C, st, condT, modw, modb, ng)
            k.dma("sp", mvo, C.MV[:].rearrange("p v m -> p (v m)"), reads=[C.t_mv], final=True)
            emit_ffn(k, C, 0, wi, wo)
            store_x(k, C, xo)
            k.finish()
        print("instructions:", k.ninstr, "dma sems:", k.nsem)
    return nc


def vec_pm(v):
    v = np.asarray(v)
    lead = v.shape[:-1]
    n = v.shape[-1] // 128
    a = v.reshape(lead + (n, 128))
    return np.ascontiguousarray(np.moveaxis(a, -1, 0))


def core_tokens_T(x, ctx, core):
    b, h = core // 2, core % 2
    cx, xl = ctx[b], x[b, h * NLAT:(h + 1) * NLAT]
    if h == 1:
        cx, xl = cx[::-1], xl[::-1]
    t = np.concatenate([cx, xl], axis=0)
    return np.ascontiguousarray(t.T)


NH = 16
NKV = 4
HD = 128
QKV_W = 3072
NKEY = NCTX + SEQ
NKB = NKEY // 128


def emit_qkv(k, C, wqkv_dram, qkg_dram, cos_dram, sin_dram, rm_dram, qT_d, kT_d, v_d):
    with Stage(k) as st:
        H = st.sb("qkv_h", [128, NCH, TOK], BF16)
        HT = [DT("h%d" % i) for i in range(3)]
        ps = [st.ps("qps%d" % i, [128, 512], F32) for i in range(8)]
        pst = [DT("qps%d" % i) for i in range(8)]
        with Stage(k) as stn:
            emit_norm_mod(k, C, stn, 1, H, HT, ps[6:8], pst[6:8])
        cs = st.sb("cs", [128, 2, NLAT], F32)
        rm = st.sb("rm", [128, 128], F32)
        g2 = st.sb("g2", [128, 2], F32)
        t_c = DT("qkvconst")
        k.dma("sp", cs[:, 0, :], cos_dram, writes=[t_c])
        k.dma("sp", cs[:, 1, :], sin_dram, writes=[t_c])
        k.dma("sp", rm[:], rm_dram, writes=[t_c])
        k.dma("sp", g2[:], qkg_dram, writes=[t_c])
        k.op("dve", lambda e: e.tensor_scalar(out=g2[:, 0:1], in0=g2[:, 0:1], scalar1=float(HD ** -0.5), scalar2=None,
                                              op0=ALU.mult), reads=[t_c], writes=[t_c])
        NW = 3
        wt = [st.sb("qw%d" % i, [128, NCH, 256], BF16) for i in range(NW)]
        wtt = [DT("qw%d" % i) for i in range(NW)]
        wv = st.sb("qwv", [128, NCH, 512], BF16)
        wvt = DT("qwv")
        w_v = wqkv_dram.rearrange("(c p) n -> p c n", p=128)
        k.dma("pool", wv[:], w_v[:, :, 2560:3072], writes=[wvt])
        sq = [st.sb("qsq%d" % i, [128, 512], F32) for i in range(2)]
        sqt = [DT("qsq") for _ in range(2)]
        rs = [st.sb("qrs%d" % i, [128, 512], F32) for i in range(2)]
        rst = [DT("qrs") for _ in range(2)]
        qn = [st.sb("qqn%d" % i, [128, 512], F32) for i in range(2)]
        qnt = [DT("qqn") for _ in range(2)]
        t1 = [st.sb("qt1%d" % i, [128, 512], F32) for i in range(2)]
        t1t = [DT("qt1") for _ in range(2)]
        ob = [st.sb("qob%d" % i, [128, 512], BF16) for i in range(3)]
        obt = [DT("qob%d" % i) for i in range(3)]
        def qunit(b, s, fc, isq, gcol, bi, t0, tl, u, o3):
            ts = slice(t0, t0 + tl)
            pq, pss, pr = ps[u], ps[2 + u], ps[4 + u]
            pqt, psst, prt = pst[u], pst[2 + u], pst[4 + u]

            def f1():
                for c in range(NCH):
                    k.op("pe", lambda e: e.matmul(pq[:, :tl], lhsT=wt[b][:, c, s * 128:(s + 1) * 128], rhs=H[:, c, ts],
                                                  start=(c == 0), stop=(c == NCH - 1)),
                         reads=[wtt[b], HT[bi]], writes=[pqt], inc=(c == NCH - 1))

            def f2():
                k.op("act", lambda e: e.activation(out=sq[u][:, :tl], in_=pq[:, :tl], func=AF.Square),
                     reads=[pqt], writes=[sqt[u]])
                k.op("pe", lambda e: e.matmul(pss[:, :tl], lhsT=C.ones_f[:], rhs=sq[u][:, :tl], start=True, stop=True),
                     reads=[sqt[u], C.t_const], writes=[psst])
                k.op("act", lambda e: e.activation(out=rs[u][:, :tl], in_=pss[:, :tl], func=AF.Sqrt,
                                                   bias=C.eps_col[:, 0:1], scale=1.0 / HD),
                     reads=[psst, C.t_const], writes=[rst[u]])
                k.op("dve", lambda e: e.reciprocal(out=rs[u][:, :tl], in_=rs[u][:, :tl]), reads=[rst[u]], writes=[rst[u]])
                k.op("dve", lambda e: e.scalar_tensor_tensor(out=qn[u][:, :tl], in0=pq[:, :tl], scalar=gcol,
                                                             in1=rs[u][:, :tl], op0=ALU.mult, op1=ALU.mult),
                     reads=[pqt, rst[u], t_c], writes=[qnt[u]])

            def f3():
                if bi == 0:
                    k.op("act", lambda e: e.copy(out=ob[o3][:, :tl], in_=qn[u][:, :tl]), reads=[qnt[u]], writes=[obt[o3]])
                else:
                    ls = slice(t0 - NCTX, t0 - NCTX + tl)
                    k.op("pe", lambda e: e.matmul(pr[:, :tl], lhsT=rm[:], rhs=qn[u][:, :tl], start=True, stop=True),
                         reads=[qnt[u], t_c], writes=[prt])
                    k.op("dve", lambda e: e.tensor_tensor(out=t1[u][:, :tl], in0=qn[u][:, :tl], in1=cs[:, 0, ls],
                                                           op=ALU.mult), reads=[qnt[u], t_c], writes=[t1t[u]])
                    k.op("dve", lambda e: e.tensor_tensor(out=qn[u][:, :tl], in0=pr[:, :tl], in1=cs[:, 1, ls],
                                                          op=ALU.mult), reads=[prt, t_c], writes=[qnt[u]])
                    k.op("dve", lambda e: e.tensor_tensor(out=ob[o3][:, :tl], in0=qn[u][:, :tl], in1=t1[u][:, :tl],
                                                          op=ALU.add), reads=[qnt[u], t1t[u]], writes=[obt[o3]])
                dst = qT_d[fc, :, ts] if isq else kT_d[fc - NH, :, ts]
                k.dma("sp", dst, ob[o3][:, :tl], reads=[obt[o3]], final=True)

            return f1, f2, f3

        pend2 = None
        pend3 = None
        nu = 0
        for ti in range(10):
            b = ti % NW
            k.dma("pool", wt[b][:], w_v[:, :, ti * 256:(ti + 1) * 256], writes=[wtt[b]])
            for s in range(2):
                fc = ti * 2 + s
                isq = fc < NH
                gcol = g2[:, 0:1] if isq else g2[:, 1:2]
                for bi, (t0, tl, v) in enumerate(TBS):
                    f1, f2, f3 = qunit(b, s, fc, isq, gcol, bi, t0, tl, nu % 2, nu % 3)
                    nu += 1
                    f1()
                    if pend3 is not None:
                        pend3()
                    pend3 = None
                    if pend2 is not None:
                        pend2[0]()
                        pend3 = pend2[1]
                    pend2 = (f2, f3)
        if pend3 is not None:
            pend3()
        if pend2 is not None:
            pend2[0]()
            pend2[1]()
        for tb in range(TOK // 128):
            u = tb % 2
            o3 = nu % 3
            nu += 1
            bi = 0 if tb < 2 else (1 if tb < 6 else 2)
            for c in range(NCH):
                k.op("pe", lambda e: e.matmul(ps[u][:, :], lhsT=H[:, c, tb * 128:(tb + 1) * 128], rhs=wv[:, c, :],
                                              start=(c == 0), stop=(c == NCH - 1)),
                     reads=[wvt, HT[bi]], writes=[pst[u]], inc=(c == NCH - 1))
            k.op("act", lambda e: e.copy(out=ob[o3][:, :], in_=ps[u][:, :]), reads=[pst[u]], writes=[obt[o3]])
            k.dma("sp", v_d[tb * 128:(tb + 1) * 128, :], ob[o3][:, :], reads=[obt[o3]], final=True)


NKEYW = NCTX + 128 + NLAT + 128


def emit_attn(k, C, window, ctx_out, qT_d, kT_all_d, v_all_d, kv_other, esink_dram, masks_dram, wo_dram):
    X = C.X
    nkey = NKEYW if window else NKEY
    nkb = nkey // 128
    with Stage(k) as st:
        KT = st.sb("KT", [128, NKV, nkey], BF16)
        V = st.sb("V", [128, nkb, 512], BF16)
        t_kv = DT("kv")
        kT_me, v_me, kT_x, v_x = kT_all_d, v_all_d, kv_other[0], kv_other[1]
        vr = lambda a: a.rearrange("(b p) f -> p b f", p=128)
        nlb = NLAT // 128
        if not window:
            for g in range(NKV):
                k.dma("sp", KT[:, g, 0:NCTX], kT_me[g][:, 0:NCTX], writes=[t_kv])
                k.dma("sp", KT[:, g, NCTX:TOK], kT_x[0][g][:, NCTX:TOK], writes=[t_kv])
                k.dma("sp", KT[:, g, TOK:NKEY], kT_x[1][g][:, NCTX:TOK], writes=[t_kv])
            k.dma("sp", V[:, 0:2, :], vr(v_me[0:NCTX, :]), writes=[t_kv])
            k.dma("sp", V[:, 2:2 + nlb, :], vr(v_x[0][NCTX:TOK, :]), writes=[t_kv])
            k.dma("sp", V[:, 2 + nlb:2 + 2 * nlb, :], vr(v_x[1][NCTX:TOK, :]), writes=[t_kv])
        else:
            k.op("pool", lambda e: e.memset(KT[:, :, NCTX:NCTX + 128], 0.0), writes=[t_kv])
            k.op("pool", lambda e: e.memset(V[:, 2, :], 0.0), writes=[t_kv])
            for g in range(NKV):
                k.dma("sp", KT[:, g, 0:NCTX], kT_me[g][:, 0:NCTX], writes=[t_kv])
                k.dma("sp", KT[:, g, NCTX + 128:NCTX + 128 + NLAT], kT_me[g][:, NCTX:TOK], writes=[t_kv])
                k.dma("sp", KT[:, g, NCTX + 128 + NLAT:NKEYW], kT_x[g][:, TOK - 128:TOK], writes=[t_kv])
            k.dma("sp", V[:, 0:2, :], vr(v_me[0:NCTX, :]), writes=[t_kv])
            k.dma("sp", V[:, 3:3 + nlb, :], vr(v_me[NCTX:TOK, :]), writes=[t_kv])
            k.dma("sp", V[:, 3 + nlb:4 + nlb, :], vr(v_x[TOK - 128:TOK, :]), writes=[t_kv])
        QT = [st.sb("QT%d" % i, [128, 4, TOK], BF16) for i in range(2)]
        QTt = [DT("QT%d" % i) for i in range(2)]
        OT = [st.sb("OT%d" % i, [128, 4, TOK], BF16) for i in range(2)]
        OTt = [DT("OT%d" % i) for i in range(2)]
        wo = [st.sb("awo%d" % i, [128, 4, D], BF16) for i in range(2)]
        wot = [DT("awo%d" % i) for i in range(2)]
        pt = [st.sb("pt%d" % i, [128, 512], BF16) for i in range(3)]
        ptt = [DT("pt%d" % i) for i in range(3)]
        rd = [st.sb("rd%d" % i, [128, 512], F32) for i in range(2)]
        rdt = [DT("rd%d" % i) for i in range(2)]
        ps = [st.ps("aps%d" % i, [128, 512], F32) for i in range(8)]
        pst = [DT("aps%d" % i) for i in range(8)]
        t_c = DT("attnconst")
        if window:
            es_ = st.sb("esink", [128, NH], F32)
            mk = st.sb("masks", [128, 4, 128], BF16)
            k.dma("sp", es_[:], esink_dram, writes=[t_c])
            k.dma("sp", mk[:], masks_dram, writes=[t_c])
            k.op("act", lambda e: e.activation(out=es_[:], in_=es_[:], func=AF.Exp), reads=[t_c], writes=[t_c])
        wo_v = wo_dram.rearrange("(h p) n -> p h n", p=128)
        ns = 0
        nunit = 0
        ny = 0
        for g in range(NKV):
            gb_ = g % 2
            for hl in range(4):
                k.dma("sp", QT[gb_][:, hl, :], qT_d[4 * g + hl], writes=[QTt[gb_]])
            k.dma("pool", wo[gb_][:], wo_v[:, 4 * g:4 * g + 4, :], writes=[wot[gb_]])
            for hl in range(4):
                h = 4 * g + hl
                units = []
                if not window:
                    for (t0, tl, v) in TBS[1:]:
                        units.append((t0, tl, [(kb, None) for kb in range(NKB)]))
                    if ctx_out:
                        units.append((0, NCTX, [(0, None), (1, None)]))
                else:
                    nqb = NLAT // 128
                    for qb in range(nqb):
                        kl = [(0, None), (1, None), (2 + qb, 2 if qb == 0 else 0), (3 + qb, None),
                              (4 + qb, 3 if qb == nqb - 1 else 1)]
                        units.append((NCTX + qb * 128, 128, kl))
                    if ctx_out:
                        units.append((0, NCTX, [(0, None), (1, None)]))
                for (t0, tl, kl) in units:
                    ts = slice(t0, t0 + tl)
                    u = nunit % 2
                    nunit += 1
                    po, pd = ps[2 + u], ps[4 + u]
                    pot, pdt = pst[2 + u], pst[4 + u]
                    def front(ki, kb, mi, s2, p3):
                        k.op("pe", lambda e: e.matmul(ps[s2][:, :tl], lhsT=KT[:, g, kb * 128:(kb + 1) * 128],
                                                      rhs=QT[gb_][:, hl, ts], start=True, stop=True),
                             reads=[t_kv, QTt[gb_]], writes=[pst[s2]])
                        k.op("act", lambda e: e.activation(out=pt[p3][:, :tl], in_=ps[s2][:, :tl], func=AF.Exp),
                             reads=[pst[s2]], writes=[ptt[p3]])
                        if mi is not None:
                            k.op("dve", lambda e: e.tensor_tensor(out=pt[p3][:, :tl], in0=pt[p3][:, :tl], in1=mk[:, mi, :tl],
                                                                  op=ALU.mult), reads=[ptt[p3], t_c], writes=[ptt[p3]])

                    def back(ki, kb, p3):
                        last = ki == len(kl) - 1
                        k.op("pe", lambda e: e.matmul(po[:, :tl], lhsT=V[:, kb, g * 128:(g + 1) * 128], rhs=pt[p3][:, :tl],
                                                      start=(ki == 0), stop=last),
                             reads=[t_kv, ptt[p3]], writes=[pot], inc=last)
                        k.op("pe", lambda e: e.matmul(pd[:, :tl], lhsT=C.ones_b[:], rhs=pt[p3][:, :tl],
                                                      start=(ki == 0), stop=last),
                             reads=[C.t_const, ptt[p3]], writes=[pdt], inc=last)
                    prev = None
                    for ki, (kb, mi) in enumerate(kl):
                        s2 = ns % 2
                        p3 = ns % 3
                        ns += 1
                        front(ki, kb, mi, s2, p3)
                        if prev is not None:
                            back(*prev)
                        prev = (ki, kb, p3)
                    back(*prev)
                    if window:
                        k.op("dve", lambda e: e.tensor_scalar(out=rd[u][:, :tl], in0=pd[:, :tl], scalar1=es_[:, h:h + 1],
                                                              scalar2=None, op0=ALU.add), reads=[pdt, t_c], writes=[rdt[u]])
                        k.op("dve", lambda e: e.reciprocal(out=rd[u][:, :tl], in_=rd[u][:, :tl]), reads=[rdt[u]], writes=[rdt[u]])
                    else:
                        k.op("dve", lambda e: e.reciprocal(out=rd[u][:, :tl], in_=pd[:, :tl]), reads=[pdt], writes=[rdt[u]])
                    k.op("dve", lambda e: e.tensor_tensor(out=OT[gb_][:, hl, ts], in0=po[:, :tl], in1=rd[u][:, :tl], op=ALU.mult),
                         reads=[pot, rdt[u]], writes=[OTt[gb_]])
            for dc in range(NCH):
                for bi, (t0, tl, v) in enumerate(TBS):
                    if bi == 0 and not ctx_out:
                        continue
                    ts = slice(t0, t0 + tl)
                    p = 6 + ny % 2
                    ny += 1
                    for hl in range(4):
                        k.op("pe", lambda e: e.matmul(ps[p][:, :tl], lhsT=wo[gb_][:, hl, dc * 128:(dc + 1) * 128],
                                                      rhs=OT[gb_][:, hl, ts], start=(hl == 0), stop=(hl == 3)),
                             reads=[wot[gb_], OTt[gb_]], writes=[pst[p]], inc=(hl == 3))
                    k.op("dve", lambda e: e.scalar_tensor_tensor(out=X[:, dc, ts], in0=ps[p][:, :tl],
                                                                 scalar=C.G[:, v, 1, dc:dc + 1], in1=X[:, dc, ts],
                                                                 op0=ALU.mult, op1=ALU.add),
                         reads=[pst[p], C.t_mv, C.XT[bi]], writes=[C.XT[bi]])


def rope_tables(half):
    t = np.arange(half * NLAT, (half + 1) * NLAT)
    if half == 1:
        t = t[::-1]
    row = (t // 64).astype(np.float32)
    col = (t % 64).astype(np.float32)
    inv = (10000.0 ** (-np.arange(32, dtype=np.float32) / 32)).astype(np.float32)
    ang = np.concatenate([row[:, None] * inv, col[:, None] * inv], axis=-1)
    ang = np.concatenate([ang, ang], axis=-1).astype(np.float32)
    return np.ascontiguousarray(np.cos(ang).T.astype(np.float32)), np.ascontiguousarray(np.sin(ang).T.astype(np.float32))


def rot_matrix():
    R = np.zeros((128, 128), np.float32)
    for m in range(64):
        R[m + 64, m] = -1.0
    for m in range(64, 128):
        R[m - 64, m] = 1.0
    return R


def win_masks(half=0):
    import ml_dtypes
    s = np.arange(128)[:, None]
    t = np.arange(128)[None, :]
    prev = (s >= t).astype(np.float32)
    nxt = (s <= t).astype(np.float32)
    z = np.zeros_like(prev)
    m = np.stack([prev, nxt, z, nxt[::-1]], axis=1)
    return np.ascontiguousarray(m).astype(ml_dtypes.bfloat16)


KINDS = ["a", "s", "m", "w"]


MH = 8
MDQK = 128
MDV = 256
NBLK = TOK // 128
LN_KSCALE = float(np.log(MDQK ** -0.5))


def emit_mlstm_proj(k, C, win_dram, wg_dram, bg_dram, qT_d, kT_d, ktok_d, vtok_d, so_d, gi_d):
    with Stage(k) as st:
        H = st.sb("m_h", [128, NCH, TOK], BF16)
        HT = [DT("h%d" % i) for i in range(3)]
        ps = [st.ps("mps%d" % i, [128, 512], F32) for i in range(8)]
        pst = [DT("mps%d" % i) for i in range(8)]
        with Stage(k) as stn:
            emit_norm_mod(k, C, stn, 1, H, HT, ps[6:8], pst[6:8])
        NW = 3
        wt = [st.sb("mw%d" % i, [128, NCH, 512], BF16) for i in range(NW)]
        wtt = [DT("mw%d" % i) for i in range(NW)]
        w_v = win_dram.rearrange("(c p) n -> p c n", p=128)
        ob = [st.sb("mob%d" % i, [128, 512], BF16) for i in range(3)]
        obt = [DT("mob%d" % i) for i in range(3)]
        of = [st.sb("mof%d" % i, [128, 512], F32) for i in range(2)]
        oft = [DT("mof%d" % i) for i in range(2)]
        nu = 0
        nw = 0
        for (c0, kind) in [(0, "q"), (512, "q"), (1024, "k"), (1536, "k"), (4096, "o"), (4608, "o"), (5120, "o"), (5632, "o")]:
            b = nw % NW
            nw += 1
            k.dma("pool", wt[b][:], w_v[:, :, c0:c0 + 512], writes=[wtt[b]])
            for s in range(4):
                fcol = c0 + s * 128
                for bi, (t0, tl, v) in enumerate(TBS):
                    ts = slice(t0, t0 + tl)
                    u = nu % 4
                    nu += 1
                    for c in range(NCH):
                        k.op("pe", lambda e: e.matmul(ps[u][:, :tl], lhsT=wt[b][:, c, s * 128:(s + 1) * 128], rhs=H[:, c, ts],
                                                      start=(c == 0), stop=(c == NCH - 1)),
                             reads=[wtt[b], HT[bi]], writes=[pst[u]], inc=(c == NCH - 1))
                    if kind == "o":
                        f2 = nu % 2
                        k.op("act", lambda e: e.activation(out=of[f2][:, :tl], in_=ps[u][:, :tl], func=AF.Sigmoid),
                             reads=[pst[u]], writes=[oft[f2]])
                        k.dma("sp", so_d[(fcol - 4096) // 128, :, ts], of[f2][:, :tl], reads=[oft[f2]], final=True)
                    else:
                        o3 = nu % 3
                        k.op("act", lambda e: e.copy(out=ob[o3][:, :tl], in_=ps[u][:, :tl]), reads=[pst[u]], writes=[obt[o3]])
                        dst = qT_d[fcol // 128, :, ts] if kind == "q" else kT_d[(fcol - 1024) // 128, :, ts]
                        k.dma("sp", dst, ob[o3][:, :tl], reads=[obt[o3]], final=True)
        wg = st.sb("m_wg", [128, NCH, 32], BF16)
        wgt = DT("m_wg")
        k.dma("pool", wg[:], wg_dram.rearrange("(c p) n -> p c n", p=128), writes=[wgt])
        bg = st.sb("m_bg", [128, 32], F32)
        k.dma("sp", bg[:], bg_dram, writes=[wgt])
        gt = [st.sb("m_gt%d" % i, [128, 32], F32) for i in range(2)]
        gtt = [DT("m_gt%d" % i) for i in range(2)]
        for (c0, kind) in [(1024, "k"), (1536, "k"), (2048, "v"), (2560, "v"), (3072, "v"), (3584, "v")]:
            b = nw % NW
            nw += 1
            k.dma("pool", wt[b][:], w_v[:, :, c0:c0 + 512], writes=[wtt[b]])
            for tb in range(NBLK):
                u = nu % 4
                o3 = nu % 3
                nu += 1
                bi = 0 if tb < 2 else (1 if tb < 6 else 2)
                for c in range(NCH):
                    k.op("pe", lambda e: e.matmul(ps[u][:, :], lhsT=H[:, c, tb * 128:(tb + 1) * 128], rhs=wt[b][:, c, :],
                                                  start=(c == 0), stop=(c == NCH - 1)),
                         reads=[wtt[b], HT[bi]], writes=[pst[u]], inc=(c == NCH - 1))
                k.op("act", lambda e: e.copy(out=ob[o3][:, :], in_=ps[u][:, :]), reads=[pst[u]], writes=[obt[o3]])
                dst = ktok_d[tb * 128:(tb + 1) * 128, c0 - 1024:c0 - 512] if kind == "k" else \
                    vtok_d[tb * 128:(tb + 1) * 128, c0 - 2048:c0 - 1536]
                k.dma("sp", dst, ob[o3][:, :], reads=[obt[o3]], final=True)
        for tb in range(NBLK):
            u = nu % 4
            f2 = nu % 2
            nu += 1
            bi = 0 if tb < 2 else (1 if tb < 6 else 2)
            for c in range(NCH):
                k.op("pe", lambda e: e.matmul(ps[u][:, 0:32], lhsT=H[:, c, tb * 128:(tb + 1) * 128], rhs=wg[:, c, :],
                                              start=(c == 0), stop=(c == NCH - 1)),
                     reads=[wgt, HT[bi]], writes=[pst[u]], inc=(c == NCH - 1))
            k.op("dve", lambda e: e.tensor_tensor(out=gt[f2][:], in0=ps[u][:, 0:32], in1=bg[:], op=ALU.add),
                 reads=[pst[u], wgt], writes=[gtt[f2]])
            k.op("act", lambda e: e.activation(out=gt[f2][:], in_=gt[f2][:], func=AF.Tanh, scale=1.0 / 15.0),
                 reads=[gtt[f2]], writes=[gtt[f2]])
            k.op("dve", lambda e: e.tensor_scalar(out=gt[f2][:], in0=gt[f2][:], scalar1=15.0, scalar2=None, op0=ALU.mult),
                 reads=[gtt[f2]], writes=[gtt[f2]])
            k.dma("sp", gi_d[tb * 128:(tb + 1) * 128, :], gt[f2][:], reads=[gtt[f2]], final=True)


def emit_mlstm_scan(k, C, st, phase, qT_d, kT_d, ktok_d, vtok_d, gi_d, tri_dram, madd_dram, state_in, state_out,
                    h_in, h_out):
    p = phase
    NB_ = 2
    QT = [st.sb("m_qT%d" % i, [128, MH, 128], BF16) for i in range(NB_)]
    KT = [st.sb("m_kT%d" % i, [128, MH, 128], BF16) for i in range(NB_)]
    KK = [st.sb("m_kk%d" % i, [128, MH * MDQK], BF16) for i in range(NB_)]
    VV = [st.sb("m_vv%d" % i, [128, MH * MDV], BF16) for i in range(NB_)]
    t_blk = [DT("m_blk%d" % i) for i in range(NB_)]
    GI = st.sb("m_gi", [128, NBLK, 32], F32)
    t_in = DT("m_in")
    k.dma("sp", GI[:], gi_d.rearrange("(b p) f -> p b f", p=128), writes=[t_in])
    tri = st.sb("m_tri", [128, 128], F32)
    madd = st.sb("m_madd", [128, 128], F32)
    t_c = DT("m_c")
    k.dma("sp", tri[:], tri_dram[p], writes=[t_c])
    k.dma("sp", madd[:], madd_dram[p], writes=[t_c])
    A_ = st.sb("m_a", [128, NBLK, MH], F32)
    IMB = st.sb("m_imb", [128, NBLK, MH], F32)
    t_g = DT("m_g")
    fsl = GI[:, :, p * 16 + 8:p * 16 + 16]
    isl = GI[:, :, p * 16:p * 16 + 8]
    k.op("act", lambda e: e.activation(out=A_[:], in_=fsl, func=AF.Exp, scale=-1.0), reads=[t_in], writes=[t_g])
    k.op("act", lambda e: e.activation(out=A_[:], in_=A_[:], func=AF.Ln, bias=C.one_col[:, 0:1], scale=1.0),
         reads=[t_g, C.t_const], writes=[t_g])
    k.op("dve", lambda e: e.tensor_scalar(out=A_[:], in0=A_[:], scalar1=-1.0, scalar2=None, op0=ALU.mult),
         reads=[t_g], writes=[t_g])
    ps = [st.ps("sps%d" % i, [128, 512], F32) for i in range(8)]
    pst = [DT("sps%d" % i) for i in range(8)]
    for blk in range(NBLK):
        u = blk % 2
        k.op("pe", lambda e: e.matmul(ps[u][:, 0:MH], lhsT=tri[:], rhs=A_[:, blk, :], start=True, stop=True),
             reads=[t_c, t_g], writes=[pst[u]])
        k.op("dve", lambda e: e.scalar_tensor_tensor(out=IMB[:, blk, :], in0=isl[:, blk, :], scalar=LN_KSCALE,
                                                     in1=ps[u][:, 0:MH], op0=ALU.add, op1=ALU.subtract),
             reads=[pst[u], t_in], writes=[t_g])
    Cf = st.sb("m_Cf", [128, MH, MDV], F32)
    Cb = st.sb("m_Cb", [128, MH, MDV], BF16)
    Nf = st.sb("m_Nf", [128, MH, 128], F32)
    Nb = st.sb("m_Nb", [128, MH, 128], BF16)
    St = [DT("m_st%d" % hh) for hh in range(MH)]
    NS = 2
    Ta = [st.sb("m_Ta%d" % i, [128, 128], F32) for i in range(NS)]
    Tat = [DT("Ta") for _ in range(NS)]
    tmp = [st.sb("m_tmp%d" % i, [128, 128], F32) for i in range(NS)]
    tmpt = [DT("tmp") for _ in range(NS)]
    Dm = [st.sb("m_Dm%d" % i, [128, 128], F32) for i in range(NS)]
    Dmt = [DT("Dm") for _ in range(NS)]
    eb = [st.sb("m_eb%d" % i, [128, 128], F32) for i in range(NS)]
    ebt = [DT("eb") for _ in range(NS)]
    Qp = [st.sb("m_Qp%d" % i, [128, 128], BF16) for i in range(NS)]
    Qpt = [DT("Qp") for _ in range(NS)]
    PT = [st.sb("m_PT%d" % i, [128, 128], BF16) for i in range(NS)]
    PTt = [DT("PT") for _ in range(NS)]
    rd = [st.sb("m_rd%d" % i, [128, 128], F32) for i in range(NS)]
    rdt = [DT("rd") for _ in range(NS)]
    wc = [st.sb("m_wc%d" % i, [128, 2], F32) for i in range(NS)]
    wct = [DT("wc") for _ in range(NS)]
    K2 = [st.sb("m_K2%d" % i, [128, 128], BF16) for i in range(NS)]
    K2t = [DT("K2") for _ in range(NS)]
    hb = [st.sb("m_hb%d" % i, [128, 2, 128], F32) for i in range(3)]
    hbt = [DT("hb%d" % i) for i in range(3)]
    hi = [st.sb("m_hi%d" % i, [128, 2, 128], F32) for i in range(3)]
    hit = [DT("hi%d" % i) for i in range(3)]

    def zero_state():
        for hh in range(MH):
            k.op("pool", lambda e: e.memset(Cf[:, hh, :], 0.0), writes=[St[hh]])
            k.op("pool", lambda e: e.memset(Cb[:, hh, :], 0.0), writes=[St[hh]])
            k.op("pool", lambda e: e.memset(Nf[:, hh, :], 0.0), writes=[St[hh]])
            k.op("pool", lambda e: e.memset(Nb[:, hh, :], 0.0), writes=[St[hh]])

    ecol = 127 if p == 0 else 0

    pSt2 = [DT("m_pS%d" % i) for i in range(2)]
    Ta3 = [st.sb("m_Ta3_%d" % i, [128, 128], F32) for i in range(3)]
    Tat3 = [DT("Ta3") for _ in range(3)]

    def make_unit(blk, bb, bs, hh, u, h3, u3):
        pA, pN, pC = ps[u3], ps[3 + u], ps[5 + u]
        pAt, pNt, pCt = pst[u3], pst[3 + u], pst[5 + u]
        pSv, pSt = pN[:, 384:512], pSt2[u]
        TaU, TatU = Ta3[u3], Tat3[u3]

        def fa1():
                k.op("act", lambda e: e.activation(out=TaU[:], in_=tri[:], func=AF.Identity, scale=A_[:, blk, hh:hh + 1]),
                     reads=[t_c, t_g], writes=[TatU])
                k.op("pe", lambda e: e.matmul(pA[:, 0:128], lhsT=C.ones_f[:], rhs=TaU[:], start=True, stop=True),
                     reads=[TatU, C.t_const], writes=[pAt])

        def fa2():
                k.op("dve", lambda e: e.tensor_tensor(out=tmp[u][:], in0=pA[:, 0:128], in1=madd[:], op=ALU.add),
                     reads=[pAt, t_c], writes=[tmpt[u]])
                k.op("act", lambda e: e.activation(out=Dm[u][:], in_=tmp[u][:], func=AF.Exp, bias=IMB[:, blk, hh:hh + 1], scale=1.0),
                     reads=[tmpt[u], t_g], writes=[Dmt[u]])
                k.op("act", lambda e: e.activation(out=eb[u][:], in_=pA[:, 0:128], func=AF.Exp), reads=[pAt], writes=[ebt[u]])
                k.op("dve", lambda e: e.tensor_tensor(out=Qp[u][:], in0=QT[bb][:, hh, :], in1=eb[u][:], op=ALU.mult),
                     reads=[t_blk[bb], ebt[u]], writes=[Qpt[u]])
                k.op("pe", lambda e: e.matmul(pSv, lhsT=KT[bb][:, hh, :], rhs=QT[bb][:, hh, :], start=True, stop=True),
                     reads=[t_blk[bb]], writes=[pSt])
                k.op("dve", lambda e: e.tensor_tensor(out=PT[u][:], in0=pSv, in1=Dm[u][:], op=ALU.mult),
                     reads=[pSt, Dmt[u]], writes=[PTt[u]])

        def fb():
                for dvh in range(2):
                    k.op("pe", lambda e: e.matmul(pN[:, dvh * 128:(dvh + 1) * 128], lhsT=VV[bb][:, hh * MDV + dvh * 128:hh * MDV + (dvh + 1) * 128],
                                                  rhs=PT[u][:], start=True, stop=False), reads=[t_blk[bb], PTt[u]], writes=[pNt], inc=False)
                    k.op("pe", lambda e: e.matmul(pN[:, dvh * 128:(dvh + 1) * 128], lhsT=Cb[:, hh, dvh * 128:(dvh + 1) * 128],
                                                  rhs=Qp[u][:], start=False, stop=True), reads=[St[hh], Qpt[u]], writes=[pNt], inc=False)
                k.op("pe", lambda e: e.matmul(pN[:, 256:384], lhsT=C.ones_b[:], rhs=PT[u][:], start=True, stop=False),
                     reads=[C.t_const, PTt[u]], writes=[pNt], inc=False)
                k.op("pe", lambda e: e.matmul(pN[:, 256:384], lhsT=Nb[:, hh, :], rhs=Qp[u][:], start=False, stop=True),
                     reads=[St[hh], Qpt[u]], writes=[pNt], inc=True)
                k.op("act", lambda e: e.activation(out=rd[u][:], in_=pN[:, 256:384], func=AF.Abs), reads=[pNt], writes=[rdt[u]])
                k.op("dve", lambda e: e.tensor_scalar(out=rd[u][:], in0=rd[u][:], scalar1=1.0, scalar2=None, op0=ALU.max),
                     reads=[rdt[u]], writes=[rdt[u]])
                k.op("dve", lambda e: e.reciprocal(out=rd[u][:], in_=rd[u][:]), reads=[rdt[u]], writes=[rdt[u]])
                if p == 1:
                    for dvh in range(2):
                        k.dma("sp", hi[h3][:, dvh, :], h_in[2 * hh + dvh, :, bs], writes=[hit[h3]])
                for dvh in range(2):
                    k.op("dve", lambda e: e.tensor_tensor(out=hb[h3][:, dvh, :], in0=pN[:, dvh * 128:(dvh + 1) * 128], in1=rd[u][:],
                                                          op=ALU.mult), reads=[pNt, rdt[u]], writes=[hbt[h3]])
                if p == 0:
                    for dvh in range(2):
                        k.dma("sp", h_out[2 * hh + dvh, :, bs], hb[h3][:, dvh, :], reads=[hbt[h3]], final=True)
                else:
                    k.op("dve", lambda e: e.tensor_tensor(out=hb[h3][:], in0=hb[h3][:], in1=hi[h3][:], op=ALU.add),
                         reads=[hit[h3]], writes=[hbt[h3]])
                    for dvh in range(2):
                        k.dma("sp", h_out[2 * hh + dvh, :, bs], hb[h3][:, dvh, :], reads=[hbt[h3]], final=True)

        def fc1():
                k.op("dve", lambda e: e.tensor_tensor(out=wc[u][:, 0:1], in0=IMB[:, blk, hh:hh + 1], in1=pA[:, ecol:ecol + 1],
                                                      op=ALU.add), reads=[pAt, t_g], writes=[wct[u]])
                k.op("act", lambda e: e.activation(out=wc[u][:, 0:1], in_=wc[u][:, 0:1], func=AF.Exp), reads=[wct[u]], writes=[wct[u]])
                k.op("act", lambda e: e.activation(out=wc[u][:, 1:2], in_=pA[:, ecol:ecol + 1], func=AF.Exp), reads=[pAt, wct[u]],
                     writes=[wct[u]])
                k.op("act", lambda e: e.activation(out=K2[u][:], in_=KK[bb][:, hh * 128:(hh + 1) * 128], func=AF.Identity,
                                                   scale=wc[u][:, 0:1]), reads=[t_blk[bb], wct[u]], writes=[K2t[u]])

        def fc2():
                k.op("pe", lambda e: e.matmul(pC[:, 0:256], lhsT=K2[u][:], rhs=VV[bb][:, hh * MDV:(hh + 1) * MDV], start=True, stop=True),
                     reads=[K2t[u], t_blk[bb]], writes=[pCt], inc=False)
                k.op("pe", lambda e: e.matmul(pC[:, 256:384], lhsT=K2[u][:], rhs=C.ones_b[:], start=True, stop=True),
                     reads=[K2t[u], C.t_const], writes=[pCt], inc=True)
                k.op("dve", lambda e: e.scalar_tensor_tensor(out=Cf[:, hh, :], in0=Cf[:, hh, :], scalar=wc[u][:, 1:2], in1=pC[:, 0:256],
                                                             op0=ALU.mult, op1=ALU.add), reads=[pCt, wct[u], St[hh]], writes=[St[hh]])
                k.op("act", lambda e: e.copy(out=Cb[:, hh, :], in_=Cf[:, hh, :]), reads=[St[hh]], writes=[St[hh]])
                k.op("dve", lambda e: e.scalar_tensor_tensor(out=Nf[:, hh, :], in0=Nf[:, hh, :], scalar=wc[u][:, 1:2], in1=pC[:, 256:384],
                                                             op0=ALU.mult, op1=ALU.add), reads=[pCt, wct[u], St[hh]], writes=[St[hh]])
                k.op("act", lambda e: e.copy(out=Nb[:, hh, :], in_=Nf[:, hh, :]), reads=[St[hh]], writes=[St[hh]])

        return fa1, fa2, fb, fc1, fc2

    if p == 0:
        order = [("z",)] + [("b", b) for b in range(NBLK)]
    else:
        order = [("z",), ("b", 1), ("b", 0), ("load",)] + [("b", b) for b in range(NBLK - 1, 1, -1)]
    n = 0
    nh = 0
    nblk = 0
    pipe = []
    LAG = [0, 1, 2, 2, 3]

    def advance(flush=False):
        nonlocal pipe
        newest = len(pipe) - 1
        for idx, stages in enumerate(pipe):
            age = newest - idx
            done = 5 - len(stages)
            while stages and (flush or LAG[done] <= age):
                stages.pop(0)()
                done += 1
        pipe = [st_ for st_ in pipe if st_]

    for it in order:
        if it[0] in ("z", "load"):
            advance(flush=True)
        if it[0] == "z":
            zero_state()
            continue
        if it[0] == "load":
            for hh in range(MH):
                k.dma("sp", Cf[:, hh, :], state_in[0][hh], writes=[St[hh]])
                k.dma("sp", Nf[:, hh, :], state_in[1][hh], writes=[St[hh]])
                k.op("act", lambda e: e.copy(out=Cb[:, hh, :], in_=Cf[:, hh, :]), reads=[St[hh]], writes=[St[hh]])
                k.op("act", lambda e: e.copy(out=Nb[:, hh, :], in_=Nf[:, hh, :]), reads=[St[hh]], writes=[St[hh]])
            continue
        blk = it[1]
        bs = slice(blk * 128, (blk + 1) * 128)
        bb = nblk % NB_
        nblk += 1
        k.dma("sp", QT[bb][:], qT_d[:, :, bs].rearrange("h p t -> p h t"), writes=[t_blk[bb]])
        k.dma("sp", KT[bb][:], kT_d[:, :, bs].rearrange("h p t -> p h t"), writes=[t_blk[bb]])
        k.dma("sp", KK[bb][:], ktok_d[bs, :], writes=[t_blk[bb]])
        k.dma("sp", VV[bb][:], vtok_d[bs, :], writes=[t_blk[bb]])
        for hh in range(MH):
            fa1, fa2, fb, fc1, fc2 = make_unit(blk, bb, bs, hh, n % NS, nh % 3, n % 3)
            n += 1
            nh += 1
            pipe.append([fa1, fa2, fb, fc1, fc2])
            advance()
    advance(flush=True)
    if p == 0:
        for hh in range(MH):
            k.dma("sp", state_out[0][hh], Cf[:, hh, :], reads=[St[hh]], final=True)
            k.dma("sp", state_out[1][hh], Nf[:, hh, :], reads=[St[hh]], final=True)


def emit_mlstm_readout(k, C, st, hs_d, so_d, ng_dram, wout_dram, ctx_out):
    X = C.X
    HN = st.sb("m_hn", [128, NCH, TOK], BF16)
    HNT = [DT("hn%d" % i) for i in range(3)]
    gn = st.sb("m_gn", [128, NCH], F32)
    t_c = DT("m_rc")
    k.dma("sp", gn[:], ng_dram, writes=[t_c])
    ps = [st.ps("rps%d" % i, [128, 512], F32) for i in range(4)]
    pst = [DT("rps%d" % i) for i in range(4)]
    sq = [st.sb("m_sq%d" % i, [128, 512], F32) for i in range(2)]
    sqt = [DT("sq") for _ in range(2)]
    rs = [st.sb("m_rs%d" % i, [128, 512], F32) for i in range(2)]
    rst = [DT("rs") for _ in range(2)]
    so = [st.sb("m_so%d" % i, [128, 2, 512], F32) for i in range(2)]
    sot = [DT("so%d" % i) for i in range(2)]
    hh_ = [st.sb("m_hh%d" % i, [128, 2, 512], F32) for i in range(2)]
    hht = [DT("hh%d" % i) for i in range(2)]
    n = 0
    for hh in range(MH):
        for bi, (t0, tl, v) in enumerate(TBS):
            if bi == 0 and not ctx_out:
                continue
            ts = slice(t0, t0 + tl)
            u = n % 2
            n += 1
            for dvh in range(2):
                k.dma("sp", so[u][:, dvh, :tl], so_d[2 * hh + dvh, :, ts], writes=[sot[u]])
                k.dma("sp", hh_[u][:, dvh, :tl], hs_d[2 * hh + dvh, :, ts], writes=[hht[u]])
            for dvh in range(2):
                q = (2 * n + dvh) % 2
                k.op("act", lambda e: e.activation(out=sq[q][:, :tl], in_=hh_[u][:, dvh, :tl], func=AF.Square),
                     reads=[hht[u]], writes=[sqt[q]])
                k.op("pe", lambda e: e.matmul(ps[u][:, :tl], lhsT=C.ones_f[:], rhs=sq[q][:, :tl], start=(dvh == 0), stop=(dvh == 1)),
                     reads=[sqt[q], C.t_const], writes=[pst[u]])
            k.op("act", lambda e: e.activation(out=rs[u][:, :tl], in_=ps[u][:, :tl], func=AF.Sqrt, bias=C.eps_col[:, 0:1],
                                               scale=1.0 / MDV), reads=[pst[u], C.t_const], writes=[rst[u]])
            k.op("dve", lambda e: e.reciprocal(out=rs[u][:, :tl], in_=rs[u][:, :tl]), reads=[rst[u]], writes=[rst[u]])
            for dvh in range(2):
                c = 2 * hh + dvh
                k.op("dve", lambda e: e.scalar_tensor_tensor(out=so[u][:, dvh, :tl], in0=so[u][:, dvh, :tl], scalar=gn[:, c:c + 1],
                                                             in1=rs[u][:, :tl], op0=ALU.mult, op1=ALU.mult),
                     reads=[sot[u], rst[u], t_c], writes=[sot[u]])
                k.op("dve", lambda e: e.tensor_tensor(out=HN[:, c, ts], in0=so[u][:, dvh, :tl], in1=hh_[u][:, dvh, :tl], op=ALU.mult),
                     reads=[sot[u], hht[u]], writes=[HNT[bi]])
    emit_outproj(k, C, st, HN, HNT, wout_dram, ctx_out, ps[2:4], pst[2:4])


def emit_outproj(k, C, st, IN, INT, w_dram, ctx_out, ps, pst):
    X = C.X
    NW = 2
    wo = [st.sb("op_w%d" % i, [128, NCH, 256], BF16) for i in range(NW)]
    wot = [DT("op_w%d" % i) for i in range(NW)]
    w_v = w_dram.rearrange("(c p) n -> p c n", p=128)
    ny = 0
    for dq in range(D // 256):
        b = dq % NW
        k.dma("pool", wo[b][:], w_v[:, :, dq * 256:(dq + 1) * 256], writes=[wot[b]])
        for dl in range(2):
            dc = dq * 2 + dl
            for bi, (t0, tl, v) in enumerate(TBS):
                if bi == 0 and not ctx_out:
                    continue
                ts = slice(t0, t0 + tl)
                p = ny % 2
                ny += 1
                for c in range(NCH):
                    k.op("pe", lambda e: e.matmul(ps[p][:, :tl], lhsT=wo[b][:, c, dl * 128:(dl + 1) * 128], rhs=IN[:, c, ts],
                                                  start=(c == 0), stop=(c == NCH - 1)),
                         reads=[wot[b], INT[bi]], writes=[pst[p]], inc=(c == NCH - 1))
                k.op("dve", lambda e: e.scalar_tensor_tensor(out=X[:, dc, ts], in0=ps[p][:, :tl], scalar=C.G[:, v, 1, dc:dc + 1],
                                                             in1=X[:, dc, ts], op0=ALU.mult, op1=ALU.add),
                     reads=[pst[p], C.t_mv, C.XT[bi]], writes=[C.XT[bi]])


def mlstm_consts():
    kk = np.arange(128)[:, None]
    tt = np.arange(128)[None, :]
    tri = np.stack([(kk <= tt), (kk >= tt)]).astype(np.float32)
    madd = ((tri - 1.0) * 1.0e4).astype(np.float32)
    return tri, madd


S5P = 64
S5T = TOK


def _eslice(seg, a, tl):
    t0, n, rev = seg
    if not rev:
        return slice(a - t0, a - t0 + tl)
    hi = n - 1 - (a - t0)
    lo = hi - tl + 1
    return slice(hi, lo - 1 if lo > 0 else None, -1)


def emit_s5_phase(k, C, p, lam_dram, bd_dram, cT_dram, dsk_dram, state_in, state_out, y_in, y_out, Z, ZT):
    segs = [(0, TOK, False)] if p == 0 else [(0, NCTX, True), (NCTX, NLAT, True)]
    with Stage(k) as st:
        ps = [st.ps("s5ps%d" % i, [128, 512], F32) for i in range(8)]
        pst = [DT("s5ps%d" % i) for i in range(8)]
        rstd = st.sb("s_rstd", [128, TOK], F32)
        t_rs = [DT("s_rs%d" % i) for i in range(3)]
        with Stage(k) as stn:
            emit_rstd(k, C, stn, rstd, t_rs, ps[6:8], pst[6:8])
        Hd = [st.sb("s_hd%d" % i, [128, TOK], BF16) for i in range(2)]
        HdT = [DT("s_hd%d" % i) for i in range(2)]
        htmp = [st.sb("s_htmp%d" % i, [128, 512], F32) for i in range(2)]
        htmpt = [DT("s_htmp%d" % i) for i in range(2)]
        LM = st.sb("s_lam", [128, 3, S5P], F32)
        t_p = DT("s5prep")
        k.dma("sp", LM[:], lam_dram[p], writes=[t_p])
        W = st.sb("s_w", [128, 16, S5P], F32)
        LR, LI, DTT, MAG, TH, CS, SN, AR1, AI, DEN, KR, KI, T1, T2, NSN, T3 = [W[:, i, :] for i in range(16)]

        def dv(fn):
            k.op("dve", fn, reads=[t_p, C.t_const], writes=[t_p])

        def ac(fn):
            k.op("act", fn, reads=[t_p, C.t_const], writes=[t_p])
        dv(lambda e: e.tensor_scalar(out=LR, in0=LM[:, 0, :], scalar1=-1e-4, scalar2=None, op0=ALU.min))
        ac(lambda e: e.activation(out=DTT, in_=LM[:, 2, :], func=AF.Exp))
        dv(lambda e: e.tensor_tensor(out=T1, in0=LR, in1=DTT, op=ALU.mult))
        ac(lambda e: e.activation(out=MAG, in_=T1, func=AF.Exp))
        dv(lambda e: e.tensor_tensor(out=TH, in0=LM[:, 1, :], in1=DTT, op=ALU.mult))
        ac(lambda e: e.activation(out=SN, in_=TH, func=AF.Sin, scale=1.0 / 8))
        ac(lambda e: e.activation(out=T1, in_=TH, func=AF.Sin, scale=1.0 / 16))
        dv(lambda e: e.tensor_tensor(out=T1, in0=T1, in1=T1, op=ALU.mult))
        dv(lambda e: e.tensor_scalar(out=CS, in0=T1, scalar1=-2.0, scalar2=1.0, op0=ALU.mult, op1=ALU.add))
        for _ in range(3):
            dv(lambda e: e.tensor_tensor(out=T1, in0=CS, in1=CS, op=ALU.mult))
            dv(lambda e: e.tensor_tensor(out=T2, in0=SN, in1=SN, op=ALU.mult))
            dv(lambda e: e.tensor_tensor(out=T3, in0=CS, in1=SN, op=ALU.mult))
            dv(lambda e: e.tensor_tensor(out=CS, in0=T1, in1=T2, op=ALU.subtract))
            dv(lambda e: e.tensor_scalar(out=SN, in0=T3, scalar1=2.0, scalar2=None, op0=ALU.mult))
        dv(lambda e: e.tensor_tensor(out=AR1, in0=MAG, in1=CS, op=ALU.mult))
        dv(lambda e: e.tensor_scalar(out=AR1, in0=AR1, scalar1=-1.0, scalar2=None, op0=ALU.add))
        dv(lambda e: e.tensor_tensor(out=AI, in0=MAG, in1=SN, op=ALU.mult))
        dv(lambda e: e.tensor_tensor(out=T1, in0=LR, in1=LR, op=ALU.mult))
        dv(lambda e: e.tensor_tensor(out=T2, in0=LM[:, 1, :], in1=LM[:, 1, :], op=ALU.mult))
        dv(lambda e: e.tensor_tensor(out=DEN, in0=T1, in1=T2, op=ALU.add))
        dv(lambda e: e.reciprocal(out=DEN, in_=DEN))
        dv(lambda e: e.tensor_tensor(out=T1, in0=AR1, in1=LR, op=ALU.mult))
        dv(lambda e: e.tensor_tensor(out=T2, in0=AI, in1=LM[:, 1, :], op=ALU.mult))
        dv(lambda e: e.tensor_tensor(out=T1, in0=T1, in1=T2, op=ALU.add))
        dv(lambda e: e.tensor_tensor(out=KR, in0=T1, in1=DEN, op=ALU.mult))
        dv(lambda e: e.tensor_tensor(out=T1, in0=AI, in1=LR, op=ALU.mult))
        dv(lambda e: e.tensor_tensor(out=T2, in0=AR1, in1=LM[:, 1, :], op=ALU.mult))
        dv(lambda e: e.tensor_tensor(out=T1, in0=T1, in1=T2, op=ALU.subtract))
        dv(lambda e: e.tensor_tensor(out=KI, in0=T1, in1=DEN, op=ALU.mult))
        dv(lambda e: e.tensor_scalar(out=NSN, in0=SN, scalar1=-1.0, scalar2=None, op0=ALU.mult))
        NLV = 11
        UP = st.sb("s_up", [128, NLV, 3, S5P], F32)
        dv(lambda e: e.tensor_copy(out=UP[:, 0, 0, :], in_=CS))
        dv(lambda e: e.tensor_copy(out=UP[:, 0, 1, :], in_=SN))
        dv(lambda e: e.tensor_copy(out=UP[:, 0, 2, :], in_=NSN))
        for l in range(1, NLV):
            dv(lambda e: e.tensor_tensor(out=T1, in0=UP[:, l - 1, 0, :], in1=UP[:, l - 1, 0, :], op=ALU.mult))
            dv(lambda e: e.tensor_tensor(out=T2, in0=UP[:, l - 1, 1, :], in1=UP[:, l - 1, 1, :], op=ALU.mult))
            dv(lambda e: e.tensor_tensor(out=UP[:, l, 0, :], in0=T1, in1=T2, op=ALU.subtract))
            dv(lambda e: e.tensor_tensor(out=T3, in0=UP[:, l - 1, 0, :], in1=UP[:, l - 1, 1, :], op=ALU.mult))
            dv(lambda e: e.tensor_scalar(out=UP[:, l, 1, :], in0=T3, scalar1=2.0, scalar2=None, op0=ALU.mult))
            dv(lambda e: e.tensor_scalar(out=UP[:, l, 2, :], in0=T3, scalar1=-2.0, scalar2=None, op0=ALU.mult))
        CR = st.sb("s_cr", [128, S5P, 16], F32)
        CI = st.sb("s_ci", [128, S5P, 16], F32)
        with Stage(k) as stc:
            CT = stc.sb("s_cT", [128, 2, S5P, 16], F32)
            k.dma("sp", CT[:], cT_dram[p], writes=[t_p])
            CW = stc.sb("s_cw", [128, S5P, 16], F32)
            krb = W[:, 10:11, :].rearrange("p o s -> p s o").to_broadcast([128, S5P, 16])
            kib = W[:, 11:12, :].rearrange("p o s -> p s o").to_broadcast([128, S5P, 16])
            dv(lambda e: e.tensor_tensor(out=CR[:], in0=CT[:, 0, :, :], in1=krb, op=ALU.mult))
            dv(lambda e: e.tensor_tensor(out=CW[:], in0=CT[:, 1, :, :], in1=kib, op=ALU.mult))
            dv(lambda e: e.tensor_tensor(out=CR[:], in0=CR[:], in1=CW[:], op=ALU.subtract))
            dv(lambda e: e.tensor_tensor(out=CI[:], in0=CT[:, 0, :, :], in1=kib, op=ALU.mult))
            dv(lambda e: e.tensor_tensor(out=CW[:], in0=CT[:, 1, :, :], in1=krb, op=ALU.mult))
            dv(lambda e: e.tensor_tensor(out=CI[:], in0=CI[:], in1=CW[:], op=ALU.add))
            dv(lambda e: e.tensor_scalar(out=CI[:], in0=CI[:], scalar1=-1.0, scalar2=None, op0=ALU.mult))
        dsk = st.sb("s_dsk", [128, NCH], F32)
        k.dma("sp", dsk[:], dsk_dram, writes=[t_p])
        SI = None
        if p == 1:
            SI = st.sb("s_si", [128, S5P, 2], F32)
            k.dma("sp", SI[:], state_in, writes=[t_p])
        SO = st.sb("s_so", [128, S5P, 2], F32)
        t_so = DT("s_so")
        E2 = [st.sb("s_E%d" % i, [128, 2, TOK], F32) for i in range(2)]
        t_E2 = [DT("s_E%d" % i) for i in range(2)]
        MB2 = [st.sb("s_magb%d" % i, [128, TOK], F32) for i in range(1)] * 2
        t_mb2 = [DT("s_magb%d" % i) for i in range(1)] * 2
        BR = [st.sb("s_br%d" % i, [128, TOK], F32) for i in range(2)]
        BI = [st.sb("s_bi%d" % i, [128, TOK], F32) for i in range(2)]
        t_b = [DT("s_b%d" % i) for i in range(2)]
        XR = [st.sb("s_xr%d" % i, [128, TOK], BF16) for i in range(1)] * 2
        XI = [st.sb("s_xi%d" % i, [128, TOK], BF16) for i in range(1)] * 2
        t_x = [DT("s_x%d" % i) for i in range(1)] * 2
        sc = [st.sb("s_sc%d" % i, [128, 512], F32) for i in range(4)]
        sct = [DT("s_sc%d" % i) for i in range(4)]
        tsc = [st.sb("s_tsc%d" % i, [128, 640], F32) for i in range(2)] * 2
        tsct = [DT("s_tsc%d" % i) for i in range(2)] * 2
        ini = st.sb("s_ini", [128, 4], F32)
        t_ini = DT("s_ini")
        BD = [st.sb("s_bd%d" % i, [128, 8, 128], BF16) for i in range(2)]
        BDt = [DT("s_bd%d" % i) for i in range(2)]
        CD = [st.sb("s_cd%d" % i, [128, 8, 128], BF16) for i in range(2)]
        CDt = [DT("s_cd%d" % i) for i in range(2)]
        yb = [st.sb("s_yb%d" % i, [128, 512], F32) for i in range(2)]
        ybt = [DT("s_yb%d" % i) for i in range(2)]
        yi = [st.sb("s_yi%d" % i, [128, 512], F32) for i in range(2)]
        yit = [DT("s_yi%d" % i) for i in range(2)]
        zb = [st.sb("s_zb%d" % i, [128, 512], BF16) for i in range(2)]
        zbt = [DT("s_zb%d" % i) for i in range(2)]

        def table_gen(tn):
            nonlocal nsc
            E, t_E, MB_, t_mb = E2[tn % 2], t_E2[tn % 2], MB2[tn % 2], t_mb2[tn % 2]
            P_ = tn
            k.op("pool", lambda e: e.memset(E[:, 0, 0:1], 1.0), writes=[t_E])
            k.op("pool", lambda e: e.memset(E[:, 1, 0:1], 0.0), writes=[t_E])
            seg = 1
            l = 0
            emax = TOK if p == 0 else NLAT
            while seg < emax:
                n_ = min(seg, emax - seg)
                s4 = nsc % 4
                nsc += 1
                ur, ui, nui = UP[:, l, 0, P_:P_ + 1], UP[:, l, 1, P_:P_ + 1], UP[:, l, 2, P_:P_ + 1]
                k.op("act", lambda e: e.activation(out=tsc[s4][:, :n_], in_=E[:, 0, 0:n_], func=AF.Identity, scale=ur),
                     reads=[t_E, t_p], writes=[tsct[s4]])
                k.op("dve", lambda e: e.scalar_tensor_tensor(out=E[:, 0, seg:seg + n_], in0=E[:, 1, 0:n_], scalar=nui, in1=tsc[s4][:, :n_],
                                                             op0=ALU.mult, op1=ALU.add), reads=[t_E, t_p, tsct[s4]], writes=[t_E])
                s4 = nsc % 4
                nsc += 1
                k.op("act", lambda e: e.activation(out=tsc[s4][:, :n_], in_=E[:, 1, 0:n_], func=AF.Identity, scale=ur),
                     reads=[t_E, t_p], writes=[tsct[s4]])
                k.op("dve", lambda e: e.scalar_tensor_tensor(out=E[:, 1, seg:seg + n_], in0=E[:, 0, 0:n_], scalar=ui, in1=tsc[s4][:, :n_],
                                                             op0=ALU.mult, op1=ALU.add), reads=[t_E, t_p, tsct[s4]], writes=[t_E])
                seg *= 2
                l += 1
                yield

        nt = 0
        nsc = 0
        nyb = 0
        ntile = 0
        npu = 0
        for dc in range(NCH):
            db = dc % 2
            emit_hchunk(k, C, 1, dc, rstd, t_rs, Hd[db], HdT[db], htmp, htmpt)
            k.dma("pool", BD[db][:], bd_dram[p, dc], writes=[BDt[db]])
            k.op("pool", lambda e: e.memset(CD[db][:], 0.0), writes=[CDt[db]])
            for j in range(4):
                P_ = 4 * dc + j
                for gp in range(2):
                    rows = slice(gp * 64, (gp + 1) * 64)
                    cols = slice((2 * j + gp) * 16, (2 * j + gp + 1) * 16)
                    k.op("act", lambda e: e.copy(out=CD[db][rows, j, cols], in_=CR[rows, P_, :]), reads=[t_p], writes=[CDt[db]])
                    k.op("act", lambda e: e.copy(out=CD[db][rows, 4 + j, cols], in_=CI[rows, P_, :]), reads=[t_p], writes=[CDt[db]])
            ypst = pst[4:7]
            for j in range(4):
                P_ = 4 * dc + j
                u = nt % 2
                nt += 1
                E, t_E, MB_, t_mb = E2[ntile % 2], t_E2[ntile % 2], MB2[ntile % 2], t_mb2[ntile % 2]
                if ntile == 0:
                    for _ in table_gen(0):
                        pass
                gen = table_gen(ntile + 1) if ntile + 1 < 4 * NCH else iter(())

                def step():
                    next(gen, None)
                ntile += 1
                k.op("act", lambda e: e.activation(out=MB_[:], in_=E[:, 0, :], func=AF.Identity, scale=0.0, bias=MAG[:, P_:P_ + 1]),
                     reads=[t_E, t_p], writes=[t_mb])
                for bi, (t0, tl, v) in enumerate(TBS):
                    ts = slice(t0, t0 + tl)
                    sg = segs[0] if (p == 0 or bi == 0) else segs[1]
                    es_ = _eslice(sg, t0, tl)
                    pu = (npu % 2) * 2
                    npu += 1
                    k.op("pe", lambda e: e.matmul(ps[pu][:, :tl], lhsT=BD[db][:, j, :], rhs=Hd[db][:, ts], start=True, stop=True),
                         reads=[BDt[db], HdT[db]], writes=[pst[pu]])
                    k.op("pe", lambda e: e.matmul(ps[pu + 1][:, :tl], lhsT=BD[db][:, 4 + j, :], rhs=Hd[db][:, ts], start=True, stop=True),
                         reads=[BDt[db], HdT[db]], writes=[pst[pu + 1]])
                    a4, b4 = nsc % 4, (nsc + 1) % 4
                    nsc += 2
                    k.op("dve", lambda e: e.tensor_tensor(out=sc[a4][:, :tl], in0=ps[pu][:, :tl], in1=E[:, 0, es_], op=ALU.mult),
                         reads=[pst[pu], t_E], writes=[sct[a4]])
                    k.op("dve", lambda e: e.tensor_tensor(out=sc[b4][:, :tl], in0=ps[pu + 1][:, :tl], in1=E[:, 1, es_], op=ALU.mult),
                         reads=[pst[pu + 1], t_E], writes=[sct[b4]])
                    k.op("dve", lambda e: e.tensor_tensor(out=BR[u][:, ts], in0=sc[a4][:, :tl], in1=sc[b4][:, :tl], op=ALU.add),
                         reads=[sct[a4], sct[b4]], writes=[t_b[u]])
                    step()
                    a4, b4 = nsc % 4, (nsc + 1) % 4
                    nsc += 2
                    k.op("dve", lambda e: e.tensor_tensor(out=sc[a4][:, :tl], in0=ps[pu + 1][:, :tl], in1=E[:, 0, es_], op=ALU.mult),
                         reads=[pst[pu + 1], t_E], writes=[sct[a4]])
                    k.op("dve", lambda e: e.tensor_tensor(out=sc[b4][:, :tl], in0=ps[pu][:, :tl], in1=E[:, 1, es_], op=ALU.mult),
                         reads=[pst[pu], t_E], writes=[sct[b4]])
                    k.op("dve", lambda e: e.tensor_tensor(out=BI[u][:, ts], in0=sc[a4][:, :tl], in1=sc[b4][:, :tl], op=ALU.subtract),
                         reads=[sct[a4], sct[b4]], writes=[t_b[u]])
                    step()
                for si, sg in enumerate(segs):
                    t0, n_, rev = sg
                    vs = slice(t0 + n_ - 1, t0 - 1 if t0 > 0 else None, -1) if rev else slice(t0, t0 + n_)
                    if p == 1 and si == 1:
                        x0r, x0i = SI[:, P_, 0:1], SI[:, P_, 1:2]
                        ur, ui, nui = UP[:, 0, 0, P_:P_ + 1], UP[:, 0, 1, P_:P_ + 1], UP[:, 0, 2, P_:P_ + 1]
                        k.op("dve", lambda e: e.tensor_tensor(out=ini[:, 2:3], in0=x0r, in1=ur, op=ALU.mult), reads=[t_p], writes=[t_ini])
                        k.op("dve", lambda e: e.scalar_tensor_tensor(out=ini[:, 0:1], in0=x0i, scalar=nui, in1=ini[:, 2:3], op0=ALU.mult, op1=ALU.add),
                             reads=[t_p, t_ini], writes=[t_ini])
                        k.op("dve", lambda e: e.tensor_tensor(out=ini[:, 3:4], in0=x0i, in1=ur, op=ALU.mult), reads=[t_p, t_ini], writes=[t_ini])
                        k.op("dve", lambda e: e.scalar_tensor_tensor(out=ini[:, 1:2], in0=x0r, scalar=ui, in1=ini[:, 3:4], op0=ALU.mult, op1=ALU.add),
                             reads=[t_p, t_ini], writes=[t_ini])
                        i_r, i_i = ini[:, 0:1], ini[:, 1:2]
                    else:
                        i_r, i_i = 0.0, 0.0
                    k.op("dve", lambda e: e.tensor_tensor_scan(out=BR[u][:, vs], data0=MB_[:, 0:n_], data1=BR[u][:, vs], initial=i_r,
                                                               op0=ALU.mult, op1=ALU.add), reads=[t_mb, t_ini], writes=[t_b[u]])
                    k.op("dve", lambda e: e.tensor_tensor_scan(out=BI[u][:, vs], data0=MB_[:, 0:n_], data1=BI[u][:, vs], initial=i_i,
                                                               op0=ALU.mult, op1=ALU.add), reads=[t_mb, t_ini], writes=[t_b[u]])
                    step()
                for bi, (t0, tl, v) in enumerate(TBS):
                    ts = slice(t0, t0 + tl)
                    sg = segs[0] if (p == 0 or bi == 0) else segs[1]
                    es_ = _eslice(sg, t0, tl)
                    a4, b4 = nsc % 4, (nsc + 1) % 4
                    nsc += 2
                    k.op("dve", lambda e: e.tensor_tensor(out=sc[a4][:, :tl], in0=BR[u][:, ts], in1=E[:, 0, es_], op=ALU.mult),
                         reads=[t_b[u], t_E], writes=[sct[a4]])
                    k.op("dve", lambda e: e.tensor_tensor(out=sc[b4][:, :tl], in0=BI[u][:, ts], in1=E[:, 1, es_], op=ALU.mult),
                         reads=[t_b[u], t_E], writes=[sct[b4]])
                    k.op("dve", lambda e: e.tensor_tensor(out=XR[u][:, ts], in0=sc[a4][:, :tl], in1=sc[b4][:, :tl], op=ALU.subtract),
                         reads=[sct[a4], sct[b4]], writes=[t_x[u]])
                    step()
                    if p == 0 and bi == 2:
                        k.op("act", lambda e: e.activation(out=ini[:, 2:3], in_=sc[a4][:, tl - 1:tl], func=AF.Identity,
                                                           bias=sc[b4][:, tl - 1:tl], scale=1.0), reads=[sct[a4], sct[b4]], writes=[t_ini])
                        k.op("dve", lambda e: e.tensor_scalar(out=SO[:, P_, 0:1], in0=sc[b4][:, tl - 1:tl], scalar1=-2.0, scalar2=ini[:, 2:3],
                                                              op0=ALU.mult, op1=ALU.add), reads=[sct[b4], t_ini], writes=[t_so])
                    a4, b4 = nsc % 4, (nsc + 1) % 4
                    nsc += 2
                    k.op("dve", lambda e: e.tensor_tensor(out=sc[a4][:, :tl], in0=BI[u][:, ts], in1=E[:, 0, es_], op=ALU.mult),
                         reads=[t_b[u], t_E], writes=[sct[a4]])
                    k.op("dve", lambda e: e.tensor_tensor(out=sc[b4][:, :tl], in0=BR[u][:, ts], in1=E[:, 1, es_], op=ALU.mult),
                         reads=[t_b[u], t_E], writes=[sct[b4]])
                    k.op("dve", lambda e: e.tensor_tensor(out=XI[u][:, ts], in0=sc[a4][:, :tl], in1=sc[b4][:, :tl], op=ALU.add),
                         reads=[sct[a4], sct[b4]], writes=[t_x[u]])
                    step()
                    if p == 0 and bi == 2:
                        k.op("dve", lambda e: e.tensor_tensor(out=SO[:, P_, 1:2], in0=sc[a4][:, tl - 1:tl], in1=sc[b4][:, tl - 1:tl], op=ALU.add),
                             reads=[sct[a4], sct[b4]], writes=[t_so])
                    yp = ps[4 + bi]
                    k.op("pe", lambda e: e.matmul(yp[:, :tl], lhsT=CD[db][:, j, :], rhs=XR[u][:, ts], start=(j == 0), stop=False),
                         reads=[CDt[db], t_x[u]], writes=[ypst[bi]], inc=False)
                    k.op("pe", lambda e: e.matmul(yp[:, :tl], lhsT=CD[db][:, 4 + j, :], rhs=XI[u][:, ts], start=False, stop=(j == 3)),
                         reads=[CDt[db], t_x[u]], writes=[ypst[bi]], inc=True)
                for _ in gen:
                    pass
            for bi, (t0, tl, v) in enumerate(TBS):
                ts = slice(t0, t0 + tl)
                q = nyb % 2
                nyb += 1
                if p == 0:
                    k.op("dve", lambda e: e.scalar_tensor_tensor(out=yb[q][:, :tl], in0=Hd[db][:, ts], scalar=dsk[:, dc:dc + 1], in1=ps[4 + bi][:, :tl],
                                                                 op0=ALU.mult, op1=ALU.add), reads=[HdT[db], t_p, ypst[bi]], writes=[ybt[q]])
                    k.dma("sp", y_out[dc, :, ts], yb[q][:, :tl], reads=[ybt[q]], final=True)
                else:
                    k.dma("sp", yi[q][:, :tl], y_in[dc, :, ts], writes=[yit[q]])
                    k.op("dve", lambda e: e.tensor_tensor(out=yb[q][:, :tl], in0=ps[4 + bi][:, :tl], in1=yi[q][:, :tl], op=ALU.add),
                         reads=[ypst[bi], yit[q]], writes=[ybt[q]])
                    k.op("dve", lambda e: e.tensor_tensor(out=yi[q][:, :tl], in0=yb[q][:, :tl], in1=yb[q][:, :tl], op=ALU.mult),
                         reads=[ybt[q]], writes=[yit[q]])
                    k.op("dve", lambda e: e.tensor_scalar(out=yi[q][:, :tl], in0=yi[q][:, :tl], scalar1=0.044715, scalar2=1.0, op0=ALU.mult, op1=ALU.add),
                         reads=[], writes=[yit[q]])
                    k.op("dve", lambda e: e.tensor_tensor(out=yi[q][:, :tl], in0=yi[q][:, :tl], in1=yb[q][:, :tl], op=ALU.mult),
                         reads=[ybt[q]], writes=[yit[q]])
                    k.op("act", lambda e: e.activation(out=yi[q][:, :tl], in_=yi[q][:, :tl], func=AF.Tanh, scale=0.7978845608028654),
                         reads=[], writes=[yit[q]])
                    k.op("act", lambda e: e.mul(out=yb[q][:, :tl], in_=yb[q][:, :tl], mul=0.5),
                         reads=[], writes=[ybt[q]])
                    k.op("dve", lambda e: e.scalar_tensor_tensor(out=zb[q][:, :tl], in0=yi[q][:, :tl], scalar=1.0, in1=yb[q][:, :tl],
                                                                 op0=ALU.add, op1=ALU.mult), reads=[yit[q], ybt[q]], writes=[zbt[q]])
                    k.dma("sp", Z[dc, :, ts], zb[q][:, :tl], reads=[zbt[q]], final=True)
        if p == 0:
            k.dma("sp", state_out, SO[:], reads=[t_so], final=True)


def emit_s5_glu(k, C, st, Z, ZT, wglu_dram):
    X = C.X
    NW = 3
    wt = [st.sb("g_w%d" % i, [128, NCH, 256], BF16) for i in range(NW)]
    wtt = [DT("g_w%d" % i) for i in range(NW)]
    sg = [st.sb("g_sg%d" % i, [128, 512], F32) for i in range(2)]
    sgt = [DT("g_sg%d" % i) for i in range(2)]
    ps = [st.ps("gps%d" % i, [128, 512], F32) for i in range(4)]
    pst = [DT("gps%d" % i) for i in range(4)]
    w_v = wglu_dram.rearrange("(c p) n -> p c n", p=128)
    nu = 0
    for dc in range(NCH):
        b = dc % NW
        k.dma("pool", wt[b][:, :, 0:128], w_v[:, :, dc * 128:(dc + 1) * 128], writes=[wtt[b]])
        k.dma("pool", wt[b][:, :, 128:256], w_v[:, :, D + dc * 128:D + (dc + 1) * 128], writes=[wtt[b]])
        for bi, (t0, tl, v) in enumerate(TBS):
            ts = slice(t0, t0 + tl)
            u = nu % 2
            nu += 1
            pa, pg = ps[2 * u], ps[2 * u + 1]
            for c in range(NCH):
                k.op("pe", lambda e: e.matmul(pa[:, :tl], lhsT=wt[b][:, c, 0:128], rhs=Z[:, c, ts], start=(c == 0), stop=(c == NCH - 1)),
                     reads=[wtt[b], ZT[bi]], writes=[pst[2 * u]], inc=(c == NCH - 1))
            for c in range(NCH):
                k.op("pe", lambda e: e.matmul(pg[:, :tl], lhsT=wt[b][:, c, 128:256], rhs=Z[:, c, ts], start=(c == 0), stop=(c == NCH - 1)),
                     reads=[wtt[b], ZT[bi]], writes=[pst[2 * u + 1]], inc=(c == NCH - 1))
            k.op("act", lambda e: e.activation(out=sg[u][:, :tl], in_=pg[:, :tl], func=AF.Sigmoid), reads=[pst[2 * u + 1]], writes=[sgt[u]])
            k.op("dve", lambda e: e.tensor_tensor(out=sg[u][:, :tl], in0=pa[:, :tl], in1=sg[u][:, :tl], op=ALU.mult),
                 reads=[pst[2 * u], sgt[u]], writes=[sgt[u]])
            k.op("dve", lambda e: e.scalar_tensor_tensor(out=X[:, dc, ts], in0=sg[u][:, :tl], scalar=C.G[:, v, 1, dc:dc + 1], in1=X[:, dc, ts],
                                                          op0=ALU.mult, op1=ALU.add), reads=[sgt[u], C.t_mv, C.XT[bi]], writes=[C.XT[bi]])


def s5_host_arrays(I, half):
    occ = 0
    dirs = [0, 1] if half == 0 else [1, 0]
    lam = np.zeros((2, 128, 3, S5P), np.float32)
    bd = np.zeros((2, NCH, 128, 8, 128), np.float32)
    cT = np.zeros((2, 128, 2, S5P, 16), np.float32)
    for s, dr in enumerate(dirs):
        for nm, idx in (("s5_lam_re", 0), ("s5_lam_im", 1)):
            a = I[nm][occ, dr].reshape(S5P, 2, 64)
            lam[s, :, idx, :] = a.transpose(1, 2, 0).reshape(128, S5P)
        ld = np.broadcast_to(I["s5_log_dt"][occ, dr].reshape(S5P, 2, 1), (S5P, 2, 64))
        lam[s, :, 2, :] = ld.transpose(1, 2, 0).reshape(128, S5P)
        for ri, nm in enumerate(("s5_b_re", "s5_b_im")):
            B = I[nm][occ, dr]
            for dc in range(NCH):
                for j in range(4):
                    for gp in range(2):
                        g = 8 * dc + 2 * j + gp
                        gl = 2 * j + gp
                        bd[s, dc, gl * 16:(gl + 1) * 16, ri * 4 + j, gp * 64:(gp + 1) * 64] = B[g].T
        for ri, nm in enumerate(("s5_c_re", "s5_c_im")):
            Cc = I[nm][occ, dr].reshape(S5P, 2, 16, 64)
            cT[s, :, ri, :, :] = Cc.transpose(1, 3, 0, 2).reshape(128, S5P, 16)
    return lam, bd, cT


NVC = 2
NACT = 4


class FProg:
    def __init__(self):
        self.nc = bass.Bass("TRN2", target_bir_lowering=False)
        self.ins = {}
        self.scr = {}
        self.outs = {}

    def inp(self, name, shape, dt=F32):
        if name in self.scr:
            return self.scr[name]
        if name not in self.ins:
            self.ins[name] = self.nc.dram_tensor(name, list(shape), dt, kind="ExternalInput").ap()
        return self.ins[name]

    def tmp(self, name, shape, dt=F32):
        if name not in self.scr:
            self.scr[name] = self.nc.dram_tensor(name, list(shape), dt).ap()
        return self.scr[name]

    def out(self, name, shape, dt=F32):
        if name not in self.outs:
            self.outs[name] = self.nc.dram_tensor(name, list(shape), dt, kind="ExternalOutput").ap()
        return self.outs[name]


SEGMENTS = [
    [("modvec", 0), ("ffn", 0, 0), ("mix1", 0)],
    [("mix2", 0), ("ffn", 0, 1), ("modvec", 1), ("ffn", 1, 0), ("mix1", 1)],
    [("mix2", 1), ("ffn", 1, 1), ("modvec", 2), ("ffn", 2, 0), ("mix1", 2)],
    [("mix2", 2), ("ffn", 2, 1), ("modvec", 3), ("ffn", 3, 0), ("mix1", 3)],
    [("mix2", 3), ("ffn", 3, 1)],
]


def set_layer(C, i):
    m = C.mvsets[i % 2]
    C.MV, C.MB, C.NG, C.A, C.G, C.t_mv = m


def emit_step(k, C, P, step, v):
    op = step[0]
    sf = "_v%d" % v
    so = "_v%d" % (1 - v)
    if op == "modvec":
        i = step[1]
        if v != 0:
            return
        set_layer(C, i)
        with Stage(k) as st:
            emit_modvec(k, C, st, P.inp("condT", [128, 2, NCH]), P.inp("modw%d" % i, [D, 9 * D]),
                        P.inp("modb%d" % i, [128, 9 * NCH]), P.inp("ng%d" % i, [128, 3, NCH]), P.inp("eye2", [2, 2]))
    elif op == "ffn":
        i, j = step[1], step[2]
        set_layer(C, i)
        emit_ffn(k, C, 0 if j == 0 else 2, P.inp("wi%d_%d" % (i, j), [D, 2 * DFF]),
                 P.inp("wo%d_%d" % (i, j), [DFF, D]), lat_only=(i == 3 and j == 1))
    elif op == "mix1":
        i = step[1]
        set_layer(C, i)
        kind = KINDS[i % 4]
        if kind in ("a", "w"):
            emit_qkv(k, C, P.inp("wqkv%d" % i, [D, QKV_W]), P.inp("qkg%d" % i, [128, 2]),
                     P.inp("cos" + sf, [128, NLAT]), P.inp("sin" + sf, [128, NLAT]), P.inp("rm", [128, 128]),
                     P.tmp("qT%d" % i + sf, [NH, 128, TOK], BF16), P.tmp("kT%d" % i + sf, [NKV, 128, TOK], BF16),
                     P.tmp("v%d" % i + sf, [TOK, 512], BF16))
        elif kind == "m":
            mq = P.tmp("mq%d" % i + sf, [MH, 128, TOK], BF16)
            mk_ = P.tmp("mk%d" % i + sf, [MH, 128, TOK], BF16)
            mkt = P.tmp("mkt%d" % i + sf, [TOK, MH * MDQK], BF16)
            mvt = P.tmp("mvt%d" % i + sf, [TOK, MH * MDV], BF16)
            mso = P.tmp("mso%d" % i + sf, [NCH, 128, TOK], F32)
            mgi = P.tmp("mgi%d" % i + sf, [TOK, 32], F32)
            emit_mlstm_proj(k, C, P.inp("mwin%d" % i, [D, 6144]), P.inp("mwg%d" % i + sf, [D, 32]),
                            P.inp("mbg%d" % i + sf, [128, 32]), mq, mk_, mkt, mvt, mso, mgi)
            with Stage(k) as st:
                emit_mlstm_scan(k, C, st, 0, mq, mk_, mkt, mvt, mgi, P.inp("mtri", [2, 128, 128]),
                                P.inp("mmadd", [2, 128, 128]), None,
                                (P.tmp("mstC%d" % i + sf, [MH, 128, MDV]), P.tmp("mstN%d" % i + sf, [MH, 128, 128])),
                                None, P.tmp("mh1_%d" % i + sf, [NCH, 128, TOK]))
        elif kind == "s":
            emit_s5_phase(k, C, 0, P.inp("s5lam" + sf, [2, 128, 3, S5P]), P.inp("s5bd" + sf, [2, NCH, 128, 8, 128]),
                          P.inp("s5cT" + sf, [2, 128, 2, S5P, 16]), P.inp("s5dsk", [128, NCH]), None,
                          P.tmp("s5st%d" % i + sf, [128, S5P, 2]), None, P.tmp("s5y1_%d" % i + sf, [NCH, 128, TOK]), None, None)
    elif op == "mix2":
        i = step[1]
        set_layer(C, i)
        kind = KINDS[i % 4]
        if kind in ("a", "w"):
            kts = [P.scr["kT%d_v%d" % (i, q)] for q in range(NVC)]
            vs = [P.scr["v%d_v%d" % (i, q)] for q in range(NVC)]
            other = (kts, vs) if kind == "a" else (kts[1 - v], vs[1 - v])
            emit_attn(k, C, kind == "w", i != 3, P.scr["qT%d" % i + sf], kts[v], vs[v], other,
                      P.inp("esink%d" % i, [128, NH]) if kind == "w" else None,
                      P.inp("masks", [128, 4, 128], BF16) if kind == "w" else None,
                      P.inp("awo%d" % i, [D, D]))
        elif kind == "m":
            hs = P.tmp("mhs%d" % i + sf, [NCH, 128, TOK])
            with Stage(k) as st:
                emit_mlstm_scan(k, C, st, 1, P.scr["mq%d" % i + sf], P.scr["mk%d" % i + sf], P.scr["mkt%d" % i + sf],
                                P.scr["mvt%d" % i + sf], P.scr["mgi%d" % i + sf],
                                P.inp("mtri", [2, 128, 128]), P.inp("mmadd", [2, 128, 128]),
                                (P.scr["mstC%d" % i + so], P.scr["mstN%d" % i + so]),
                                None, P.scr["mh1_%d" % i + sf], hs)
            with Stage(k) as st:
                emit_mlstm_readout(k, C, st, hs, P.scr["mso%d" % i + sf],
                                   P.inp("mng%d" % i, [128, NCH]), P.inp("mwout%d" % i, [D, D]), True)
        elif kind == "s":
            zd = P.tmp("s5z%d" % i + sf, [NCH, 128, TOK], BF16)
            emit_s5_phase(k, C, 1, P.inp("s5lam" + sf, [2, 128, 3, S5P]), P.inp("s5bd" + sf, [2, NCH, 128, 8, 128]),
                          P.inp("s5cT" + sf, [2, 128, 2, S5P, 16]), P.inp("s5dsk", [128, NCH]),
                          P.scr["s5st%d" % i + so], None, P.scr["s5y1_%d" % i + sf], None, zd, None)
            with Stage(k) as st:
                Z = st.sb("s_z", [128, NCH, TOK], BF16)
                ZT = [DT("z%d" % q) for q in range(3)]
                for q, (t0, tl, vv) in enumerate(TBS):
                    k.dma("sp", Z[:, :, t0:t0 + tl], zd[:, :, t0:t0 + tl].rearrange("c p t -> p c t"), writes=[ZT[q]])
                emit_s5_glu(k, C, st, Z, ZT, P.inp("s5wglu", [D, 2 * D]))
    else:
        raise ValueError(op)


def build_fused(segments=None, nvc=NVC):
    segments = SEGMENTS if segments is None else segments
    P = FProg()
    nc = P.nc
    with ExitStack() as es:
        k = KB(nc, es)
        C = Ctx()
        with Stage(k) as st0:
            C.X = st0.sb("X", [128, NCH, TOK], F32)
            C.XT = [DT("x%d" % i) for i in range(3)]
            C.mvsets = []
            for q in range(2):
                C.mvsets.append((st0.sb("MV%d" % q, [128, 2, 9 * NCH], F32), st0.sb("MB%d" % q, [128, 9 * NCH], F32),
                                 st0.sb("NG%d" % q, [128, 3, NCH], F32), st0.sb("A%d" % q, [128, 2, 3, NCH], F32),
                                 st0.sb("G%d" % q, [128, 2, 3, NCH], F32), DT("mv%d" % q)))
            setup_consts(k, st0, C)
            for si, seg in enumerate(segments):
                last = si == len(segments) - 1
                for v in range(nvc):
                    sf = "_v%d" % v
                    src = P.inp("xT" + sf, [D, TOK]) if si == 0 else P.scr["xpark" + sf]
                    load_x(k, C, src)
                    for step in seg:
                        emit_step(k, C, P, step, v)
                    if last:
                        store_x(k, C, P.out("xo" + sf, [D, NLAT]), True)
                    else:
                        store_x(k, C, P.tmp("xpark" + sf, [D, TOK]), False)
                    k.barrier()
            k.finish()
        P.ninstr = k.ninstr
    return P


def host_inputs(I, b, names):
    out = {}
    s5 = None
    for name in names:
        v = None
        base = name
        if len(name) > 3 and name[-3:-1] == "_v":
            v = int(name[-1])
            base = name[:-3]
        if base == "xT":
            out[name] = core_tokens_T(I["x"], I["ctx"], 2 * b + v)
        elif base == "condT":
            out[name] = vec_pm(np.stack([I["c"][b], I["c_ctx"]]))
        elif base == "rm":
            out[name] = rot_matrix()
        elif base == "eye2":
            out[name] = np.eye(2, dtype=np.float32)
        elif base == "masks":
            out[name] = win_masks()
        elif base == "cos":
            out[name] = rope_tables(v)[0]
        elif base == "sin":
            out[name] = rope_tables(v)[1]
        elif base == "mtri":
            out[name] = mlstm_consts()[0]
        elif base == "mmadd":
            out[name] = mlstm_consts()[1]
        elif base in ("s5lam", "s5bd", "s5cT"):
            if s5 is None:
                s5 = [s5_host_arrays(I, h) for h in range(2)]
            out[name] = s5[v][("s5lam", "s5bd", "s5cT").index(base)]
        elif base == "s5dsk":
            out[name] = vec_pm(I["s5_d"][0])
        elif base == "s5wglu":
            out[name] = I["s5_w_glu"][0]
        else:
            i = int(base[-3]) if base[-2] == "_" else int(base[-1])
            bb = base[:-3] if base[-2] == "_" else base[:-1]
            pre = "a_" if i % 4 == 0 else "w_"
            if bb == "modw":
                out[name] = I["mod_w"][i]
            elif bb == "modb":
                out[name] = vec_pm(I["mod_b"][i])
            elif bb == "ng":
                out[name] = vec_pm(I["norm_g"][i])
            elif bb == "wi":
                out[name] = I["ffn_wi"][i, int(base[-1])]
            elif bb == "wo":
                out[name] = I["ffn_wo"][i, int(base[-1])]
            elif bb == "wqkv":
                out[name] = I[pre + "wqkv"][0]
            elif bb == "qkg":
                out[name] = np.ascontiguousarray(I[pre + "qk_g"][0].T)
            elif bb == "awo":
                out[name] = I[pre + "wo"][0]
            elif bb == "esink":
                out[name] = np.ascontiguousarray(np.broadcast_to(I["w_sink"][0][None, :], (128, NH)))
            elif bb == "mwin":
                out[name] = I["m_w_in"][0]
            elif bb in ("mwg", "mbg"):
                perm = np.arange(32) if v == 0 else np.concatenate([np.arange(16, 32), np.arange(0, 16)])
                if bb == "mwg":
                    out[name] = np.ascontiguousarray(I["m_w_gate"][0][:, perm])
                else:
                    out[name] = np.ascontiguousarray(np.broadcast_to(I["m_b_gate"][0][perm][None, :], (128, 32)))
            elif bb == "mng":
                out[name] = vec_pm(I["m_norm_g"][0])
            elif bb == "mwout":
                out[name] = I["m_w_out"][0]
            else:
                raise KeyError(name)
    return out


def kernel(**inputs):
    I = {k_: np.asarray(v) for k_, v in inputs.items()}
    P = build_fused()
    names = list(P.ins)
    in_maps = [host_inputs(I, b, names) for b in range(NACT)]
    res = run_bass_kernel_spmd(P.nc, in_maps, core_ids=list(range(NACT)))
    B = I["x"].shape[0]
    out = np.empty((B, SEQ, D), np.float32)
    for b in range(NACT):
        for v in range(NVC):
            y = res.results[b]["xo_v%d" % v].T
            if v == 1:
                y = y[::-1]
            out[b, v * NLAT:(v + 1) * NLAT] = y
    return out
```

```python
import numpy as np
from contextlib import ExitStack
import concourse.bass as bass
import concourse.mybir as mybir
from concourse.bass_utils import run_bass_kernel_spmd

F32 = mybir.dt.float32
BF16 = mybir.dt.bfloat16
AF = mybir.ActivationFunctionType
ALU = mybir.AluOpType
AX = mybir.AxisListType

D = 2048
NCH = 16
DFF = 5632
NFC = 44
NCTX = 256
NLAT = 1024
TOK = NCTX + NLAT
SEQ = 2048
EPS = 1e-6
TBS = [(0, 256, 1), (256, 512, 0), (768, 512, 0)]
NCORES = 8


class DT:
    __slots__ = ("name", "w", "r", "dsem", "dcnt")

    def __init__(self, name=""):
        self.name = name
        self.w = {}
        self.r = {}
        self.dsem = None
        self.dcnt = 0


class KB:
    def __init__(self, nc, es):
        self.nc = nc
        self.es = es
        self.engs = {"pe": nc.tensor, "act": nc.scalar, "dve": nc.vector, "pool": nc.gpsimd, "sp": nc.sync}
        self.sem = {n: es.enter_context(nc.semaphore("s_" + n)) for n in ("pe", "act", "dve", "pool")}
        self.cnt = {n: 0 for n in self.sem}
        self.waited = {n: {} for n in self.engs}
        self.bound = []
        self.all_sems = []
        self.free_sems = []
        self.scount = {}
        self.final = []
        self.nsem = 0
        self.ninstr = 0
        self.uid = 0

    def _wait(self, e, deps):
        need = {}
        for d in deps:
            for s, v in d.items():
                if need.get(s, 0) < v:
                    need[s] = v
        w = self.waited[e]
        for s, v in need.items():
            if e == "pe" and s is self.sem["pe"]:
                continue
            if w.get(s, 0) >= v:
                continue
            self.engs[e].wait_ge(s, v)
            w[s] = v

    def op(self, e, fn, reads=(), writes=(), inc=True):
        deps = [t.w for t in reads]
        for t in writes:
            deps.append(t.w)
            deps.append(t.r)
        self._wait(e, deps)
        ins = fn(self.engs[e])
        self.ninstr += 1
        s = self.sem[e]
        if inc:
            self.cnt[e] += 1
            ins.then_inc(s, 1)
            v = self.cnt[e]
        else:
            v = self.cnt[e] + 1
        for t in reads:
            t.r[s] = v
        for t in writes:
            t.w[s] = v
            t.r = {}
        return ins

    def dma(self, q, out, in_, reads=(), writes=(), final=False):
        deps = [t.w for t in reads]
        for t in writes:
            deps.append(t.w)
            deps.append(t.r)
        self._wait(q, deps)
        t0 = (list(writes) + list(reads))[0]
        if t0.dsem is None:
            if self.free_sems:
                t0.dsem = self.free_sems.pop()
            else:
                t0.dsem = self.es.enter_context(self.nc.semaphore("d%d" % self.nsem))
                self.nsem += 1
                self.all_sems.append(t0.dsem)
                self.scount[t0.dsem] = 0
            self.bound.append(t0)
        self.scount[t0.dsem] += 16
        cnt = self.scount[t0.dsem]
        ins = self.engs[q].dma_start(out=out, in_=in_).then_inc(t0.dsem, 16)
        self.ninstr += 1
        for t in reads:
            t.r[t0.dsem] = cnt
        for t in writes:
            t.w[t0.dsem] = cnt
            t.r = {}
        if final:
            self.final.append({t0.dsem: cnt})
        return ins

    def barrier(self):
        allt = {self.sem[n]: self.cnt[n] for n in self.sem if self.cnt[n] > 0}
        for sm in self.all_sems:
            if self.scount[sm] > 0:
                allt[sm] = self.scount[sm]
        for e in self.engs:
            self._wait(e, [allt])
        for t in self.bound:
            t.dsem = None
        self.bound = []
        self.free_sems = list(self.all_sems)

    def finish(self):
        self._wait("sp", self.final)


class Stage:
    def __init__(self, k):
        self.k = k
        self.es = ExitStack()

    def __enter__(self):
        self.es.__enter__()
        return self

    def __exit__(self, *a):
        self.k.barrier()
        return self.es.__exit__(*a)

    def sb(self, name, shape, dt):
        self.k.uid += 1
        return self.es.enter_context(self.k.nc.sbuf_tensor("sb%d_%s" % (self.k.uid, name), list(shape), dt))

    def ps(self, name, shape, dt=F32):
        self.k.uid += 1
        return self.es.enter_context(self.k.nc.psum_tensor("ps%d_%s" % (self.k.uid, name), list(shape), dt))


class Ctx:
    pass


def setup_consts(k, st, C):
    nc = k.nc
    C.ones_f = st.sb("ones_f", [128, 128], F32)
    C.ones_b = st.sb("ones_b", [128, 128], BF16)
    C.t_const = DT("const")
    k.op("pool", lambda e: e.memset(C.ones_f[:], 1.0), writes=[C.t_const])
    k.op("pool", lambda e: e.memset(C.ones_b[:], 1.0), writes=[C.t_const])
    C.eps_col = st.sb("eps_col", [128, 2], F32)
    k.op("pool", lambda e: e.memset(C.eps_col[:], EPS), writes=[C.t_const])
    C.one_col = st.sb("one_col", [128, 2], F32)
    k.op("pool", lambda e: e.memset(C.one_col[:], 1.0), writes=[C.t_const])


def emit_modvec(k, C, st, condT_dram, modw_dram, modb_dram, ng_dram, i2_dram):
    C.t_mv = DT("mv")
    cond = st.sb("cond", [128, 2, NCH], F32)
    condb = st.sb("condb", [128, NCH, 2], BF16)
    t_cond = DT("cond")
    k.dma("sp", cond[:], condT_dram, writes=[t_cond])
    t_ng = DT("ng")
    k.dma("sp", C.NG[:], ng_dram, writes=[C.t_mv])
    k.dma("sp", C.MB[:], modb_dram, writes=[C.t_mv])
    t_condb = DT("condb")
    for v in range(2):
        k.op("act", lambda e: e.activation(out=condb[:, :, v], in_=cond[:, v, :], func=AF.Silu),
             reads=[t_cond], writes=[t_condb])
    NW = 3
    CW = 512
    wt = [st.sb("mw%d" % i, [128, NCH, CW], BF16) for i in range(NW)]
    wtt = [DT("mw%d" % i) for i in range(NW)]
    rows = st.sb("mvrows", [2, 9 * D], F32)
    t_rows = DT("mvrows")
    i2 = st.sb("mvi2", [2, 2], F32)
    k.dma("sp", i2[:], i2_dram, writes=[t_rows])
    pr = [st.ps("mvpr%d" % i, [128, 512], F32) for i in range(2)]
    prt = [DT("mvpr%d" % i) for i in range(2)]
    ps = st.ps("mvps", [128, 512], F32)
    pst = DT("mvps")
    modw_v = modw_dram.rearrange("(c p) n -> p c n", p=128)
    ntile = (9 * D) // CW
    for ti in range(ntile):
        b = ti % NW
        u = ti % 2
        k.dma("pool", wt[b][:], modw_v[:, :, ti * CW:(ti + 1) * CW], writes=[wtt[b]])
        for kc in range(NCH):
            k.op("pe", lambda e: e.matmul(pr[u][0:2, :], lhsT=condb[:, kc, :], rhs=wt[b][:, kc, :], start=(kc == 0),
                                          stop=(kc == NCH - 1)),
                 reads=[wtt[b], t_condb], writes=[prt[u]], inc=(kc == NCH - 1))
        k.op("act", lambda e: e.copy(out=rows[:, ti * CW:(ti + 1) * CW], in_=pr[u][0:2, :]), reads=[prt[u]], writes=[t_rows])
    for cc in range(9 * NCH):
        k.op("pe", lambda e: e.matmul(ps[:, cc * 2:cc * 2 + 2], lhsT=rows[:, cc * 128:(cc + 1) * 128], rhs=i2[:],
                                      start=True, stop=True), reads=[t_rows], writes=[pst], inc=(cc == 9 * NCH - 1))
    psv = ps[:, 0:288].rearrange("p (mc v) -> p v mc", v=2)
    for v in range(2):
        k.op("dve", lambda e: e.tensor_tensor(out=C.MV[:, v, :], in0=psv[:, v, :], in1=C.MB[:], op=ALU.add),
             reads=[pst, C.t_mv], writes=[C.t_mv])
    for v in range(2):
        for j in range(3):
            sc = C.MV[:, v, (3 * j + 1) * NCH:(3 * j + 2) * NCH]
            k.op("dve", lambda e: e.tensor_scalar(out=C.A[:, v, j, :], in0=sc, scalar1=1.0, scalar2=1.0,
                                                  op0=ALU.add, op1=ALU.mult), reads=[C.t_mv], writes=[C.t_mv])
            k.op("dve", lambda e: e.tensor_tensor(out=C.A[:, v, j, :], in0=C.A[:, v, j, :], in1=C.NG[:, j, :],
                                                  op=ALU.mult), reads=[C.t_mv], writes=[C.t_mv])
            gt = C.MV[:, v, (3 * j + 2) * NCH:(3 * j + 3) * NCH]
            k.op("dve", lambda e: e.tensor_scalar(out=C.G[:, v, j, :], in0=gt, scalar1=(1.0 if j == 1 else 0.5),
                                                  scalar2=None, op0=ALU.mult), reads=[C.t_mv], writes=[C.t_mv])


def emit_norm_mod(k, C, st, j, H, HT, pss, psst, tbs=None):
    X = C.X
    sq = [st.sb("sq%d_%d" % (j, i), [128, 512], F32) for i in range(2)]
    sqt = [DT("sq") for _ in range(2)]
    tmp = [st.sb("nt%d_%d" % (j, i), [128, 512], F32) for i in range(2)]
    tmpt = [DT("nt") for _ in range(2)]
    rstd = st.sb("rstd%d" % j, [128, TOK], F32)
    n = 0
    for bi, (t0, tl, v) in enumerate(TBS):
        if tbs is not None and bi not in tbs:
            continue
        ts = slice(t0, t0 + tl)
        pb = bi % 2
        for c in range(NCH):
            b = n % 2
            n += 1
            k.op("act", lambda e: e.activation(out=sq[b][:, :tl], in_=X[:, c, ts], func=AF.Square),
                 reads=[C.XT[bi]], writes=[sqt[b]])
            k.op("pe", lambda e: e.matmul(pss[pb][:, :tl], lhsT=C.ones_f[:], rhs=sq[b][:, :tl], start=(c == 0),
                                          stop=(c == NCH - 1)),
                 reads=[sqt[b], C.t_const], writes=[psst[pb]], inc=True)
        t_r = DT("rstd")
        k.op("act", lambda e: e.activation(out=rstd[:, ts], in_=pss[pb][:, :tl], func=AF.Sqrt,
                                           bias=C.eps_col[:, 0:1], scale=1.0 / D),
             reads=[psst[pb], C.t_const], writes=[t_r])
        k.op("dve", lambda e: e.reciprocal(out=rstd[:, ts], in_=rstd[:, ts]), reads=[t_r], writes=[t_r])
        for c in range(NCH):
            b = n % 2
            n += 1
            k.op("dve", lambda e: e.scalar_tensor_tensor(out=tmp[b][:, :tl], in0=X[:, c, ts],
                                                         scalar=C.A[:, v, j, c:c + 1], in1=rstd[:, ts],
                                                         op0=ALU.mult, op1=ALU.mult),
                 reads=[C.XT[bi], t_r, C.t_mv], writes=[tmpt[b]])
            k.op("act", lambda e: e.activation(out=H[:, c, ts], in_=tmp[b][:, :tl], func=AF.Identity,
                                               bias=C.MV[:, v, 3 * j * NCH + c:3 * j * NCH + c + 1], scale=1.0),
                 reads=[tmpt[b], C.t_mv], writes=[HT[bi]])


def emit_rstd(k, C, st, rstd, t_rs, pss, psst):
    X = C.X
    sq = [st.sb("rsq%d" % i, [128, 512], F32) for i in range(2)]
    sqt = [DT("rsq") for _ in range(2)]
    n = 0
    for bi, (t0, tl, v) in enumerate(TBS):
        ts = slice(t0, t0 + tl)
        pb = bi % 2
        for c in range(NCH):
            b = n % 2
            n += 1
            k.op("act", lambda e: e.activation(out=sq[b][:, :tl], in_=X[:, c, ts], func=AF.Square),
                 reads=[C.XT[bi]], writes=[sqt[b]])
            k.op("pe", lambda e: e.matmul(pss[pb][:, :tl], lhsT=C.ones_f[:], rhs=sq[b][:, :tl], start=(c == 0),
                                          stop=(c == NCH - 1)),
                 reads=[sqt[b], C.t_const], writes=[psst[pb]], inc=True)
        k.op("act", lambda e: e.activation(out=rstd[:, ts], in_=pss[pb][:, :tl], func=AF.Sqrt,
                                           bias=C.eps_col[:, 0:1], scale=1.0 / D),
             reads=[psst[pb], C.t_const], writes=[t_rs[bi]])
        k.op("dve", lambda e: e.reciprocal(out=rstd[:, ts], in_=rstd[:, ts]), reads=[t_rs[bi]], writes=[t_rs[bi]])


def emit_hchunk(k, C, j, c, rstd, t_rs, Hc, HcT, tmp, tmpt):
    X = C.X
    for bi, (t0, tl, v) in enumerate(TBS):
        ts = slice(t0, t0 + tl)
        b = bi % 2
        k.op("dve", lambda e: e.scalar_tensor_tensor(out=tmp[b][:, :tl], in0=X[:, c, ts],
                                                     scalar=C.A[:, v, j, c:c + 1], in1=rstd[:, ts],
                                                     op0=ALU.mult, op1=ALU.mult),
             reads=[C.XT[bi], t_rs[bi], C.t_mv], writes=[tmpt[b]])
        k.op("act", lambda e: e.activation(out=Hc[:, ts], in_=tmp[b][:, :tl], func=AF.Identity,
                                           bias=C.MV[:, v, 3 * j * NCH + c:3 * j * NCH + c + 1], scale=1.0),
             reads=[tmpt[b], C.t_mv], writes=[HcT])


def emit_ffn(k, C, j, wi_dram, wo_dram, lat_only=False):
    X = C.X
    NG_ = 4
    GC = NFC // NG_
    with Stage(k) as st:
        H = st.sb("ffn_h", [128, NCH, TOK], BF16)
        HT = [DT("h%d" % i) for i in range(3)]
        ps = [st.ps("fps%d" % i, [128, 512], F32) for i in range(8)]
        pst = [DT("fps%d" % i) for i in range(8)]
        tbs = [1, 2] if lat_only else [0, 1, 2]
        with Stage(k) as stn:
            emit_norm_mod(k, C, stn, j, H, HT, ps[6:8], pst[6:8], tbs=tbs)
        act = st.sb("ffn_act", [128, GC, TOK], BF16)
        actT = [DT("act%d" % i) for i in range(3)]
        NWI = 3
        wi = [st.sb("wi%d" % i, [128, NCH, 256], BF16) for i in range(NWI)]
        wit = [DT("wi%d" % i) for i in range(NWI)]
        NWO = 3
        WOC = 256
        wo = [st.sb("wo%d" % i, [128, GC, WOC], BF16) for i in range(NWO)]
        wot = [DT("wo%d" % i) for i in range(NWO)]
        sg = [st.sb("sg%d" % i, [128, 512], F32) for i in range(2)]
        sgt = [DT("sg") for _ in range(2)]
        wi_v = wi_dram.rearrange("(c p) n -> p c n", p=128)
        wo_v = wo_dram.rearrange("(f p) n -> p f n", p=128)
        nwi = 0
        nwo = 0
        nu = 0
        ny = 0
        for g in range(NG_):
            for fl in range(GC):
                f = g * GC + fl
                b = nwi % NWI
                nwi += 1
                k.dma("pool", wi[b][:, :, 0:128], wi_v[:, :, f * 128:(f + 1) * 128], writes=[wit[b]])
                k.dma("pool", wi[b][:, :, 128:256], wi_v[:, :, DFF + f * 128:DFF + (f + 1) * 128], writes=[wit[b]])
                for bi, (t0, tl, v) in enumerate(TBS):
                    if bi not in tbs:
                        continue
                    ts = slice(t0, t0 + tl)
                    pa = (nu % 3) * 2
                    pg = pa + 1
                    sb_ = nu % 2
                    nu += 1
                    for c in range(NCH):
                        k.op("pe", lambda e: e.matmul(ps[pa][:, :tl], lhsT=wi[b][:, c, 0:128], rhs=H[:, c, ts],
                                                      start=(c == 0), stop=(c == NCH - 1)),
                             reads=[wit[b], HT[bi]], writes=[pst[pa]], inc=(c == NCH - 1))
                    for c in range(NCH):
                        k.op("pe", lambda e: e.matmul(ps[pg][:, :tl], lhsT=wi[b][:, c, 128:256], rhs=H[:, c, ts],
                                                      start=(c == 0), stop=(c == NCH - 1)),
                             reads=[wit[b], HT[bi]], writes=[pst[pg]], inc=(c == NCH - 1))
                    k.op("act", lambda e: e.activation(out=sg[sb_][:, :tl], in_=ps[pg][:, :tl], func=AF.Silu),
                         reads=[pst[pg]], writes=[sgt[sb_]])
                    k.op("dve", lambda e: e.tensor_tensor(out=act[:, fl, ts], in0=ps[pa][:, :tl], in1=sg[sb_][:, :tl],
                                                          op=ALU.mult),
                         reads=[pst[pa], sgt[sb_]], writes=[actT[bi]])
            for dq in range(D // WOC):
                b = nwo % NWO
                nwo += 1
                k.dma("pool", wo[b][:], wo_v[:, g * GC:(g + 1) * GC, dq * WOC:(dq + 1) * WOC], writes=[wot[b]])
                for dl in range(WOC // 128):
                    dc = dq * (WOC // 128) + dl
                    for bi, (t0, tl, v) in enumerate(TBS):
                        if bi not in tbs:
                            continue
                        ts = slice(t0, t0 + tl)
                        p = ny % 6
                        ny += 1
                        for fl in range(GC):
                            k.op("pe", lambda e: e.matmul(ps[p][:, :tl], lhsT=wo[b][:, fl, dl * 128:(dl + 1) * 128],
                                                          rhs=act[:, fl, ts], start=(fl == 0), stop=(fl == GC - 1)),
                                 reads=[wot[b], actT[bi]], writes=[pst[p]], inc=(fl == GC - 1))
                        k.op("dve", lambda e: e.scalar_tensor_tensor(out=X[:, dc, ts], in0=ps[p][:, :tl],
                                                                     scalar=C.G[:, v, j, dc:dc + 1], in1=X[:, dc, ts],
                                                                     op0=ALU.mult, op1=ALU.add),
                             reads=[pst[p], C.t_mv, C.XT[bi]], writes=[C.XT[bi]])


def alloc_persistent(k, st, C):
    C.X = st.sb("X", [128, NCH, TOK], F32)
    C.XT = [DT("x%d" % i) for i in range(3)]
    C.MV = st.sb("MV", [128, 2, 9 * NCH], F32)
    C.MB = st.sb("MB", [128, 9 * NCH], F32)
    C.NG = st.sb("NG", [128, 3, NCH], F32)
    C.A = st.sb("A", [128, 2, 3, NCH], F32)
    C.G = st.sb("G", [128, 2, 3, NCH], F32)
    setup_consts(k, st, C)


def load_x(k, C, xT_dram):
    xv = xT_dram.rearrange("(c p) t -> p c t", p=128)
    for bi, (t0, tl, v) in enumerate(TBS):
        k.dma("sp", C.X[:, :, t0:t0 + tl], xv[:, :, t0:t0 + tl], writes=[C.XT[bi]])


def store_x(k, C, xo_dram, lat_only=False):
    xv = xo_dram.rearrange("(c p) t -> p c t", p=128)
    for bi, (t0, tl, v) in enumerate(TBS):
        if lat_only and bi == 0:
            continue
        o0 = t0 - (NCTX if lat_only else 0)
        k.dma("sp", xv[:, :, o0:o0 + tl], C.X[:, :, t0:t0 + tl], reads=[C.XT[bi]], final=True)


def build_test_ffn():
    nc = bass.Bass("TRN2", target_bir_lowering=False)
    xT = nc.dram_tensor("xT", [D, TOK], F32, kind="ExternalInput").ap()
    condT = nc.dram_tensor("condT", [128, 2, NCH], F32, kind="ExternalInput").ap()
    modw = nc.dram_tensor("modw", [D, 9 * D], F32, kind="ExternalInput").ap()
    modb = nc.dram_tensor("modb", [128, 9 * NCH], F32, kind="ExternalInput").ap()
    ng = nc.dram_tensor("ng", [128, 3, NCH], F32, kind="ExternalInput").ap()
    wi = nc.dram_tensor("wi", [D, 2 * DFF], F32, kind="ExternalInput").ap()
    wo = nc.dram_tensor("wo", [DFF, D], F32, kind="ExternalInput").ap()
    xo = nc.dram_tensor("xo", [D, TOK], F32, kind="ExternalOutput").ap()
    mvo = nc.dram_tensor("mvo", [128, 2 * 9 * NCH], F32, kind="ExternalOutput").ap()
    with ExitStack() as es:
        k = KB(nc, es)
        C = Ctx()
        with Stage(k) as st0:
            alloc_persistent(k, st0, C)
            load_x(k, C, xT)
            with Stage(k) as st:
                emit_modvec(k, C, st, condT, modw, modb, ng)
            k.dma("sp", mvo, C.MV[:].rearrange("p v m -> p (v m)"), reads=[C.t_mv], final=True)
            emit_ffn(k, C, 0, wi, wo)
            store_x(k, C, xo)
            k.finish()
        print("instructions:", k.ninstr, "dma sems:", k.nsem)
    return nc


def vec_pm(v):
    v = np.asarray(v)
    lead = v.shape[:-1]
    n = v.shape[-1] // 128
    a = v.reshape(lead + (n, 128))
    return np.ascontiguousarray(np.moveaxis(a, -1, 0))


def core_tokens_T(x, ctx, core):
    b, h = core // 2, core % 2
    cx, xl = ctx[b], x[b, h * NLAT:(h + 1) * NLAT]
    if h == 1:
        cx, xl = cx[::-1], xl[::-1]
    t = np.concatenate([cx, xl], axis=0)
    return np.ascontiguousarray(t.T)


NH = 16
NKV = 4
HD = 128
QKV_W = 3072
NKEY = NCTX + SEQ
NKB = NKEY // 128


def emit_qkv(k, C, wqkv_dram, qkg_dram, cos_dram, sin_dram, rm_dram, qT_d, kT_d, v_d):
    with Stage(k) as st:
        H = st.sb("qkv_h", [128, NCH, TOK], BF16)
        HT = [DT("h%d" % i) for i in range(3)]
        ps = [st.ps("qps%d" % i, [128, 512], F32) for i in range(8)]
        pst = [DT("qps%d" % i) for i in range(8)]
        with Stage(k) as stn:
            emit_norm_mod(k, C, stn, 1, H, HT, ps[6:8], pst[6:8])
        cs = st.sb("cs", [128, 2, NLAT], F32)
        rm = st.sb("rm", [128, 128], F32)
        g2 = st.sb("g2", [128, 2], F32)
        t_c = DT("qkvconst")
        k.dma("sp", cs[:, 0, :], cos_dram, writes=[t_c])
        k.dma("sp", cs[:, 1, :], sin_dram, writes=[t_c])
        k.dma("sp", rm[:], rm_dram, writes=[t_c])
        k.dma("sp", g2[:], qkg_dram, writes=[t_c])
        k.op("dve", lambda e: e.tensor_scalar(out=g2[:, 0:1], in0=g2[:, 0:1], scalar1=float(HD ** -0.5), scalar2=None,
                                              op0=ALU.mult), reads=[t_c], writes=[t_c])
        NW = 3
        wt = [st.sb("qw%d" % i, [128, NCH, 256], BF16) for i in range(NW)]
        wtt = [DT("qw%d" % i) for i in range(NW)]
        wv = st.sb("qwv", [128, NCH, 512], BF16)
        wvt = DT("qwv")
        w_v = wqkv_dram.rearrange("(c p) n -> p c n", p=128)
        k.dma("pool", wv[:], w_v[:, :, 2560:3072], writes=[wvt])
        sq = [st.sb("qsq%d" % i, [128, 512], F32) for i in range(2)]
        sqt = [DT("qsq") for _ in range(2)]
        rs = [st.sb("qrs%d" % i, [128, 512], F32) for i in range(2)]
        rst = [DT("qrs") for _ in range(2)]
        qn = [st.sb("qqn%d" % i, [128, 512], F32) for i in range(2)]
        qnt = [DT("qqn") for _ in range(2)]
        t1 = [st.sb("qt1%d" % i, [128, 512], F32) for i in range(2)]
        t1t = [DT("qt1") for _ in range(2)]
        ob = [st.sb("qob%d" % i, [128, 512], BF16) for i in range(3)]
        obt = [DT("qob%d" % i) for i in range(3)]
        def qunit(b, s, fc, isq, gcol, bi, t0, tl, u, o3):
            ts = slice(t0, t0 + tl)
            pq, pss, pr = ps[u], ps[2 + u], ps[4 + u]
            pqt, psst, prt = pst[u], pst[2 + u], pst[4 + u]

            def f1():
                for c in range(NCH):
                    k.op("pe", lambda e: e.matmul(pq[:, :tl], lhsT=wt[b][:, c, s * 128:(s + 1) * 128], rhs=H[:, c, ts],
                                                  start=(c == 0), stop=(c == NCH - 1)),
                         reads=[wtt[b], HT[bi]], writes=[pqt], inc=(c == NCH - 1))

            def f2():
                k.op("act", lambda e: e.activation(out=sq[u][:, :tl], in_=pq[:, :tl], func=AF.Square),
                     reads=[pqt], writes=[sqt[u]])
                k.op("pe", lambda e: e.matmul(pss[:, :tl], lhsT=C.ones_f[:], rhs=sq[u][:, :tl], start=True, stop=True),
                     reads=[sqt[u], C.t_const], writes=[psst])
                k.op("act", lambda e: e.activation(out=rs[u][:, :tl], in_=pss[:, :tl], func=AF.Sqrt,
                                                   bias=C.eps_col[:, 0:1], scale=1.0 / HD),
                     reads=[psst, C.t_const], writes=[rst[u]])
                k.op("dve", lambda e: e.reciprocal(out=rs[u][:, :tl], in_=rs[u][:, :tl]), reads=[rst[u]], writes=[rst[u]])
                k.op("dve", lambda e: e.scalar_tensor_tensor(out=qn[u][:, :tl], in0=pq[:, :tl], scalar=gcol,
                                                             in1=rs[u][:, :tl], op0=ALU.mult, op1=ALU.mult),
                     reads=[pqt, rst[u], t_c], writes=[qnt[u]])

            def f3():
                if bi == 0:
                    k.op("act", lambda e: e.copy(out=ob[o3][:, :tl], in_=qn[u][:, :tl]), reads=[qnt[u]], writes=[obt[o3]])
                else:
                    ls = slice(t0 - NCTX, t0 - NCTX + tl)
                    k.op("pe", lambda e: e.matmul(pr[:, :tl], lhsT=rm[:], rhs=qn[u][:, :tl], start=True, stop=True),
                         reads=[qnt[u], t_c], writes=[prt])
                    k.op("dve", lambda e: e.tensor_tensor(out=t1[u][:, :tl], in0=qn[u][:, :tl], in1=cs[:, 0, ls],
                                                           op=ALU.mult), reads=[qnt[u], t_c], writes=[t1t[u]])
                    k.op("dve", lambda e: e.tensor_tensor(out=qn[u][:, :tl], in0=pr[:, :tl], in1=cs[:, 1, ls],
                                                          op=ALU.mult), reads=[prt, t_c], writes=[qnt[u]])
                    k.op("dve", lambda e: e.tensor_tensor(out=ob[o3][:, :tl], in0=qn[u][:, :tl], in1=t1[u][:, :tl],
                                                          op=ALU.add), reads=[qnt[u], t1t[u]], writes=[obt[o3]])
                dst = qT_d[fc, :, ts] if isq else kT_d[fc - NH, :, ts]
                k.dma("sp", dst, ob[o3][:, :tl], reads=[obt[o3]], final=True)

            return f1, f2, f3

        pend2 = None
        pend3 = None
        nu = 0
        for ti in range(10):
            b = ti % NW
            k.dma("pool", wt[b][:], w_v[:, :, ti * 256:(ti + 1) * 256], writes=[wtt[b]])
            for s in range(2):
                fc = ti * 2 + s
                isq = fc < NH
                gcol = g2[:, 0:1] if isq else g2[:, 1:2]
                for bi, (t0, tl, v) in enumerate(TBS):
                    f1, f2, f3 = qunit(b, s, fc, isq, gcol, bi, t0, tl, nu % 2, nu % 3)
                    nu += 1
                    f1()
                    if pend3 is not None:
                        pend3()
                    pend3 = None
                    if pend2 is not None:
                        pend2[0]()
                        pend3 = pend2[1]
                    pend2 = (f2, f3)
        if pend3 is not None:
            pend3()
        if pend2 is not None:
            pend2[0]()
            pend2[1]()
        for tb in range(TOK // 128):
            u = tb % 2
            o3 = nu % 3
            nu += 1
            bi = 0 if tb < 2 else (1 if tb < 6 else 2)
            for c in range(NCH):
                k.op("pe", lambda e: e.matmul(ps[u][:, :], lhsT=H[:, c, tb * 128:(tb + 1) * 128], rhs=wv[:, c, :],
                                              start=(c == 0), stop=(c == NCH - 1)),
                     reads=[wvt, HT[bi]], writes=[pst[u]], inc=(c == NCH - 1))
            k.op("act", lambda e: e.copy(out=ob[o3][:, :], in_=ps[u][:, :]), reads=[pst[u]], writes=[obt[o3]])
            k.dma("sp", v_d[tb * 128:(tb + 1) * 128, :], ob[o3][:, :], reads=[obt[o3]], final=True)


NKEYW = NCTX + 128 + NLAT + 128


def emit_attn(k, C, window, ctx_out, qT_d, kT_all_d, v_all_d, kv_other, esink_dram, masks_dram, wo_dram):
    X = C.X
    nkey = NKEYW if window else NKEY
    nkb = nkey // 128
    with Stage(k) as st:
        KT = st.sb("KT", [128, NKV, nkey], BF16)
        V = st.sb("V", [128, nkb, 512], BF16)
        t_kv = DT("kv")
        kT_me, v_me, kT_x, v_x = kT_all_d, v_all_d, kv_other[0], kv_other[1]
        vr = lambda a: a.rearrange("(b p) f -> p b f", p=128)
        nlb = NLAT // 128
        if not window:
            for g in range(NKV):
                k.dma("sp", KT[:, g, 0:NCTX], kT_me[g][:, 0:NCTX], writes=[t_kv])
                k.dma("sp", KT[:, g, NCTX:TOK], kT_x[0][g][:, NCTX:TOK], writes=[t_kv])
                k.dma("sp", KT[:, g, TOK:NKEY], kT_x[1][g][:, NCTX:TOK], writes=[t_kv])
            k.dma("sp", V[:, 0:2, :], vr(v_me[0:NCTX, :]), writes=[t_kv])
            k.dma("sp", V[:, 2:2 + nlb, :], vr(v_x[0][NCTX:TOK, :]), writes=[t_kv])
            k.dma("sp", V[:, 2 + nlb:2 + 2 * nlb, :], vr(v_x[1][NCTX:TOK, :]), writes=[t_kv])
        else:
            k.op("pool", lambda e: e.memset(KT[:, :, NCTX:NCTX + 128], 0.0), writes=[t_kv])
            k.op("pool", lambda e: e.memset(V[:, 2, :], 0.0), writes=[t_kv])
            for g in range(NKV):
                k.dma("sp", KT[:, g, 0:NCTX], kT_me[g][:, 0:NCTX], writes=[t_kv])
                k.dma("sp", KT[:, g, NCTX + 128:NCTX + 128 + NLAT], kT_me[g][:, NCTX:TOK], writes=[t_kv])
                k.dma("sp", KT[:, g, NCTX + 128 + NLAT:NKEYW], kT_x[g][:, TOK - 128:TOK], writes=[t_kv])
            k.dma("sp", V[:, 0:2, :], vr(v_me[0:NCTX, :]), writes=[t_kv])
            k.dma("sp", V[:, 3:3 + nlb, :], vr(v_me[NCTX:TOK, :]), writes=[t_kv])
            k.dma("sp", V[:, 3 + nlb:4 + nlb, :], vr(v_x[TOK - 128:TOK, :]), writes=[t_kv])
        QT = [st.sb("QT%d" % i, [128, 4, TOK], BF16) for i in range(2)]
        QTt = [DT("QT%d" % i) for i in range(2)]
        OT = [st.sb("OT%d" % i, [128, 4, TOK], BF16) for i in range(2)]
        OTt = [DT("OT%d" % i) for i in range(2)]
        wo = [st.sb("awo%d" % i, [128, 4, D], BF16) for i in range(2)]
        wot = [DT("awo%d" % i) for i in range(2)]
        pt = [st.sb("pt%d" % i, [128, 512], BF16) for i in range(3)]
        ptt = [DT("pt%d" % i) for i in range(3)]
        rd = [st.sb("rd%d" % i, [128, 512], F32) for i in range(2)]
        rdt = [DT("rd%d" % i) for i in range(2)]
        ps = [st.ps("aps%d" % i, [128, 512], F32) for i in range(8)]
        pst = [DT("aps%d" % i) for i in range(8)]
        t_c = DT("attnconst")
        if window:
            es_ = st.sb("esink", [128, NH], F32)
            mk = st.sb("masks", [128, 4, 128], BF16)
            k.dma("sp", es_[:], esink_dram, writes=[t_c])
            k.dma("sp", mk[:], masks_dram, writes=[t_c])
            k.op("act", lambda e: e.activation(out=es_[:], in_=es_[:], func=AF.Exp), reads=[t_c], writes=[t_c])
        wo_v = wo_dram.rearrange("(h p) n -> p h n", p=128)
        ns = 0
        nunit = 0
        ny = 0
        for g in range(NKV):
            gb_ = g % 2
            for hl in range(4):
                k.dma("sp", QT[gb_][:, hl, :], qT_d[4 * g + hl], writes=[QTt[gb_]])
            k.dma("pool", wo[gb_][:], wo_v[:, 4 * g:4 * g + 4, :], writes=[wot[gb_]])
            for hl in range(4):
                h = 4 * g + hl
                units = []
                if not window:
                    for (t0, tl, v) in TBS[1:]:
                        units.append((t0, tl, [(kb, None) for kb in range(NKB)]))
                    if ctx_out:
                        units.append((0, NCTX, [(0, None), (1, None)]))
                else:
                    nqb = NLAT // 128
                    for qb in range(nqb):
                        kl = [(0, None), (1, None), (2 + qb, 2 if qb == 0 else 0), (3 + qb, None),
                              (4 + qb, 3 if qb == nqb - 1 else 1)]
                        units.append((NCTX + qb * 128, 128, kl))
                    if ctx_out:
                        units.append((0, NCTX, [(0, None), (1, None)]))
                for (t0, tl, kl) in units:
                    ts = slice(t0, t0 + tl)
                    u = nunit % 2
                    nunit += 1
                    po, pd = ps[2 + u], ps[4 + u]
                    pot, pdt = pst[2 + u], pst[4 + u]
                    def front(ki, kb, mi, s2, p3):
                        k.op("pe", lambda e: e.matmul(ps[s2][:, :tl], lhsT=KT[:, g, kb * 128:(kb + 1) * 128],
                                                      rhs=QT[gb_][:, hl, ts], start=True, stop=True),
                             reads=[t_kv, QTt[gb_]], writes=[pst[s2]])
                        k.op("act", lambda e: e.activation(out=pt[p3][:, :tl], in_=ps[s2][:, :tl], func=AF.Exp),
                             reads=[pst[s2]], writes=[ptt[p3]])
                        if mi is not None:
                            k.op("dve", lambda e: e.tensor_tensor(out=pt[p3][:, :tl], in0=pt[p3][:, :tl], in1=mk[:, mi, :tl],
                                                                  op=ALU.mult), reads=[ptt[p3], t_c], writes=[ptt[p3]])

                    def back(ki, kb, p3):
                        last = ki == len(kl) - 1
                        k.op("pe", lambda e: e.matmul(po[:, :tl], lhsT=V[:, kb, g * 128:(g + 1) * 128], rhs=pt[p3][:, :tl],
                                                      start=(ki == 0), stop=last),
                             reads=[t_kv, ptt[p3]], writes=[pot], inc=last)
                        k.op("pe", lambda e: e.matmul(pd[:, :tl], lhsT=C.ones_b[:], rhs=pt[p3][:, :tl],
                                                      start=(ki == 0), stop=last),
                             reads=[C.t_const, ptt[p3]], writes=[pdt], inc=last)
                    prev = None
                    for ki, (kb, mi) in enumerate(kl):
                        s2 = ns % 2
                        p3 = ns % 3
                        ns += 1
                        front(ki, kb, mi, s2, p3)
                        if prev is not None:
                            back(*prev)
                        prev = (ki, kb, p3)
                    back(*prev)
                    if window:
                        k.op("dve", lambda e: e.tensor_scalar(out=rd[u][:, :tl], in0=pd[:, :tl], scalar1=es_[:, h:h + 1],
                                                              scalar2=None, op0=ALU.add), reads=[pdt, t_c], writes=[rdt[u]])
                        k.op("dve", lambda e: e.reciprocal(out=rd[u][:, :tl], in_=rd[u][:, :tl]), reads=[rdt[u]], writes=[rdt[u]])
                    else:
                        k.op("dve", lambda e: e.reciprocal(out=rd[u][:, :tl], in_=pd[:, :tl]), reads=[pdt], writes=[rdt[u]])
                    k.op("dve", lambda e: e.tensor_tensor(out=OT[gb_][:, hl, ts], in0=po[:, :tl], in1=rd[u][:, :tl], op=ALU.mult),
                         reads=[pot, rdt[u]], writes=[OTt[gb_]])
            for dc in range(NCH):
                for bi, (t0, tl, v) in enumerate(TBS):
                    if bi == 0 and not ctx_out:
                        continue
                    ts = slice(t0, t0 + tl)
                    p = 6 + ny % 2
                    ny += 1
                    for hl in range(4):
                        k.op("pe", lambda e: e.matmul(ps[p][:, :tl], lhsT=wo[gb_][:, hl, dc * 128:(dc + 1) * 128],
                                                      rhs=OT[gb_][:, hl, ts], start=(hl == 0), stop=(hl == 3)),
                             reads=[wot[gb_], OTt[gb_]], writes=[pst[p]], inc=(hl == 3))
                    k.op("dve", lambda e: e.scalar_tensor_tensor(out=X[:, dc, ts], in0=ps[p][:, :tl],
                                                                 scalar=C.G[:, v, 1, dc:dc + 1], in1=X[:, dc, ts],
                                                                 op0=ALU.mult, op1=ALU.add),
                         reads=[pst[p], C.t_mv, C.XT[bi]], writes=[C.XT[bi]])


def rope_tables(half):
    t = np.arange(half * NLAT, (half + 1) * NLAT)
    if half == 1:
        t = t[::-1]
    row = (t // 64).astype(np.float32)
    col = (t % 64).astype(np.float32)
    inv = (10000.0 ** (-np.arange(32, dtype=np.float32) / 32)).astype(np.float32)
    ang = np.concatenate([row[:, None] * inv, col[:, None] * inv], axis=-1)
    ang = np.concatenate([ang, ang], axis=-1).astype(np.float32)
    return np.ascontiguousarray(np.cos(ang).T.astype(np.float32)), np.ascontiguousarray(np.sin(ang).T.astype(np.float32))


def rot_matrix():
    R = np.zeros((128, 128), np.float32)
    for m in range(64):
        R[m + 64, m] = -1.0
    for m in range(64, 128):
        R[m - 64, m] = 1.0
    return R


def win_masks(half=0):
    import ml_dtypes
    s = np.arange(128)[:, None]
    t = np.arange(128)[None, :]
    prev = (s >= t).astype(np.float32)
    nxt = (s <= t).astype(np.float32)
    z = np.zeros_like(prev)
    m = np.stack([prev, nxt, z, nxt[::-1]], axis=1)
    return np.ascontiguousarray(m).astype(ml_dtypes.bfloat16)


KINDS = ["a", "s", "m", "w"]


MH = 8
MDQK = 128
MDV = 256
NBLK = TOK // 128
LN_KSCALE = float(np.log(MDQK ** -0.5))


def emit_mlstm_proj(k, C, win_dram, wg_dram, bg_dram, qT_d, kT_d, ktok_d, vtok_d, so_d, gi_d):
    with Stage(k) as st:
        H = st.sb("m_h", [128, NCH, TOK], BF16)
        HT = [DT("h%d" % i) for i in range(3)]
        ps = [st.ps("mps%d" % i, [128, 512], F32) for i in range(8)]
        pst = [DT("mps%d" % i) for i in range(8)]
        with Stage(k) as stn:
            emit_norm_mod(k, C, stn, 1, H, HT, ps[6:8], pst[6:8])
        NW = 3
        wt = [st.sb("mw%d" % i, [128, NCH, 512], BF16) for i in range(NW)]
        wtt = [DT("mw%d" % i) for i in range(NW)]
        w_v = win_dram.rearrange("(c p) n -> p c n", p=128)
        ob = [st.sb("mob%d" % i, [128, 512], BF16) for i in range(3)]
        obt = [DT("mob%d" % i) for i in range(3)]
        of = [st.sb("mof%d" % i, [128, 512], F32) for i in range(2)]
        oft = [DT("mof%d" % i) for i in range(2)]
        nu = 0
        nw = 0
        for (c0, kind) in [(0, "q"), (512, "q"), (1024, "k"), (1536, "k"), (4096, "o"), (4608, "o"), (5120, "o"), (5632, "o")]:
            b = nw % NW
            nw += 1
            k.dma("pool", wt[b][:], w_v[:, :, c0:c0 + 512], writes=[wtt[b]])
            for s in range(4):
                fcol = c0 + s * 128
                for bi, (t0, tl, v) in enumerate(TBS):
                    ts = slice(t0, t0 + tl)
                    u = nu % 4
                    nu += 1
                    for c in range(NCH):
                        k.op("pe", lambda e: e.matmul(ps[u][:, :tl], lhsT=wt[b][:, c, s * 128:(s + 1) * 128], rhs=H[:, c, ts],
                                                      start=(c == 0), stop=(c == NCH - 1)),
                             reads=[wtt[b], HT[bi]], writes=[pst[u]], inc=(c == NCH - 1))
                    if kind == "o":
                        f2 = nu % 2
                        k.op("act", lambda e: e.activation(out=of[f2][:, :tl], in_=ps[u][:, :tl], func=AF.Sigmoid),
                             reads=[pst[u]], writes=[oft[f2]])
                        k.dma("sp", so_d[(fcol - 4096) // 128, :, ts], of[f2][:, :tl], reads=[oft[f2]], final=True)
                    else:
                        o3 = nu % 3
                        k.op("act", lambda e: e.copy(out=ob[o3][:, :tl], in_=ps[u][:, :tl]), reads=[pst[u]], writes=[obt[o3]])
                        dst = qT_d[fcol // 128, :, ts] if kind == "q" else kT_d[(fcol - 1024) // 128, :, ts]
                        k.dma("sp", dst, ob[o3][:, :tl], reads=[obt[o3]], final=True)
        wg = st.sb("m_wg", [128, NCH, 32], BF16)
        wgt = DT("m_wg")
        k.dma("pool", wg[:], wg_dram.rearrange("(c p) n -> p c n", p=128), writes=[wgt])
        bg = st.sb("m_bg", [128, 32], F32)
        k.dma("sp", bg[:], bg_dram, writes=[wgt])
        gt = [st.sb("m_gt%d" % i, [128, 32], F32) for i in range(2)]
        gtt = [DT("m_gt%d" % i) for i in range(2)]
        for (c0, kind) in [(1024, "k"), (1536, "k"), (2048, "v"), (2560, "v"), (3072, "v"), (3584, "v")]:
            b = nw % NW
            nw += 1
            k.dma("pool", wt[b][:], w_v[:, :, c0:c0 + 512], writes=[wtt[b]])
            for tb in range(NBLK):
                u = nu % 4
                o3 = nu % 3
                nu += 1
                bi = 0 if tb < 2 else (1 if tb < 6 else 2)
                for c in range(NCH):
                    k.op("pe", lambda e: e.matmul(ps[u][:, :], lhsT=H[:, c, tb * 128:(tb + 1) * 128], rhs=wt[b][:, c, :],
                                                  start=(c == 0), stop=(c == NCH - 1)),
                         reads=[wtt[b], HT[bi]], writes=[pst[u]], inc=(c == NCH - 1))
                k.op("act", lambda e: e.copy(out=ob[o3][:, :], in_=ps[u][:, :]), reads=[pst[u]], writes=[obt[o3]])
                dst = ktok_d[tb * 128:(tb + 1) * 128, c0 - 1024:c0 - 512] if kind == "k" else \
                    vtok_d[tb * 128:(tb + 1) * 128, c0 - 2048:c0 - 1536]
                k.dma("sp", dst, ob[o3][:, :], reads=[obt[o3]], final=True)
        for tb in range(NBLK):
            u = nu % 4
            f2 = nu % 2
            nu += 1
            bi = 0 if tb < 2 else (1 if tb < 6 else 2)
            for c in range(NCH):
                k.op("pe", lambda e: e.matmul(ps[u][:, 0:32], lhsT=H[:, c, tb * 128:(tb + 1) * 128], rhs=wg[:, c, :],
                                              start=(c == 0), stop=(c == NCH - 1)),
                     reads=[wgt, HT[bi]], writes=[pst[u]], inc=(c == NCH - 1))
            k.op("dve", lambda e: e.tensor_tensor(out=gt[f2][:], in0=ps[u][:, 0:32], in1=bg[:], op=ALU.add),
                 reads=[pst[u], wgt], writes=[gtt[f2]])
            k.op("act", lambda e: e.activation(out=gt[f2][:], in_=gt[f2][:], func=AF.Tanh, scale=1.0 / 15.0),
                 reads=[gtt[f2]], writes=[gtt[f2]])
            k.op("dve", lambda e: e.tensor_scalar(out=gt[f2][:], in0=gt[f2][:], scalar1=15.0, scalar2=None, op0=ALU.mult),
                 reads=[gtt[f2]], writes=[gtt[f2]])
            k.dma("sp", gi_d[tb * 128:(tb + 1) * 128, :], gt[f2][:], reads=[gtt[f2]], final=True)


def emit_mlstm_scan(k, C, st, phase, qT_d, kT_d, ktok_d, vtok_d, gi_d, tri_dram, madd_dram, state_in, state_out,
                    h_in, h_out):
    p = phase
    NB_ = 2
    QT = [st.sb("m_qT%d" % i, [128, MH, 128], BF16) for i in range(NB_)]
    KT = [st.sb("m_kT%d" % i, [128, MH, 128], BF16) for i in range(NB_)]
    KK = [st.sb("m_kk%d" % i, [128, MH * MDQK], BF16) for i in range(NB_)]
    VV = [st.sb("m_vv%d" % i, [128, MH * MDV], BF16) for i in range(NB_)]
    t_blk = [DT("m_blk%d" % i) for i in range(NB_)]
    GI = st.sb("m_gi", [128, NBLK, 32], F32)
    t_in = DT("m_in")
    k.dma("sp", GI[:], gi_d.rearrange("(b p) f -> p b f", p=128), writes=[t_in])
    tri = st.sb("m_tri", [128, 128], F32)
    madd = st.sb("m_madd", [128, 128], F32)
    t_c = DT("m_c")
    k.dma("sp", tri[:], tri_dram[p], writes=[t_c])
    k.dma("sp", madd[:], madd_dram[p], writes=[t_c])
    A_ = st.sb("m_a", [128, NBLK, MH], F32)
    IMB = st.sb("m_imb", [128, NBLK, MH], F32)
    t_g = DT("m_g")
    fsl = GI[:, :, p * 16 + 8:p * 16 + 16]
    isl = GI[:, :, p * 16:p * 16 + 8]
    k.op("act", lambda e: e.activation(out=A_[:], in_=fsl, func=AF.Exp, scale=-1.0), reads=[t_in], writes=[t_g])
    k.op("act", lambda e: e.activation(out=A_[:], in_=A_[:], func=AF.Ln, bias=C.one_col[:, 0:1], scale=1.0),
         reads=[t_g, C.t_const], writes=[t_g])
    k.op("dve", lambda e: e.tensor_scalar(out=A_[:], in0=A_[:], scalar1=-1.0, scalar2=None, op0=ALU.mult),
         reads=[t_g], writes=[t_g])
    ps = [st.ps("sps%d" % i, [128, 512], F32) for i in range(8)]
    pst = [DT("sps%d" % i) for i in range(8)]
    for blk in range(NBLK):
        u = blk % 2
        k.op("pe", lambda e: e.matmul(ps[u][:, 0:MH], lhsT=tri[:], rhs=A_[:, blk, :], start=True, stop=True),
             reads=[t_c, t_g], writes=[pst[u]])
        k.op("dve", lambda e: e.scalar_tensor_tensor(out=IMB[:, blk, :], in0=isl[:, blk, :], scalar=LN_KSCALE,
                                                     in1=ps[u][:, 0:MH], op0=ALU.add, op1=ALU.subtract),
             reads=[pst[u], t_in], writes=[t_g])
    Cf = st.sb("m_Cf", [128, MH, MDV], F32)
    Cb = st.sb("m_Cb", [128, MH, MDV], BF16)
    Nf = st.sb("m_Nf", [128, MH, 128], F32)
    Nb = st.sb("m_Nb", [128, MH, 128], BF16)
    St = [DT("m_st%d" % hh) for hh in range(MH)]
    NS = 2
    Ta = [st.sb("m_Ta%d" % i, [128, 128], F32) for i in range(NS)]
    Tat = [DT("Ta") for _ in range(NS)]
    tmp = [st.sb("m_tmp%d" % i, [128, 128], F32) for i in range(NS)]
    tmpt = [DT("tmp") for _ in range(NS)]
    Dm = [st.sb("m_Dm%d" % i, [128, 128], F32) for i in range(NS)]
    Dmt = [DT("Dm") for _ in range(NS)]
    eb = [st.sb("m_eb%d" % i, [128, 128], F32) for i in range(NS)]
    ebt = [DT("eb") for _ in range(NS)]
    Qp = [st.sb("m_Qp%d" % i, [128, 128], BF16) for i in range(NS)]
    Qpt = [DT("Qp") for _ in range(NS)]
    PT = [st.sb("m_PT%d" % i, [128, 128], BF16) for i in range(NS)]
    PTt = [DT("PT") for _ in range(NS)]
    rd = [st.sb("m_rd%d" % i, [128, 128], F32) for i in range(NS)]
    rdt = [DT("rd") for _ in range(NS)]
    wc = [st.sb("m_wc%d" % i, [128, 2], F32) for i in range(NS)]
    wct = [DT("wc") for _ in range(NS)]
    K2 = [st.sb("m_K2%d" % i, [128, 128], BF16) for i in range(NS)]
    K2t = [DT("K2") for _ in range(NS)]
    hb = [st.sb("m_hb%d" % i, [128, 2, 128], F32) for i in range(3)]
    hbt = [DT("hb%d" % i) for i in range(3)]
    hi = [st.sb("m_hi%d" % i, [128, 2, 128], F32) for i in range(3)]
    hit = [DT("hi%d" % i) for i in range(3)]

    def zero_state():
        for hh in range(MH):
            k.op("pool", lambda e: e.memset(Cf[:, hh, :], 0.0), writes=[St[hh]])
            k.op("pool", lambda e: e.memset(Cb[:, hh, :], 0.0), writes=[St[hh]])
            k.op("pool", lambda e: e.memset(Nf[:, hh, :], 0.0), writes=[St[hh]])
            k.op("pool", lambda e: e.memset(Nb[:, hh, :], 0.0), writes=[St[hh]])

    ecol = 127 if p == 0 else 0

    pSt2 = [DT("m_pS%d" % i) for i in range(2)]
    Ta3 = [st.sb("m_Ta3_%d" % i, [128, 128], F32) for i in range(3)]
    Tat3 = [DT("Ta3") for _ in range(3)]

    def make_unit(blk, bb, bs, hh, u, h3, u3):
        pA, pN, pC = ps[u3], ps[3 + u], ps[5 + u]
        pAt, pNt, pCt = pst[u3], pst[3 + u], pst[5 + u]
        pSv, pSt = pN[:, 384:512], pSt2[u]
        TaU, TatU = Ta3[u3], Tat3[u3]

        def fa1():
                k.op("act", lambda e: e.activation(out=TaU[:], in_=tri[:], func=AF.Identity, scale=A_[:, blk, hh:hh + 1]),
                     reads=[t_c, t_g], writes=[TatU])
                k.op("pe", lambda e: e.matmul(pA[:, 0:128], lhsT=C.ones_f[:], rhs=TaU[:], start=True, stop=True),
                     reads=[TatU, C.t_const], writes=[pAt])

        def fa2():
                k.op("dve", lambda e: e.tensor_tensor(out=tmp[u][:], in0=pA[:, 0:128], in1=madd[:], op=ALU.add),
                     reads=[pAt, t_c], writes=[tmpt[u]])
                k.op("act", lambda e: e.activation(out=Dm[u][:], in_=tmp[u][:], func=AF.Exp, bias=IMB[:, blk, hh:hh + 1], scale=1.0),
                     reads=[tmpt[u], t_g], writes=[Dmt[u]])
                k.op("act", lambda e: e.activation(out=eb[u][:], in_=pA[:, 0:128], func=AF.Exp), reads=[pAt], writes=[ebt[u]])
                k.op("dve", lambda e: e.tensor_tensor(out=Qp[u][:], in0=QT[bb][:, hh, :], in1=eb[u][:], op=ALU.mult),
                     reads=[t_blk[bb], ebt[u]], writes=[Qpt[u]])
                k.op("pe", lambda e: e.matmul(pSv, lhsT=KT[bb][:, hh, :], rhs=QT[bb][:, hh, :], start=True, stop=True),
                     reads=[t_blk[bb]], writes=[pSt])
                k.op("dve", lambda e: e.tensor_tensor(out=PT[u][:], in0=pSv, in1=Dm[u][:], op=ALU.mult),
                     reads=[pSt, Dmt[u]], writes=[PTt[u]])

        def fb():
                for dvh in range(2):
                    k.op("pe", lambda e: e.matmul(pN[:, dvh * 128:(dvh + 1) * 128], lhsT=VV[bb][:, hh * MDV + dvh * 128:hh * MDV + (dvh + 1) * 128],
                                                  rhs=PT[u][:], start=True, stop=False), reads=[t_blk[bb], PTt[u]], writes=[pNt], inc=False)
                    k.op("pe", lambda e: e.matmul(pN[:, dvh * 128:(dvh + 1) * 128], lhsT=Cb[:, hh, dvh * 128:(dvh + 1) * 128],
                                                  rhs=Qp[u][:], start=False, stop=True), reads=[St[hh], Qpt[u]], writes=[pNt], inc=False)
                k.op("pe", lambda e: e.matmul(pN[:, 256:384], lhsT=C.ones_b[:], rhs=PT[u][:], start=True, stop=False),
                     reads=[C.t_const, PTt[u]], writes=[pNt], inc=False)
                k.op("pe", lambda e: e.matmul(pN[:, 256:384], lhsT=Nb[:, hh, :], rhs=Qp[u][:], start=False, stop=True),
                     reads=[St[hh], Qpt[u]], writes=[pNt], inc=True)
                k.op("act", lambda e: e.activation(out=rd[u][:], in_=pN[:, 256:384], func=AF.Abs), reads=[pNt], writes=[rdt[u]])
                k.op("dve", lambda e: e.tensor_scalar(out=rd[u][:], in0=rd[u][:], scalar1=1.0, scalar2=None, op0=ALU.max),
                     reads=[rdt[u]], writes=[rdt[u]])
                k.op("dve", lambda e: e.reciprocal(out=rd[u][:], in_=rd[u][:]), reads=[rdt[u]], writes=[rdt[u]])
                if p == 1:
                    for dvh in range(2):
                        k.dma("sp", hi[h3][:, dvh, :], h_in[2 * hh + dvh, :, bs], writes=[hit[h3]])
                for dvh in range(2):
                    k.op("dve", lambda e: e.tensor_tensor(out=hb[h3][:, dvh, :], in0=pN[:, dvh * 128:(dvh + 1) * 128], in1=rd[u][:],
                                                          op=ALU.mult), reads=[pNt, rdt[u]], writes=[hbt[h3]])
                if p == 0:
                    for dvh in range(2):
                        k.dma("sp", h_out[2 * hh + dvh, :, bs], hb[h3][:, dvh, :], reads=[hbt[h3]], final=True)
                else:
                    k.op("dve", lambda e: e.tensor_tensor(out=hb[h3][:], in0=hb[h3][:], in1=hi[h3][:], op=ALU.add),
                         reads=[hit[h3]], writes=[hbt[h3]])
                    for dvh in range(2):
                        k.dma("sp", h_out[2 * hh + dvh, :, bs], hb[h3][:, dvh, :], reads=[hbt[h3]], final=True)

        def fc1():
                k.op("dve", lambda e: e.tensor_tensor(out=wc[u][:, 0:1], in0=IMB[:, blk, hh:hh + 1], in1=pA[:, ecol:ecol + 1],
                                                      op=ALU.add), reads=[pAt, t_g], writes=[wct[u]])
                k.op("act", lambda e: e.activation(out=wc[u][:, 0:1], in_=wc[u][:, 0:1], func=AF.Exp), reads=[wct[u]], writes=[wct[u]])
                k.op("act", lambda e: e.activation(out=wc[u][:, 1:2], in_=pA[:, ecol:ecol + 1], func=AF.Exp), reads=[pAt, wct[u]],
                     writes=[wct[u]])
                k.op("act", lambda e: e.activation(out=K2[u][:], in_=KK[bb][:, hh * 128:(hh + 1) * 128], func=AF.Identity,
                                                   scale=wc[u][:, 0:1]), reads=[t_blk[bb], wct[u]], writes=[K2t[u]])

        def fc2():
                k.op("pe", lambda e: e.matmul(pC[:, 0:256], lhsT=K2[u][:], rhs=VV[bb][:, hh * MDV:(hh + 1) * MDV], start=True, stop=True),
                     reads=[K2t[u], t_blk[bb]], writes=[pCt], inc=False)
                k.op("pe", lambda e: e.matmul(pC[:, 256:384], lhsT=K2[u][:], rhs=C.ones_b[:], start=True, stop=True),
                     reads=[K2t[u], C.t_const], writes=[pCt], inc=True)
                k.op("dve", lambda e: e.scalar_tensor_tensor(out=Cf[:, hh, :], in0=Cf[:, hh, :], scalar=wc[u][:, 1:2], in1=pC[:, 0:256],
                                                             op0=ALU.mult, op1=ALU.add), reads=[pCt, wct[u], St[hh]], writes=[St[hh]])
                k.op("act", lambda e: e.copy(out=Cb[:, hh, :], in_=Cf[:, hh, :]), reads=[St[hh]], writes=[St[hh]])
                k.op("dve", lambda e: e.scalar_tensor_tensor(out=Nf[:, hh, :], in0=Nf[:, hh, :], scalar=wc[u][:, 1:2], in1=pC[:, 256:384],
                                                             op0=ALU.mult, op1=ALU.add), reads=[pCt, wct[u], St[hh]], writes=[St[hh]])
                k.op("act", lambda e: e.copy(out=Nb[:, hh, :], in_=Nf[:, hh, :]), reads=[St[hh]], writes=[St[hh]])

        return fa1, fa2, fb, fc1, fc2

    if p == 0:
        order = [("z",)] + [("b", b) for b in range(NBLK)]
    else:
        order = [("z",), ("b", 1), ("b", 0), ("load",)] + [("b", b) for b in range(NBLK - 1, 1, -1)]
    n = 0
    nh = 0
    nblk = 0
    pipe = []
    LAG = [0, 1, 2, 2, 3]

    def advance(flush=False):
        nonlocal pipe
        newest = len(pipe) - 1
        for idx, stages in enumerate(pipe):
            age = newest - idx
            done = 5 - len(stages)
            while stages and (flush or LAG[done] <= age):
                stages.pop(0)()
                done += 1
        pipe = [st_ for st_ in pipe if st_]

    for it in order:
        if it[0] in ("z", "load"):
            advance(flush=True)
        if it[0] == "z":
            zero_state()
            continue
        if it[0] == "load":
            for hh in range(MH):
                k.dma("sp", Cf[:, hh, :], state_in[0][hh], writes=[St[hh]])
                k.dma("sp", Nf[:, hh, :], state_in[1][hh], writes=[St[hh]])
                k.op("act", lambda e: e.copy(out=Cb[:, hh, :], in_=Cf[:, hh, :]), reads=[St[hh]], writes=[St[hh]])
                k.op("act", lambda e: e.copy(out=Nb[:, hh, :], in_=Nf[:, hh, :]), reads=[St[hh]], writes=[St[hh]])
            continue
        blk = it[1]
        bs = slice(blk * 128, (blk + 1) * 128)
        bb = nblk % NB_
        nblk += 1
        k.dma("sp", QT[bb][:], qT_d[:, :, bs].rearrange("h p t -> p h t"), writes=[t_blk[bb]])
        k.dma("sp", KT[bb][:], kT_d[:, :, bs].rearrange("h p t -> p h t"), writes=[t_blk[bb]])
        k.dma("sp", KK[bb][:], ktok_d[bs, :], writes=[t_blk[bb]])
        k.dma("sp", VV[bb][:], vtok_d[bs, :], writes=[t_blk[bb]])
        for hh in range(MH):
            fa1, fa2, fb, fc1, fc2 = make_unit(blk, bb, bs, hh, n % NS, nh % 3, n % 3)
            n += 1
            nh += 1
            pipe.append([fa1, fa2, fb, fc1, fc2])
            advance()
    advance(flush=True)
    if p == 0:
        for hh in range(MH):
            k.dma("sp", state_out[0][hh], Cf[:, hh, :], reads=[St[hh]], final=True)
            k.dma("sp", state_out[1][hh], Nf[:, hh, :], reads=[St[hh]], final=True)


def emit_mlstm_readout(k, C, st, hs_d, so_d, ng_dram, wout_dram, ctx_out):
    X = C.X
    HN = st.sb("m_hn", [128, NCH, TOK], BF16)
    HNT = [DT("hn%d" % i) for i in range(3)]
    gn = st.sb("m_gn", [128, NCH], F32)
    t_c = DT("m_rc")
    k.dma("sp", gn[:], ng_dram, writes=[t_c])
    ps = [st.ps("rps%d" % i, [128, 512], F32) for i in range(4)]
    pst = [DT("rps%d" % i) for i in range(4)]
    sq = [st.sb("m_sq%d" % i, [128, 512], F32) for i in range(2)]
    sqt = [DT("sq") for _ in range(2)]
    rs = [st.sb("m_rs%d" % i, [128, 512], F32) for i in range(2)]
    rst = [DT("rs") for _ in range(2)]
    so = [st.sb("m_so%d" % i, [128, 2, 512], F32) for i in range(2)]
    sot = [DT("so%d" % i) for i in range(2)]
    hh_ = [st.sb("m_hh%d" % i, [128, 2, 512], F32) for i in range(2)]
    hht = [DT("hh%d" % i) for i in range(2)]
    n = 0
    for hh in range(MH):
        for bi, (t0, tl, v) in enumerate(TBS):
            if bi == 0 and not ctx_out:
                continue
            ts = slice(t0, t0 + tl)
            u = n % 2
            n += 1
            for dvh in range(2):
                k.dma("sp", so[u][:, dvh, :tl], so_d[2 * hh + dvh, :, ts], writes=[sot[u]])
                k.dma("sp", hh_[u][:, dvh, :tl], hs_d[2 * hh + dvh, :, ts], writes=[hht[u]])
            for dvh in range(2):
                q = (2 * n + dvh) % 2
                k.op("act", lambda e: e.activation(out=sq[q][:, :tl], in_=hh_[u][:, dvh, :tl], func=AF.Square),
                     reads=[hht[u]], writes=[sqt[q]])
                k.op("pe", lambda e: e.matmul(ps[u][:, :tl], lhsT=C.ones_f[:], rhs=sq[q][:, :tl], start=(dvh == 0), stop=(dvh == 1)),
                     reads=[sqt[q], C.t_const], writes=[pst[u]])
            k.op("act", lambda e: e.activation(out=rs[u][:, :tl], in_=ps[u][:, :tl], func=AF.Sqrt, bias=C.eps_col[:, 0:1],
                                               scale=1.0 / MDV), reads=[pst[u], C.t_const], writes=[rst[u]])
            k.op("dve", lambda e: e.reciprocal(out=rs[u][:, :tl], in_=rs[u][:, :tl]), reads=[rst[u]], writes=[rst[u]])
            for dvh in range(2):
                c = 2 * hh + dvh
                k.op("dve", lambda e: e.scalar_tensor_tensor(out=so[u][:, dvh, :tl], in0=so[u][:, dvh, :tl], scalar=gn[:, c:c + 1],
                                                             in1=rs[u][:, :tl], op0=ALU.mult, op1=ALU.mult),
                     reads=[sot[u], rst[u], t_c], writes=[sot[u]])
                k.op("dve", lambda e: e.tensor_tensor(out=HN[:, c, ts], in0=so[u][:, dvh, :tl], in1=hh_[u][:, dvh, :tl], op=ALU.mult),
                     reads=[sot[u], hht[u]], writes=[HNT[bi]])
    emit_outproj(k, C, st, HN, HNT, wout_dram, ctx_out, ps[2:4], pst[2:4])


def emit_outproj(k, C, st, IN, INT, w_dram, ctx_out, ps, pst):
    X = C.X
    NW = 2
    wo = [st.sb("op_w%d" % i, [128, NCH, 256], BF16) for i in range(NW)]
    wot = [DT("op_w%d" % i) for i in range(NW)]
    w_v = w_dram.rearrange("(c p) n -> p c n", p=128)
    ny = 0
    for dq in range(D // 256):
        b = dq % NW
        k.dma("pool", wo[b][:], w_v[:, :, dq * 256:(dq + 1) * 256], writes=[wot[b]])
        for dl in range(2):
            dc = dq * 2 + dl
            for bi, (t0, tl, v) in enumerate(TBS):
                if bi == 0 and not ctx_out:
                    continue
                ts = slice(t0, t0 + tl)
                p = ny % 2
                ny += 1
                for c in range(NCH):
                    k.op("pe", lambda e: e.matmul(ps[p][:, :tl], lhsT=wo[b][:, c, dl * 128:(dl + 1) * 128], rhs=IN[:, c, ts],
                                                  start=(c == 0), stop=(c == NCH - 1)),
                         reads=[wot[b], INT[bi]], writes=[pst[p]], inc=(c == NCH - 1))
                k.op("dve", lambda e: e.scalar_tensor_tensor(out=X[:, dc, ts], in0=ps[p][:, :tl], scalar=C.G[:, v, 1, dc:dc + 1],
                                                             in1=X[:, dc, ts], op0=ALU.mult, op1=ALU.add),
                     reads=[pst[p], C.t_mv, C.XT[bi]], writes=[C.XT[bi]])


def mlstm_consts():
    kk = np.arange(128)[:, None]
    tt = np.arange(128)[None, :]
    tri = np.stack([(kk <= tt), (kk >= tt)]).astype(np.float32)
    madd = ((tri - 1.0) * 1.0e4).astype(np.float32)
    return tri, madd


S5P = 64
S5T = TOK


def _eslice(seg, a, tl):
    t0, n, rev = seg
    if not rev:
        return slice(a - t0, a - t0 + tl)
    hi = n - 1 - (a - t0)
    lo = hi - tl + 1
    return slice(hi, lo - 1 if lo > 0 else None, -1)


def emit_s5_phase(k, C, p, lam_dram, bd_dram, cT_dram, dsk_dram, state_in, state_out, y_in, y_out, Z, ZT):
    segs = [(0, TOK, False)] if p == 0 else [(0, NCTX, True), (NCTX, NLAT, True)]
    with Stage(k) as st:
        ps = [st.ps("s5ps%d" % i, [128, 512], F32) for i in range(8)]
        pst = [DT("s5ps%d" % i) for i in range(8)]
        rstd = st.sb("s_rstd", [128, TOK], F32)
        t_rs = [DT("s_rs%d" % i) for i in range(3)]
        with Stage(k) as stn:
            emit_rstd(k, C, stn, rstd, t_rs, ps[6:8], pst[6:8])
        Hd = [st.sb("s_hd%d" % i, [128, TOK], BF16) for i in range(2)]
        HdT = [DT("s_hd%d" % i) for i in range(2)]
        htmp = [st.sb("s_htmp%d" % i, [128, 512], F32) for i in range(2)]
        htmpt = [DT("s_htmp%d" % i) for i in range(2)]
        LM = st.sb("s_lam", [128, 3, S5P], F32)
        t_p = DT("s5prep")
        k.dma("sp", LM[:], lam_dram[p], writes=[t_p])
        W = st.sb("s_w", [128, 16, S5P], F32)
        LR, LI, DTT, MAG, TH, CS, SN, AR1, AI, DEN, KR, KI, T1, T2, NSN, T3 = [W[:, i, :] for i in range(16)]

        def dv(fn):
            k.op("dve", fn, reads=[t_p, C.t_const], writes=[t_p])

        def ac(fn):
            k.op("act", fn, reads=[t_p, C.t_const], writes=[t_p])
        dv(lambda e: e.tensor_scalar(out=LR, in0=LM[:, 0, :], scalar1=-1e-4, scalar2=None, op0=ALU.min))
        ac(lambda e: e.activation(out=DTT, in_=LM[:, 2, :], func=AF.Exp))
        dv(lambda e: e.tensor_tensor(out=T1, in0=LR, in1=DTT, op=ALU.mult))
        ac(lambda e: e.activation(out=MAG, in_=T1, func=AF.Exp))
        dv(lambda e: e.tensor_tensor(out=TH, in0=LM[:, 1, :], in1=DTT, op=ALU.mult))
        ac(lambda e: e.activation(out=SN, in_=TH, func=AF.Sin, scale=1.0 / 8))
        ac(lambda e: e.activation(out=T1, in_=TH, func=AF.Sin, scale=1.0 / 16))
        dv(lambda e: e.tensor_tensor(out=T1, in0=T1, in1=T1, op=ALU.mult))
        dv(lambda e: e.tensor_scalar(out=CS, in0=T1, scalar1=-2.0, scalar2=1.0, op0=ALU.mult, op1=ALU.add))
        for _ in range(3):
            dv(lambda e: e.tensor_tensor(out=T1, in0=CS, in1=CS, op=ALU.mult))
            dv(lambda e: e.tensor_tensor(out=T2, in0=SN, in1=SN, op=ALU.mult))
            dv(lambda e: e.tensor_tensor(out=T3, in0=CS, in1=SN, op=ALU.mult))
            dv(lambda e: e.tensor_tensor(out=CS, in0=T1, in1=T2, op=ALU.subtract))
            dv(lambda e: e.tensor_scalar(out=SN, in0=T3, scalar1=2.0, scalar2=None, op0=ALU.mult))
        dv(lambda e: e.tensor_tensor(out=AR1, in0=MAG, in1=CS, op=ALU.mult))
        dv(lambda e: e.tensor_scalar(out=AR1, in0=AR1, scalar1=-1.0, scalar2=None, op0=ALU.add))
        dv(lambda e: e.tensor_tensor(out=AI, in0=MAG, in1=SN, op=ALU.mult))
        dv(lambda e: e.tensor_tensor(out=T1, in0=LR, in1=LR, op=ALU.mult))
        dv(lambda e: e.tensor_tensor(out=T2, in0=LM[:, 1, :], in1=LM[:, 1, :], op=ALU.mult))
        dv(lambda e: e.tensor_tensor(out=DEN, in0=T1, in1=T2, op=ALU.add))
        dv(lambda e: e.reciprocal(out=DEN, in_=DEN))
        dv(lambda e: e.tensor_tensor(out=T1, in0=AR1, in1=LR, op=ALU.mult))
        dv(lambda e: e.tensor_tensor(out=T2, in0=AI, in1=LM[:, 1, :], op=ALU.mult))
        dv(lambda e: e.tensor_tensor(out=T1, in0=T1, in1=T2, op=ALU.add))
        dv(lambda e: e.tensor_tensor(out=KR, in0=T1, in1=DEN, op=ALU.mult))
        dv(lambda e: e.tensor_tensor(out=T1, in0=AI, in1=LR, op=ALU.mult))
        dv(lambda e: e.tensor_tensor(out=T2, in0=AR1, in1=LM[:, 1, :], op=ALU.mult))
        dv(lambda e: e.tensor_tensor(out=T1, in0=T1, in1=T2, op=ALU.subtract))
        dv(lambda e: e.tensor_tensor(out=KI, in0=T1, in1=DEN, op=ALU.mult))
        dv(lambda e: e.tensor_scalar(out=NSN, in0=SN, scalar1=-1.0, scalar2=None, op0=ALU.mult))
        NLV = 11
        UP = st.sb("s_up", [128, NLV, 3, S5P], F32)
        dv(lambda e: e.tensor_copy(out=UP[:, 0, 0, :], in_=CS))
        dv(lambda e: e.tensor_copy(out=UP[:, 0, 1, :], in_=SN))
        dv(lambda e: e.tensor_copy(out=UP[:, 0, 2, :], in_=NSN))
        for l in range(1, NLV):
            dv(lambda e: e.tensor_tensor(out=T1, in0=UP[:, l - 1, 0, :], in1=UP[:, l - 1, 0, :], op=ALU.mult))
            dv(lambda e: e.tensor_tensor(out=T2, in0=UP[:, l - 1, 1, :], in1=UP[:, l - 1, 1, :], op=ALU.mult))
            dv(lambda e: e.tensor_tensor(out=UP[:, l, 0, :], in0=T1, in1=T2, op=ALU.subtract))
            dv(lambda e: e.tensor_tensor(out=T3, in0=UP[:, l - 1, 0, :], in1=UP[:, l - 1, 1, :], op=ALU.mult))
            dv(lambda e: e.tensor_scalar(out=UP[:, l, 1, :], in0=T3, scalar1=2.0, scalar2=None, op0=ALU.mult))
            dv(lambda e: e.tensor_scalar(out=UP[:, l, 2, :], in0=T3, scalar1=-2.0, scalar2=None, op0=ALU.mult))
        CR = st.sb("s_cr", [128, S5P, 16], F32)
        CI = st.sb("s_ci", [128, S5P, 16], F32)
        with Stage(k) as stc:
            CT = stc.sb("s_cT", [128, 2, S5P, 16], F32)
            k.dma("sp", CT[:], cT_dram[p], writes=[t_p])
            CW = stc.sb("s_cw", [128, S5P, 16], F32)
            krb = W[:, 10:11, :].rearrange("p o s -> p s o").to_broadcast([128, S5P, 16])
            kib = W[:, 11:12, :].rearrange("p o s -> p s o").to_broadcast([128, S5P, 16])
            dv(lambda e: e.tensor_tensor(out=CR[:], in0=CT[:, 0, :, :], in1=krb, op=ALU.mult))
            dv(lambda e: e.tensor_tensor(out=CW[:], in0=CT[:, 1, :, :], in1=kib, op=ALU.mult))
            dv(lambda e: e.tensor_tensor(out=CR[:], in0=CR[:], in1=CW[:], op=ALU.subtract))
            dv(lambda e: e.tensor_tensor(out=CI[:], in0=CT[:, 0, :, :], in1=kib, op=ALU.mult))
            dv(lambda e: e.tensor_tensor(out=CW[:], in0=CT[:, 1, :, :], in1=krb, op=ALU.mult))
            dv(lambda e: e.tensor_tensor(out=CI[:], in0=CI[:], in1=CW[:], op=ALU.add))
            dv(lambda e: e.tensor_scalar(out=CI[:], in0=CI[:], scalar1=-1.0, scalar2=None, op0=ALU.mult))
        dsk = st.sb("s_dsk", [128, NCH], F32)
        k.dma("sp", dsk[:], dsk_dram, writes=[t_p])
        SI = None
        if p == 1:
            SI = st.sb("s_si", [128, S5P, 2], F32)
            k.dma("sp", SI[:], state_in, writes=[t_p])
        SO = st.sb("s_so", [128, S5P, 2], F32)
        t_so = DT("s_so")
        E2 = [st.sb("s_E%d" % i, [128, 2, TOK], F32) for i in range(2)]
        t_E2 = [DT("s_E%d" % i) for i in range(2)]
        MB2 = [st.sb("s_magb%d" % i, [128, TOK], F32) for i in range(1)] * 2
        t_mb2 = [DT("s_magb%d" % i) for i in range(1)] * 2
        BR = [st.sb("s_br%d" % i, [128, TOK], F32) for i in range(2)]
        BI = [st.sb("s_bi%d" % i, [128, TOK], F32) for i in range(2)]
        t_b = [DT("s_b%d" % i) for i in range(2)]
        XR = [st.sb("s_xr%d" % i, [128, TOK], BF16) for i in range(1)] * 2
        XI = [st.sb("s_xi%d" % i, [128, TOK], BF16) for i in range(1)] * 2
        t_x = [DT("s_x%d" % i) for i in range(1)] * 2
        sc = [st.sb("s_sc%d" % i, [128, 512], F32) for i in range(4)]
        sct = [DT("s_sc%d" % i) for i in range(4)]
        tsc = [st.sb("s_tsc%d" % i, [128, 640], F32) for i in range(2)] * 2
        tsct = [DT("s_tsc%d" % i) for i in range(2)] * 2
        ini = st.sb("s_ini", [128, 4], F32)
        t_ini = DT("s_ini")
        BD = [st.sb("s_bd%d" % i, [128, 8, 128], BF16) for i in range(2)]
        BDt = [DT("s_bd%d" % i) for i in range(2)]
        CD = [st.sb("s_cd%d" % i, [128, 8, 128], BF16) for i in range(2)]
        CDt = [DT("s_cd%d" % i) for i in range(2)]
        yb = [st.sb("s_yb%d" % i, [128, 512], F32) for i in range(2)]
        ybt = [DT("s_yb%d" % i) for i in range(2)]
        yi = [st.sb("s_yi%d" % i, [128, 512], F32) for i in range(2)]
        yit = [DT("s_yi%d" % i) for i in range(2)]
        zb = [st.sb("s_zb%d" % i, [128, 512], BF16) for i in range(2)]
        zbt = [DT("s_zb%d" % i) for i in range(2)]

        def table_gen(tn):
            nonlocal nsc
            E, t_E, MB_, t_mb = E2[tn % 2], t_E2[tn % 2], MB2[tn % 2], t_mb2[tn % 2]
            P_ = tn
            k.op("pool", lambda e: e.memset(E[:, 0, 0:1], 1.0), writes=[t_E])
            k.op("pool", lambda e: e.memset(E[:, 1, 0:1], 0.0), writes=[t_E])
            seg = 1
            l = 0
            emax = TOK if p == 0 else NLAT
            while seg < emax:
                n_ = min(seg, emax - seg)
                ur, ui, nui = UP[:, l, 0, P_:P_ + 1], UP[:, l, 1, P_:P_ + 1], UP[:, l, 2, P_:P_ + 1]
                k.op("act", lambda e: e.activation(out=tsc[0][:, :n_], in_=E[:, 0, 0:n_], func=AF.Identity, scale=ur),
                     reads=[t_E, t_p], writes=[tsct[0]])
                k.op("act", lambda e: e.activation(out=tsc[1][:, :n_], in_=E[:, 1, 0:n_], func=AF.Identity, scale=ur),
                     reads=[t_E, t_p], writes=[tsct[1]])
                yield
                k.op("dve", lambda e: e.scalar_tensor_tensor(out=E[:, 0, seg:seg + n_], in0=E[:, 1, 0:n_], scalar=nui, in1=tsc[0][:, :n_],
                                                             op0=ALU.mult, op1=ALU.add), reads=[t_E, t_p, tsct[0]], writes=[t_E])
                k.op("dve", lambda e: e.scalar_tensor_tensor(out=E[:, 1, seg:seg + n_], in0=E[:, 0, 0:n_], scalar=ui, in1=tsc[1][:, :n_],
                                                             op0=ALU.mult, op1=ALU.add), reads=[t_E, t_p, tsct[1]], writes=[t_E])
                seg *= 2
                l += 1

        nt = 0
        nsc = 0
        nyb = 0
        ntile = 0
        for dc in range(NCH):
            db = dc % 2
            emit_hchunk(k, C, 1, dc, rstd, t_rs, Hd[db], HdT[db], htmp, htmpt)
            k.dma("pool", BD[db][:], bd_dram[p, dc], writes=[BDt[db]])
            k.op("pool", lambda e: e.memset(CD[db][:], 0.0), writes=[CDt[db]])
            for j in range(4):
                P_ = 4 * dc + j
                for gp in range(2):
                    rows = slice(gp * 64, (gp + 1) * 64)
                    cols = slice((2 * j + gp) * 16, (2 * j + gp + 1) * 16)
                    k.op("act", lambda e: e.copy(out=CD[db][rows, j, cols], in_=CR[rows, P_, :]), reads=[t_p], writes=[CDt[db]])
                    k.op("act", lambda e: e.copy(out=CD[db][rows, 4 + j, cols], in_=CI[rows, P_, :]), reads=[t_p], writes=[CDt[db]])
            ypst = pst[4:7]
            for j in range(4):
                P_ = 4 * dc + j
                u = nt % 2
                nt += 1
                E, t_E, MB_, t_mb = E2[ntile % 2], t_E2[ntile % 2], MB2[ntile % 2], t_mb2[ntile % 2]
                if ntile == 0:
                    for _ in table_gen(0):
                        pass
                gen = table_gen(ntile + 1) if ntile + 1 < 4 * NCH else iter(())

                def step():
                    next(gen, None)
                ntile += 1
                k.op("act", lambda e: e.activation(out=MB_[:], in_=E[:, 0, :], func=AF.Identity, scale=0.0, bias=MAG[:, P_:P_ + 1]),
                     reads=[t_E, t_p], writes=[t_mb])
                for bi, (t0, tl, v) in enumerate(TBS):
                    ts = slice(t0, t0 + tl)
                    sg = segs[0] if (p == 0 or bi == 0) else segs[1]
                    es_ = _eslice(sg, t0, tl)
                    pu = (nsc % 2) * 2
                    k.op("pe", lambda e: e.matmul(ps[pu][:, :tl], lhsT=BD[db][:, j, :], rhs=Hd[db][:, ts], start=True, stop=True),
                         reads=[BDt[db], HdT[db]], writes=[pst[pu]])
                    k.op("pe", lambda e: e.matmul(ps[pu + 1][:, :tl], lhsT=BD[db][:, 4 + j, :], rhs=Hd[db][:, ts], start=True, stop=True),
                         reads=[BDt[db], HdT[db]], writes=[pst[pu + 1]])
                    a4, b4 = nsc % 4, (nsc + 1) % 4
                    nsc += 2
                    k.op("dve", lambda e: e.tensor_tensor(out=sc[a4][:, :tl], in0=ps[pu][:, :tl], in1=E[:, 0, es_], op=ALU.mult),
                         reads=[pst[pu], t_E], writes=[sct[a4]])
                    k.op("dve", lambda e: e.tensor_tensor(out=sc[b4][:, :tl], in0=ps[pu + 1][:, :tl], in1=E[:, 1, es_], op=ALU.mult),
                         reads=[pst[pu + 1], t_E], writes=[sct[b4]])
                    k.op("dve", lambda e: e.tensor_tensor(out=BR[u][:, ts], in0=sc[a4][:, :tl], in1=sc[b4][:, :tl], op=ALU.add),
                         reads=[sct[a4], sct[b4]], writes=[t_b[u]])
                    step()
                    a4, b4 = nsc % 4, (nsc + 1) % 4
                    nsc += 2
                    k.op("dve", lambda e: e.tensor_tensor(out=sc[a4][:, :tl], in0=ps[pu + 1][:, :tl], in1=E[:, 0, es_], op=ALU.mult),
                         reads=[pst[pu + 1], t_E], writes=[sct[a4]])
                    k.op("dve", lambda e: e.tensor_tensor(out=sc[b4][:, :tl], in0=ps[pu][:, :tl], in1=E[:, 1, es_], op=ALU.mult),
                         reads=[pst[pu], t_E], writes=[sct[b4]])
                    k.op("dve", lambda e: e.tensor_tensor(out=BI[u][:, ts], in0=sc[a4][:, :tl], in1=sc[b4][:, :tl], op=ALU.subtract),
                         reads=[sct[a4], sct[b4]], writes=[t_b[u]])
                    step()
                for si, sg in enumerate(segs):
                    t0, n_, rev = sg
                    vs = slice(t0 + n_ - 1, t0 - 1 if t0 > 0 else None, -1) if rev else slice(t0, t0 + n_)
                    if p == 1 and si == 1:
                        x0r, x0i = SI[:, P_, 0:1], SI[:, P_, 1:2]
                        ur, ui, nui = UP[:, 0, 0, P_:P_ + 1], UP[:, 0, 1, P_:P_ + 1], UP[:, 0, 2, P_:P_ + 1]
                        k.op("dve", lambda e: e.tensor_tensor(out=ini[:, 2:3], in0=x0r, in1=ur, op=ALU.mult), reads=[t_p], writes=[t_ini])
                        k.op("dve", lambda e: e.scalar_tensor_tensor(out=ini[:, 0:1], in0=x0i, scalar=nui, in1=ini[:, 2:3], op0=ALU.mult, op1=ALU.add),
                             reads=[t_p, t_ini], writes=[t_ini])
                        k.op("dve", lambda e: e.tensor_tensor(out=ini[:, 3:4], in0=x0i, in1=ur, op=ALU.mult), reads=[t_p, t_ini], writes=[t_ini])
                        k.op("dve", lambda e: e.scalar_tensor_tensor(out=ini[:, 1:2], in0=x0r, scalar=ui, in1=ini[:, 3:4], op0=ALU.mult, op1=ALU.add),
                             reads=[t_p, t_ini], writes=[t_ini])
                        i_r, i_i = ini[:, 0:1], ini[:, 1:2]
                    else:
                        i_r, i_i = 0.0, 0.0
                    k.op("dve", lambda e: e.tensor_tensor_scan(out=BR[u][:, vs], data0=MB_[:, 0:n_], data1=BR[u][:, vs], initial=i_r,
                                                               op0=ALU.mult, op1=ALU.add), reads=[t_mb, t_ini], writes=[t_b[u]])
                    k.op("dve", lambda e: e.tensor_tensor_scan(out=BI[u][:, vs], data0=MB_[:, 0:n_], data1=BI[u][:, vs], initial=i_i,
                                                               op0=ALU.mult, op1=ALU.add), reads=[t_mb, t_ini], writes=[t_b[u]])
                    step()
                for bi, (t0, tl, v) in enumerate(TBS):
                    ts = slice(t0, t0 + tl)
                    sg = segs[0] if (p == 0 or bi == 0) else segs[1]
                    es_ = _eslice(sg, t0, tl)
                    a4, b4 = nsc % 4, (nsc + 1) % 4
                    nsc += 2
                    k.op("dve", lambda e: e.tensor_tensor(out=sc[a4][:, :tl], in0=BR[u][:, ts], in1=E[:, 0, es_], op=ALU.mult),
                         reads=[t_b[u], t_E], writes=[sct[a4]])
                    k.op("dve", lambda e: e.tensor_tensor(out=sc[b4][:, :tl], in0=BI[u][:, ts], in1=E[:, 1, es_], op=ALU.mult),
                         reads=[t_b[u], t_E], writes=[sct[b4]])
                    k.op("dve", lambda e: e.tensor_tensor(out=XR[u][:, ts], in0=sc[a4][:, :tl], in1=sc[b4][:, :tl], op=ALU.subtract),
                         reads=[sct[a4], sct[b4]], writes=[t_x[u]])
                    step()
                    if p == 0 and bi == 2:
                        k.op("act", lambda e: e.activation(out=ini[:, 2:3], in_=sc[a4][:, tl - 1:tl], func=AF.Identity,
                                                           bias=sc[b4][:, tl - 1:tl], scale=1.0), reads=[sct[a4], sct[b4]], writes=[t_ini])
                        k.op("dve", lambda e: e.tensor_scalar(out=SO[:, P_, 0:1], in0=sc[b4][:, tl - 1:tl], scalar1=-2.0, scalar2=ini[:, 2:3],
                                                              op0=ALU.mult, op1=ALU.add), reads=[sct[b4], t_ini], writes=[t_so])
                    a4, b4 = nsc % 4, (nsc + 1) % 4
                    nsc += 2
                    k.op("dve", lambda e: e.tensor_tensor(out=sc[a4][:, :tl], in0=BI[u][:, ts], in1=E[:, 0, es_], op=ALU.mult),
                         reads=[t_b[u], t_E], writes=[sct[a4]])
                    k.op("dve", lambda e: e.tensor_tensor(out=sc[b4][:, :tl], in0=BR[u][:, ts], in1=E[:, 1, es_], op=ALU.mult),
                         reads=[t_b[u], t_E], writes=[sct[b4]])
                    k.op("dve", lambda e: e.tensor_tensor(out=XI[u][:, ts], in0=sc[a4][:, :tl], in1=sc[b4][:, :tl], op=ALU.add),
                         reads=[sct[a4], sct[b4]], writes=[t_x[u]])
                    step()
                    if p == 0 and bi == 2:
                        k.op("dve", lambda e: e.tensor_tensor(out=SO[:, P_, 1:2], in0=sc[a4][:, tl - 1:tl], in1=sc[b4][:, tl - 1:tl], op=ALU.add),
                             reads=[sct[a4], sct[b4]], writes=[t_so])
                    yp = ps[4 + bi]
                    k.op("pe", lambda e: e.matmul(yp[:, :tl], lhsT=CD[db][:, j, :], rhs=XR[u][:, ts], start=(j == 0), stop=False),
                         reads=[CDt[db], t_x[u]], writes=[ypst[bi]], inc=False)
                    k.op("pe", lambda e: e.matmul(yp[:, :tl], lhsT=CD[db][:, 4 + j, :], rhs=XI[u][:, ts], start=False, stop=(j == 3)),
                         reads=[CDt[db], t_x[u]], writes=[ypst[bi]], inc=True)
                for _ in gen:
                    pass
            for bi, (t0, tl, v) in enumerate(TBS):
                ts = slice(t0, t0 + tl)
                q = nyb % 2
                nyb += 1
                if p == 0:
                    k.op("dve", lambda e: e.scalar_tensor_tensor(out=yb[q][:, :tl], in0=Hd[db][:, ts], scalar=dsk[:, dc:dc + 1], in1=ps[4 + bi][:, :tl],
                                                                 op0=ALU.mult, op1=ALU.add), reads=[HdT[db], t_p, ypst[bi]], writes=[ybt[q]])
                    k.dma("sp", y_out[dc, :, ts], yb[q][:, :tl], reads=[ybt[q]], final=True)
                else:
                    k.dma("sp", yi[q][:, :tl], y_in[dc, :, ts], writes=[yit[q]])
                    k.op("dve", lambda e: e.tensor_tensor(out=yb[q][:, :tl], in0=ps[4 + bi][:, :tl], in1=yi[q][:, :tl], op=ALU.add),
                         reads=[ypst[bi], yit[q]], writes=[ybt[q]])
                    k.op("dve", lambda e: e.tensor_tensor(out=yi[q][:, :tl], in0=yb[q][:, :tl], in1=yb[q][:, :tl], op=ALU.mult),
                         reads=[ybt[q]], writes=[yit[q]])
                    k.op("dve", lambda e: e.tensor_scalar(out=yi[q][:, :tl], in0=yi[q][:, :tl], scalar1=0.044715, scalar2=1.0, op0=ALU.mult, op1=ALU.add),
                         reads=[], writes=[yit[q]])
                    k.op("dve", lambda e: e.tensor_tensor(out=yi[q][:, :tl], in0=yi[q][:, :tl], in1=yb[q][:, :tl], op=ALU.mult),
                         reads=[ybt[q]], writes=[yit[q]])
                    k.op("act", lambda e: e.activation(out=yi[q][:, :tl], in_=yi[q][:, :tl], func=AF.Tanh, scale=0.7978845608028654),
                         reads=[], writes=[yit[q]])
                    k.op("act", lambda e: e.mul(out=yb[q][:, :tl], in_=yb[q][:, :tl], mul=0.5),
                         reads=[], writes=[ybt[q]])
                    k.op("dve", lambda e: e.scalar_tensor_tensor(out=zb[q][:, :tl], in0=yi[q][:, :tl], scalar=1.0, in1=yb[q][:, :tl],
                                                                 op0=ALU.add, op1=ALU.mult), reads=[yit[q], ybt[q]], writes=[zbt[q]])
                    k.dma("sp", Z[dc, :, ts], zb[q][:, :tl], reads=[zbt[q]], final=True)
        if p == 0:
            k.dma("sp", state_out, SO[:], reads=[t_so], final=True)


def emit_s5_glu(k, C, st, Z, ZT, wglu_dram):
    X = C.X
    NW = 3
    wt = [st.sb("g_w%d" % i, [128, NCH, 256], BF16) for i in range(NW)]
    wtt = [DT("g_w%d" % i) for i in range(NW)]
    sg = [st.sb("g_sg%d" % i, [128, 512], F32) for i in range(2)]
    sgt = [DT("g_sg%d" % i) for i in range(2)]
    ps = [st.ps("gps%d" % i, [128, 512], F32) for i in range(4)]
    pst = [DT("gps%d" % i) for i in range(4)]
    w_v = wglu_dram.rearrange("(c p) n -> p c n", p=128)
    nu = 0
    for dc in range(NCH):
        b = dc % NW
        k.dma("pool", wt[b][:, :, 0:128], w_v[:, :, dc * 128:(dc + 1) * 128], writes=[wtt[b]])
        k.dma("pool", wt[b][:, :, 128:256], w_v[:, :, D + dc * 128:D + (dc + 1) * 128], writes=[wtt[b]])
        for bi, (t0, tl, v) in enumerate(TBS):
            ts = slice(t0, t0 + tl)
            u = nu % 2
            nu += 1
            pa, pg = ps[2 * u], ps[2 * u + 1]
            for c in range(NCH):
                k.op("pe", lambda e: e.matmul(pa[:, :tl], lhsT=wt[b][:, c, 0:128], rhs=Z[:, c, ts], start=(c == 0), stop=(c == NCH - 1)),
                     reads=[wtt[b], ZT[bi]], writes=[pst[2 * u]], inc=(c == NCH - 1))
            for c in range(NCH):
                k.op("pe", lambda e: e.matmul(pg[:, :tl], lhsT=wt[b][:, c, 128:256], rhs=Z[:, c, ts], start=(c == 0), stop=(c == NCH - 1)),
                     reads=[wtt[b], ZT[bi]], writes=[pst[2 * u + 1]], inc=(c == NCH - 1))
            k.op("act", lambda e: e.activation(out=sg[u][:, :tl], in_=pg[:, :tl], func=AF.Sigmoid), reads=[pst[2 * u + 1]], writes=[sgt[u]])
            k.op("dve", lambda e: e.tensor_tensor(out=sg[u][:, :tl], in0=pa[:, :tl], in1=sg[u][:, :tl], op=ALU.mult),
                 reads=[pst[2 * u], sgt[u]], writes=[sgt[u]])
            k.op("dve", lambda e: e.scalar_tensor_tensor(out=X[:, dc, ts], in0=sg[u][:, :tl], scalar=C.G[:, v, 1, dc:dc + 1], in1=X[:, dc, ts],
                                                          op0=ALU.mult, op1=ALU.add), reads=[sgt[u], C.t_mv, C.XT[bi]], writes=[C.XT[bi]])


def s5_host_arrays(I, half):
    occ = 0
    dirs = [0, 1] if half == 0 else [1, 0]
    lam = np.zeros((2, 128, 3, S5P), np.float32)
    bd = np.zeros((2, NCH, 128, 8, 128), np.float32)
    cT = np.zeros((2, 128, 2, S5P, 16), np.float32)
    for s, dr in enumerate(dirs):
        for nm, idx in (("s5_lam_re", 0), ("s5_lam_im", 1)):
            a = I[nm][occ, dr].reshape(S5P, 2, 64)
            lam[s, :, idx, :] = a.transpose(1, 2, 0).reshape(128, S5P)
        ld = np.broadcast_to(I["s5_log_dt"][occ, dr].reshape(S5P, 2, 1), (S5P, 2, 64))
        lam[s, :, 2, :] = ld.transpose(1, 2, 0).reshape(128, S5P)
        for ri, nm in enumerate(("s5_b_re", "s5_b_im")):
            B = I[nm][occ, dr]
            for dc in range(NCH):
                for j in range(4):
                    for gp in range(2):
                        g = 8 * dc + 2 * j + gp
                        gl = 2 * j + gp
                        bd[s, dc, gl * 16:(gl + 1) * 16, ri * 4 + j, gp * 64:(gp + 1) * 64] = B[g].T
        for ri, nm in enumerate(("s5_c_re", "s5_c_im")):
            Cc = I[nm][occ, dr].reshape(S5P, 2, 16, 64)
            cT[s, :, ri, :, :] = Cc.transpose(1, 3, 0, 2).reshape(128, S5P, 16)
    return lam, bd, cT


NVC = 2
NACT = 4


class FProg:
    def __init__(self):
        self.nc = bass.Bass("TRN2", target_bir_lowering=False)
        self.ins = {}
        self.scr = {}
        self.outs = {}

    def inp(self, name, shape, dt=F32):
        if name in self.scr:
            return self.scr[name]
        if name not in self.ins:
            self.ins[name] = self.nc.dram_tensor(name, list(shape), dt, kind="ExternalInput").ap()
        return self.ins[name]

    def tmp(self, name, shape, dt=F32):
        if name not in self.scr:
            self.scr[name] = self.nc.dram_tensor(name, list(shape), dt).ap()
        return self.scr[name]

    def out(self, name, shape, dt=F32):
        if name not in self.outs:
            self.outs[name] = self.nc.dram_tensor(name, list(shape), dt, kind="ExternalOutput").ap()
        return self.outs[name]


SEGMENTS = [
    [("modvec", 0), ("ffn", 0, 0), ("mix1", 0)],
    [("mix2", 0), ("ffn", 0, 1), ("modvec", 1), ("ffn", 1, 0), ("mix1", 1)],
    [("mix2", 1), ("ffn", 1, 1), ("modvec", 2), ("ffn", 2, 0), ("mix1", 2)],
    [("mix2", 2), ("ffn", 2, 1), ("modvec", 3), ("ffn", 3, 0), ("mix1", 3)],
    [("mix2", 3), ("ffn", 3, 1)],
]


def set_layer(C, i):
    m = C.mvsets[i % 2]
    C.MV, C.MB, C.NG, C.A, C.G, C.t_mv = m


def emit_step(k, C, P, step, v):
    op = step[0]
    sf = "_v%d" % v
    so = "_v%d" % (1 - v)
    if op == "modvec":
        i = step[1]
        if v != 0:
            return
        set_layer(C, i)
        with Stage(k) as st:
            emit_modvec(k, C, st, P.inp("condT", [128, 2, NCH]), P.inp("modw%d" % i, [D, 9 * D]),
                        P.inp("modb%d" % i, [128, 9 * NCH]), P.inp("ng%d" % i, [128, 3, NCH]), P.inp("eye2", [2, 2]))
    elif op == "ffn":
        i, j = step[1], step[2]
        set_layer(C, i)
        emit_ffn(k, C, 0 if j == 0 else 2, P.inp("wi%d_%d" % (i, j), [D, 2 * DFF]),
                 P.inp("wo%d_%d" % (i, j), [DFF, D]), lat_only=(i == 3 and j == 1))
    elif op == "mix1":
        i = step[1]
        set_layer(C, i)
        kind = KINDS[i % 4]
        if kind in ("a", "w"):
            emit_qkv(k, C, P.inp("wqkv%d" % i, [D, QKV_W]), P.inp("qkg%d" % i, [128, 2]),
                     P.inp("cos" + sf, [128, NLAT]), P.inp("sin" + sf, [128, NLAT]), P.inp("rm", [128, 128]),
                     P.tmp("qT%d" % i + sf, [NH, 128, TOK], BF16), P.tmp("kT%d" % i + sf, [NKV, 128, TOK], BF16),
                     P.tmp("v%d" % i + sf, [TOK, 512], BF16))
        elif kind == "m":
            mq = P.tmp("mq%d" % i + sf, [MH, 128, TOK], BF16)
            mk_ = P.tmp("mk%d" % i + sf, [MH, 128, TOK], BF16)
            mkt = P.tmp("mkt%d" % i + sf, [TOK, MH * MDQK], BF16)
            mvt = P.tmp("mvt%d" % i + sf, [TOK, MH * MDV], BF16)
            mso = P.tmp("mso%d" % i + sf, [NCH, 128, TOK], F32)
            mgi = P.tmp("mgi%d" % i + sf, [TOK, 32], F32)
            emit_mlstm_proj(k, C, P.inp("mwin%d" % i, [D, 6144]), P.inp("mwg%d" % i + sf, [D, 32]),
                            P.inp("mbg%d" % i + sf, [128, 32]), mq, mk_, mkt, mvt, mso, mgi)
            with Stage(k) as st:
                emit_mlstm_scan(k, C, st, 0, mq, mk_, mkt, mvt, mgi, P.inp("mtri", [2, 128, 128]),
                                P.inp("mmadd", [2, 128, 128]), None,
                                (P.tmp("mstC%d" % i + sf, [MH, 128, MDV]), P.tmp("mstN%d" % i + sf, [MH, 128, 128])),
                                None, P.tmp("mh1_%d" % i + sf, [NCH, 128, TOK]))
        elif kind == "s":
            emit_s5_phase(k, C, 0, P.inp("s5lam" + sf, [2, 128, 3, S5P]), P.inp("s5bd" + sf, [2, NCH, 128, 8, 128]),
                          P.inp("s5cT" + sf, [2, 128, 2, S5P, 16]), P.inp("s5dsk", [128, NCH]), None,
                          P.tmp("s5st%d" % i + sf, [128, S5P, 2]), None, P.tmp("s5y1_%d" % i + sf, [NCH, 128, TOK]), None, None)
    elif op == "mix2":
        i = step[1]
        set_layer(C, i)
        kind = KINDS[i % 4]
        if kind in ("a", "w"):
            kts = [P.scr["kT%d_v%d" % (i, q)] for q in range(NVC)]
            vs = [P.scr["v%d_v%d" % (i, q)] for q in range(NVC)]
            other = (kts, vs) if kind == "a" else (kts[1 - v], vs[1 - v])
            emit_attn(k, C, kind == "w", i != 3, P.scr["qT%d" % i + sf], kts[v], vs[v], other,
                      P.inp("esink%d" % i, [128, NH]) if kind == "w" else None,
                      P.inp("masks", [128, 4, 128], BF16) if kind == "w" else None,
                      P.inp("awo%d" % i, [D, D]))
        elif kind == "m":
            hs = P.tmp("mhs%d" % i + sf, [NCH, 128, TOK])
            with Stage(k) as st:
                emit_mlstm_scan(k, C, st, 1, P.scr["mq%d" % i + sf], P.scr["mk%d" % i + sf], P.scr["mkt%d" % i + sf],
                                P.scr["mvt%d" % i + sf], P.scr["mgi%d" % i + sf],
                                P.inp("mtri", [2, 128, 128]), P.inp("mmadd", [2, 128, 128]),
                                (P.scr["mstC%d" % i + so], P.scr["mstN%d" % i + so]),
                                None, P.scr["mh1_%d" % i + sf], hs)
            with Stage(k) as st:
                emit_mlstm_readout(k, C, st, hs, P.scr["mso%d" % i + sf],
                                   P.inp("mng%d" % i, [128, NCH]), P.inp("mwout%d" % i, [D, D]), True)
        elif kind == "s":
            zd = P.tmp("s5z%d" % i + sf, [NCH, 128, TOK], BF16)
            emit_s5_phase(k, C, 1, P.inp("s5lam" + sf, [2, 128, 3, S5P]), P.inp("s5bd" + sf, [2, NCH, 128, 8, 128]),
                          P.inp("s5cT" + sf, [2, 128, 2, S5P, 16]), P.inp("s5dsk", [128, NCH]),
                          P.scr["s5st%d" % i + so], None, P.scr["s5y1_%d" % i + sf], None, zd, None)
            with Stage(k) as st:
                Z = st.sb("s_z", [128, NCH, TOK], BF16)
                ZT = [DT("z%d" % q) for q in range(3)]
                for q, (t0, tl, vv) in enumerate(TBS):
                    k.dma("sp", Z[:, :, t0:t0 + tl], zd[:, :, t0:t0 + tl].rearrange("c p t -> p c t"), writes=[ZT[q]])
                emit_s5_glu(k, C, st, Z, ZT, P.inp("s5wglu", [D, 2 * D]))
    else:
        raise ValueError(op)


def build_fused(segments=None, nvc=NVC):
    segments = SEGMENTS if segments is None else segments
    P = FProg()
    nc = P.nc
    with ExitStack() as es:
        k = KB(nc, es)
        C = Ctx()
        with Stage(k) as st0:
            C.X = st0.sb("X", [128, NCH, TOK], F32)
            C.XT = [DT("x%d" % i) for i in range(3)]
            C.mvsets = []
            for q in range(2):
                C.mvsets.append((st0.sb("MV%d" % q, [128, 2, 9 * NCH], F32), st0.sb("MB%d" % q, [128, 9 * NCH], F32),
                                 st0.sb("NG%d" % q, [128, 3, NCH], F32), st0.sb("A%d" % q, [128, 2, 3, NCH], F32),
                                 st0.sb("G%d" % q, [128, 2, 3, NCH], F32), DT("mv%d" % q)))
            setup_consts(k, st0, C)
            for si, seg in enumerate(segments):
                last = si == len(segments) - 1
                for v in range(nvc):
                    sf = "_v%d" % v
                    src = P.inp("xT" + sf, [D, TOK]) if si == 0 else P.scr["xpark" + sf]
                    load_x(k, C, src)
                    for step in seg:
                        emit_step(k, C, P, step, v)
                    if last:
                        store_x(k, C, P.out("xo" + sf, [D, NLAT]), True)
                    else:
                        store_x(k, C, P.tmp("xpark" + sf, [D, TOK]), False)
                    k.barrier()
            k.finish()
        P.ninstr = k.ninstr
    return P


def host_inputs(I, b, names):
    out = {}
    s5 = None
    for name in names:
        v = None
        base = name
        if len(name) > 3 and name[-3:-1] == "_v":
            v = int(name[-1])
            base = name[:-3]
        if base == "xT":
            out[name] = core_tokens_T(I["x"], I["ctx"], 2 * b + v)
        elif base == "condT":
            out[name] = vec_pm(np.stack([I["c"][b], I["c_ctx"]]))
        elif base == "rm":
            out[name] = rot_matrix()
        elif base == "eye2":
            out[name] = np.eye(2, dtype=np.float32)
        elif base == "masks":
            out[name] = win_masks()
        elif base == "cos":
            out[name] = rope_tables(v)[0]
        elif base == "sin":
            out[name] = rope_tables(v)[1]
        elif base == "mtri":
            out[name] = mlstm_consts()[0]
        elif base == "mmadd":
            out[name] = mlstm_consts()[1]
        elif base in ("s5lam", "s5bd", "s5cT"):
            if s5 is None:
                s5 = [s5_host_arrays(I, h) for h in range(2)]
            out[name] = s5[v][("s5lam", "s5bd", "s5cT").index(base)]
        elif base == "s5dsk":
            out[name] = vec_pm(I["s5_d"][0])
        elif base == "s5wglu":
            out[name] = I["s5_w_glu"][0]
        else:
            i = int(base[-3]) if base[-2] == "_" else int(base[-1])
            bb = base[:-3] if base[-2] == "_" else base[:-1]
            pre = "a_" if i % 4 == 0 else "w_"
            if bb == "modw":
                out[name] = I["mod_w"][i]
            elif bb == "modb":
                out[name] = vec_pm(I["mod_b"][i])
            elif bb == "ng":
                out[name] = vec_pm(I["norm_g"][i])
            elif bb == "wi":
                out[name] = I["ffn_wi"][i, int(base[-1])]
            elif bb == "wo":
                out[name] = I["ffn_wo"][i, int(base[-1])]
            elif bb == "wqkv":
                out[name] = I[pre + "wqkv"][0]
            elif bb == "qkg":
                out[name] = np.ascontiguousarray(I[pre + "qk_g"][0].T)
            elif bb == "awo":
                out[name] = I[pre + "wo"][0]
            elif bb == "esink":
                out[name] = np.ascontiguousarray(np.broadcast_to(I["w_sink"][0][None, :], (128, NH)))
            elif bb == "mwin":
                out[name] = I["m_w_in"][0]
            elif bb in ("mwg", "mbg"):
                perm = np.arange(32) if v == 0 else np.concatenate([np.arange(16, 32), np.arange(0, 16)])
                if bb == "mwg":
                    out[name] = np.ascontiguousarray(I["m_w_gate"][0][:, perm])
                else:
                    out[name] = np.ascontiguousarray(np.broadcast_to(I["m_b_gate"][0][perm][None, :], (128, 32)))
            elif bb == "mng":
                out[name] = vec_pm(I["m_norm_g"][0])
            elif bb == "mwout":
                out[name] = I["m_w_out"][0]
            else:
                raise KeyError(name)
    return out


def kernel(**inputs):
    I = {k_: np.asarray(v) for k_, v in inputs.items()}
    P = build_fused()
    names = list(P.ins)
    in_maps = [host_inputs(I, b, names) for b in range(NACT)]
    res = run_bass_kernel_spmd(P.nc, in_maps, core_ids=list(range(NACT)))
    B = I["x"].shape[0]
    out = np.empty((B, SEQ, D), np.float32)
    for b in range(NACT):
        for v in range(NVC):
            y = res.results[b]["xo_v%d" % v].T
            if v == 1:
                y = y[::-1]
            out[b, v * NLAT:(v + 1) * NLAT] = y
    return out
```
